# Optimizing a Trainium2 kernel written in Bass

```python
import math
import jax, jax.numpy as jnp
from jax import lax
import numpy as np

D_MODEL = 2048
BATCH = 2
SEQ = 4096
DEPTH = 4
DEC_BATCH = 32
DEC_SEQ = 8
PAST_LEN = 16384
PAGE_SIZE = 128

NORM_EPS = 1e-6
N_MIXERS = 3
EXPAND = 2
D_INNER = EXPAND * D_MODEL
N_A = (DEPTH + 2) // N_MIXERS
N_B = (DEPTH + 1) // N_MIXERS
N_C = DEPTH // N_MIXERS

A_HEAD_DIM = 64
A_HEADS = D_INNER // A_HEAD_DIM
A_KV_HEADS = 8
A_GROUP = A_HEADS // A_KV_HEADS
A_Q = A_HEADS * A_HEAD_DIM
A_KV = A_KV_HEADS * A_HEAD_DIM
A_IN = A_Q + 2 * A_KV + D_INNER
A_SCALE = A_HEAD_DIM ** -0.5
WINDOW = 128
A_BLOCK = WINDOW
N_BUCKETS = 32
MAX_EXACT = N_BUCKETS // 2
MAX_DISTANCE = 128

B_HEAD = 64
B_HEADS = D_INNER // B_HEAD
B_LORA = max(32, int(round(1.8 * D_MODEL ** 0.5 / 32)) * 32)
B_LN_EPS = 64e-5
N_LERP = 6

C_HEADS = 8
C_KEY = D_INNER // 2
C_DK = C_KEY // C_HEADS
C_DV = D_INNER // C_HEADS
C_GATE_RANK = 16
C_GATE_NORM = 16.0
C_CHUNK = 64
C_IN = 2 * C_KEY + 2 * D_INNER + C_GATE_RANK

kernel_name = 'hybrid_swa_rwkv7_gla_decode_step'


def rmsnorm(x, g, eps=NORM_EPS):
    xf = x.astype(jnp.float32)
    y = xf * lax.rsqrt(jnp.mean(xf * xf, axis=-1, keepdims=True) + eps)
    return (y * g.astype(jnp.float32)).astype(x.dtype)


def t5_bucket(dist):
    d = jnp.maximum(dist, 0)
    large = MAX_EXACT + (jnp.log(jnp.maximum(d, 1).astype(jnp.float32) / MAX_EXACT)
                         / math.log(MAX_DISTANCE / MAX_EXACT) * (N_BUCKETS - MAX_EXACT)).astype(jnp.int32)
    return jnp.where(d < MAX_EXACT, d, jnp.minimum(large, N_BUCKETS - 1))


def sink_attention(q, k, v, bias, valid, sinks):
    s = jnp.einsum('bqkgd,bskd->bkgqs', q, k, preferred_element_type=jnp.float32) * A_SCALE
    b = bias.reshape(bias.shape[0], bias.shape[1], A_KV_HEADS, A_GROUP)
    s = s + jnp.transpose(b, (2, 3, 0, 1)).astype(jnp.float32)
    s = jnp.where(valid, s, -jnp.inf)
    sink = sinks.astype(jnp.float32).reshape(A_KV_HEADS, A_GROUP)[:, :, None, None]
    m = jnp.maximum(jnp.max(s, axis=-1, keepdims=True), sink)
    p = jnp.exp(s - m)
    p = p / (jnp.sum(p, axis=-1, keepdims=True) + jnp.exp(sink - m))
    return jnp.einsum('bkgqs,bskd->bqkgd', p.astype(v.dtype), v)


def banded_attention(q, k, v, rel_bias, sinks):
    B, T = q.shape[:2]
    nb = T // A_BLOCK

    def blocks(a):
        return jnp.moveaxis(a.reshape((B, nb, A_BLOCK) + a.shape[2:]), 1, 0)

    def with_prev(a):
        prev = jnp.concatenate([jnp.zeros_like(a[:, :A_BLOCK]), a[:, :T - A_BLOCK]], axis=1)
        return jnp.concatenate([blocks(prev), blocks(a)], axis=2)

    r = jnp.arange(A_BLOCK)[:, None]
    c = jnp.arange(2 * A_BLOCK)[None, :]
    dist = A_BLOCK + r - c
    band = (dist >= 0) & (dist < WINDOW)
    bias = rel_bias[t5_bucket(dist)]

    def one_block(args):
        i, qb, kb, vb = args
        valid = band & ((i > 0) | (c >= A_BLOCK))
        return sink_attention(qb, kb, vb, bias, valid, sinks)

    o = lax.map(one_block, (jnp.arange(nb), blocks(q), with_prev(k), with_prev(v)))
    return jnp.moveaxis(o, 0, 1).reshape(B, T, A_Q)


def attn_branch(h, w_in, q_g, k_g, sinks, w_out, rel_bias, cache_k=None, cache_v=None):
    B, T, _ = h.shape
    q, k, v, gate = jnp.split(h @ w_in, [A_Q, A_Q + A_KV, A_Q + 2 * A_KV], axis=-1)
    q = rmsnorm(q.reshape(B, T, A_KV_HEADS, A_GROUP, A_HEAD_DIM), q_g)
    k = rmsnorm(k.reshape(B, T, A_KV_HEADS, A_HEAD_DIM), k_g)
    v = v.reshape(B, T, A_KV_HEADS, A_HEAD_DIM)
    if cache_k is None:
        o = banded_attention(q, k, v, rel_bias, sinks)
        k_all, v_all = k, v
    else:
        k_all = jnp.concatenate([cache_k.astype(k.dtype), k], axis=1)
        v_all = jnp.concatenate([cache_v.astype(v.dtype), v], axis=1)
        q_pos = PAST_LEN + jnp.arange(T)
        k_pos = PAST_LEN - WINDOW + jnp.arange(WINDOW + T)
        dist = q_pos[:, None] - k_pos[None, :]
        valid = (dist >= 0) & (dist < WINDOW)
        o = sink_attention(q, k_all, v_all, rel_bias[t5_bucket(dist)], valid, sinks).reshape(B, T, A_Q)
    y = (o * jax.nn.silu(gate)) @ w_out
    return y, k_all[:, -WINDOW:], v_all[:, -WINDOW:]


def rwkv_scan(S0, r, decay, k, v, kk, a):
    def step(S, inp):
        r_t, w_t, k_t, v_t, kk_t, a_t = inp
        s_a = jnp.einsum('bhij,bhj->bhi', S, -kk_t)
        S = (S * w_t[:, :, None, :] + s_a[..., None] * (kk_t * a_t)[:, :, None, :]
             + v_t[..., None] * k_t[:, :, None, :])
        return S, jnp.einsum('bhij,bhj->bhi', S, r_t)
    S, o = lax.scan(step, S0, tuple(jnp.moveaxis(t, 1, 0) for t in (r, decay, k, v, kk, a)))
    return S, jnp.moveaxis(o, 0, 1)


def rwkv_branch(h, h_prev, S0, mu, w_rkvg, w_lora_down, w_lora_up, w0, a0, k_k, k_a, r_k,
                ln_g, ln_b, w_out):
    B, T, _ = h.shape
    f32 = jnp.float32
    xx = jnp.concatenate([h_prev[:, None].astype(h.dtype), h[:, :-1]], axis=1) - h
    xm = h[None] + xx[None] * mu[:, None, None, :].astype(h.dtype)
    r, k, v, g = jnp.einsum('cbtd,cde->cbte', xm[:4], w_rkvg)
    lw, la = jnp.einsum('cbtd,cdr->cbtr', xm[4:], w_lora_down)
    w_log = -jax.nn.softplus(-(w0 + jnp.tanh(lw) @ w_lora_up[0]).astype(f32)) - 0.5
    decay = jnp.exp(-jnp.exp(w_log))
    a = jax.nn.sigmoid((a0 + la @ w_lora_up[1]).astype(f32))
    r, k, v, decay, a = [t.astype(f32).reshape(B, T, B_HEADS, B_HEAD) for t in (r, k, v, decay, a)]
    kkr = k * k_k.astype(f32).reshape(B_HEADS, B_HEAD)
    kk = kkr / jnp.maximum(jnp.sqrt(jnp.sum(kkr * kkr, axis=-1, keepdims=True)), 1e-12)
    k = k * (1.0 + (a - 1.0) * k_a.astype(f32).reshape(B_HEADS, B_HEAD))
    S, o = rwkv_scan(S0.astype(f32), r, decay, k, v, kk, a)
    mean = jnp.mean(o, axis=-1, keepdims=True)
    var = jnp.mean(jnp.square(o - mean), axis=-1, keepdims=True)
    o = (o - mean) * lax.rsqrt(var + B_LN_EPS)
    o = o * ln_g.astype(f32).reshape(B_HEADS, B_HEAD) + ln_b.astype(f32).reshape(B_HEADS, B_HEAD)
    o = o + jnp.sum(r * k * r_k.astype(f32), axis=-1, keepdims=True) * v
    y = (o.reshape(B, T, D_INNER).astype(h.dtype) * jax.nn.silu(g)) @ w_out
    return y, S, h[:, -1]


def gla_chunk(S, inp):
    q, k, v, g = inp
    C = q.shape[2]
    b = jnp.cumsum(g, axis=2)
    causal = jnp.tril(jnp.ones((C, C), bool))
    o_inter = jnp.einsum('bhtd,bhde->bhte', q * jnp.exp(b), S)
    diff = jnp.where(causal[:, :, None], b[:, :, :, None, :] - b[:, :, None, :, :], -jnp.inf)
    att = jnp.einsum('bhtsd,bhsd->bhts', q[:, :, :, None, :] * jnp.exp(diff), k)
    o = o_inter + jnp.einsum('bhts,bhse->bhte', att, v)
    b_last = b[:, :, -1:, :]
    S = jnp.exp(b_last)[:, :, 0, :, None] * S + jnp.einsum('bhsd,bhse->bhde', k * jnp.exp(b_last - b), v)
    return S, o


def gla_branch(h, S0, w_in, w_gk_up, b_gk, o_g, w_out):
    B, T, _ = h.shape
    f32 = jnp.float32
    q, k, v, gate, gk_low = jnp.split(
        h @ w_in, [C_KEY, 2 * C_KEY, 2 * C_KEY + D_INNER, 2 * C_KEY + 2 * D_INNER], axis=-1)
    gk = jax.nn.log_sigmoid((gk_low @ w_gk_up + b_gk).astype(f32)) / C_GATE_NORM
    chunk = min(C_CHUNK, T)
    nc = T // chunk

    def to_chunks(t, dh):
        return t.astype(f32).reshape(B, nc, chunk, C_HEADS, dh).transpose(1, 0, 3, 2, 4)

    xs = (to_chunks(q, C_DK) * C_DK ** -0.5, to_chunks(k, C_DK), to_chunks(v, C_DV), to_chunks(gk, C_DK))
    S, o = lax.scan(gla_chunk, S0.astype(f32), xs)
    o = rmsnorm(o.transpose(1, 0, 3, 2, 4).reshape(B, T, C_HEADS, C_DV), o_g)
    y = (o.reshape(B, T, D_INNER).astype(h.dtype) * jax.nn.silu(gate)) @ w_out
    return y, S


def setup_inputs(seed: int = 0) -> dict:
    key = jax.random.key(seed)
    ks = iter(jax.random.split(key, 40))
    f32 = jnp.float32

    def nrm(shape, scale):
        return scale * jax.random.normal(next(ks), shape, f32)

    def gain(shape):
        return 1.0 + 0.05 * jax.random.normal(next(ks), shape, f32)

    return {
        'x_prompt': nrm((BATCH, SEQ, D_MODEL), 1.0),
        'x_sample': nrm((DEC_BATCH, DEC_SEQ, D_MODEL), 1.0),
        'cache_k_win': nrm((N_A, DEC_BATCH, WINDOW, A_KV_HEADS, A_HEAD_DIM), 1.0),
        'cache_v_win': nrm((N_A, DEC_BATCH, WINDOW, A_KV_HEADS, A_HEAD_DIM), 1.0),
        'state_wkv': nrm((N_B, DEC_BATCH, B_HEADS, B_HEAD, B_HEAD), 0.1),
        'state_shift': nrm((N_B, DEC_BATCH, D_MODEL), 1.0),
        'state_gla': nrm((N_C, DEC_BATCH, C_HEADS, C_DK, C_DV), 1.0),
        'norm_g': gain((DEPTH, D_MODEL)),
        'rel_bias': nrm((N_BUCKETS, A_HEADS), 0.5),
        'w_in_a': nrm((N_A, D_MODEL, A_IN), D_MODEL ** -0.5),
        'q_norm_g': gain((N_A, A_HEAD_DIM)),
        'k_norm_g': gain((N_A, A_HEAD_DIM)),
        'sinks': nrm((N_A, A_HEADS), 1.0),
        'w_out_a': nrm((N_A, D_INNER, D_MODEL), D_INNER ** -0.5),
        'mu_b': jax.random.uniform(next(ks), (N_B, N_LERP, D_MODEL), f32),
        'w_rkvg_b': nrm((N_B, 4, D_MODEL, D_INNER), D_MODEL ** -0.5),
        'w_lora_down_b': nrm((N_B, 2, D_MODEL, B_LORA), D_MODEL ** -0.5),
        'w_lora_up_b': nrm((N_B, 2, B_LORA, D_INNER), 0.3 * B_LORA ** -0.5),
        'w0_b': -1.0 + nrm((N_B, D_INNER), 0.5),
        'a0_b': nrm((N_B, D_INNER), 0.1),
        'k_k_b': 0.85 + nrm((N_B, D_INNER), 0.05),
        'k_a_b': gain((N_B, D_INNER)),
        'r_k_b': nrm((N_B, B_HEADS, B_HEAD), 0.1),
        'ln_x_g_b': gain((N_B, D_INNER)),
        'ln_x_b_b': nrm((N_B, D_INNER), 0.01),
        'w_out_b': nrm((N_B, D_INNER, D_MODEL), D_INNER ** -0.5),
        'w_in_c': nrm((N_C, D_MODEL, C_IN), D_MODEL ** -0.5),
        'w_gk_up_c': nrm((N_C, C_GATE_RANK, C_KEY), C_GATE_RANK ** -0.5),
        'b_gk_c': nrm((N_C, C_KEY), 0.1),
        'o_norm_g_c': gain((N_C, C_DV)),
        'w_out_c': nrm((N_C, D_INNER, D_MODEL), D_INNER ** -0.5),
    }


def reference(x_prompt, x_sample, cache_k_win, cache_v_win, state_wkv, state_shift, state_gla,
              norm_g, rel_bias, w_in_a, q_norm_g, k_norm_g, sinks, w_out_a,
              mu_b, w_rkvg_b, w_lora_down_b, w_lora_up_b, w0_b, a0_b, k_k_b, k_a_b, r_k_b,
              ln_x_g_b, ln_x_b_b, w_out_b, w_in_c, w_gk_up_c, b_gk_c, o_norm_g_c, w_out_c):
    xp, xs = x_prompt, x_sample
    kwp, vwp, kws, vws = [], [], [], []
    wkvp, shp, wkvs, shs = [], [], [], []
    glap, glas = [], []
    for layer in range(DEPTH):
        kind, j = layer % N_MIXERS, layer // N_MIXERS
        hp = rmsnorm(xp, norm_g[layer])
        hs = rmsnorm(xs, norm_g[layer])
        if kind == 0:
            wa = (w_in_a[j], q_norm_g[j], k_norm_g[j], sinks[j], w_out_a[j], rel_bias)
            yp, kp_, vp_ = attn_branch(hp, *wa)
            ys, ks_, vs_ = attn_branch(hs, *wa, cache_k_win[j], cache_v_win[j])
            kwp.append(kp_); vwp.append(vp_); kws.append(ks_); vws.append(vs_)
        elif kind == 1:
            wb = (mu_b[j], w_rkvg_b[j], w_lora_down_b[j], w_lora_up_b[j], w0_b[j], a0_b[j],
                  k_k_b[j], k_a_b[j], r_k_b[j], ln_x_g_b[j], ln_x_b_b[j], w_out_b[j])
            B = hp.shape[0]
            yp, Sp, lp = rwkv_branch(hp, jnp.zeros((B, D_MODEL), hp.dtype),
                                     jnp.zeros((B, B_HEADS, B_HEAD, B_HEAD), jnp.float32), *wb)
            ys, Ss, ls = rwkv_branch(hs, state_shift[j], state_wkv[j], *wb)
            wkvp.append(Sp); shp.append(lp); wkvs.append(Ss); shs.append(ls)
        else:
            wc = (w_in_c[j], w_gk_up_c[j], b_gk_c[j], o_norm_g_c[j], w_out_c[j])
            B = hp.shape[0]
            yp, Sp = gla_branch(hp, jnp.zeros((B, C_HEADS, C_DK, C_DV), jnp.float32), *wc)
            ys, Ss = gla_branch(hs, state_gla[j], *wc)
            glap.append(Sp); glas.append(Ss)
        xp = xp + yp.astype(xp.dtype)
        xs = xs + ys.astype(xs.dtype)
    return (xp, xs,
            jnp.stack(kwp), jnp.stack(vwp), jnp.stack(kws), jnp.stack(vws),
            jnp.stack(wkvp), jnp.stack(shp), jnp.stack(wkvs), jnp.stack(shs),
            jnp.stack(glap), jnp.stack(glas))
```

```python
import contextlib
import math
import numpy as np
import concourse.bass as bass
import concourse.mybir as mybir
from concourse.bass_utils import run_bass_kernel_spmd

F32 = mybir.dt.float32
BF16 = mybir.dt.bfloat16
ALU = mybir.AluOpType
AF = mybir.ActivationFunctionType
AX = mybir.AxisListType

D = 2048
DI = 4096
KC = D // 128
IC = DI // 128
WINDOW = 128
NEG = -30000.0
NORM_EPS = 1e-6


class _Stop(Exception):
    pass


class TT:
    def __init__(self, h, name=""):
        self.h = h
        self.name = name
        self.w = None
        self.r = {}
        self.psum = False

    def __getitem__(self, idx):
        return self.h[idx]


class KB:
    NDS = 12
    LIMIT = 10 ** 9

    def __init__(self, nc, es):
        self.nc = nc
        self.es = es
        self.eng = {"pe": nc.tensor, "act": nc.scalar, "dve": nc.vector, "pool": nc.gpsimd, "sp": nc.sync}
        self.sets = [{}, {}]
        self.phase = {}
        self.cnt = {}
        self.seen = {}
        self.dj = {}
        self.ep = 0
        self.nbar = 0
        for si in range(2):
            for e in self.eng:
                self.sets[si][e] = es.enter_context(nc.semaphore("c%d_%s" % (si, e)))
            for q in ("sp", "pool"):
                for j in range(self.NDS):
                    self.sets[si][("d", q, j)] = es.enter_context(nc.semaphore("d%d_%s_%d" % (si, q, j)))
        for e in self.eng:
            self.phase[e] = es.enter_context(nc.semaphore("ph_" + e))
            self.cnt[e] = 0
            self.seen[e] = {}
        for q in ("sp", "pool"):
            self.dj[q] = 0
        self.nalloc = 0
        self.dead = False
        self.last_dma_tok = None

    def semh(self, k):
        return self.sets[self.ep % 2][k]

    def sb(self, shape, dt, name=None, es=None):
        self.nalloc += 1
        name = (name or "t") + "_%d" % self.nalloc
        return TT((es or self.es).enter_context(self.nc.sbuf_tensor(name, list(shape), dt)), name)

    def ps(self, shape, dt, name=None):
        self.nalloc += 1
        name = name or "p%d" % self.nalloc
        t = TT(self.es.enter_context(self.nc.psum_tensor(name, list(shape), dt)), name)
        t.psum = True
        return t

    def barrier(self):
        if self.dead:
            return
        toks = {}
        for e in self.eng:
            if self.cnt[e] > 0:
                toks[e] = self.cnt[e]
        for q in ("sp", "pool"):
            for j in range(self.NDS):
                n = (self.dj[q] - j + self.NDS - 1) // self.NDS if self.dj[q] > j else 0
                if n > 0:
                    toks[("d", q, j)] = 16 * n
        for e in self.eng:
            for k, v in toks.items():
                if k == e:
                    continue
                self._wait(e, (k, v, self.ep))

    def epoch_barrier(self):
        self.barrier()
        nxt = (self.ep + 1) % 2
        self.nbar += 1
        for e in self.eng:
            self.eng[e].sem_clear(self.sets[nxt][e])
            if e in ("sp", "pool"):
                for j in range(self.NDS):
                    self.eng[e].sem_clear(self.sets[nxt][("d", e, j)])
            self.eng[e].sem_inc(self.phase[e], 1)
        for e in self.eng:
            for f in self.eng:
                if f != e:
                    self.eng[e].wait_ge(self.phase[f], self.nbar)
        self.ep += 1
        for e in self.eng:
            self.cnt[e] = 0
            self.seen[e] = {}
        for q in ("sp", "pool"):
            self.dj[q] = 0

    def maybe_epoch(self):
        if max(self.cnt.values()) >= self.LIMIT or max(self.dj.values()) >= self.LIMIT // 2:
            self.epoch_barrier()

    def _wait(self, E, tok):
        if self.dead or tok is None:
            return
        k, v, ep = tok
        if ep < self.ep:
            return
        if self.seen[E].get(k, 0) >= v:
            return
        self.eng[E].wait_ge(self.semh(k), v)
        self.seen[E][k] = v

    def _deps(self, E, reads, writes):
        need = {}

        def add(tok):
            if tok is None:
                return
            k, v, ep = tok
            if ep < self.ep:
                return
            if need.get(k, 0) < v:
                need[k] = v

        for t in reads:
            add(t.w)
            if t.psum:
                for rk, tok in t.r.items():
                    if rk != E:
                        add(tok)
        for t in writes:
            add(t.w)
            for tok in t.r.values():
                add(tok)
        for k, v in need.items():
            if k == "pe" and E == "pe":
                continue
            self._wait(E, (k, v, self.ep))

    def _done(self, E, ins, reads, writes):
        self.cnt[E] += 1
        ins.then_inc(self.semh(E), 1)
        tok = (E, self.cnt[E], self.ep)
        for t in writes:
            t.w = tok
            t.r = {}
        for t in reads:
            if t not in writes:
                t.r[E] = tok
        return tok

    def op(self, E, fn, reads=(), writes=()):
        if self.dead:
            return None
        self.maybe_epoch()
        self._deps(E, reads, writes)
        ins = fn()
        return self._done(E, ins, reads, writes)

    def mm(self, out_t, mms, reads):
        return self.mm_multi(out_t, [mms], reads)

    def mm_multi(self, out_t, groups, reads):
        if self.dead:
            return None
        self.maybe_epoch()
        self._deps("pe", reads, [out_t])
        ins = None
        for mms in groups:
            n = len(mms)
            for i, (o, l, r) in enumerate(mms):
                ins = self.nc.tensor.matmul(o, lhsT=l, rhs=r, start=(i == 0), stop=(i == n - 1))
        return self._done("pe", ins, reads, [out_t])

    def transpose(self, out_t, items, reads):
        if self.dead:
            return None
        self.maybe_epoch()
        self._deps("pe", reads, [out_t])
        ins = None
        for (o, i, idn) in items:
            ins = self.nc.tensor.transpose(o, i, idn)
        return self._done("pe", ins, reads, [out_t])

    def dma(self, q, out_ap, in_ap, reads=(), writes=(), **kw):
        if self.dead:
            return None
        self.maybe_epoch()
        self._deps(q, reads, writes)
        i = self.dj[q]
        self.dj[q] += 1
        j = i % self.NDS
        rnd = i // self.NDS
        key = ("d", q, j)
        if rnd > 0:
            self._wait(q, (key, 16 * rnd, self.ep))
        self.eng[q].dma_start(out=out_ap, in_=in_ap, **kw).then_inc(self.semh(key), 16)
        tok = (key, 16 * (rnd + 1), self.ep)
        for t in writes:
            t.w = tok
            t.r = {}
        for t in reads:
            t.r[key] = tok
        self.last_dma_tok = tok
        return tok

    def finish(self, toks):
        self.dead = False
        for tok in toks:
            if tok is not None:
                self._wait("sp", tok)


def t5_bucket_np(d):
    d = np.maximum(d, 0)
    large = 16 + (np.log(np.maximum(d, 1).astype(np.float32) / np.float32(16)) / np.float32(math.log(128 / 16))
                  * np.float32(16)).astype(np.int32)
    return np.where(d < 16, d, np.minimum(large, 31))


def host_consts():
    c = {}
    c["ident"] = np.eye(128, dtype=np.float32)
    bo = np.zeros((128, 128), np.float32)
    bo[:64, :64] = 1.0
    bo[64:, 64:] = 1.0
    c["blockones"] = bo
    oh = np.zeros((33, 384), np.float32)
    for m in range(384):
        dist = m - 128
        if 0 <= dist < 128:
            oh[int(t5_bucket_np(np.array(dist))), m] = 1.0
        else:
            oh[32, m] = NEG
    c["bucket_oh"] = oh
    t = np.arange(64)
    mk = np.zeros((3, 64, 64), np.float32)
    mk[0] = (t[:, None] < t[None, :])
    mk[1] = (t[None, :] < t[:, None])
    mk[2] = (t[:, None] <= t[None, :])
    c["masks"] = mk
    return c


def build_program(cfg):
    TP = cfg["TP"]
    layers = cfg["layers"]
    NSQ = 4
    NST = NSQ * 8
    nA = sum(1 for k in layers if k == 0)
    nB = sum(1 for k in layers if k == 1)
    nC = sum(1 for k in layers if k == 2)
    NL = len(layers)
    assert TP % 512 == 0
    NT = TP // 512

    nc = bass.Bass("TRN2", target_bir_lowering=False)

    def din(name, shape, dt=F32):
        return nc.dram_tensor(name, list(shape), dt, kind="ExternalInput").ap()

    def dout(name, shape, dt=F32):
        return nc.dram_tensor(name, list(shape), dt, kind="ExternalOutput").ap()

    I = {}
    I["xp"] = din("xp", [TP, D])
    I["xs"] = din("xs", [NST, D])
    I["norm_g"] = din("norm_g", [NL, D])
    I["ident"] = din("ident", [128, 128])
    I["blockones"] = din("blockones", [128, 128])
    I["bucket_oh"] = din("bucket_oh", [33, 384])
    if nA:
        A_IN = DI + 1024 + DI
        I["cache_k"] = din("cache_k", [nA, NSQ, 128, 512])
        I["cache_v"] = din("cache_v", [nA, NSQ, 128, 512])
        I["rel_bias"] = din("rel_bias", [32, 64])
        I["w_in_a"] = din("w_in_a", [nA, D, A_IN])
        I["q_norm_g"] = din("q_norm_g", [nA, 64])
        I["k_norm_g"] = din("k_norm_g", [nA, 64])
        I["sinks"] = din("sinks", [nA, 64])
        I["w_out_a"] = din("w_out_a", [nA, DI, D])
    if nB:
        I["state_wkv"] = din("state_wkv", [nB, NSQ, 64, 64, 64])
        I["state_shift"] = din("state_shift", [nB, NSQ, D])
        I["mu_b"] = din("mu_b", [nB, 6, D])
        I["w_rkvg_b"] = din("w_rkvg_b", [nB, 4, D, DI])
        I["w_lora_down_b"] = din("w_lora_down_b", [nB, 2, D, 96])
        I["w_lora_up_b"] = din("w_lora_up_b", [nB, 2, 96, DI])
        for nm in ("w0_b", "a0_b", "k_k_b", "k_a_b", "r_k_b", "ln_x_g_b", "ln_x_b_b"):
            I[nm] = din(nm, [nB, DI])
        I["w_out_b"] = din("w_out_b", [nB, DI, D])
        I["masks"] = din("masks", [3, 64, 64])
    if nC:
        C_IN = 2 * 2048 + 2 * DI + 16
        I["state_gla"] = din("state_gla", [nC, NSQ, 8, 256, 512])
        I["w_in_c"] = din("w_in_c", [nC, D, C_IN])
        I["w_gk_up_c"] = din("w_gk_up_c", [nC, 16, 2048])
        I["b_gk_c"] = din("b_gk_c", [nC, 2048])
        I["o_norm_g_c"] = din("o_norm_g_c", [nC, 512])
        I["w_out_c"] = din("w_out_c", [nC, DI, D])
        if "masks" not in I:
            I["masks"] = din("masks", [3, 64, 64])
    O = {}
    if nC:
        O["glap"] = dout("glap", [nC, 8, 256, 512])
        O["glas"] = dout("glas", [nC, NSQ, 8, 256, 512])
    if nB:
        O["wkvp"] = dout("wkvp", [nB, 64, 64, 64])
        O["shp"] = dout("shp", [nB, D])
        O["wkvs"] = dout("wkvs", [nB, NSQ, 64, 64, 64])
        O["shs"] = dout("shs", [nB, NSQ, D])
    O["yp"] = dout("yp", [TP, D])
    O["ys"] = dout("ys", [NST, D])
    if nA:
        O["kwp"] = dout("kwp", [nA, 128, 512])
        O["vwp"] = dout("vwp", [nA, 128, 512])
        O["kws"] = dout("kws", [nA, NSQ, 128, 512])
        O["vws"] = dout("vws", [nA, NSQ, 128, 512])
    zrep = nc.dram_tensor("zrep", [64, 128 * 384], F32, kind="Internal").ap() if nA else None

    es = contextlib.ExitStack()
    with es:
        kb = KB(nc, es)
        V = nc.vector
        S = nc.scalar
        G = nc.gpsimd
        out_regions = []

        def chk(stage):
            if cfg.get("stop") == stage:
                kb.dead = True

        yp_t = [TT(None, "yp%d" % t) for t in range(NT)]
        ys_t = TT(None, "ys")
        out_regions += yp_t + [ys_t]

        ident_f = kb.sb([128, 128], F32, "ident_f")
        ident_b = kb.sb([128, 128], BF16, "ident_b")
        bones_b = kb.sb([128, 128], BF16, "bones_b")
        ones_b = kb.sb([128, 128], BF16, "ones_b")
        gT = kb.sb([128, NL, KC], F32, "gT")
        kb.dma("sp", ident_f[:], I["ident"], writes=[ident_f])
        kb.dma("pool", ident_b[:], I["ident"], writes=[ident_b])
        kb.dma("pool", bones_b[:], I["blockones"], writes=[bones_b])
        bones_f = kb.sb([128, 128], F32, "bones_f")
        kb.dma("sp", bones_f[:], I["blockones"], writes=[bones_f])
        kb.op("dve", lambda: V.memset(ones_b[:], 1.0), writes=[ones_b])
        kb.dma("sp", gT[:], I["norm_g"].rearrange("l (kc p) -> p l kc", p=128), writes=[gT],
               allow_slow_non_contiguous=True)

        xt = kb.sb([128, 4, D], F32, "xt")
        hT = kb.sb([128, KC, 512], BF16, "hT")
        og = kb.sb([128, IC, 512], BF16, "og")
        wbuf = [kb.sb([128, 8192], BF16, "wbuf%d" % i) for i in range(2)]
        wsel = [0]
        stat = kb.sb([128, 16], F32, "stat")
        PSA = [kb.ps([128, 512], F32, "psA%d" % i) for i in range(4)]
        PSB = [kb.ps([128, 1024], BF16, "psB%d" % i) for i in range(2)]
        PSX = [kb.ps([128, 512], F32, "psX%d" % i) for i in range(2)]
        psa_i = [0]
        psb_i = [0]

        def psA():
            psa_i[0] += 1
            return PSA[psa_i[0] % len(PSA)]

        def psB():
            psb_i[0] += 1
            return PSB[psb_i[0] % len(PSB)]

        def next_w():
            wsel[0] += 1
            return wbuf[wsel[0] % 2]

        def load_norm(l, src_ap, ntok, src_reads):
            nsub = (ntok + 127) // 128
            pp = min(ntok, 128)
            if ntok >= 128:
                kb.dma("sp", xt[:, 0:nsub, :], src_ap.rearrange("(s p) d -> p s d", p=128), reads=src_reads, writes=[xt])
            else:
                kb.dma("sp", xt[0:pp, 0, :], src_ap, reads=src_reads, writes=[xt])
            for s in range(nsub):
                kb.op("act", lambda s=s: S.activation(out=hT.h[0:pp, 0:4, :].rearrange("p a b -> p (a b)"), in_=xt[0:pp, s, :],
                                                     func=AF.Square, accum_out=stat[0:pp, s:s + 1]),
                      reads=[xt], writes=[hT, stat])
            kb.op("dve", lambda: V.tensor_scalar(out=stat[0:pp, 4:4 + nsub], in0=stat[0:pp, 0:nsub], scalar1=1.0 / D,
                                                 scalar2=NORM_EPS, op0=ALU.mult, op1=ALU.add), reads=[stat], writes=[stat])
            kb.op("act", lambda: S.activation(out=stat[0:pp, 8:8 + nsub], in_=stat[0:pp, 4:4 + nsub], func=AF.Sqrt),
                  reads=[stat], writes=[stat])
            kb.op("dve", lambda: V.reciprocal(out=stat[0:pp, 12:12 + nsub], in_=stat[0:pp, 8:8 + nsub]),
                  reads=[stat], writes=[stat])
            xn = og.h[:].rearrange("p a b -> p (a b)")
            for s in range(nsub):
                kb.op("dve", lambda s=s: V.tensor_scalar(out=xn[0:pp, s * D:(s + 1) * D], in0=xt[0:pp, s, :],
                                                         scalar1=stat[0:pp, 12 + s:13 + s], scalar2=None, op0=ALU.mult),
                      reads=[xt, stat], writes=[og])
            for kc in range(KC):
                pt = psB()
                kb.transpose(pt, [(pt[:, s * 128:s * 128 + pp], xn[0:pp, s * D + kc * 128:s * D + (kc + 1) * 128],
                                   ident_b[0:pp, 0:pp]) for s in range(nsub)], reads=[og, ident_b])
                kb.op("act", lambda kc=kc, pt=pt: S.activation(out=hT[:, kc, 0:ntok], in_=pt[:, 0:ntok], func=AF.Copy,
                                                                scale=gT[:, l, kc:kc + 1]),
                      reads=[pt, gT], writes=[hT])

        def load_w_in(w_ap, col0, ncols):
            wt = next_w()
            view = wt.h[:, 0:KC * ncols].rearrange("p (k n) -> p k n", k=KC)
            kb.dma("pool", view, w_ap.rearrange("(k p) n -> p k n", p=128)[:, :, col0:col0 + ncols], writes=[wt])
            return wt, view

        def out_proj(w_ap, ntok, dst_ap, dst_t):
            nsub = (ntok + 127) // 128
            pp = min(ntok, 128)
            for cb in range(D // 256):
                wt = next_w()
                view = wt.h[:, 0:IC * 256].rearrange("p (k n) -> p k n", k=IC)
                kb.dma("pool", view, w_ap.rearrange("(k p) n -> p k n", p=128)[:, :, cb * 256:(cb + 1) * 256], writes=[wt])
                for s in range(nsub):
                    pt = psA()
                    kb.mm(pt, [(pt[0:pp, 0:256], og[:, ic, s * 128:s * 128 + pp], view[:, ic, :]) for ic in range(IC)],
                          reads=[og, wt])
                    kb.op("dve", lambda s=s, pt=pt, cb=cb: V.tensor_tensor(out=xt[0:pp, s, cb * 256:(cb + 1) * 256],
                                                                            in0=pt[0:pp, 0:256],
                                                                            in1=xt[0:pp, s, cb * 256:(cb + 1) * 256], op=ALU.add),
                          reads=[pt, xt], writes=[xt])
            if ntok >= 128:
                kb.dma("sp", dst_ap.rearrange("(s p) d -> p s d", p=128), xt[:, 0:nsub, :], reads=[xt], writes=[dst_t])
            else:
                kb.dma("sp", dst_ap, xt[0:pp, 0, :], reads=[xt], writes=[dst_t])

        AT = {}

        def attn_alloc(les):
            def sb(shape, dt, name):
                return kb.sb(shape, dt, name, es=les)
            AT["BTp"] = sb([128, 64, 128], BF16, "BTp")
            AT["BTc"] = sb([128, 64, 128], BF16, "BTc")
            with contextlib.ExitStack() as zes:
                rb33 = kb.sb([33, 64], F32, "rb33", es=zes)
                oh33 = kb.sb([33, 384], F32, "oh33", es=zes)
                zsb = kb.sb([64, 384], F32, "zsb", es=zes)
                kb.op("dve", lambda: V.memset(rb33[:], 1.0), writes=[rb33])
                kb.dma("sp", rb33[0:32, :], I["rel_bias"], writes=[rb33])
                kb.dma("sp", oh33[:], I["bucket_oh"], writes=[oh33])
                pz = psA()
                kb.mm(pz, [(pz[0:64, 0:384], rb33[:, :], oh33[:, :])], reads=[rb33, oh33])
                kb.op("dve", lambda: V.tensor_copy(out=zsb[:], in_=pz[0:64, 0:384]), reads=[pz], writes=[zsb])
                zr3 = zrep.rearrange("h (r m) -> h r m", m=384)
                zrp = kb.sb([64, 16, 384], F32, "zrp", es=zes)
                kb.op("dve", lambda: V.tensor_copy(out=zrp[:], in_=zsb[:, :].unsqueeze(1).to_broadcast([64, 16, 384])),
                      reads=[zsb], writes=[zrp])
                ztoks = []
                for r0 in range(0, 128, 16):
                    kb.dma("sp", zr3[:, r0:r0 + 16, :], zrp[:], reads=[zrp], writes=[])
                    ztoks.append(kb.last_dma_tok)
                for tk in ztoks:
                    kb._wait("pool", tk)
                for (BT, off) in ((AT["BTc"], 128), (AT["BTp"], 256)):
                    src = bass.AP(tensor=zrep.tensor, offset=zrep.offset + off, ap=[[383, 128], [128 * 384, 64], [1, 128]])
                    kb.dma("pool", BT[:], src, reads=[zrep_t], writes=[BT])
                kb.barrier()
            chk("bt")
            AT["kT"] = sb([128, 8, 640], BF16, "kT")
            AT["vtk"] = sb([128, 5, 512], BF16, "vtk")
            AT["knf"] = sb([128, 8, 128], F32, "knf")
            AT["vlf"] = sb([128, 512], F32, "vlf")
            AT["qn"] = sb([128, 4, 512], BF16, "qn")
            AT["sg"] = sb([128, 4, 512], BF16, "sg")
            AT["sqb"] = sb([128, 512], BF16, "sqb")
            AT["rsd"] = sb([128, 512], F32, "rsd")
            AT["pT"] = [sb([128, 512], BF16, "pT%d" % i) for i in range(2)]
            AT["tmpf"] = [sb([128, 512], F32, "tmpf%d" % i) for i in range(2)]
            AT["den"] = sb([128, 512], F32, "den")
            gq = AT["gq"] = sb([128, nA], F32, "gq")
            gk = AT["gk"] = sb([128, nA], F32, "gk")
            esk = AT["esk"] = sb([128, nA, 32], F32, "esk")
            for par in range(2):
                kb.dma("sp", gq[par * 64:(par + 1) * 64, :], I["q_norm_g"].rearrange("l d -> d l"), writes=[gq],
                       allow_slow_non_contiguous=True)
                kb.dma("sp", gk[par * 64:(par + 1) * 64, :], I["k_norm_g"].rearrange("l d -> d l"), writes=[gk],
                       allow_slow_non_contiguous=True)
            kb.op("dve", lambda: V.memset(esk[:], 0.0), writes=[esk])
            for par in range(2):
                sk = I["sinks"].rearrange("l (c two) -> two l c", two=2)[par:par + 1]
                kb.dma("sp", esk[par * 64:par * 64 + 1, :, :], sk, writes=[esk], allow_slow_non_contiguous=True)
            pe_ = psA()
            kb.mm(pe_, [(pe_[:, 0:nA * 32], bones_f[:, :], esk[:].rearrange("p l c -> p (l c)"))], reads=[bones_f, esk])
            kb.op("act", lambda: S.activation(out=esk[:].rearrange("p l c -> p (l c)"), in_=pe_[:, 0:nA * 32], func=AF.Exp),
                  reads=[pe_], writes=[esk])
            kb.op("dve", lambda: V.tensor_scalar(out=gq[:], in0=gq[:], scalar1=0.125, scalar2=None, op0=ALU.mult),
                  reads=[gq], writes=[gq])
            chk("esk")
            AT["kcs"] = sb([128, 8, 128], BF16, "kcs")
            AT["kcs_f"] = AT["tmpf"][0]
            AT["vcs_f"] = AT["tmpf"][1]
            AT["vns"] = sb([8, NSQ, 512], BF16, "vns")
            AT["vnsf"] = sb([32, 512], F32, "vnsf")
            AT["kcT_bufs"] = [sb([128, 8, 128], BF16, "kcTb%d" % i) for i in range(NSQ)]
            AT["vcs_bufs"] = [sb([128, 512], BF16, "vcsb%d" % i) for i in range(NSQ)]
            AT["ktok"] = AT["den"]
            AT["pti"] = [0]
            AT["tfi"] = [0]

        class _A:
            def __getattr__(self, k):
                return AT[k]
        A_ = _A()

        def headnorm(ps_t, ntok, gcol, gt, out_ap, out_t, f32_out=None):
            sqb, rsd = AT["sqb"], AT["rsd"]
            kb.op("act", lambda: S.activation(out=sqb[:, 0:ntok], in_=ps_t[:, 0:ntok], func=AF.Square),
                  reads=[ps_t], writes=[sqb])
            p2 = psA()
            kb.mm(p2, [(p2[:, 0:ntok], bones_b[:, :], sqb[:, 0:ntok])], reads=[bones_b, sqb])
            kb.op("dve", lambda: V.tensor_scalar(out=rsd[:, 0:ntok], in0=p2[:, 0:ntok], scalar1=1.0 / 64, scalar2=NORM_EPS,
                                                 op0=ALU.mult, op1=ALU.add), reads=[p2], writes=[rsd])
            kb.op("act", lambda: S.activation(out=rsd[:, 0:ntok], in_=rsd[:, 0:ntok], func=AF.Sqrt), reads=[rsd], writes=[rsd])
            kb.op("dve", lambda: V.reciprocal(out=rsd[:, 0:ntok], in_=rsd[:, 0:ntok]), reads=[rsd], writes=[rsd])
            if f32_out is not None:
                fo_ap, fo_t, c0, c1 = f32_out
                kb.op("dve", lambda: V.scalar_tensor_tensor(out=fo_ap, in0=ps_t[:, c0:c1], scalar=gcol,
                                                            in1=rsd[:, c0:c1], op0=ALU.mult, op1=ALU.mult),
                      reads=[ps_t, rsd, gt], writes=[fo_t])
            kb.op("dve", lambda: V.scalar_tensor_tensor(out=out_ap, in0=ps_t[:, 0:ntok], scalar=gcol, in1=rsd[:, 0:ntok],
                                                        op0=ALU.mult, op1=ALU.mult),
                  reads=[ps_t, rsd, gt], writes=[out_t])

        def attn_block(j, kh, QB, q_cols, kprev, kcur, vprev, vcur, ncur, og_cols, first, xr=()):
            qn, sg, den, esk, pT, tmpf = AT["qn"], AT["sg"], AT["den"], AT["esk"], AT["pT"], AT["tmpf"]
            po = PSX[0]
            pd = PSX[1]
            groups_o = []
            groups_d = []
            preads = []
            W4 = 4 * QB
            for par in range(2):
                pts = []
                for which in ((0, 1) if not first else (1,)):
                    nk = 128 if which == 0 else ncur
                    lk = kprev(par) if which == 0 else kcur(par)
                    sc = psA()
                    kb.mm(sc, [(sc[0:nk, 0:W4].rearrange("p (c q) -> p c q", c=4), lk,
                                qn[par * 64:(par + 1) * 64, :, q_cols])], reads=[AT["kT"], qn] + list(xr))
                    BT = AT["BTp"] if which == 0 else AT["BTc"]
                    h0 = kh * 8 + par
                    tf = tmpf[AT["tfi"][0] % 2]
                    AT["tfi"][0] += 1
                    kb.op("dve", lambda sc=sc, tf=tf, BT=BT, nk=nk, h0=h0: V.tensor_tensor(
                        out=tf[0:nk, 0:W4].rearrange("p (c q) -> p c q", c=4),
                        in0=sc[0:nk, 0:W4].rearrange("p (c q) -> p c q", c=4),
                        in1=BT[0:nk, h0:h0 + 7:2, 0:QB], op=ALU.add), reads=[sc, BT], writes=[tf])
                    p = pT[AT["pti"][0] % 2]
                    AT["pti"][0] += 1
                    kb.op("act", lambda tf=tf, p=p, nk=nk: S.activation(out=p[0:nk, 0:W4], in_=tf[0:nk, 0:W4], func=AF.Exp),
                          reads=[tf], writes=[p])
                    pts.append((p, nk, which))
                go = []
                gd = []
                preads = []
                for (p, nk, which) in pts:
                    vv = vprev if which == 0 else vcur
                    go.append((po[par * 64:(par + 1) * 64, 0:W4], vv, p[0:nk, 0:W4]))
                    gd.append((pd[par * 64:(par + 1) * 64, 0:W4], ones_b[0:nk, 0:64], p[0:nk, 0:W4]))
                    preads.append(p)
                kb.mm(po, go, reads=preads + [AT["vtk"]] + list(xr))
                kb.mm(pd, gd, reads=preads + [ones_b])
            kb.op("dve", lambda: V.tensor_tensor(out=den[:, 0:W4].rearrange("p (c q) -> p c q", c=4),
                                                 in0=pd[:, 0:W4].rearrange("p (c q) -> p c q", c=4),
                                                 in1=esk[:, j, kh * 4:kh * 4 + 4].unsqueeze(2).to_broadcast([128, 4, QB]),
                                                 op=ALU.add), reads=[pd, esk], writes=[den])
            kb.op("dve", lambda: V.reciprocal(out=den[:, 0:W4], in_=den[:, 0:W4]), reads=[den], writes=[den])
            kb.op("dve", lambda: V.tensor_tensor(out=den[:, 0:W4], in0=po[:, 0:W4], in1=den[:, 0:W4], op=ALU.mult),
                  reads=[po, den], writes=[den])
            kb.op("dve", lambda: V.tensor_tensor(out=og[:, kh * 4:kh * 4 + 4, og_cols],
                                                 in0=den[:, 0:W4].rearrange("p (c q) -> p c q", c=4),
                                                 in1=sg[:, :, q_cols], op=ALU.mult), reads=[den, sg], writes=[og])

        def attn_layer_tile(l, j, ntok, ti, last):
            kT, vtk, knf, vlf, qn, sg, gq, gk, vns, vnsf = (AT[k] for k in
                                                            ("kT", "vtk", "knf", "vlf", "qn", "sg", "gq", "gk", "vns", "vnsf"))
            w_in = I["w_in_a"][j]
            sample = ntok < 128
            for kh in range(8):
                wt = next_w()
                view = wt.h[:, 0:KC * 128].rearrange("p (k n) -> p k n", k=KC)
                src = w_in.rearrange("(k p) n -> p k n", p=128)[:, :, DI + kh * 64:DI + kh * 64 + 64]
                kb.dma("pool", view[:, :, 0:64], src, writes=[wt])
                kb.dma("pool", view[:, :, 64:128], src, writes=[wt])
                pk = psA()
                kb.mm(pk, [(pk[:, 0:ntok], view[:, kc, :], hT[:, kc, 0:ntok]) for kc in range(KC)], reads=[wt, hT])
                if not sample:
                    headnorm(pk, ntok, gk[:, j:j + 1], gk, kT[:, kh, 128:640], kT,
                             f32_out=(knf[:, kh, :], knf, 384, 512) if last else None)
                else:
                    headnorm(pk, ntok, gk[:, j:j + 1], gk, kT[:, kh, 0:ntok], kT,
                             f32_out=(knf[:, kh, 0:ntok], knf, 0, ntok))
            chk("ak")
            wt, view = load_w_in(w_in, DI + 512, 512)
            chk("av0")
            if not sample:
                for s_ in range(4):
                    pv = psA()
                    kb.mm(pv, [(pv[:, :], hT[:, kc, s_ * 128:(s_ + 1) * 128], view[:, kc, :]) for kc in range(KC)], reads=[wt, hT])
                    chk("av1")
                    kb.op("act", lambda s_=s_, pv=pv: S.copy(out=vtk[:, 1 + s_, :], in_=pv[:, :]), reads=[pv], writes=[vtk])
                    chk("av2")
                    if s_ == 1:
                        chk("av3")
                    if s_ == 3:
                        chk("av4")
                    if last and s_ == 3:
                        kb.op("dve", lambda pv=pv: V.tensor_copy(out=vlf[:], in_=pv[:, :]), reads=[pv], writes=[vlf])
                        chk("av5")
            else:
                for sq in range(NSQ):
                    pv = psA()
                    kb.mm(pv, [(pv[0:8, :], hT[:, kc, sq * 8:(sq + 1) * 8], view[:, kc, :]) for kc in range(KC)], reads=[wt, hT])
                    kb.op("act", lambda sq=sq, pv=pv: S.copy(out=vns[:, sq, :], in_=pv[0:8, :]), reads=[pv], writes=[vns])
                pv = psA()
                kb.mm(pv, [(pv[0:NST, :], hT[:, kc, 0:NST], view[:, kc, :]) for kc in range(KC)], reads=[wt, hT])
                kb.op("dve", lambda pv=pv: V.tensor_copy(out=vnsf[:], in_=pv[0:NST, :]), reads=[pv], writes=[vnsf])
            chk("av")
            for kh in range(8):
                wt, view = load_w_in(w_in, kh * 512, 512)
                for c in range(4):
                    pq = psA()
                    kb.mm(pq, [(pq[:, 0:ntok], view[:, kc, c * 128:(c + 1) * 128], hT[:, kc, 0:ntok]) for kc in range(KC)],
                          reads=[wt, hT])
                    headnorm(pq, ntok, gq[:, j:j + 1], gq, qn[:, c, 0:ntok], qn)
                wt, view = load_w_in(w_in, DI + 1024 + kh * 512, 512)
                for c in range(4):
                    pg = psA()
                    kb.mm(pg, [(pg[:, 0:ntok], view[:, kc, c * 128:(c + 1) * 128], hT[:, kc, 0:ntok]) for kc in range(KC)],
                          reads=[wt, hT])
                    kb.op("act", lambda c=c, pg=pg: S.activation(out=sg[:, c, 0:ntok], in_=pg[:, 0:ntok], func=AF.Silu),
                          reads=[pg], writes=[sg])
                chk("aq")
                if not sample:
                    for b in range(4):
                        if b == 1:
                            chk("ab0")
                        if b == 2:
                            chk("ab1")
                        attn_block(j, kh, 128, slice(b * 128, (b + 1) * 128),
                                   lambda par, b=b: kT[par * 64:(par + 1) * 64, kh, b * 128:(b + 1) * 128],
                                   lambda par, b=b: kT[par * 64:(par + 1) * 64, kh, (b + 1) * 128:(b + 2) * 128],
                                   vtk[:, b, kh * 64:(kh + 1) * 64], vtk[:, b + 1, kh * 64:(kh + 1) * 64], 128,
                                   slice(b * 128, (b + 1) * 128), first=(ti == 0 and b == 0))
                else:
                    for sq in range(NSQ):
                        attn_block(j, kh, 8, slice(sq * 8, (sq + 1) * 8),
                                   lambda par, sq=sq: AT["kcT_bufs"][sq][par * 64:(par + 1) * 64, kh, :],
                                   lambda par, sq=sq: kT[par * 64:(par + 1) * 64, kh, sq * 8:(sq + 1) * 8],
                                   AT["vcs_bufs"][sq][:, kh * 64:(kh + 1) * 64], vns[:, sq, kh * 64:(kh + 1) * 64], 8,
                                   slice(sq * 8, (sq + 1) * 8), first=False,
                                   xr=[AT["kcT_bufs"][sq], AT["vcs_bufs"][sq], vns])
            if not sample:
                kb.op("pool", lambda: G.tensor_copy(out=kT[:, :, 0:128], in_=kT[:, :, 512:640]), reads=[kT], writes=[kT])
                kb.op("pool", lambda: G.tensor_copy(out=vtk[:, 0, :], in_=vtk[:, 4, :]), reads=[vtk], writes=[vtk])

        def attn_sample_prep(j):
            kcs, kcs_f, vcs_f = AT["kcs"], AT["kcs_f"], AT["vcs_f"]
            for sq in range(NSQ):
                kb.dma("sp", kcs_f[:], I["cache_k"][j, sq], writes=[kcs_f])
                kb.dma("sp", vcs_f[:], I["cache_v"][j, sq], writes=[vcs_f])
                kcv = kcs_f.h[:].rearrange("p (k d) -> p k d", k=8)
                kb.op("dve", lambda kcv=kcv: V.tensor_copy(out=kcs[:, :, 0:64], in_=kcv), reads=[kcs_f], writes=[kcs])
                kb.op("dve", lambda kcv=kcv: V.tensor_copy(out=kcs[:, :, 64:128], in_=kcv), reads=[kcs_f], writes=[kcs])
                kct = AT["kcT_bufs"][sq]
                vc = AT["vcs_bufs"][sq]
                kb.op("act", lambda vc=vc: S.copy(out=vc[:], in_=vcs_f[:]), reads=[vcs_f], writes=[vc])
                for k2 in range(0, 8, 4):
                    pt = psB()
                    kb.transpose(pt, [(pt[:, i * 128:(i + 1) * 128], kcs[:, k2 + i, :], ident_b[:, :]) for i in range(4)],
                                 reads=[kcs, ident_b])
                    kb.op("act", lambda pt=pt, kct=kct, k2=k2: S.copy(out=kct[:, k2:k2 + 4, :],
                                                                    in_=pt[:, 0:512].rearrange("p (k n) -> p k n", k=4)),
                          reads=[pt], writes=[kct])
                kb.dma("sp", O["kws"][j, sq, 0:120, :], I["cache_k"][j, sq, 8:128, :], writes=[])
                kws_toks.append(kb_last_tok("sp"))
                kb.dma("sp", O["vws"][j, sq, 0:120, :], I["cache_v"][j, sq, 8:128, :], writes=[])
                kws_toks.append(kb_last_tok("sp"))

        def kb_last_tok(q):
            return kb.last_dma_tok

        kws_toks = []
        if nA:
            kws_t = TT(None, "kws")
            vws_t = TT(None, "vws")
            kwp_t = TT(None, "kwp")
            vwp_t = TT(None, "vwp")
            zrep_t = TT(None, "zrep")
            out_regions += [kws_t, vws_t, kwp_t, vwp_t]

        def attn_outputs_prompt(j):
            knf, vlf, ktok = AT["knf"], AT["vlf"], AT["ktok"]
            pt = psA()
            kb.transpose(pt, [(pt[:, kh * 64:(kh + 1) * 64], knf[0:64, kh, :], ident_f[0:64, 0:64]) for kh in range(8)],
                         reads=[knf, ident_f])
            kb.op("dve", lambda: V.tensor_copy(out=ktok[:], in_=pt[:, :]), reads=[pt], writes=[ktok])
            kb.dma("sp", O["kwp"][j], ktok[:], reads=[ktok], writes=[])
            kws_toks.append(kb_last_tok("sp"))
            kb.dma("sp", O["vwp"][j], vlf[:], reads=[vlf], writes=[])
            kws_toks.append(kb_last_tok("sp"))

        def attn_outputs_sample(j):
            knf, vnsf, ktok = AT["knf"], AT["vnsf"], AT["ktok"]
            pt = psA()
            kb.transpose(pt, [(pt[0:NST, kh * 64:(kh + 1) * 64], knf[0:64, kh, 0:NST], ident_f[0:64, 0:64])
                              for kh in range(8)], reads=[knf, ident_f])
            kb.op("dve", lambda: V.tensor_copy(out=ktok[0:NST, :], in_=pt[0:NST, :]), reads=[pt], writes=[ktok])
            for sq in range(NSQ):
                kb.dma("sp", O["kws"][j, sq, 120:128, :], ktok[sq * 8:(sq + 1) * 8, :], reads=[ktok], writes=[])
                kws_toks.append(kb_last_tok("sp"))
                kb.dma("sp", O["vws"][j, sq, 120:128, :], vnsf[sq * 8:(sq + 1) * 8, :], reads=[vnsf], writes=[])
                kws_toks.append(kb_last_tok("sp"))


        BT_ = {}
        KAP = 0.6065306597126334

        def rwkv_alloc(les, j):
            def sb(shape, dt, name):
                return kb.sb(shape, dt, name, es=les)
            B = BT_
            B["ST"] = sb([128, 32, 64], F32, "ST")
            B["STs"] = sb([128, NSQ, 64], F32, "STs")
            kb.op("dve", lambda: V.memset(B["ST"][:], 0.0), writes=[B["ST"]])
            B["hlast"] = sb([128, KC], BF16, "hlast")
            kb.op("dve", lambda: V.memset(B["hlast"][:], 0.0), writes=[B["hlast"]])
            B["dlt"] = sb([128, KC, 256], BF16, "dlt")
            B["xm"] = [sb([128, KC, 256], BF16, "xm%d" % c) for c in range(5)]
            B["xm"].append(B["xm"][4])
            B["mu"] = sb([128, 6, KC], F32, "mu")
            kb.dma("sp", B["mu"][:], I["mu_b"][j].rearrange("c (k p) -> p c k", p=128), writes=[B["mu"]],
                   allow_slow_non_contiguous=True)
            B["pv"] = sb([128, 7, 32], F32, "pv")
            for wi, nm in enumerate(("w0_b", "a0_b", "k_k_b", "k_a_b", "r_k_b", "ln_x_g_b", "ln_x_b_b")):
                kb.dma("sp", B["pv"][:, wi, :], I[nm][j].rearrange("(c p) -> p c", p=128), writes=[B["pv"]],
                       allow_slow_non_contiguous=True)
            B["wd"] = sb([128, 2, KC, 96], BF16, "wd")
            for c in range(2):
                kb.dma("pool", B["wd"][:, c, :, :], I["w_lora_down_b"][j, c].rearrange("(k p) n -> p k n", p=128), writes=[B["wd"]])
            B["wu"] = [sb([96, 2, 128], BF16, "wu%d" % i) for i in range(2)]
            B["lw"] = sb([96, 2, 256], BF16, "lwlow")
            B["mk"] = sb([128, 3, 64], F32, "mk")
            kb.dma("sp", B["mk"][0:64], I["masks"].rearrange("m a b -> a m b"), writes=[B["mk"]])
            kb.dma("sp", B["mk"][64:128], I["masks"].rearrange("m a b -> a m b"), writes=[B["mk"]])
            for nm in ("r", "k", "kk", "k2", "a", "sgw", "cs", "t1", "t2", "Em", "Ep", "E0", "bon", "of"):
                B[nm] = sb([128, 256], F32, "f_" + nm)
            for nm in ("rt", "at", "bt", "kt", "bh", "kh", "vb", "sq"):
                B[nm] = sb([128, 256], BF16, "b_" + nm)
            B["ones64"] = sb([128, 64], F32, "ones64")
            kb.op("dve", lambda: V.memset(B["ones64"][:], 1.0), writes=[B["ones64"]])
            for nm in ("LT", "L", "LT2", "L2", "AkT", "Grb", "Grk", "Vt", "bht", "kht", "Y", "Y2", "Sb"):
                B[nm] = sb([128, 64], BF16, "m_" + nm)
            B["wrk"] = wbuf
            B["Sin"] = B["of"]
            B["Sout"] = B["bon"]
            B["shf"] = sb([128, KC, NSQ], F32, "shf")
            B["wri"] = [0]

        def rwkv_mix_inputs(j, col0, T, first_cols):
            B = BT_
            for c in range(6):
                for kc in range(KC):
                    kb.op("dve", lambda c=c, kc=kc: V.scalar_tensor_tensor(out=B["xm"][c][:, kc, 0:T], in0=B["dlt"][:, kc, 0:T],
                                                                            scalar=B["mu"][:, c, kc:kc + 1], in1=hT[:, kc, col0:col0 + T],
                                                                            op0=ALU.mult, op1=ALU.add),
                          reads=[B["dlt"], B["mu"], hT], writes=[B["xm"][c]])
                if c >= 4:
                    cc_ = c - 4
                    pl = psA()
                    kb.mm(pl, [(pl[0:96, 0:T], B["wd"][:, cc_, kc, :], B["xm"][c][:, kc, 0:T]) for kc in range(KC)],
                          reads=[B["wd"], B["xm"][c]])
                    kb.op("act", lambda cc_=cc_, pl=pl: S.activation(out=B["lw"][:, cc_, 0:T], in_=pl[0:96, 0:T],
                                                                    func=(AF.Tanh if cc_ == 0 else AF.Copy)), reads=[pl], writes=[B["lw"]])

        def rwkv_pair(j, pr, T, C, og_cols, ST, st_ap):
            B = BT_
            pvp = B["pv"]
            w4 = I["w_rkvg_b"][j]
            ps_in = {}
            for c in range(4):
                wt = next_w()
                wv = wt.h[:, 0:KC * 128].rearrange("p (k n) -> p k n", k=KC)
                kb.dma("pool", wv, w4[c].rearrange("(k p) n -> p k n", p=128)[:, :, pr * 128:(pr + 1) * 128], writes=[wt])
                pp_ = psA()
                kb.mm(pp_, [(pp_[:, 0:T], wv[:, kc, :], B["xm"][c][:, kc, 0:T]) for kc in range(KC)], reads=[wt, B["xm"][c]])
                if c == 0:
                    kb.op("act", lambda pp_=pp_: S.copy(out=B["r"][:, 0:T], in_=pp_[:, 0:T]), reads=[pp_], writes=[B["r"]])
                elif c == 1:
                    kb.op("act", lambda pp_=pp_: S.copy(out=B["k"][:, 0:T], in_=pp_[:, 0:T]), reads=[pp_], writes=[B["k"]])
                elif c == 2:
                    kb.op("act", lambda pp_=pp_: S.copy(out=B["vb"][:, 0:T], in_=pp_[:, 0:T]), reads=[pp_], writes=[B["vb"]])
                else:
                    kb.op("act", lambda pp_=pp_: S.activation(out=og[:, pr, og_cols], in_=pp_[:, 0:T], func=AF.Silu),
                          reads=[pp_], writes=[og])
            chk("bproj")
            r, k, kk, k2, a, sgw, cs, t1, t2, Em, Ep, E0, bon = (B[n] for n in
                                                                 ("r", "k", "kk", "k2", "a", "sgw", "cs", "t1", "t2", "Em", "Ep", "E0", "bon"))
            wu = B["wu"][pr % 2]
            kb.dma("pool", wu[:], I["w_lora_up_b"][j][:, :, pr * 128:(pr + 1) * 128].rearrange("c r n -> r c n"), writes=[wu])
            for c, dst, wi in ((0, sgw, 0), (1, a, 1)):
                pl = psA()
                kb.mm(pl, [(pl[:, 0:T], wu[:, c, :], B["lw"][:, c, 0:T])], reads=[wu, B["lw"]])
                kb.op("act", lambda pl=pl, dst=dst, wi=wi: S.activation(out=dst[:, 0:T], in_=pl[:, 0:T], func=AF.Sigmoid,
                                                                       bias=pvp[:, wi, pr:pr + 1]), reads=[pl, pvp], writes=[dst])
            kb.op("dve", lambda: V.tensor_scalar(out=t1[:, 0:T], in0=k[:, 0:T], scalar1=pvp[:, 2, pr:pr + 1], scalar2=None, op0=ALU.mult),
                  reads=[k, pvp], writes=[t1])
            kb.op("act", lambda: S.activation(out=B["sq"][:, 0:T], in_=t1[:, 0:T], func=AF.Square), reads=[t1], writes=[B["sq"]])
            pn = psA()
            kb.mm(pn, [(pn[:, 0:T], bones_b[:, :], B["sq"][:, 0:T])], reads=[bones_b, B["sq"]])
            kb.op("act", lambda: S.activation(out=t2[:, 0:T], in_=pn[:, 0:T], func=AF.Sqrt), reads=[pn], writes=[t2])
            kb.op("dve", lambda: V.tensor_scalar(out=t2[:, 0:T], in0=t2[:, 0:T], scalar1=1e-12, scalar2=None, op0=ALU.max),
                  reads=[t2], writes=[t2])
            kb.op("dve", lambda: V.reciprocal(out=t2[:, 0:T], in_=t2[:, 0:T]), reads=[t2], writes=[t2])
            kb.op("dve", lambda: V.tensor_tensor(out=kk[:, 0:T], in0=t1[:, 0:T], in1=t2[:, 0:T], op=ALU.mult), reads=[t1, t2], writes=[kk])
            kb.op("dve", lambda: V.tensor_scalar(out=t1[:, 0:T], in0=a[:, 0:T], scalar1=-1.0, scalar2=pvp[:, 3, pr:pr + 1],
                                                 op0=ALU.add, op1=ALU.mult), reads=[a, pvp], writes=[t1])
            kb.op("dve", lambda: V.scalar_tensor_tensor(out=k2[:, 0:T], in0=t1[:, 0:T], scalar=1.0, in1=k[:, 0:T], op0=ALU.add, op1=ALU.mult),
                  reads=[t1, k], writes=[k2])
            kb.op("dve", lambda: V.scalar_tensor_tensor(out=B["sq"][:, 0:T], in0=r[:, 0:T], scalar=pvp[:, 4, pr:pr + 1], in1=k2[:, 0:T],
                                                        op0=ALU.mult, op1=ALU.mult), reads=[r, k2, pvp], writes=[B["sq"]])
            pb = psA()
            kb.mm(pb, [(pb[:, 0:T], bones_b[:, :], B["sq"][:, 0:T])], reads=[bones_b, B["sq"]])
            kb.op("dve", lambda: V.tensor_tensor(out=bon[:, 0:T], in0=pb[:, 0:T], in1=B["vb"][:, 0:T], op=ALU.mult), reads=[pb, B["vb"]], writes=[bon])
            nch = T // C
            for ci in range(nch):
                kb.op("dve", lambda ci=ci: V.tensor_tensor_scan(out=cs[:, ci * C:(ci + 1) * C], data0=B["ones64"][:, 0:C],
                                                                 data1=sgw[:, ci * C:(ci + 1) * C], initial=0.0, op0=ALU.mult, op1=ALU.add),
                      reads=[sgw, B["ones64"]], writes=[cs])
            kb.op("act", lambda: S.activation(out=Em[:, 0:T], in_=cs[:, 0:T], func=AF.Exp, scale=-KAP), reads=[cs], writes=[Em])
            kb.op("act", lambda: S.activation(out=Ep[:, 0:T], in_=cs[:, 0:T], func=AF.Exp, scale=KAP), reads=[cs], writes=[Ep])
            kb.op("dve", lambda: V.tensor_tensor(out=t1[:, 0:T], in0=cs[:, 0:T], in1=sgw[:, 0:T], op=ALU.subtract), reads=[cs, sgw], writes=[t1])
            kb.op("act", lambda: S.activation(out=E0[:, 0:T], in_=t1[:, 0:T], func=AF.Exp, scale=-KAP), reads=[t1], writes=[E0])
            kb.op("dve", lambda: V.tensor_tensor(out=B["rt"][:, 0:T], in0=r[:, 0:T], in1=Em[:, 0:T], op=ALU.mult), reads=[r, Em], writes=[B["rt"]])
            kb.op("dve", lambda: V.scalar_tensor_tensor(out=B["at"][:, 0:T], in0=kk[:, 0:T], scalar=-1.0, in1=E0[:, 0:T], op0=ALU.mult, op1=ALU.mult),
                  reads=[kk, E0], writes=[B["at"]])
            kb.op("dve", lambda: V.tensor_tensor(out=t1[:, 0:T], in0=kk[:, 0:T], in1=a[:, 0:T], op=ALU.mult), reads=[kk, a], writes=[t1])
            kb.op("dve", lambda: V.tensor_tensor(out=B["bt"][:, 0:T], in0=t1[:, 0:T], in1=Ep[:, 0:T], op=ALU.mult), reads=[t1, Ep], writes=[B["bt"]])
            kb.op("dve", lambda: V.tensor_tensor(out=B["kt"][:, 0:T], in0=k2[:, 0:T], in1=Ep[:, 0:T], op=ALU.mult), reads=[k2, Ep], writes=[B["kt"]])
            for ci in range(nch):
                cc = slice(ci * C, (ci + 1) * C)
                wc = Em[:, (ci + 1) * C - 1:(ci + 1) * C]
                kb.op("dve", lambda cc=cc, wc=wc: V.tensor_scalar(out=B["bh"][:, cc], in0=B["bt"][:, cc], scalar1=wc, scalar2=None, op0=ALU.mult),
                      reads=[B["bt"], Em], writes=[B["bh"]])
                kb.op("dve", lambda cc=cc, wc=wc: V.tensor_scalar(out=B["kh"][:, cc], in0=B["kt"][:, cc], scalar1=wc, scalar2=None, op0=ALU.mult),
                      reads=[B["kt"], Em], writes=[B["kh"]])
            chk("bprep")
            nlev = int(math.log2(C))
            mk = B["mk"]
            po = PSX[0]
            for ci in range(nch):
                cc = slice(ci * C, (ci + 1) * C)
                for hp in range(2):
                    P = slice(hp * 64, (hp + 1) * 64)
                    TBs = slice(hp * 64, hp * 64 + C)
                    at_, bt_, kt_, rt_ = B["at"][P, cc], B["bt"][P, cc], B["kt"][P, cc], B["rt"][P, cc]
                    if ci == 0 and hp == 1:
                        chk("bhc")

                    def mmask(dst, lh, rh, mi, rds):
                        pm = psA()
                        kb.mm(pm, [(pm[TBs, 0:C], lh, rh)], reads=rds)
                        kb.op("dve", lambda pm=pm, dst=dst, mi=mi: V.tensor_tensor(out=dst[TBs, 0:C], in0=pm[TBs, 0:C], in1=mk[TBs, mi, 0:C], op=ALU.mult),
                              reads=[pm, mk], writes=[dst])
                    mmask(B["LT"], bt_, at_, 0, [B["bt"], B["at"]])
                    mmask(B["L"], at_, bt_, 1, [B["bt"], B["at"]])
                    mmask(B["AkT"], kt_, at_, 0, [B["kt"], B["at"]])
                    mmask(B["Grb"], bt_, rt_, 2, [B["bt"], B["rt"]])
                    mmask(B["Grk"], kt_, rt_, 2, [B["kt"], B["rt"]])
                    for src_, dst_ in ((B["vb"], B["Vt"]), (B["bh"], B["bht"]), (B["kh"], B["kht"])):
                        pt = psA()
                        kb.mm(pt, [(pt[TBs, 0:64], src_[P, cc], ident_b[P, P])], reads=[src_, ident_b])
                        kb.op("act", lambda pt=pt, dst_=dst_: S.copy(out=dst_[TBs, 0:64], in_=pt[TBs, 0:64]), reads=[pt], writes=[dst_])
                    kb.op("act", lambda P=P, ci=ci: S.copy(out=B["Sb"][P, :], in_=st_ap(ci, P)), reads=[ST], writes=[B["Sb"]])
                    py = psA()
                    kb.mm(py, [(py[TBs, 0:64], at_, B["Sb"][P, :]), (py[TBs, 0:64], B["AkT"][TBs, 0:C], B["Vt"][TBs, 0:64])],
                          reads=[B["at"], B["Sb"], B["AkT"], B["Vt"]])
                    Ycur, Yoth = B["Y"], B["Y2"]
                    kb.op("dve", lambda py=py, Ycur=Ycur: V.tensor_copy(out=Ycur[TBs, 0:64], in_=py[TBs, 0:64]), reads=[py], writes=[Ycur])
                    LTc, Lc, LTn, Ln = B["LT"], B["L"], B["LT2"], B["L2"]
                    for lev in range(nlev):
                        py = psA()
                        kb.mm(py, [(py[TBs, 0:64], ident_b[TBs, TBs], Ycur[TBs, 0:64]), (py[TBs, 0:64], LTc[TBs, 0:C], Ycur[TBs, 0:64])],
                              reads=[ident_b, Ycur, LTc])
                        kb.op("dve", lambda py=py, Yoth=Yoth: V.tensor_copy(out=Yoth[TBs, 0:64], in_=py[TBs, 0:64]), reads=[py], writes=[Yoth])
                        Ycur, Yoth = Yoth, Ycur
                        if lev < nlev - 1:
                            p1 = psA()
                            kb.mm(p1, [(p1[TBs, 0:C], Lc[TBs, 0:C], LTc[TBs, 0:C])], reads=[Lc, LTc])
                            if lev < nlev - 2:
                                p2 = psA()
                                kb.mm(p2, [(p2[TBs, 0:C], LTc[TBs, 0:C], Lc[TBs, 0:C])], reads=[Lc, LTc])
                                kb.op("act", lambda p2=p2, Ln=Ln: S.copy(out=Ln[TBs, 0:C], in_=p2[TBs, 0:C]), reads=[p2], writes=[Ln])
                            kb.op("dve", lambda p1=p1, LTn=LTn: V.tensor_copy(out=LTn[TBs, 0:C], in_=p1[TBs, 0:C]), reads=[p1], writes=[LTn])
                            LTc, LTn = LTn, LTc
                            Lc, Ln = Ln, Lc
                    U = Ycur
                    kb.mm(po, [(po[P, cc], B["Sb"][P, :], rt_), (po[P, cc], U[TBs, 0:64], B["Grb"][TBs, 0:C]),
                               (po[P, cc], B["Vt"][TBs, 0:64], B["Grk"][TBs, 0:C])],
                          reads=[B["Sb"], B["rt"], U, B["Grb"], B["Vt"], B["Grk"]])
                    pd_ = psA()
                    kb.mm(pd_, [(pd_[P, 0:64], B["bht"][TBs, 0:64], U[TBs, 0:64]), (pd_[P, 0:64], B["kht"][TBs, 0:64], B["Vt"][TBs, 0:64])],
                          reads=[B["bht"], U, B["kht"], B["Vt"]])
                    wc = Em[P, (ci + 1) * C - 1:(ci + 1) * C]
                    kb.op("dve", lambda pd_=pd_, wc=wc, P=P, ci=ci: V.scalar_tensor_tensor(out=st_ap(ci, P), in0=st_ap(ci, P), scalar=wc, in1=pd_[P, 0:64],
                                                                                     op0=ALU.mult, op1=ALU.add), reads=[ST, Em, pd_], writes=[ST])
            chk("bchunks")
            of = B["of"]
            kb.op("act", lambda: S.copy(out=B["sq"][:, 0:T], in_=po[:, 0:T]), reads=[po], writes=[B["sq"]])
            pm_ = psA()
            kb.mm(pm_, [(pm_[:, 0:T], bones_b[:, :], B["sq"][:, 0:T])], reads=[bones_b, B["sq"]])
            kb.op("act", lambda: S.copy(out=of[:, 0:T], in_=po[:, 0:T]), reads=[po], writes=[of])
            kb.op("dve", lambda: V.scalar_tensor_tensor(out=of[:, 0:T], in0=pm_[:, 0:T], scalar=-1.0 / 64, in1=of[:, 0:T], op0=ALU.mult, op1=ALU.add),
                  reads=[pm_, of], writes=[of])
            kb.op("act", lambda: S.activation(out=B["sq"][:, 0:T], in_=of[:, 0:T], func=AF.Square), reads=[of], writes=[B["sq"]])
            pv_ = psA()
            kb.mm(pv_, [(pv_[:, 0:T], bones_b[:, :], B["sq"][:, 0:T])], reads=[bones_b, B["sq"]])
            kb.op("dve", lambda: V.tensor_scalar(out=t1[:, 0:T], in0=pv_[:, 0:T], scalar1=1.0 / 64, scalar2=64e-5, op0=ALU.mult, op1=ALU.add),
                  reads=[pv_], writes=[t1])
            kb.op("act", lambda: S.activation(out=t1[:, 0:T], in_=t1[:, 0:T], func=AF.Sqrt), reads=[t1], writes=[t1])
            kb.op("dve", lambda: V.reciprocal(out=t1[:, 0:T], in_=t1[:, 0:T]), reads=[t1], writes=[t1])
            kb.op("dve", lambda: V.tensor_tensor(out=of[:, 0:T], in0=of[:, 0:T], in1=t1[:, 0:T], op=ALU.mult), reads=[of, t1], writes=[of])
            kb.op("dve", lambda: V.tensor_scalar(out=of[:, 0:T], in0=of[:, 0:T], scalar1=pvp[:, 5, pr:pr + 1], scalar2=pvp[:, 6, pr:pr + 1],
                                                 op0=ALU.mult, op1=ALU.add), reads=[of, pvp], writes=[of])
            kb.op("dve", lambda: V.tensor_tensor(out=of[:, 0:T], in0=of[:, 0:T], in1=bon[:, 0:T], op=ALU.add), reads=[of, bon], writes=[of])
            kb.op("dve", lambda: V.tensor_tensor(out=og[:, pr, og_cols], in0=of[:, 0:T], in1=og[:, pr, og_cols], op=ALU.mult), reads=[of, og], writes=[og])

        def rwkv_layer_tile(l, j, ntok, ti):
            B = BT_
            sample = ntok < 128
            dlt = B["dlt"]
            if not sample:
                for half in range(2):
                    c0 = half * 256
                    kb.op("dve", lambda c0=c0: V.tensor_tensor(out=dlt[:, :, 1:256], in0=hT[:, :, c0:c0 + 255], in1=hT[:, :, c0 + 1:c0 + 256],
                                                               op=ALU.subtract), reads=[hT], writes=[dlt])
                    if half == 0:
                        kb.op("dve", lambda: V.tensor_tensor(out=dlt[:, :, 0], in0=B["hlast"][:, :], in1=hT[:, :, 0], op=ALU.subtract),
                              reads=[hT, B["hlast"]], writes=[dlt])
                    else:
                        kb.op("dve", lambda: V.tensor_tensor(out=dlt[:, :, 0], in0=hT[:, :, 255], in1=hT[:, :, 256], op=ALU.subtract),
                              reads=[hT], writes=[dlt])
                    rwkv_mix_inputs(j, c0, 256, None)
                    chk("bmix")
                    for pr in range(32):
                        if pr == 1:
                            chk("bpair")
                        rwkv_pair(j, pr, 256, 64, slice(half * 256, (half + 1) * 256), B["ST"], lambda ci, P, pr=pr: B["ST"][P, pr, :])
                kb.op("dve", lambda: V.tensor_copy(out=B["hlast"][:, :], in_=hT[:, :, 511]), reads=[hT], writes=[B["hlast"]])
            else:
                shf = B["t1"]
                kb.dma("sp", dlt[:, :, 256:256 + NSQ], I["state_shift"][j].rearrange("s (k p) -> p k s", p=128), writes=[dlt],
                       allow_slow_non_contiguous=True) if False else None
                stg = B["t2"]
                for sq in range(NSQ):
                    kb.dma("sp", stg[:, sq * KC:(sq + 1) * KC], I["state_shift"][j, sq].rearrange("(k p) -> p k", p=128),
                           writes=[stg], allow_slow_non_contiguous=True)
                stg3 = stg.h[:, 0:KC * NSQ].rearrange("p (s k) -> p k s", k=KC)
                h4 = hT.h[:, :, 0:NST].rearrange("p k (s t) -> p k s t", t=8)
                d4 = dlt.h[:, :, 0:NST].rearrange("p k (s t) -> p k s t", t=8)
                kb.op("dve", lambda: V.tensor_tensor(out=d4[:, :, :, 1:8], in0=h4[:, :, :, 0:7], in1=h4[:, :, :, 1:8], op=ALU.subtract),
                      reads=[hT], writes=[dlt])
                kb.op("dve", lambda: V.tensor_tensor(out=d4[:, :, :, 0], in0=stg3, in1=h4[:, :, :, 0], op=ALU.subtract),
                      reads=[hT, stg], writes=[dlt])
                rwkv_mix_inputs(j, 0, NST, None)
                Sin, STp, Sout = B["Sin"], B["STs"], B["Sout"]
                Sin_v = Sin.h[:, 0:NSQ * 64].rearrange("p (s i) -> p s i", s=NSQ)
                Sout_v = Sout.h[:, 0:NSQ * 64].rearrange("p (s i) -> p s i", s=NSQ)
                for pr in range(32):
                    kb.dma("sp", Sin_v, I["state_wkv"][j, :, 2 * pr:2 * pr + 2].rearrange("s h i j -> (h i) s j"), writes=[Sin])
                    for hp in range(2):
                        pt = psA()
                        kb.mm_multi(pt, [[(pt[hp * 64:(hp + 1) * 64, sq * 64:(sq + 1) * 64], Sin_v[hp * 64:(hp + 1) * 64, sq, :],
                                           ident_f[hp * 64:(hp + 1) * 64, hp * 64:(hp + 1) * 64])] for sq in range(NSQ)],
                                    reads=[Sin, ident_f])
                        kb.op("dve", lambda pt=pt, hp=hp: V.tensor_copy(out=STp[hp * 64:(hp + 1) * 64, 0:NSQ, :],
                                                                        in_=pt[hp * 64:(hp + 1) * 64, 0:NSQ * 64].rearrange("p (s i) -> p s i", s=NSQ)),
                              reads=[pt], writes=[STp])
                    rwkv_pair(j, pr, NST, 8, slice(0, NST), STp, lambda ci, P: STp[P, ci, :])
                    for hp in range(2):
                        pt = psA()
                        kb.mm_multi(pt, [[(pt[hp * 64:(hp + 1) * 64, sq * 64:(sq + 1) * 64], STp[hp * 64:(hp + 1) * 64, sq, :],
                                           ident_f[hp * 64:(hp + 1) * 64, hp * 64:(hp + 1) * 64])] for sq in range(NSQ)],
                                    reads=[STp, ident_f])
                        kb.op("dve", lambda pt=pt, hp=hp: V.tensor_copy(out=Sout_v[hp * 64:(hp + 1) * 64],
                                                                        in_=pt[hp * 64:(hp + 1) * 64, 0:NSQ * 64].rearrange("p (s i) -> p s i", s=NSQ)),
                              reads=[pt], writes=[Sout])
                    kb.dma("sp", O["wkvs"][j, :, 2 * pr:2 * pr + 2].rearrange("s h i j -> (h i) s j"), Sout_v, reads=[Sout], writes=[])
                    kws_toks.append(kb_last_tok("sp"))
                shf = B["shf"]
                kb.op("dve", lambda: V.tensor_copy(out=shf[:, :, 0:NSQ], in_=h4[:, :, :, 7]), reads=[hT], writes=[shf])
                for sq in range(NSQ):
                    kb.dma("sp", O["shs"][j, sq].rearrange("(k p) -> p k", p=128), shf[:, :, sq], reads=[shf], writes=[],
                           allow_slow_non_contiguous=True)
                    kws_toks.append(kb_last_tok("sp"))

        def rwkv_outputs_prompt(j):
            B = BT_
            ST, Sout, shf = B["ST"], B["Sout"], B["shf"]
            Sout_v = Sout.h[:, 0:256].rearrange("p (s i) -> p s i", s=4)
            for p4 in range(0, 32, 4):
                for hp in range(2):
                    pt = psA()
                    kb.mm_multi(pt, [[(pt[hp * 64:(hp + 1) * 64, q * 64:(q + 1) * 64], ST[hp * 64:(hp + 1) * 64, p4 + q, :],
                                       ident_f[hp * 64:(hp + 1) * 64, hp * 64:(hp + 1) * 64])] for q in range(4)],
                                reads=[ST, ident_f])
                    kb.op("dve", lambda pt=pt, hp=hp: V.tensor_copy(out=Sout_v[hp * 64:(hp + 1) * 64],
                                                                    in_=pt[hp * 64:(hp + 1) * 64, 0:256].rearrange("p (s i) -> p s i", s=4)),
                          reads=[pt], writes=[Sout])
                kb.dma("sp", O["wkvp"][j, 2 * p4:2 * p4 + 8].rearrange("(q h) i j -> (h i) q j", h=2), Sout_v, reads=[Sout], writes=[])
                kws_toks.append(kb_last_tok("sp"))
            kb.op("dve", lambda: V.tensor_copy(out=shf[:, :, 0], in_=hT[:, :, 511]), reads=[hT], writes=[shf])
            kb.dma("sp", O["shp"][j].rearrange("(k p) -> p k", p=128), shf[:, :, 0], reads=[shf], writes=[],
                   allow_slow_non_contiguous=True)
            kws_toks.append(kb_last_tok("sp"))


        CT_ = {}

        def gla_alloc(les, j):
            def sb(shape, dt, name):
                return kb.sb(shape, dt, name, es=les)
            Cc = CT_
            Cc["S"] = sb([128, 16, 512], F32, "glaS")
            kb.op("dve", lambda: V.memset(Cc["S"][:], 0.0), writes=[Cc["S"]])
            Cc["Ss"] = sb([128, 2, 512], F32, "glaSs")
            Cc["Sb"] = sb([128, 2, 512], BF16, "glaSb")
            Cc["wgk"] = sb([16, 256], F32, "wgk")
            Cc["bgk"] = sb([128, 16], F32, "bgk")
            kb.dma("sp", Cc["bgk"][:], I["b_gk_c"][j].rearrange("(c p) -> p c", p=128), writes=[Cc["bgk"]], allow_slow_non_contiguous=True)
            Cc["og_g"] = sb([128, 4], F32, "og_g")
            kb.dma("sp", Cc["og_g"][:], I["o_norm_g_c"][j].rearrange("(c p) -> p c", p=128), writes=[Cc["og_g"]], allow_slow_non_contiguous=True)
            Cc["gkl"] = sb([16, 512], F32, "gkl")
            Cc["wgl"] = sb([128, KC, 16], BF16, "wgl")
            kb.dma("pool", Cc["wgl"][:], I["w_in_c"][j].rearrange("(k p) n -> p k n", p=128)[:, :, 12288:12304], writes=[Cc["wgl"]])
            Cc["mk"] = sb([64, 64], F32, "mkc")
            kb.dma("sp", Cc["mk"][:], I["masks"][2], writes=[Cc["mk"]])
            for nm in ("q", "k", "b", "e1", "e2"):
                Cc[nm] = sb([128, 2, 512], F32, "c_" + nm)
            for nm in ("qt", "kt", "kh"):
                Cc[nm] = sb([128, 2, 512], BF16, "cb_" + nm)
            Cc["of"] = sb([128, 4, 512], F32, "c_of")
            Cc["osq"] = sb([128, 4, 512], BF16, "c_osq")
            Cc["vt"] = sb([64, 8, 512], BF16, "c_vt")
            Cc["att"] = sb([64, 64], BF16, "c_att")
            Cc["kht"] = sb([64, 256], BF16, "c_kht")
            Cc["rs"] = sb([128, 512], F32, "c_rs")
            Cc["ones64"] = sb([128, 64], F32, "c_ones64")
            kb.op("dve", lambda: V.memset(Cc["ones64"][:], 1.0), writes=[Cc["ones64"]])

        def gla_head(j, h, ntok, C, state_fn, og_cols):
            Cc = CT_
            w_in = I["w_in_c"][j]
            nch = ntok // C
            q, k, b, e1, e2, qt, kt, khh, of, vt = (Cc[n] for n in ("q", "k", "b", "e1", "e2", "qt", "kt", "kh", "of", "vt"))
            for (dst, col0) in ((q, h * 256), (k, 2048 + h * 256)):
                wt, view = load_w_in(w_in, col0, 256)
                for dc in range(2):
                    pq = psA()
                    kb.mm(pq, [(pq[:, 0:ntok], view[:, kc, dc * 128:(dc + 1) * 128], hT[:, kc, 0:ntok]) for kc in range(KC)], reads=[wt, hT])
                    kb.op("act", lambda pq=pq, dst=dst, dc=dc: S.copy(out=dst[:, dc, 0:ntok], in_=pq[:, 0:ntok]), reads=[pq], writes=[dst])
            kb.dma("sp", Cc["wgk"][:], I["w_gk_up_c"][j][:, h * 256:(h + 1) * 256], writes=[Cc["wgk"]])
            for dc in range(2):
                pg = psA()
                cch = 2 * h + dc
                kb.mm(pg, [(pg[:, 0:ntok], Cc["wgk"][:, dc * 128:(dc + 1) * 128], Cc["gkl"][:, 0:ntok])], reads=[Cc["wgk"], Cc["gkl"]])
                kb.op("act", lambda pg=pg, dc=dc, cch=cch: S.activation(out=e1[:, dc, 0:ntok], in_=pg[:, 0:ntok], func=AF.Sigmoid,
                                                                       bias=Cc["bgk"][:, cch:cch + 1]), reads=[pg, Cc["bgk"]], writes=[e1])
            kb.op("act", lambda: S.activation(out=e1[:, :, 0:ntok], in_=e1[:, :, 0:ntok], func=AF.Ln), reads=[e1], writes=[e1])
            for dc in range(2):
                for ci in range(nch):
                    kb.op("dve", lambda dc=dc, ci=ci: V.tensor_tensor_scan(out=b[:, dc, ci * C:(ci + 1) * C], data0=Cc["ones64"][:, 0:C],
                                                                           data1=e1[:, dc, ci * C:(ci + 1) * C], initial=0.0,
                                                                           op0=ALU.mult, op1=ALU.add), reads=[e1, Cc["ones64"]], writes=[b])
            kb.op("act", lambda: S.activation(out=e1[:, :, 0:ntok], in_=b[:, :, 0:ntok], func=AF.Exp, scale=1.0 / 16), reads=[b], writes=[e1])
            kb.op("act", lambda: S.activation(out=e2[:, :, 0:ntok], in_=b[:, :, 0:ntok], func=AF.Exp, scale=-1.0 / 16), reads=[b], writes=[e2])
            kb.op("dve", lambda: V.scalar_tensor_tensor(out=qt[:, :, 0:ntok], in0=q[:, :, 0:ntok], scalar=1.0 / 16, in1=e1[:, :, 0:ntok],
                                                        op0=ALU.mult, op1=ALU.mult), reads=[q, e1], writes=[qt])
            kb.op("dve", lambda: V.tensor_tensor(out=kt[:, :, 0:ntok], in0=k[:, :, 0:ntok], in1=e2[:, :, 0:ntok], op=ALU.mult), reads=[k, e2], writes=[kt])
            for dc in range(2):
                for ci in range(nch):
                    kb.op("dve", lambda dc=dc, ci=ci: V.tensor_scalar(out=khh[:, dc, ci * C:(ci + 1) * C], in0=kt[:, dc, ci * C:(ci + 1) * C],
                                                                      scalar1=e1[:, dc, (ci + 1) * C - 1:(ci + 1) * C], scalar2=None, op0=ALU.mult),
                          reads=[kt, e1], writes=[khh])
            wt, view = load_w_in(w_in, 4096 + h * 512, 512)
            for ci in range(nch):
                pv = psA()
                kb.mm(pv, [(pv[0:C, :], hT[:, kc, ci * C:(ci + 1) * C], view[:, kc, :]) for kc in range(KC)], reads=[wt, hT])
                kb.op("act", lambda ci=ci, pv=pv: S.copy(out=vt[0:C, ci, :], in_=pv[0:C, :]), reads=[pv], writes=[vt])
            wt, view = load_w_in(w_in, 8192 + h * 512, 512)
            for ec in range(4):
                pg = psA()
                kb.mm(pg, [(pg[:, 0:ntok], view[:, kc, ec * 128:(ec + 1) * 128], hT[:, kc, 0:ntok]) for kc in range(KC)], reads=[wt, hT])
                kb.op("act", lambda ec=ec, pg=pg: S.activation(out=og[:, h * 4 + ec, og_cols], in_=pg[:, 0:ntok], func=AF.Silu),
                      reads=[pg], writes=[og])
            for ci in range(nch):
                cc = slice(ci * C, (ci + 1) * C)
                St, Sap = state_fn(ci, "load")
                kb.op("act", lambda Sap=Sap: S.copy(out=Cc["Sb"][:], in_=Sap), reads=[St], writes=[Cc["Sb"]])
                pa = psA()
                kb.mm(pa, [(pa[0:C, 0:C], kt[:, dc, cc], qt[:, dc, cc]) for dc in range(2)], reads=[kt, qt])
                kb.op("dve", lambda pa=pa: V.tensor_tensor(out=Cc["att"][0:C, 0:C], in0=pa[0:C, 0:C], in1=Cc["mk"][0:C, 0:C], op=ALU.mult),
                      reads=[pa, Cc["mk"]], writes=[Cc["att"]])
                po = psA()
                groups = []
                for ec in range(4):
                    o_ap = po[:, ec * C:(ec + 1) * C]
                    groups.append([(o_ap, Cc["Sb"][:, dc, ec * 128:(ec + 1) * 128], qt[:, dc, cc]) for dc in range(2)]
                                  + [(o_ap, vt[0:C, ci, ec * 128:(ec + 1) * 128], Cc["att"][0:C, 0:C])])
                kb.mm_multi(po, groups, reads=[Cc["Sb"], qt, vt, Cc["att"]])
                kb.op("act", lambda po=po, cc=cc: S.copy(out=of[:, :, cc], in_=po[:, 0:4 * C].rearrange("p (e t) -> p e t", e=4)),
                      reads=[po], writes=[of])
                pt = psB()
                kb.transpose(pt, [(pt[0:C, dc * 128:(dc + 1) * 128], khh[:, dc, cc], ident_b[:, :]) for dc in range(2)], reads=[khh, ident_b])
                kb.op("act", lambda pt=pt: S.copy(out=Cc["kht"][0:C, :], in_=pt[0:C, 0:256]), reads=[pt], writes=[Cc["kht"]])
                for dc in range(2):
                    ps_ = psA()
                    kb.mm(ps_, [(ps_[:, :], Cc["kht"][0:C, dc * 128:(dc + 1) * 128], vt[0:C, ci, :])], reads=[Cc["kht"], vt])
                    St, Sap = state_fn(ci, "store")
                    kb.op("dve", lambda ps_=ps_, Sap=Sap, dc=dc, ci=ci: V.scalar_tensor_tensor(
                        out=Sap[:, dc, :], in0=Sap[:, dc, :], scalar=e1[:, dc, (ci + 1) * C - 1:(ci + 1) * C], in1=ps_[:, :],
                        op0=ALU.mult, op1=ALU.add), reads=[St, e1, ps_], writes=[St])
                state_fn(ci, "done")
            osq, rs = Cc["osq"], Cc["rs"]
            kb.op("act", lambda: S.activation(out=osq[:, :, 0:ntok], in_=of[:, :, 0:ntok], func=AF.Square), reads=[of], writes=[osq])
            pn = psA()
            kb.mm(pn, [(pn[:, 0:ntok], ones_b[:, :], osq[:, ec, 0:ntok]) for ec in range(4)], reads=[ones_b, osq])
            kb.op("dve", lambda: V.tensor_scalar(out=rs[:, 0:ntok], in0=pn[:, 0:ntok], scalar1=1.0 / 512, scalar2=NORM_EPS, op0=ALU.mult, op1=ALU.add),
                  reads=[pn], writes=[rs])
            kb.op("act", lambda: S.activation(out=rs[:, 0:ntok], in_=rs[:, 0:ntok], func=AF.Sqrt), reads=[rs], writes=[rs])
            kb.op("dve", lambda: V.reciprocal(out=rs[:, 0:ntok], in_=rs[:, 0:ntok]), reads=[rs], writes=[rs])
            for ec in range(4):
                kb.op("dve", lambda ec=ec: V.scalar_tensor_tensor(out=of[:, ec, 0:ntok], in0=of[:, ec, 0:ntok], scalar=Cc["og_g"][:, ec:ec + 1],
                                                                   in1=rs[:, 0:ntok], op0=ALU.mult, op1=ALU.mult), reads=[of, Cc["og_g"], rs], writes=[of])
                kb.op("dve", lambda ec=ec: V.tensor_tensor(out=og[:, h * 4 + ec, og_cols], in0=of[:, ec, 0:ntok], in1=og[:, h * 4 + ec, og_cols],
                                                            op=ALU.mult), reads=[of, og], writes=[og])

        def gla_layer_tile(l, j, ntok, ti):
            Cc = CT_
            sample = ntok < 128
            pl = psA()
            kb.mm(pl, [(pl[0:16, 0:ntok], Cc["wgl"][:, kc, :], hT[:, kc, 0:ntok]) for kc in range(KC)], reads=[Cc["wgl"], hT])
            kb.op("act", lambda: S.copy(out=Cc["gkl"][:, 0:ntok], in_=pl[0:16, 0:ntok]), reads=[pl], writes=[Cc["gkl"]])
            for h in range(8):
                if not sample:
                    gla_head(j, h, ntok, 64, lambda ci, what, h=h: (Cc["S"], Cc["S"][:, 2 * h:2 * h + 2, :]), slice(0, ntok))
                else:
                    def sfn(ci, what, h=h):
                        if what == "load":
                            kb.dma("sp", Cc["Ss"][:], I["state_gla"][j, ci, h].rearrange("(c p) e -> p c e", p=128), writes=[Cc["Ss"]])
                        elif what == "done":
                            kb.dma("sp", O["glas"][j, ci, h].rearrange("(c p) e -> p c e", p=128), Cc["Ss"][:], reads=[Cc["Ss"]], writes=[])
                            kws_toks.append(kb_last_tok("sp"))
                        return (Cc["Ss"], Cc["Ss"][:, :, :])
                    gla_head(j, h, ntok, 8, sfn, slice(0, ntok))

        def gla_outputs_prompt(j):
            Cc = CT_
            for h in range(8):
                kb.dma("sp", O["glap"][j, h].rearrange("(c p) e -> p c e", p=128), Cc["S"][:, 2 * h:2 * h + 2, :], reads=[Cc["S"]], writes=[])
                kws_toks.append(kb_last_tok("sp"))

        try:
            cnt_kind = {0: 0, 1: 0, 2: 0}
            for l, kind in enumerate(layers):
                j = cnt_kind[kind]
                cnt_kind[kind] += 1
                with contextlib.ExitStack() as les:
                    if kind == 0:
                        attn_alloc(les)
                    elif kind == 1:
                        rwkv_alloc(les, j)
                    else:
                        gla_alloc(les, j)
                    for ti in range(NT + 1):
                        sample = ti == NT
                        ntok = NST if sample else 512
                        if sample:
                            src = I["xs"] if l == 0 else O["ys"]
                            src_t = [] if l == 0 else [ys_t]
                            dst, dst_t = O["ys"], ys_t
                        else:
                            src = (I["xp"] if l == 0 else O["yp"])[ti * 512:(ti + 1) * 512, :]
                            src_t = [] if l == 0 else [yp_t[ti]]
                            dst, dst_t = O["yp"][ti * 512:(ti + 1) * 512, :], yp_t[ti]
                        chk("alloc")
                        load_norm(l, src, ntok, src_t)
                        chk("norm")
                        if kind == 0:
                            if sample:
                                attn_sample_prep(j)
                            attn_layer_tile(l, j, ntok, ti, last=(ti == NT - 1))
                            chk("atile")
                            if ti == NT - 1:
                                attn_outputs_prompt(j)
                            if sample:
                                attn_outputs_sample(j)
                            out_proj(I["w_out_a"][j], ntok, dst, dst_t)
                        elif kind == 1:
                            rwkv_layer_tile(l, j, ntok, ti)
                            if ti == NT - 1:
                                rwkv_outputs_prompt(j)
                            out_proj(I["w_out_b"][j], ntok, dst, dst_t)
                        else:
                            gla_layer_tile(l, j, ntok, ti)
                            if ti == NT - 1:
                                gla_outputs_prompt(j)
                            out_proj(I["w_out_c"][j], ntok, dst, dst_t)
                    kb.barrier()
        except _Stop:
            pass
        kb.finish([t.w for t in out_regions] + kws_toks)
    return nc


_PROG = {}


def _get_prog():
    if "nc" not in _PROG:
        _PROG["nc"] = build_program(dict(TP=4096, layers=[0, 1, 2, 0]))
    return _PROG["nc"]


def kernel(**inputs):
    f32 = np.float32
    A = {k: np.ascontiguousarray(np.asarray(v, dtype=f32)) for k, v in inputs.items()}
    n = 8
    consts = host_consts()
    shared = dict(consts)
    for nm in ("norm_g", "rel_bias", "w_in_a", "q_norm_g", "k_norm_g", "sinks", "w_out_a", "mu_b", "w_rkvg_b", "w_lora_down_b",
               "w_lora_up_b", "w0_b", "a0_b", "k_k_b", "k_a_b", "ln_x_g_b", "ln_x_b_b", "w_out_b", "w_in_c", "w_gk_up_c",
               "b_gk_c", "o_norm_g_c", "w_out_c"):
        shared[nm] = A[nm]
    shared["r_k_b"] = A["r_k_b"].reshape(A["r_k_b"].shape[0], -1)
    zeros_p = np.zeros((4096, D), f32)
    in_maps = []
    for c in range(n):
        m = dict(shared)
        m["xp"] = A["x_prompt"][c] if c < 2 else zeros_p
        sl = slice(4 * c, 4 * c + 4)
        m["xs"] = A["x_sample"][sl].reshape(32, D)
        m["cache_k"] = np.ascontiguousarray(A["cache_k_win"][:, sl].reshape(2, 4, 128, 512))
        m["cache_v"] = np.ascontiguousarray(A["cache_v_win"][:, sl].reshape(2, 4, 128, 512))
        m["state_wkv"] = np.ascontiguousarray(A["state_wkv"][:, sl])
        m["state_shift"] = np.ascontiguousarray(A["state_shift"][:, sl])
        m["state_gla"] = np.ascontiguousarray(A["state_gla"][:, sl])
        in_maps.append(m)
    nc = _get_prog()
    res = run_bass_kernel_spmd(nc, in_maps, core_ids=list(range(n)))
    R = res.results

    def cat_s(key, shape_tail, lead=True):
        return np.concatenate([np.asarray(R[c][key], f32) for c in range(n)], axis=1)

    y_prompt = np.stack([np.asarray(R[c]["yp"], f32) for c in range(2)], axis=0)
    y_sample = np.concatenate([np.asarray(R[c]["ys"], f32).reshape(4, 8, D) for c in range(n)], axis=0)
    kwp = np.stack([np.asarray(R[c]["kwp"], f32) for c in range(2)], axis=1).reshape(2, 2, 128, 8, 64)
    vwp = np.stack([np.asarray(R[c]["vwp"], f32) for c in range(2)], axis=1).reshape(2, 2, 128, 8, 64)
    kws = cat_s("kws", None).reshape(2, 32, 128, 8, 64)
    vws = cat_s("vws", None).reshape(2, 32, 128, 8, 64)
    wkvp = np.stack([np.asarray(R[c]["wkvp"], f32) for c in range(2)], axis=1)
    shp = np.stack([np.asarray(R[c]["shp"], f32) for c in range(2)], axis=1)
    wkvs = cat_s("wkvs", None)
    shs = cat_s("shs", None)
    glap = np.stack([np.asarray(R[c]["glap"], f32) for c in range(2)], axis=1)
    glas = cat_s("glas", None)
    return (y_prompt, y_sample, kwp, vwp, kws, vws, wkvp, shp, wkvs, shs, glap, glas)
```

```python
import contextlib
import math
import numpy as np
import concourse.bass as bass
import concourse.mybir as mybir
from concourse.bass_utils import run_bass_kernel_spmd

F32 = mybir.dt.float32
BF16 = mybir.dt.bfloat16
ALU = mybir.AluOpType
AF = mybir.ActivationFunctionType
AX = mybir.AxisListType

D = 2048
DI = 4096
KC = D // 128
IC = DI // 128
WINDOW = 128
NEG = -30000.0
NORM_EPS = 1e-6


class _Stop(Exception):
    pass


class TT:
    def __init__(self, h, name=""):
        self.h = h
        self.name = name
        self.w = None
        self.r = {}
        self.psum = False
        self.pending = False

    def __getitem__(self, idx):
        return self.h[idx]


class AliasTT:
    def __init__(self, base, view):
        self.base = base
        self.h = view
        self.name = base.name + "_alias"
        self.psum = base.psum

    def __getitem__(self, idx):
        return self.h[idx]

    @property
    def w(self):
        return self.base.w

    @w.setter
    def w(self, v):
        self.base.w = v

    @property
    def r(self):
        return self.base.r

    @r.setter
    def r(self, v):
        self.base.r = v


class KB:
    NDS = 12
    LIMIT = 10 ** 9

    def __init__(self, nc, es):
        self.nc = nc
        self.es = es
        self.eng = {"pe": nc.tensor, "act": nc.scalar, "dve": nc.vector, "pool": nc.gpsimd, "sp": nc.sync}
        self.sets = [{}, {}]
        self.phase = {}
        self.cnt = {}
        self.seen = {}
        self.dj = {}
        self.ep = 0
        self.nbar = 0
        for si in range(2):
            for e in self.eng:
                self.sets[si][e] = es.enter_context(nc.semaphore("c%d_%s" % (si, e)))
            for q in ("sp", "pool"):
                for j in range(self.NDS):
                    self.sets[si][("d", q, j)] = es.enter_context(nc.semaphore("d%d_%s_%d" % (si, q, j)))
        for e in self.eng:
            self.phase[e] = es.enter_context(nc.semaphore("ph_" + e))
            self.cnt[e] = 0
            self.seen[e] = {}
        for q in ("sp", "pool"):
            self.dj[q] = 0
        self.nalloc = 0
        self.dead = False
        self.last_dma_tok = None

    def semh(self, k):
        return self.sets[self.ep % 2][k]

    def sb(self, shape, dt, name=None, es=None):
        self.nalloc += 1
        name = (name or "t") + "_%d" % self.nalloc
        return TT((es or self.es).enter_context(self.nc.sbuf_tensor(name, list(shape), dt)), name)

    def ps(self, shape, dt, name=None, es=None):
        self.nalloc += 1
        name = (name or "p") + "_%d" % self.nalloc
        t = TT((es or self.es).enter_context(self.nc.psum_tensor(name, list(shape), dt)), name)
        t.psum = True
        return t

    def barrier(self):
        if self.dead:
            return
        toks = {}
        for e in self.eng:
            if self.cnt[e] > 0:
                toks[e] = self.cnt[e]
        for q in ("sp", "pool"):
            for j in range(self.NDS):
                n = (self.dj[q] - j + self.NDS - 1) // self.NDS if self.dj[q] > j else 0
                if n > 0:
                    toks[("d", q, j)] = 16 * n
        for e in self.eng:
            for k, v in toks.items():
                if k == e:
                    continue
                self._wait(e, (k, v, self.ep))

    def epoch_barrier(self):
        self.barrier()
        nxt = (self.ep + 1) % 2
        self.nbar += 1
        for e in self.eng:
            self.eng[e].sem_clear(self.sets[nxt][e])
            if e in ("sp", "pool"):
                for j in range(self.NDS):
                    self.eng[e].sem_clear(self.sets[nxt][("d", e, j)])
            self.eng[e].sem_inc(self.phase[e], 1)
        for e in self.eng:
            for f in self.eng:
                if f != e:
                    self.eng[e].wait_ge(self.phase[f], self.nbar)
        self.ep += 1
        for e in self.eng:
            self.cnt[e] = 0
            self.seen[e] = {}
        for q in ("sp", "pool"):
            self.dj[q] = 0

    def maybe_epoch(self):
        if max(self.cnt.values()) >= self.LIMIT or max(self.dj.values()) >= self.LIMIT // 2:
            self.epoch_barrier()

    def _wait(self, E, tok):
        if self.dead or tok is None:
            return
        k, v, ep = tok
        if ep < self.ep:
            return
        if self.seen[E].get(k, 0) >= v:
            return
        self.eng[E].wait_ge(self.semh(k), v)
        self.seen[E][k] = v

    def _deps(self, E, reads, writes):
        need = {}

        def add(tok):
            if tok is None:
                return
            k, v, ep = tok
            if ep < self.ep:
                return
            if need.get(k, 0) < v:
                need[k] = v

        for t in reads:
            add(t.w)
            if t.psum:
                for rk, tok in t.r.items():
                    if rk != E:
                        add(tok)
        for t in writes:
            add(t.w)
            for tok in t.r.values():
                add(tok)
        for k, v in need.items():
            if k == "pe" and E == "pe":
                continue
            self._wait(E, (k, v, self.ep))

    def _done(self, E, ins, reads, writes):
        if E == "pe":
            for t in writes:
                t.pending = True
        else:
            for t in reads:
                if t.psum:
                    t.pending = False
        self.cnt[E] += 1
        ins.then_inc(self.semh(E), 1)
        tok = (E, self.cnt[E], self.ep)
        for t in writes:
            t.w = tok
            t.r = {}
        for t in reads:
            if t not in writes:
                t.r[E] = tok
        return tok

    def op(self, E, fn, reads=(), writes=()):
        if self.dead:
            return None
        self.maybe_epoch()
        self._deps(E, reads, writes)
        ins = fn()
        return self._done(E, ins, reads, writes)

    def mm(self, out_t, mms, reads):
        return self.mm_multi(out_t, [mms], reads)

    def mm_multi(self, out_t, groups, reads):
        if self.dead:
            return None
        self.maybe_epoch()
        self._deps("pe", reads, [out_t])
        ins = None
        for mms in groups:
            n = len(mms)
            for i, (o, l, r) in enumerate(mms):
                ins = self.nc.tensor.matmul(o, lhsT=l, rhs=r, start=(i == 0), stop=(i == n - 1))
        return self._done("pe", ins, reads, [out_t])

    def transpose(self, out_t, items, reads):
        if self.dead:
            return None
        self.maybe_epoch()
        self._deps("pe", reads, [out_t])
        ins = None
        for (o, i, idn) in items:
            ins = self.nc.tensor.transpose(o, i, idn)
        return self._done("pe", ins, reads, [out_t])

    def dma(self, q, out_ap, in_ap, reads=(), writes=(), **kw):
        if self.dead:
            return None
        self.maybe_epoch()
        self._deps(q, reads, writes)
        i = self.dj[q]
        self.dj[q] += 1
        j = i % self.NDS
        rnd = i // self.NDS
        key = ("d", q, j)
        if rnd > 0:
            self._wait(q, (key, 16 * rnd, self.ep))
        self.eng[q].dma_start(out=out_ap, in_=in_ap, **kw).then_inc(self.semh(key), 16)
        tok = (key, 16 * (rnd + 1), self.ep)
        for t in writes:
            t.w = tok
            t.r = {}
        for t in reads:
            t.r[key] = tok
        self.last_dma_tok = tok
        return tok

    def finish(self, toks):
        self.dead = False
        for tok in toks:
            if tok is not None:
                self._wait("sp", tok)


def t5_bucket_np(d):
    d = np.maximum(d, 0)
    large = 16 + (np.log(np.maximum(d, 1).astype(np.float32) / np.float32(16)) / np.float32(math.log(128 / 16))
                  * np.float32(16)).astype(np.int32)
    return np.where(d < 16, d, np.minimum(large, 31))


def host_consts():
    c = {}
    c["ident"] = np.eye(128, dtype=np.float32)
    bo = np.zeros((128, 128), np.float32)
    bo[:64, :64] = 1.0
    bo[64:, 64:] = 1.0
    c["blockones"] = bo
    oh = np.zeros((33, 384), np.float32)
    for m in range(384):
        dist = m - 128
        if 0 <= dist < 128:
            oh[int(t5_bucket_np(np.array(dist))), m] = 1.0
        else:
            oh[32, m] = NEG
    c["bucket_oh"] = oh
    t = np.arange(64)
    mk = np.zeros((3, 64, 64), np.float32)
    mk[0] = (t[:, None] < t[None, :])
    mk[1] = (t[None, :] < t[:, None])
    mk[2] = (t[:, None] <= t[None, :])
    c["masks"] = mk
    return c


def build_program(cfg):
    TP = cfg["TP"]
    layers = cfg["layers"]
    NSQ = 4
    NST = NSQ * 8
    nA = sum(1 for k in layers if k == 0)
    nB = sum(1 for k in layers if k == 1)
    nC = sum(1 for k in layers if k == 2)
    NL = len(layers)
    assert TP % 512 == 0
    NT = TP // 512

    nc = bass.Bass("TRN2", target_bir_lowering=False)

    def din(name, shape, dt=F32):
        return nc.dram_tensor(name, list(shape), dt, kind="ExternalInput").ap()

    def dout(name, shape, dt=F32):
        return nc.dram_tensor(name, list(shape), dt, kind="ExternalOutput").ap()

    I = {}
    I["xp"] = din("xp", [TP, D])
    I["xs"] = din("xs", [NST, D])
    I["norm_g"] = din("norm_g", [NL, D])
    I["ident"] = din("ident", [128, 128])
    I["blockones"] = din("blockones", [128, 128])
    I["bucket_oh"] = din("bucket_oh", [33, 384])
    if nA:
        A_IN = DI + 1024 + DI
        I["cache_k"] = din("cache_k", [nA, NSQ, 128, 512])
        I["cache_v"] = din("cache_v", [nA, NSQ, 128, 512])
        I["rel_bias"] = din("rel_bias", [32, 64])
        I["w_in_a"] = din("w_in_a", [nA, D, A_IN])
        I["q_norm_g"] = din("q_norm_g", [nA, 64])
        I["k_norm_g"] = din("k_norm_g", [nA, 64])
        I["sinks"] = din("sinks", [nA, 64])
        I["w_out_a"] = din("w_out_a", [nA, DI, D])
    if nB:
        I["state_wkv"] = din("state_wkv", [nB, NSQ, 64, 64, 64])
        I["state_shift"] = din("state_shift", [nB, NSQ, D])
        I["mu_b"] = din("mu_b", [nB, 6, D])
        I["w_rkvg_b"] = din("w_rkvg_b", [nB, 4, D, DI])
        I["w_lora_down_b"] = din("w_lora_down_b", [nB, 2, D, 96])
        I["w_lora_up_b"] = din("w_lora_up_b", [nB, 2, 96, DI])
        for nm in ("w0_b", "a0_b", "k_k_b", "k_a_b", "r_k_b", "ln_x_g_b", "ln_x_b_b"):
            I[nm] = din(nm, [nB, DI])
        I["w_out_b"] = din("w_out_b", [nB, DI, D])
        I["masks"] = din("masks", [3, 64, 64])
    if nC:
        C_IN = 2 * 2048 + 2 * DI + 16
        I["state_gla"] = din("state_gla", [nC, NSQ, 8, 256, 512])
        I["w_in_c"] = din("w_in_c", [nC, D, C_IN])
        I["w_gk_up_c"] = din("w_gk_up_c", [nC, 16, 2048])
        I["b_gk_c"] = din("b_gk_c", [nC, 2048])
        I["o_norm_g_c"] = din("o_norm_g_c", [nC, 512])
        I["w_out_c"] = din("w_out_c", [nC, DI, D])
        if "masks" not in I:
            I["masks"] = din("masks", [3, 64, 64])
    O = {}
    if nC:
        O["glap"] = dout("glap", [nC, 8, 256, 512])
        O["glas"] = dout("glas", [nC, NSQ, 8, 256, 512])
    if nB:
        O["wkvp"] = dout("wkvp", [nB, 64, 64, 64])
        O["shp"] = dout("shp", [nB, D])
        O["wkvs"] = dout("wkvs", [nB, NSQ, 64, 64, 64])
        O["shs"] = dout("shs", [nB, NSQ, D])
    O["yp"] = dout("yp", [TP, D])
    O["ys"] = dout("ys", [NST, D])
    if nA:
        O["kwp"] = dout("kwp", [nA, 128, 512])
        O["vwp"] = dout("vwp", [nA, 128, 512])
        O["kws"] = dout("kws", [nA, NSQ, 128, 512])
        O["vws"] = dout("vws", [nA, NSQ, 128, 512])
    zrep = nc.dram_tensor("zrep", [64, 128 * 384], F32, kind="Internal").ap() if nA else None

    es = contextlib.ExitStack()
    with es:
        kb = KB(nc, es)
        V = nc.vector
        S = nc.scalar
        G = nc.gpsimd
        out_regions = []

        def chk(stage):
            if cfg.get("stop") == stage:
                kb.dead = True

        yp_t = [TT(None, "yp%d" % t) for t in range(NT)]
        ys_t = TT(None, "ys")
        out_regions += yp_t + [ys_t]

        ident_f = kb.sb([128, 128], F32, "ident_f")
        ident_b = kb.sb([128, 128], BF16, "ident_b")
        bones_b = kb.sb([128, 128], BF16, "bones_b")
        ones_b = kb.sb([128, 128], BF16, "ones_b")
        gT = kb.sb([128, NL, KC], F32, "gT")
        kb.dma("sp", ident_f[:], I["ident"], writes=[ident_f])
        kb.dma("pool", ident_b[:], I["ident"], writes=[ident_b])
        kb.dma("pool", bones_b[:], I["blockones"], writes=[bones_b])
        bones_f = kb.sb([128, 128], F32, "bones_f")
        kb.dma("sp", bones_f[:], I["blockones"], writes=[bones_f])
        kb.op("dve", lambda: V.memset(ones_b[:], 1.0), writes=[ones_b])
        kb.dma("sp", gT[:], I["norm_g"].rearrange("l (kc p) -> p l kc", p=128), writes=[gT],
               allow_slow_non_contiguous=True)

        xt = kb.sb([128, 4, D], F32, "xt")
        hT = kb.sb([128, KC, 512], BF16, "hT")
        og = kb.sb([128, IC, 512], BF16, "og")
        wbuf = [kb.sb([128, 8192], BF16, "wbuf%d" % i) for i in range(2)]
        wsel = [0]
        stat = kb.sb([128, 16], F32, "stat")
        PSA = []
        PSB = []
        PSX = []
        psa_i = [0]
        psb_i = [0]

        def layer_psum(kind, les):
            na, nb_, nx = {0: (4, 2, 2), 1: (5, 1, 2), 2: (6, 2, 0)}[kind]
            PSA[:] = [kb.ps([128, 512], F32, "psA%d" % i, es=les) for i in range(na)]
            PSB[:] = [kb.ps([128, 1024], BF16, "psB%d" % i, es=les) for i in range(nb_)]
            PSX[:] = [kb.ps([128, 512], F32, "psX%d" % i, es=les) for i in range(nx)]

        def psA():
            psa_i[0] += 1
            t = PSA[psa_i[0] % len(PSA)]
            assert kb.dead or not t.pending, "PSUM tile handed out before its previous evacuation was emitted"
            return t

        def psB():
            psb_i[0] += 1
            t = PSB[psb_i[0] % len(PSB)]
            assert kb.dead or not t.pending, "PSUM tile handed out before its previous evacuation was emitted"
            return t

        def next_w():
            wsel[0] += 1
            return wbuf[wsel[0] % 2]

        def load_norm(l, src_ap, ntok, src_reads):
            nsub = (ntok + 127) // 128
            pp = min(ntok, 128)
            if ntok >= 128:
                kb.dma("sp", xt[:, 0:nsub, :], src_ap.rearrange("(s p) d -> p s d", p=128), reads=src_reads, writes=[xt])
            else:
                kb.dma("sp", xt[0:pp, 0, :], src_ap, reads=src_reads, writes=[xt])
            for s in range(nsub):
                kb.op("act", lambda s=s: S.activation(out=hT.h[0:pp, 0:4, :].rearrange("p a b -> p (a b)"), in_=xt[0:pp, s, :],
                                                     func=AF.Square, accum_out=stat[0:pp, s:s + 1]),
                      reads=[xt], writes=[hT, stat])
            kb.op("dve", lambda: V.tensor_scalar(out=stat[0:pp, 4:4 + nsub], in0=stat[0:pp, 0:nsub], scalar1=1.0 / D,
                                                 scalar2=NORM_EPS, op0=ALU.mult, op1=ALU.add), reads=[stat], writes=[stat])
            kb.op("act", lambda: S.activation(out=stat[0:pp, 8:8 + nsub], in_=stat[0:pp, 4:4 + nsub], func=AF.Sqrt),
                  reads=[stat], writes=[stat])
            kb.op("dve", lambda: V.reciprocal(out=stat[0:pp, 12:12 + nsub], in_=stat[0:pp, 8:8 + nsub]),
                  reads=[stat], writes=[stat])
            xn = og.h[:].rearrange("p a b -> p (a b)")
            for s in range(nsub):
                kb.op("dve", lambda s=s: V.tensor_scalar(out=xn[0:pp, s * D:(s + 1) * D], in0=xt[0:pp, s, :],
                                                         scalar1=stat[0:pp, 12 + s:13 + s], scalar2=None, op0=ALU.mult),
                      reads=[xt, stat], writes=[og])
            for kc in range(KC):
                pt = psB()
                kb.transpose(pt, [(pt[:, s * 128:s * 128 + pp], xn[0:pp, s * D + kc * 128:s * D + (kc + 1) * 128],
                                   ident_b[0:pp, 0:pp]) for s in range(nsub)], reads=[og, ident_b])
                kb.op("act", lambda kc=kc, pt=pt: S.activation(out=hT[:, kc, 0:ntok], in_=pt[:, 0:ntok], func=AF.Copy,
                                                                scale=gT[:, l, kc:kc + 1]),
                      reads=[pt, gT], writes=[hT])

        def load_w_in(w_ap, col0, ncols):
            wt = next_w()
            view = wt.h[:, 0:KC * ncols].rearrange("p (k n) -> p k n", k=KC)
            kb.dma("pool", view, w_ap.rearrange("(k p) n -> p k n", p=128)[:, :, col0:col0 + ncols], writes=[wt])
            return wt, view

        def out_proj(w_ap, ntok, dst_ap, dst_t):
            nsub = (ntok + 127) // 128
            pp = min(ntok, 128)
            for cb in range(D // 256):
                wt = next_w()
                view = wt.h[:, 0:IC * 256].rearrange("p (k n) -> p k n", k=IC)
                kb.dma("pool", view, w_ap.rearrange("(k p) n -> p k n", p=128)[:, :, cb * 256:(cb + 1) * 256], writes=[wt])
                for s in range(nsub):
                    pt = psA()
                    kb.mm(pt, [(pt[0:pp, 0:256], og[:, ic, s * 128:s * 128 + pp], view[:, ic, :]) for ic in range(IC)],
                          reads=[og, wt])
                    kb.op("dve", lambda s=s, pt=pt, cb=cb: V.tensor_tensor(out=xt[0:pp, s, cb * 256:(cb + 1) * 256],
                                                                            in0=pt[0:pp, 0:256],
                                                                            in1=xt[0:pp, s, cb * 256:(cb + 1) * 256], op=ALU.add),
                          reads=[pt, xt], writes=[xt])
            if ntok >= 128:
                kb.dma("sp", dst_ap.rearrange("(s p) d -> p s d", p=128), xt[:, 0:nsub, :], reads=[xt], writes=[dst_t])
            else:
                kb.dma("sp", dst_ap, xt[0:pp, 0, :], reads=[xt], writes=[dst_t])

        AT = {}

        def attn_alloc(les):
            def sb(shape, dt, name):
                return kb.sb(shape, dt, name, es=les)
            AT["BTp"] = sb([128, 64, 128], BF16, "BTp")
            AT["BTc"] = sb([128, 64, 128], BF16, "BTc")
            with contextlib.ExitStack() as zes:
                rb33 = kb.sb([33, 64], F32, "rb33", es=zes)
                oh33 = kb.sb([33, 384], F32, "oh33", es=zes)
                zsb = kb.sb([64, 384], F32, "zsb", es=zes)
                kb.op("dve", lambda: V.memset(rb33[:], 1.0), writes=[rb33])
                kb.dma("sp", rb33[0:32, :], I["rel_bias"], writes=[rb33])
                kb.dma("sp", oh33[:], I["bucket_oh"], writes=[oh33])
                pz = psA()
                kb.mm(pz, [(pz[0:64, 0:384], rb33[:, :], oh33[:, :])], reads=[rb33, oh33])
                kb.op("dve", lambda: V.tensor_copy(out=zsb[:], in_=pz[0:64, 0:384]), reads=[pz], writes=[zsb])
                zr3 = zrep.rearrange("h (r m) -> h r m", m=384)
                zrp = kb.sb([64, 16, 384], F32, "zrp", es=zes)
                kb.op("dve", lambda: V.tensor_copy(out=zrp[:], in_=zsb[:, :].unsqueeze(1).to_broadcast([64, 16, 384])),
                      reads=[zsb], writes=[zrp])
                ztoks = []
                for r0 in range(0, 128, 16):
                    kb.dma("sp", zr3[:, r0:r0 + 16, :], zrp[:], reads=[zrp], writes=[])
                    ztoks.append(kb.last_dma_tok)
                for tk in ztoks:
                    kb._wait("pool", tk)
                for (BT, off) in ((AT["BTc"], 128), (AT["BTp"], 256)):
                    src = bass.AP(tensor=zrep.tensor, offset=zrep.offset + off, ap=[[383, 128], [128 * 384, 64], [1, 128]])
                    kb.dma("pool", BT[:], src, reads=[zrep_t], writes=[BT])
                kb.barrier()
            chk("bt")
            AT["kT"] = sb([128, 8, 640], BF16, "kT")
            AT["vtk"] = sb([128, 5, 512], BF16, "vtk")
            AT["knf"] = sb([128, 8, 128], F32, "knf")
            AT["vlf"] = sb([128, 512], F32, "vlf")
            AT["qn"] = sb([128, 4, 512], BF16, "qn")
            AT["sg"] = sb([128, 4, 512], BF16, "sg")
            AT["sqb"] = sb([128, 512], BF16, "sqb")
            AT["rsd"] = sb([128, 512], F32, "rsd")
            AT["pT"] = [sb([128, 512], BF16, "pT%d" % i) for i in range(2)]
            AT["tmpf"] = [sb([128, 512], F32, "tmpf%d" % i) for i in range(2)]
            AT["den"] = sb([128, 512], F32, "den")
            gq = AT["gq"] = sb([128, nA], F32, "gq")
            gk = AT["gk"] = sb([128, nA], F32, "gk")
            esk = AT["esk"] = sb([128, nA, 32], F32, "esk")
            for par in range(2):
                kb.dma("sp", gq[par * 64:(par + 1) * 64, :], I["q_norm_g"].rearrange("l d -> d l"), writes=[gq],
                       allow_slow_non_contiguous=True)
                kb.dma("sp", gk[par * 64:(par + 1) * 64, :], I["k_norm_g"].rearrange("l d -> d l"), writes=[gk],
                       allow_slow_non_contiguous=True)
            kb.op("dve", lambda: V.memset(esk[:], 0.0), writes=[esk])
            for par in range(2):
                sk = I["sinks"].rearrange("l (c two) -> two l c", two=2)[par:par + 1]
                kb.dma("sp", esk[par * 64:par * 64 + 1, :, :], sk, writes=[esk], allow_slow_non_contiguous=True)
            pe_ = psA()
            kb.mm(pe_, [(pe_[:, 0:nA * 32], bones_f[:, :], esk[:].rearrange("p l c -> p (l c)"))], reads=[bones_f, esk])
            kb.op("act", lambda: S.activation(out=esk[:].rearrange("p l c -> p (l c)"), in_=pe_[:, 0:nA * 32], func=AF.Exp),
                  reads=[pe_], writes=[esk])
            kb.op("dve", lambda: V.tensor_scalar(out=gq[:], in0=gq[:], scalar1=0.125, scalar2=None, op0=ALU.mult),
                  reads=[gq], writes=[gq])
            chk("esk")
            AT["kcs"] = sb([128, 8, 128], BF16, "kcs")
            AT["kcs_f"] = AT["tmpf"][0]
            AT["vcs_f"] = AT["tmpf"][1]
            AT["vns"] = sb([8, NSQ, 512], BF16, "vns")
            AT["vnsf"] = sb([32, 512], F32, "vnsf")
            AT["kcT_bufs"] = [sb([128, 8, 128], BF16, "kcTb%d" % i) for i in range(NSQ)]
            AT["vcs_bufs"] = [sb([128, 512], BF16, "vcsb%d" % i) for i in range(NSQ)]
            AT["ktok"] = AT["den"]
            AT["pti"] = [0]
            AT["tfi"] = [0]

        class _A:
            def __getattr__(self, k):
                return AT[k]
        A_ = _A()

        def headnorm(ps_t, ntok, gcol, gt, out_ap, out_t, f32_out=None):
            sqb, rsd = AT["sqb"], AT["rsd"]
            kb.op("act", lambda: S.activation(out=sqb[:, 0:ntok], in_=ps_t[:, 0:ntok], func=AF.Square),
                  reads=[ps_t], writes=[sqb])
            p2 = psA()
            kb.mm(p2, [(p2[:, 0:ntok], bones_b[:, :], sqb[:, 0:ntok])], reads=[bones_b, sqb])
            kb.op("dve", lambda: V.tensor_scalar(out=rsd[:, 0:ntok], in0=p2[:, 0:ntok], scalar1=1.0 / 64, scalar2=NORM_EPS,
                                                 op0=ALU.mult, op1=ALU.add), reads=[p2], writes=[rsd])
            kb.op("act", lambda: S.activation(out=rsd[:, 0:ntok], in_=rsd[:, 0:ntok], func=AF.Sqrt), reads=[rsd], writes=[rsd])
            kb.op("dve", lambda: V.reciprocal(out=rsd[:, 0:ntok], in_=rsd[:, 0:ntok]), reads=[rsd], writes=[rsd])
            if f32_out is not None:
                fo_ap, fo_t, c0, c1 = f32_out
                kb.op("dve", lambda: V.scalar_tensor_tensor(out=fo_ap, in0=ps_t[:, c0:c1], scalar=gcol,
                                                            in1=rsd[:, c0:c1], op0=ALU.mult, op1=ALU.mult),
                      reads=[ps_t, rsd, gt], writes=[fo_t])
            kb.op("dve", lambda: V.scalar_tensor_tensor(out=out_ap, in0=ps_t[:, 0:ntok], scalar=gcol, in1=rsd[:, 0:ntok],
                                                        op0=ALU.mult, op1=ALU.mult),
                  reads=[ps_t, rsd, gt], writes=[out_t])

        def attn_block(j, kh, QB, q_cols, kprev, kcur, vprev, vcur, ncur, og_cols, first, xr=()):
            qn, sg, den, esk, pT, tmpf = AT["qn"], AT["sg"], AT["den"], AT["esk"], AT["pT"], AT["tmpf"]
            po = PSX[0]
            pd = PSX[1]
            groups_o = []
            groups_d = []
            preads = []
            W4 = 4 * QB
            for par in range(2):
                pts = []
                for which in ((0, 1) if not first else (1,)):
                    nk = 128 if which == 0 else ncur
                    lk = kprev(par) if which == 0 else kcur(par)
                    sc = psA()
                    kb.mm(sc, [(sc[0:nk, 0:W4].rearrange("p (c q) -> p c q", c=4), lk,
                                qn[par * 64:(par + 1) * 64, :, q_cols])], reads=[AT["kT"], qn] + list(xr))
                    BT = AT["BTp"] if which == 0 else AT["BTc"]
                    h0 = kh * 8 + par
                    tf = tmpf[AT["tfi"][0] % 2]
                    AT["tfi"][0] += 1
                    kb.op("dve", lambda sc=sc, tf=tf, BT=BT, nk=nk, h0=h0: V.tensor_tensor(
                        out=tf[0:nk, 0:W4].rearrange("p (c q) -> p c q", c=4),
                        in0=sc[0:nk, 0:W4].rearrange("p (c q) -> p c q", c=4),
                        in1=BT[0:nk, h0:h0 + 7:2, 0:QB], op=ALU.add), reads=[sc, BT], writes=[tf])
                    p = pT[AT["pti"][0] % 2]
                    AT["pti"][0] += 1
                    kb.op("act", lambda tf=tf, p=p, nk=nk: S.activation(out=p[0:nk, 0:W4], in_=tf[0:nk, 0:W4], func=AF.Exp),
                          reads=[tf], writes=[p])
                    pts.append((p, nk, which))
                go = []
                gd = []
                preads = []
                for (p, nk, which) in pts:
                    vv = vprev if which == 0 else vcur
                    go.append((po[par * 64:(par + 1) * 64, 0:W4], vv, p[0:nk, 0:W4]))
                    gd.append((pd[par * 64:(par + 1) * 64, 0:W4], ones_b[0:nk, 0:64], p[0:nk, 0:W4]))
                    preads.append(p)
                kb.mm(po, go, reads=preads + [AT["vtk"]] + list(xr))
                kb.mm(pd, gd, reads=preads + [ones_b])
            kb.op("dve", lambda: V.tensor_tensor(out=den[:, 0:W4].rearrange("p (c q) -> p c q", c=4),
                                                 in0=pd[:, 0:W4].rearrange("p (c q) -> p c q", c=4),
                                                 in1=esk[:, j, kh * 4:kh * 4 + 4].unsqueeze(2).to_broadcast([128, 4, QB]),
                                                 op=ALU.add), reads=[pd, esk], writes=[den])
            kb.op("dve", lambda: V.reciprocal(out=den[:, 0:W4], in_=den[:, 0:W4]), reads=[den], writes=[den])
            kb.op("dve", lambda: V.tensor_tensor(out=den[:, 0:W4], in0=po[:, 0:W4], in1=den[:, 0:W4], op=ALU.mult),
                  reads=[po, den], writes=[den])
            kb.op("dve", lambda: V.tensor_tensor(out=og[:, kh * 4:kh * 4 + 4, og_cols],
                                                 in0=den[:, 0:W4].rearrange("p (c q) -> p c q", c=4),
                                                 in1=sg[:, :, q_cols], op=ALU.mult), reads=[den, sg], writes=[og])

        def attn_layer_tile(l, j, ntok, ti, last):
            kT, vtk, knf, vlf, qn, sg, gq, gk, vns, vnsf = (AT[k] for k in
                                                            ("kT", "vtk", "knf", "vlf", "qn", "sg", "gq", "gk", "vns", "vnsf"))
            w_in = I["w_in_a"][j]
            sample = ntok < 128
            for kh in range(8):
                wt = next_w()
                view = wt.h[:, 0:KC * 128].rearrange("p (k n) -> p k n", k=KC)
                src = w_in.rearrange("(k p) n -> p k n", p=128)[:, :, DI + kh * 64:DI + kh * 64 + 64]
                kb.dma("pool", view[:, :, 0:64], src, writes=[wt])
                kb.dma("pool", view[:, :, 64:128], src, writes=[wt])
                pk = psA()
                kb.mm(pk, [(pk[:, 0:ntok], view[:, kc, :], hT[:, kc, 0:ntok]) for kc in range(KC)], reads=[wt, hT])
                if not sample:
                    headnorm(pk, ntok, gk[:, j:j + 1], gk, kT[:, kh, 128:640], kT,
                             f32_out=(knf[:, kh, :], knf, 384, 512) if last else None)
                else:
                    headnorm(pk, ntok, gk[:, j:j + 1], gk, kT[:, kh, 0:ntok], kT,
                             f32_out=(knf[:, kh, 0:ntok], knf, 0, ntok))
            chk("ak")
            wt, view = load_w_in(w_in, DI + 512, 512)
            chk("av0")
            if not sample:
                for s_ in range(4):
                    pv = psA()
                    kb.mm(pv, [(pv[:, :], hT[:, kc, s_ * 128:(s_ + 1) * 128], view[:, kc, :]) for kc in range(KC)], reads=[wt, hT])
                    chk("av1")
                    kb.op("act", lambda s_=s_, pv=pv: S.copy(out=vtk[:, 1 + s_, :], in_=pv[:, :]), reads=[pv], writes=[vtk])
                    chk("av2")
                    if s_ == 1:
                        chk("av3")
                    if s_ == 3:
                        chk("av4")
                    if last and s_ == 3:
                        kb.op("dve", lambda pv=pv: V.tensor_copy(out=vlf[:], in_=pv[:, :]), reads=[pv], writes=[vlf])
                        chk("av5")
            else:
                for sq in range(NSQ):
                    pv = psA()
                    kb.mm(pv, [(pv[0:8, :], hT[:, kc, sq * 8:(sq + 1) * 8], view[:, kc, :]) for kc in range(KC)], reads=[wt, hT])
                    kb.op("act", lambda sq=sq, pv=pv: S.copy(out=vns[:, sq, :], in_=pv[0:8, :]), reads=[pv], writes=[vns])
                pv = psA()
                kb.mm(pv, [(pv[0:NST, :], hT[:, kc, 0:NST], view[:, kc, :]) for kc in range(KC)], reads=[wt, hT])
                kb.op("dve", lambda pv=pv: V.tensor_copy(out=vnsf[:], in_=pv[0:NST, :]), reads=[pv], writes=[vnsf])
            chk("av")
            for kh in range(8):
                wt, view = load_w_in(w_in, kh * 512, 512)
                for c in range(4):
                    pq = psA()
                    kb.mm(pq, [(pq[:, 0:ntok], view[:, kc, c * 128:(c + 1) * 128], hT[:, kc, 0:ntok]) for kc in range(KC)],
                          reads=[wt, hT])
                    headnorm(pq, ntok, gq[:, j:j + 1], gq, qn[:, c, 0:ntok], qn)
                wt, view = load_w_in(w_in, DI + 1024 + kh * 512, 512)
                for c in range(4):
                    pg = psA()
                    kb.mm(pg, [(pg[:, 0:ntok], view[:, kc, c * 128:(c + 1) * 128], hT[:, kc, 0:ntok]) for kc in range(KC)],
                          reads=[wt, hT])
                    kb.op("act", lambda c=c, pg=pg: S.activation(out=sg[:, c, 0:ntok], in_=pg[:, 0:ntok], func=AF.Silu),
                          reads=[pg], writes=[sg])
                chk("aq")
                if not sample:
                    for b in range(4):
                        if b == 1:
                            chk("ab0")
                        if b == 2:
                            chk("ab1")
                        attn_block(j, kh, 128, slice(b * 128, (b + 1) * 128),
                                   lambda par, b=b: kT[par * 64:(par + 1) * 64, kh, b * 128:(b + 1) * 128],
                                   lambda par, b=b: kT[par * 64:(par + 1) * 64, kh, (b + 1) * 128:(b + 2) * 128],
                                   vtk[:, b, kh * 64:(kh + 1) * 64], vtk[:, b + 1, kh * 64:(kh + 1) * 64], 128,
                                   slice(b * 128, (b + 1) * 128), first=(ti == 0 and b == 0))
                else:
                    for sq in range(NSQ):
                        attn_block(j, kh, 8, slice(sq * 8, (sq + 1) * 8),
                                   lambda par, sq=sq: AT["kcT_bufs"][sq][par * 64:(par + 1) * 64, kh, :],
                                   lambda par, sq=sq: kT[par * 64:(par + 1) * 64, kh, sq * 8:(sq + 1) * 8],
                                   AT["vcs_bufs"][sq][:, kh * 64:(kh + 1) * 64], vns[:, sq, kh * 64:(kh + 1) * 64], 8,
                                   slice(sq * 8, (sq + 1) * 8), first=False,
                                   xr=[AT["kcT_bufs"][sq], AT["vcs_bufs"][sq], vns])
            if not sample:
                kb.op("pool", lambda: G.tensor_copy(out=kT[:, :, 0:128], in_=kT[:, :, 512:640]), reads=[kT], writes=[kT])
                kb.op("pool", lambda: G.tensor_copy(out=vtk[:, 0, :], in_=vtk[:, 4, :]), reads=[vtk], writes=[vtk])

        def attn_sample_prep(j):
            kcs, kcs_f, vcs_f = AT["kcs"], AT["kcs_f"], AT["vcs_f"]
            for sq in range(NSQ):
                kb.dma("sp", kcs_f[:], I["cache_k"][j, sq], writes=[kcs_f])
                kb.dma("sp", vcs_f[:], I["cache_v"][j, sq], writes=[vcs_f])
                kcv = kcs_f.h[:].rearrange("p (k d) -> p k d", k=8)
                kb.op("dve", lambda kcv=kcv: V.tensor_copy(out=kcs[:, :, 0:64], in_=kcv), reads=[kcs_f], writes=[kcs])
                kb.op("dve", lambda kcv=kcv: V.tensor_copy(out=kcs[:, :, 64:128], in_=kcv), reads=[kcs_f], writes=[kcs])
                kct = AT["kcT_bufs"][sq]
                vc = AT["vcs_bufs"][sq]
                kb.op("act", lambda vc=vc: S.copy(out=vc[:], in_=vcs_f[:]), reads=[vcs_f], writes=[vc])
                for k2 in range(0, 8, 4):
                    pt = psB()
                    kb.transpose(pt, [(pt[:, i * 128:(i + 1) * 128], kcs[:, k2 + i, :], ident_b[:, :]) for i in range(4)],
                                 reads=[kcs, ident_b])
                    kb.op("act", lambda pt=pt, kct=kct, k2=k2: S.copy(out=kct[:, k2:k2 + 4, :],
                                                                    in_=pt[:, 0:512].rearrange("p (k n) -> p k n", k=4)),
                          reads=[pt], writes=[kct])
                kb.dma("sp", O["kws"][j, sq, 0:120, :], I["cache_k"][j, sq, 8:128, :], writes=[])
                kws_toks.append(kb_last_tok("sp"))
                kb.dma("sp", O["vws"][j, sq, 0:120, :], I["cache_v"][j, sq, 8:128, :], writes=[])
                kws_toks.append(kb_last_tok("sp"))

        def kb_last_tok(q):
            return kb.last_dma_tok

        kws_toks = []
        if nA:
            kws_t = TT(None, "kws")
            vws_t = TT(None, "vws")
            kwp_t = TT(None, "kwp")
            vwp_t = TT(None, "vwp")
            zrep_t = TT(None, "zrep")
            out_regions += [kws_t, vws_t, kwp_t, vwp_t]

        def attn_outputs_prompt(j):
            knf, vlf, ktok = AT["knf"], AT["vlf"], AT["ktok"]
            pt = psA()
            kb.transpose(pt, [(pt[:, kh * 64:(kh + 1) * 64], knf[0:64, kh, :], ident_f[0:64, 0:64]) for kh in range(8)],
                         reads=[knf, ident_f])
            kb.op("dve", lambda: V.tensor_copy(out=ktok[:], in_=pt[:, :]), reads=[pt], writes=[ktok])
            kb.dma("sp", O["kwp"][j], ktok[:], reads=[ktok], writes=[])
            kws_toks.append(kb_last_tok("sp"))
            kb.dma("sp", O["vwp"][j], vlf[:], reads=[vlf], writes=[])
            kws_toks.append(kb_last_tok("sp"))

        def attn_outputs_sample(j):
            knf, vnsf, ktok = AT["knf"], AT["vnsf"], AT["ktok"]
            pt = psA()
            kb.transpose(pt, [(pt[0:NST, kh * 64:(kh + 1) * 64], knf[0:64, kh, 0:NST], ident_f[0:64, 0:64])
                              for kh in range(8)], reads=[knf, ident_f])
            kb.op("dve", lambda: V.tensor_copy(out=ktok[0:NST, :], in_=pt[0:NST, :]), reads=[pt], writes=[ktok])
            for sq in range(NSQ):
                kb.dma("sp", O["kws"][j, sq, 120:128, :], ktok[sq * 8:(sq + 1) * 8, :], reads=[ktok], writes=[])
                kws_toks.append(kb_last_tok("sp"))
                kb.dma("sp", O["vws"][j, sq, 120:128, :], vnsf[sq * 8:(sq + 1) * 8, :], reads=[vnsf], writes=[])
                kws_toks.append(kb_last_tok("sp"))


        BT_ = {}
        KAP = 0.6065306597126334

        def rwkv_alloc(les, j):
            def sb(shape, dt, name):
                return kb.sb(shape, dt, name, es=les)
            B = BT_
            B["ST"] = sb([128, 32, 64], F32, "ST")
            B["STs"] = sb([128, NSQ, 64], F32, "STs")
            B["STs2"] = [B["STs"], TT(B["STs"].h, "STs_b")]
            B["ST2"] = [B["ST"], TT(B["ST"].h, "ST_b")]
            kb.op("dve", lambda: V.memset(B["ST"][:], 0.0), writes=B["ST2"])
            B["hlast"] = sb([128, KC], BF16, "hlast")
            kb.op("dve", lambda: V.memset(B["hlast"][:], 0.0), writes=[B["hlast"]])
            B["dlt"] = AliasTT(wbuf[0], wbuf[0].h[:, 0:KC * 256].rearrange("p (k n) -> p k n", k=KC))
            B["xm"] = [sb([128, KC, 256], BF16, "xm%d" % c) for c in range(5)]
            B["xm"].append(B["xm"][4])
            B["mu"] = sb([128, 6, KC], F32, "mu")
            kb.dma("sp", B["mu"][:], I["mu_b"][j].rearrange("c (k p) -> p c k", p=128), writes=[B["mu"]],
                   allow_slow_non_contiguous=True)
            B["pv"] = sb([128, 7, 32], F32, "pv")
            for wi, nm in enumerate(("w0_b", "a0_b", "k_k_b", "k_a_b", "r_k_b", "ln_x_g_b", "ln_x_b_b")):
                kb.dma("sp", B["pv"][:, wi, :], I[nm][j].rearrange("(c p) -> p c", p=128), writes=[B["pv"]],
                       allow_slow_non_contiguous=True)
            B["wd"] = sb([128, 2, KC, 96], BF16, "wd")
            for c in range(2):
                kb.dma("pool", B["wd"][:, c, :, :], I["w_lora_down_b"][j, c].rearrange("(k p) n -> p k n", p=128), writes=[B["wd"]])
            B["wu"] = [sb([96, 2, 128], BF16, "wu%d" % i) for i in range(2)]
            B["lw"] = sb([96, 2, 256], BF16, "lwlow")
            B["mk"] = sb([128, 3, 64], F32, "mk")
            kb.dma("sp", B["mk"][0:64], I["masks"].rearrange("m a b -> a m b"), writes=[B["mk"]])
            kb.dma("sp", B["mk"][64:128], I["masks"].rearrange("m a b -> a m b"), writes=[B["mk"]])
            for nm in ("r", "k", "kk", "k2", "a", "sgw", "cs", "t1", "t2", "Ep", "of", "gt1"):
                B[nm] = sb([128, 256], F32, "f_" + nm)
            B["E0"] = B["t2"]
            for nm in ("Em", "bon"):
                B[nm] = [sb([128, 256], F32, "f_%s%d" % (nm, i)) for i in range(2)]
            for nm in ("sq", "gsq"):
                B[nm] = sb([128, 256], BF16, "b_" + nm)
            for nm in ("rt", "at", "bt", "kt", "bh", "kh", "vb"):
                B[nm] = [sb([128, 256], BF16, "b_%s%d" % (nm, i)) for i in range(2)]
            B["ones64"] = sb([128, 64], F32, "ones64")
            kb.op("dve", lambda: V.memset(B["ones64"][:], 1.0), writes=[B["ones64"]])
            for nm in ("P", "AkT", "Grb", "Grk", "Vt", "bht", "kht"):
                t_ = sb([128, 4, 64], BF16, "h_" + nm)
                B[nm] = [t_, TT(t_.h, t_.name + "_b")]
            for nm in ("Y", "U", "Sb"):
                t_ = sb([128, 64], BF16, "m_" + nm)
                B[nm] = [t_, TT(t_.h, t_.name + "_b")]
            B["WK"] = []
            for st_ in range(4):
                row = []
                for nm in ("L0", "L1", "LT0", "LT1", "P0", "P1"):
                    t_ = sb([128, 64], BF16, "wk%d_%s" % (st_, nm))
                    row.append([t_, TT(t_.h, t_.name + "_b")])
                B["WK"].append(row)
            B["wrk"] = wbuf
            B["Sin"] = B["of"]
            B["Sout"] = B["kk"]
            B["shf"] = sb([128, KC, NSQ], F32, "shf")
            B["wri"] = [0]

        def rwkv_mix_inputs(j, col0, T, first_cols):
            B = BT_
            for c in range(6):
                for kc in range(KC):
                    kb.op("dve", lambda c=c, kc=kc: V.scalar_tensor_tensor(out=B["xm"][c][:, kc, 0:T], in0=B["dlt"][:, kc, 0:T],
                                                                            scalar=B["mu"][:, c, kc:kc + 1], in1=hT[:, kc, col0:col0 + T],
                                                                            op0=ALU.mult, op1=ALU.add),
                          reads=[B["dlt"], B["mu"], hT], writes=[B["xm"][c]])
                if c >= 4:
                    cc_ = c - 4
                    pl = psA()
                    kb.mm(pl, [(pl[0:96, 0:T], B["wd"][:, cc_, kc, :], B["xm"][c][:, kc, 0:T]) for kc in range(KC)],
                          reads=[B["wd"], B["xm"][c]])
                    kb.op("act", lambda cc_=cc_, pl=pl: S.activation(out=B["lw"][:, cc_, 0:T], in_=pl[0:96, 0:T],
                                                                    func=(AF.Tanh if cc_ == 0 else AF.Copy)), reads=[pl], writes=[B["lw"]])

        def rwkv_s1(j, pr, T, C, og_cols):
            B = BT_
            bs = pr % 2
            pvp = B["pv"]
            w4 = I["w_rkvg_b"][j]
            ps_in = {}
            for c in range(4):
                wt = next_w()
                wv = wt.h[:, 0:KC * 128].rearrange("p (k n) -> p k n", k=KC)
                kb.dma("pool", wv, w4[c].rearrange("(k p) n -> p k n", p=128)[:, :, pr * 128:(pr + 1) * 128], writes=[wt])
                yield
                pp_ = psA()
                kb.mm(pp_, [(pp_[:, 0:T], wv[:, kc, :], B["xm"][c][:, kc, 0:T]) for kc in range(KC)], reads=[wt, B["xm"][c]])
                if c == 0:
                    kb.op("act", lambda pp_=pp_: S.copy(out=B["r"][:, 0:T], in_=pp_[:, 0:T]), reads=[pp_], writes=[B["r"]])
                    yield
                elif c == 1:
                    kb.op("act", lambda pp_=pp_: S.copy(out=B["k"][:, 0:T], in_=pp_[:, 0:T]), reads=[pp_], writes=[B["k"]])
                    yield
                elif c == 2:
                    kb.op("act", lambda pp_=pp_: S.copy(out=B["vb"][bs][:, 0:T], in_=pp_[:, 0:T]), reads=[pp_], writes=[B["vb"][bs]])
                    yield
                else:
                    kb.op("act", lambda pp_=pp_: S.activation(out=og[:, pr, og_cols], in_=pp_[:, 0:T], func=AF.Silu),
                          reads=[pp_], writes=[og])
                    yield
            chk("bproj")
            r, k, kk, k2, a, sgw, cs, t1, t2, Ep, E0 = (B[n] for n in ("r", "k", "kk", "k2", "a", "sgw", "cs", "t1", "t2", "Ep", "E0"))
            Em, bon = B["Em"][bs], B["bon"][bs]
            wu = B["wu"][pr % 2]
            kb.dma("pool", wu[:], I["w_lora_up_b"][j][:, :, pr * 128:(pr + 1) * 128].rearrange("c r n -> r c n"), writes=[wu])
            yield
            for c, dst, wi in ((0, sgw, 0), (1, a, 1)):
                pl = psA()
                kb.mm(pl, [(pl[:, 0:T], wu[:, c, :], B["lw"][:, c, 0:T])], reads=[wu, B["lw"]])
                kb.op("act", lambda pl=pl, dst=dst, wi=wi: S.activation(out=dst[:, 0:T], in_=pl[:, 0:T], func=AF.Sigmoid,
                                                                       bias=pvp[:, wi, pr:pr + 1]), reads=[pl, pvp], writes=[dst])
                yield
            kb.op("dve", lambda: V.tensor_scalar(out=t1[:, 0:T], in0=k[:, 0:T], scalar1=pvp[:, 2, pr:pr + 1], scalar2=None, op0=ALU.mult),
                  reads=[k, pvp], writes=[t1])
            yield
            kb.op("act", lambda: S.activation(out=B["sq"][:, 0:T], in_=t1[:, 0:T], func=AF.Square), reads=[t1], writes=[B["sq"]])
            yield
            pn = psA()
            kb.mm(pn, [(pn[:, 0:T], bones_b[:, :], B["sq"][:, 0:T])], reads=[bones_b, B["sq"]])
            kb.op("act", lambda: S.activation(out=t2[:, 0:T], in_=pn[:, 0:T], func=AF.Sqrt), reads=[pn], writes=[t2])
            yield
            kb.op("dve", lambda: V.tensor_scalar(out=t2[:, 0:T], in0=t2[:, 0:T], scalar1=1e-12, scalar2=None, op0=ALU.max),
                  reads=[t2], writes=[t2])
            yield
            kb.op("dve", lambda: V.reciprocal(out=t2[:, 0:T], in_=t2[:, 0:T]), reads=[t2], writes=[t2])
            yield
            kb.op("dve", lambda: V.tensor_tensor(out=kk[:, 0:T], in0=t1[:, 0:T], in1=t2[:, 0:T], op=ALU.mult), reads=[t1, t2], writes=[kk])
            yield
            kb.op("dve", lambda: V.tensor_scalar(out=t1[:, 0:T], in0=a[:, 0:T], scalar1=-1.0, scalar2=pvp[:, 3, pr:pr + 1],
                                                 op0=ALU.add, op1=ALU.mult), reads=[a, pvp], writes=[t1])
            yield
            kb.op("dve", lambda: V.scalar_tensor_tensor(out=k2[:, 0:T], in0=t1[:, 0:T], scalar=1.0, in1=k[:, 0:T], op0=ALU.add, op1=ALU.mult),
                  reads=[t1, k], writes=[k2])
            yield
            kb.op("dve", lambda: V.scalar_tensor_tensor(out=B["sq"][:, 0:T], in0=r[:, 0:T], scalar=pvp[:, 4, pr:pr + 1], in1=k2[:, 0:T],
                                                        op0=ALU.mult, op1=ALU.mult), reads=[r, k2, pvp], writes=[B["sq"]])
            yield
            pb = psA()
            kb.mm(pb, [(pb[:, 0:T], bones_b[:, :], B["sq"][:, 0:T])], reads=[bones_b, B["sq"]])
            kb.op("dve", lambda: V.tensor_tensor(out=bon[:, 0:T], in0=pb[:, 0:T], in1=B["vb"][bs][:, 0:T], op=ALU.mult), reads=[pb, B["vb"][bs]], writes=[bon])
            yield
            nch = T // C
            for ci in range(nch):
                kb.op("dve", lambda ci=ci: V.tensor_tensor_scan(out=cs[:, ci * C:(ci + 1) * C], data0=B["ones64"][:, 0:C],
                                                                 data1=sgw[:, ci * C:(ci + 1) * C], initial=0.0, op0=ALU.mult, op1=ALU.add),
                      reads=[sgw, B["ones64"]], writes=[cs])
                yield
            kb.op("act", lambda: S.activation(out=Em[:, 0:T], in_=cs[:, 0:T], func=AF.Exp, scale=-KAP), reads=[cs], writes=[Em])
            yield
            kb.op("act", lambda: S.activation(out=Ep[:, 0:T], in_=cs[:, 0:T], func=AF.Exp, scale=KAP), reads=[cs], writes=[Ep])
            yield
            kb.op("dve", lambda: V.tensor_tensor(out=t1[:, 0:T], in0=cs[:, 0:T], in1=sgw[:, 0:T], op=ALU.subtract), reads=[cs, sgw], writes=[t1])
            yield
            kb.op("act", lambda: S.activation(out=E0[:, 0:T], in_=t1[:, 0:T], func=AF.Exp, scale=-KAP), reads=[t1], writes=[E0])
            yield
            kb.op("dve", lambda: V.tensor_tensor(out=B["rt"][bs][:, 0:T], in0=r[:, 0:T], in1=Em[:, 0:T], op=ALU.mult), reads=[r, Em], writes=[B["rt"][bs]])
            yield
            kb.op("dve", lambda: V.scalar_tensor_tensor(out=B["at"][bs][:, 0:T], in0=kk[:, 0:T], scalar=-1.0, in1=E0[:, 0:T], op0=ALU.mult, op1=ALU.mult),
                  reads=[kk, E0], writes=[B["at"][bs]])
            yield
            kb.op("dve", lambda: V.tensor_tensor(out=t1[:, 0:T], in0=kk[:, 0:T], in1=a[:, 0:T], op=ALU.mult), reads=[kk, a], writes=[t1])
            yield
            kb.op("dve", lambda: V.tensor_tensor(out=B["bt"][bs][:, 0:T], in0=t1[:, 0:T], in1=Ep[:, 0:T], op=ALU.mult), reads=[t1, Ep], writes=[B["bt"][bs]])
            yield
            kb.op("dve", lambda: V.tensor_tensor(out=B["kt"][bs][:, 0:T], in0=k2[:, 0:T], in1=Ep[:, 0:T], op=ALU.mult), reads=[k2, Ep], writes=[B["kt"][bs]])
            yield
            for ci in range(nch):
                cc = slice(ci * C, (ci + 1) * C)
                wc = Em[:, (ci + 1) * C - 1:(ci + 1) * C]
                kb.op("dve", lambda cc=cc, wc=wc: V.tensor_scalar(out=B["bh"][bs][:, cc], in0=B["bt"][bs][:, cc], scalar1=wc, scalar2=None, op0=ALU.mult),
                      reads=[B["bt"][bs], Em], writes=[B["bh"][bs]])
                yield
                kb.op("dve", lambda cc=cc, wc=wc: V.tensor_scalar(out=B["kh"][bs][:, cc], in0=B["kt"][bs][:, cc], scalar1=wc, scalar2=None, op0=ALU.mult),
                      reads=[B["kt"][bs], Em], writes=[B["kh"][bs]])
                yield
        def rwkv_s2(j, pr, T, C, og_cols, ST, st_ap):
            B = BT_
            bs = pr % 2
            pvp = B["pv"]
            nch = T // C
            Em, bon, t1 = B["Em"][bs], B["bon"][bs], B["gt1"]
            chk("bprep")
            nlev = int(math.log2(C))
            mk = B["mk"]

            def run_rr(gens):
                gens = list(gens)
                while gens:
                    for g in list(gens):
                        try:
                            next(g)
                        except StopIteration:
                            gens.remove(g)
                    yield

            def phaseA(ci, hp):
                cc = slice(ci * C, (ci + 1) * C)
                P = slice(hp * 64, (hp + 1) * 64)
                TBs = slice(hp * 64, hp * 64 + C)
                at_, bt_, kt_, rt_ = B["at"][bs][P, cc], B["bt"][bs][P, cc], B["kt"][bs][P, cc], B["rt"][bs][P, cc]
                wk = B["WK"][ci]
                L_ = [wk[0][hp], wk[1][hp]]
                LT_ = [wk[2][hp], wk[3][hp]]
                Pp = [wk[4][hp], wk[5][hp]]

                def mmask(dst, dap, lh, rh, mi, rds):
                    pm = psA()
                    kb.mm(pm, [(pm[TBs, 0:C], lh, rh)], reads=rds)
                    kb.op("dve", lambda: V.tensor_tensor(out=dap, in0=pm[TBs, 0:C], in1=mk[TBs, mi, 0:C], op=ALU.mult),
                          reads=[pm, mk], writes=[dst])
                    yield
                yield from mmask(LT_[0], LT_[0][TBs, 0:C], bt_, at_, 0, [B["bt"][bs], B["at"][bs]])
                yield from mmask(L_[0], L_[0][TBs, 0:C], at_, bt_, 1, [B["bt"][bs], B["at"][bs]])
                yield from mmask(B["AkT"][hp], B["AkT"][hp][TBs, ci, 0:C], kt_, at_, 0, [B["kt"][bs], B["at"][bs]])
                yield from mmask(B["Grb"][hp], B["Grb"][hp][TBs, ci, 0:C], bt_, rt_, 2, [B["bt"][bs], B["rt"][bs]])
                yield from mmask(B["Grk"][hp], B["Grk"][hp][TBs, ci, 0:C], kt_, rt_, 2, [B["kt"][bs], B["rt"][bs]])
                for src_, dst_ in ((B["vb"][bs], B["Vt"][hp]), (B["bh"][bs], B["bht"][hp]), (B["kh"][bs], B["kht"][hp])):
                    pt = psA()
                    kb.mm(pt, [(pt[TBs, 0:64], src_[P, cc], ident_b[P, P])], reads=[src_, ident_b])
                    kb.op("act", lambda pt=pt, dst_=dst_: S.copy(out=dst_[TBs, ci, :], in_=pt[TBs, 0:64]), reads=[pt], writes=[dst_])
                    yield
                kb.op("dve", lambda: V.tensor_tensor(out=Pp[0][TBs, 0:C], in0=LT_[0][TBs, 0:C], in1=ident_b[TBs, TBs], op=ALU.add),
                      reads=[LT_[0], ident_b], writes=[Pp[0]])
                yield
                cur = 0
                for lev in range(1, nlev):
                    nx = 1 - cur
                    last = lev == nlev - 1
                    p2 = psA()
                    kb.mm(p2, [(p2[TBs, 0:C], LT_[cur][TBs, 0:C], L_[cur][TBs, 0:C])], reads=[L_[cur], LT_[cur]])
                    if not last:
                        p1 = psA()
                        kb.mm(p1, [(p1[TBs, 0:C], L_[cur][TBs, 0:C], LT_[cur][TBs, 0:C])], reads=[L_[cur], LT_[cur]])
                    kb.op("act", lambda p2=p2, nx=nx: S.copy(out=L_[nx][TBs, 0:C], in_=p2[TBs, 0:C]), reads=[p2], writes=[L_[nx]])
                    if not last:
                        kb.op("dve", lambda p1=p1, nx=nx: V.tensor_copy(out=LT_[nx][TBs, 0:C], in_=p1[TBs, 0:C]), reads=[p1], writes=[LT_[nx]])
                        yield
                    p3 = psA()
                    kb.mm(p3, [(p3[TBs, 0:C], ident_b[TBs, TBs], Pp[cur][TBs, 0:C]), (p3[TBs, 0:C], L_[nx][TBs, 0:C], Pp[cur][TBs, 0:C])],
                          reads=[ident_b, Pp[cur], L_[nx]])
                    if last:
                        kb.op("dve", lambda p3=p3: V.tensor_copy(out=B["P"][hp][TBs, ci, 0:C], in_=p3[TBs, 0:C]), reads=[p3], writes=[B["P"][hp]])
                    else:
                        kb.op("dve", lambda p3=p3, nx=nx: V.tensor_copy(out=Pp[nx][TBs, 0:C], in_=p3[TBs, 0:C]), reads=[p3], writes=[Pp[nx]])
                    yield
                    cur = nx

            def phaseB(ci, hp):
                cc = slice(ci * C, (ci + 1) * C)
                P = slice(hp * 64, (hp + 1) * 64)
                TBs = slice(hp * 64, hp * 64 + C)
                at_, rt_ = B["at"][bs][P, cc], B["rt"][bs][P, cc]
                Sb, Y, U = B["Sb"][hp], B["Y"][hp], B["U"][hp]
                po = PSX[hp]
                kb.op("act", lambda: S.copy(out=Sb[P, :], in_=st_ap(ci, P)), reads=[ST[hp]], writes=[Sb])
                yield
                py = psA()
                kb.mm(py, [(py[TBs, 0:64], at_, Sb[P, :]), (py[TBs, 0:64], B["AkT"][hp][TBs, ci, 0:C], B["Vt"][hp][TBs, ci, :])],
                      reads=[B["at"][bs], Sb, B["AkT"][hp], B["Vt"][hp]])
                kb.op("dve", lambda: V.tensor_copy(out=Y[TBs, 0:64], in_=py[TBs, 0:64]), reads=[py], writes=[Y])
                yield
                pu = psA()
                kb.mm(pu, [(pu[TBs, 0:64], B["P"][hp][TBs, ci, 0:C], Y[TBs, 0:64])], reads=[B["P"][hp], Y])
                kb.op("dve", lambda: V.tensor_copy(out=U[TBs, 0:64], in_=pu[TBs, 0:64]), reads=[pu], writes=[U])
                yield
                kb.mm(po, [(po[P, cc], Sb[P, :], rt_), (po[P, cc], U[TBs, 0:64], B["Grb"][hp][TBs, ci, 0:C]),
                           (po[P, cc], B["Vt"][hp][TBs, ci, :], B["Grk"][hp][TBs, ci, 0:C])],
                      reads=[Sb, B["rt"][bs], U, B["Grb"][hp], B["Vt"][hp], B["Grk"][hp]])
                pd_ = psA()
                kb.mm(pd_, [(pd_[P, 0:64], B["bht"][hp][TBs, ci, :], U[TBs, 0:64]), (pd_[P, 0:64], B["kht"][hp][TBs, ci, :], B["Vt"][hp][TBs, ci, :])],
                      reads=[B["bht"][hp], U, B["kht"][hp], B["Vt"][hp]])
                wc = Em[P, (ci + 1) * C - 1:(ci + 1) * C]
                kb.op("dve", lambda: V.scalar_tensor_tensor(out=st_ap(ci, P), in0=st_ap(ci, P), scalar=wc, in1=pd_[P, 0:64],
                                                            op0=ALU.mult, op1=ALU.add), reads=[ST[hp], Em, pd_], writes=[ST[hp]])
                yield

            yield from run_rr([phaseA(ci, hp) for ci in range(nch) for hp in range(2)])
            for ci in range(nch):
                yield from run_rr([phaseB(ci, hp) for hp in range(2)])
            chk("bchunks")
            of = B["of"]
            for hp in range(2):
                P = slice(hp * 64, (hp + 1) * 64)
                kb.op("act", lambda hp=hp, P=P: S.copy(out=B["gsq"][P, 0:T], in_=PSX[hp][P, 0:T]), reads=[PSX[hp]], writes=[B["gsq"]])
                yield
                kb.op("act", lambda hp=hp, P=P: S.copy(out=of[P, 0:T], in_=PSX[hp][P, 0:T]), reads=[PSX[hp]], writes=[of])
                yield
            pm_ = psA()
            kb.mm(pm_, [(pm_[:, 0:T], bones_b[:, :], B["gsq"][:, 0:T])], reads=[bones_b, B["gsq"]])
            kb.op("dve", lambda: V.scalar_tensor_tensor(out=of[:, 0:T], in0=pm_[:, 0:T], scalar=-1.0 / 64, in1=of[:, 0:T], op0=ALU.mult, op1=ALU.add),
                  reads=[pm_, of], writes=[of])
            yield
            kb.op("act", lambda: S.activation(out=B["gsq"][:, 0:T], in_=of[:, 0:T], func=AF.Square), reads=[of], writes=[B["gsq"]])
            yield
            pv_ = psA()
            kb.mm(pv_, [(pv_[:, 0:T], bones_b[:, :], B["gsq"][:, 0:T])], reads=[bones_b, B["gsq"]])
            kb.op("dve", lambda: V.tensor_scalar(out=t1[:, 0:T], in0=pv_[:, 0:T], scalar1=1.0 / 64, scalar2=64e-5, op0=ALU.mult, op1=ALU.add),
                  reads=[pv_], writes=[t1])
            yield
            kb.op("act", lambda: S.activation(out=t1[:, 0:T], in_=t1[:, 0:T], func=AF.Sqrt), reads=[t1], writes=[t1])
            yield
            kb.op("dve", lambda: V.reciprocal(out=t1[:, 0:T], in_=t1[:, 0:T]), reads=[t1], writes=[t1])
            yield
            kb.op("dve", lambda: V.tensor_tensor(out=of[:, 0:T], in0=of[:, 0:T], in1=t1[:, 0:T], op=ALU.mult), reads=[of, t1], writes=[of])
            yield
            kb.op("dve", lambda: V.tensor_scalar(out=of[:, 0:T], in0=of[:, 0:T], scalar1=pvp[:, 5, pr:pr + 1], scalar2=pvp[:, 6, pr:pr + 1],
                                                 op0=ALU.mult, op1=ALU.add), reads=[of, pvp], writes=[of])
            yield
            kb.op("dve", lambda: V.tensor_tensor(out=of[:, 0:T], in0=of[:, 0:T], in1=bon[:, 0:T], op=ALU.add), reads=[of, bon], writes=[of])
            yield
            kb.op("dve", lambda: V.tensor_tensor(out=og[:, pr, og_cols], in0=of[:, 0:T], in1=og[:, pr, og_cols], op=ALU.mult), reads=[of, og], writes=[og])
            yield


        def _exhaust(g):
            for _ in g:
                pass

        def _interleave(g1, g2):
            d1 = d2 = False
            while not (d1 and d2):
                if not d1:
                    try:
                        next(g1)
                    except StopIteration:
                        d1 = True
                if not d2:
                    try:
                        next(g2)
                    except StopIteration:
                        d2 = True

        def rwkv_layer_tile(l, j, ntok, ti):
            B = BT_
            sample = ntok < 128
            dlt = B["dlt"]
            if not sample:
                for half in range(2):
                    c0 = half * 256
                    kb.op("dve", lambda c0=c0: V.tensor_tensor(out=dlt[:, :, 1:256], in0=hT[:, :, c0:c0 + 255], in1=hT[:, :, c0 + 1:c0 + 256],
                                                               op=ALU.subtract), reads=[hT], writes=[dlt])
                    if half == 0:
                        kb.op("dve", lambda: V.tensor_tensor(out=dlt[:, :, 0], in0=B["hlast"][:, :], in1=hT[:, :, 0], op=ALU.subtract),
                              reads=[hT, B["hlast"]], writes=[dlt])
                    else:
                        kb.op("dve", lambda: V.tensor_tensor(out=dlt[:, :, 0], in0=hT[:, :, 255], in1=hT[:, :, 256], op=ALU.subtract),
                              reads=[hT], writes=[dlt])
                    rwkv_mix_inputs(j, c0, 256, None)
                    chk("bmix")
                    prev = None
                    for pr in range(32):
                        g1 = rwkv_s1(j, pr, 256, 64, slice(half * 256, (half + 1) * 256))
                        if prev is None:
                            _exhaust(g1)
                        else:
                            _interleave(prev, g1)
                        prev = rwkv_s2(j, pr, 256, 64, slice(half * 256, (half + 1) * 256), B["ST2"], lambda ci, P, pr=pr: B["ST"][P, pr, :])
                    _exhaust(prev)
                kb.op("dve", lambda: V.tensor_copy(out=B["hlast"][:, :], in_=hT[:, :, 511]), reads=[hT], writes=[B["hlast"]])
            else:
                shf = B["t1"]
                kb.dma("sp", dlt[:, :, 256:256 + NSQ], I["state_shift"][j].rearrange("s (k p) -> p k s", p=128), writes=[dlt],
                       allow_slow_non_contiguous=True) if False else None
                stg = B["t2"]
                for sq in range(NSQ):
                    kb.dma("sp", stg[:, sq * KC:(sq + 1) * KC], I["state_shift"][j, sq].rearrange("(k p) -> p k", p=128),
                           writes=[stg], allow_slow_non_contiguous=True)
                stg3 = stg.h[:, 0:KC * NSQ].rearrange("p (s k) -> p k s", k=KC)
                h4 = hT.h[:, :, 0:NST].rearrange("p k (s t) -> p k s t", t=8)
                d4 = dlt.h[:, :, 0:NST].rearrange("p k (s t) -> p k s t", t=8)
                kb.op("dve", lambda: V.tensor_tensor(out=d4[:, :, :, 1:8], in0=h4[:, :, :, 0:7], in1=h4[:, :, :, 1:8], op=ALU.subtract),
                      reads=[hT], writes=[dlt])
                kb.op("dve", lambda: V.tensor_tensor(out=d4[:, :, :, 0], in0=stg3, in1=h4[:, :, :, 0], op=ALU.subtract),
                      reads=[hT, stg], writes=[dlt])
                rwkv_mix_inputs(j, 0, NST, None)
                Sin, STp, Sout = B["Sin"], B["STs"], B["Sout"]
                Sin_v = Sin.h[:, 0:NSQ * 64].rearrange("p (s i) -> p s i", s=NSQ)
                Sout_v = Sout.h[:, 0:NSQ * 64].rearrange("p (s i) -> p s i", s=NSQ)
                for pr in range(32):
                    kb.dma("sp", Sin_v, I["state_wkv"][j, :, 2 * pr:2 * pr + 2].rearrange("s h i j -> (h i) s j"), writes=[Sin])
                    for hp in range(2):
                        pt = psA()
                        kb.mm_multi(pt, [[(pt[hp * 64:(hp + 1) * 64, sq * 64:(sq + 1) * 64], Sin_v[hp * 64:(hp + 1) * 64, sq, :],
                                           ident_f[hp * 64:(hp + 1) * 64, hp * 64:(hp + 1) * 64])] for sq in range(NSQ)],
                                    reads=[Sin, ident_f])
                        kb.op("dve", lambda pt=pt, hp=hp: V.tensor_copy(out=STp[hp * 64:(hp + 1) * 64, 0:NSQ, :],
                                                                        in_=pt[hp * 64:(hp + 1) * 64, 0:NSQ * 64].rearrange("p (s i) -> p s i", s=NSQ)),
                              reads=[pt], writes=[B["STs2"][hp]])
                    _exhaust(rwkv_s1(j, pr, NST, 8, slice(0, NST)))
                    _exhaust(rwkv_s2(j, pr, NST, 8, slice(0, NST), B["STs2"], lambda ci, P: STp[P, ci, :]))
                    for hp in range(2):
                        pt = psA()
                        kb.mm_multi(pt, [[(pt[hp * 64:(hp + 1) * 64, sq * 64:(sq + 1) * 64], STp[hp * 64:(hp + 1) * 64, sq, :],
                                           ident_f[hp * 64:(hp + 1) * 64, hp * 64:(hp + 1) * 64])] for sq in range(NSQ)],
                                    reads=[B["STs2"][hp], ident_f])
                        kb.op("dve", lambda pt=pt, hp=hp: V.tensor_copy(out=Sout_v[hp * 64:(hp + 1) * 64],
                                                                        in_=pt[hp * 64:(hp + 1) * 64, 0:NSQ * 64].rearrange("p (s i) -> p s i", s=NSQ)),
                              reads=[pt], writes=[Sout])
                    kb.dma("sp", O["wkvs"][j, :, 2 * pr:2 * pr + 2].rearrange("s h i j -> (h i) s j"), Sout_v, reads=[Sout], writes=[])
                    kws_toks.append(kb_last_tok("sp"))
                shf = B["shf"]
                kb.op("dve", lambda: V.tensor_copy(out=shf[:, :, 0:NSQ], in_=h4[:, :, :, 7]), reads=[hT], writes=[shf])
                for sq in range(NSQ):
                    kb.dma("sp", O["shs"][j, sq].rearrange("(k p) -> p k", p=128), shf[:, :, sq], reads=[shf], writes=[],
                           allow_slow_non_contiguous=True)
                    kws_toks.append(kb_last_tok("sp"))

        def rwkv_outputs_prompt(j):
            B = BT_
            ST, Sout, shf = B["ST"], B["Sout"], B["shf"]
            Sout_v = Sout.h[:, 0:256].rearrange("p (s i) -> p s i", s=4)
            for p4 in range(0, 32, 4):
                for hp in range(2):
                    pt = psA()
                    kb.mm_multi(pt, [[(pt[hp * 64:(hp + 1) * 64, q * 64:(q + 1) * 64], ST[hp * 64:(hp + 1) * 64, p4 + q, :],
                                       ident_f[hp * 64:(hp + 1) * 64, hp * 64:(hp + 1) * 64])] for q in range(4)],
                                reads=[B["ST2"][hp], ident_f])
                    kb.op("dve", lambda pt=pt, hp=hp: V.tensor_copy(out=Sout_v[hp * 64:(hp + 1) * 64],
                                                                    in_=pt[hp * 64:(hp + 1) * 64, 0:256].rearrange("p (s i) -> p s i", s=4)),
                          reads=[pt], writes=[Sout])
                kb.dma("sp", O["wkvp"][j, 2 * p4:2 * p4 + 8].rearrange("(q h) i j -> (h i) q j", h=2), Sout_v, reads=[Sout], writes=[])
                kws_toks.append(kb_last_tok("sp"))
            kb.op("dve", lambda: V.tensor_copy(out=shf[:, :, 0], in_=hT[:, :, 511]), reads=[hT], writes=[shf])
            kb.dma("sp", O["shp"][j].rearrange("(k p) -> p k", p=128), shf[:, :, 0], reads=[shf], writes=[],
                   allow_slow_non_contiguous=True)
            kws_toks.append(kb_last_tok("sp"))


        CT_ = {}

        def gla_alloc(les, j):
            def sb(shape, dt, name):
                return kb.sb(shape, dt, name, es=les)
            Cc = CT_
            Cc["S"] = sb([128, 16, 512], F32, "glaS")
            kb.op("dve", lambda: V.memset(Cc["S"][:], 0.0), writes=[Cc["S"]])
            Cc["Ss"] = sb([128, 2, 512], F32, "glaSs")
            Cc["Sb"] = sb([128, 2, 512], BF16, "glaSb")
            Cc["wgk"] = sb([16, 256], F32, "wgk")
            Cc["bgk"] = sb([128, 16], F32, "bgk")
            kb.dma("sp", Cc["bgk"][:], I["b_gk_c"][j].rearrange("(c p) -> p c", p=128), writes=[Cc["bgk"]], allow_slow_non_contiguous=True)
            Cc["og_g"] = sb([128, 4], F32, "og_g")
            kb.dma("sp", Cc["og_g"][:], I["o_norm_g_c"][j].rearrange("(c p) -> p c", p=128), writes=[Cc["og_g"]], allow_slow_non_contiguous=True)
            Cc["gkl"] = sb([16, 512], F32, "gkl")
            Cc["wgl"] = sb([128, KC, 16], BF16, "wgl")
            kb.dma("pool", Cc["wgl"][:], I["w_in_c"][j].rearrange("(k p) n -> p k n", p=128)[:, :, 12288:12304], writes=[Cc["wgl"]])
            Cc["mk"] = sb([64, 64], F32, "mkc")
            kb.dma("sp", Cc["mk"][:], I["masks"][2], writes=[Cc["mk"]])
            for nm in ("q", "k", "b", "e1", "e2"):
                Cc[nm] = sb([128, 2, 512], F32, "c_" + nm)
            for nm in ("qt", "kt", "kh"):
                Cc[nm] = sb([128, 2, 512], BF16, "cb_" + nm)
            Cc["of"] = sb([128, 4, 512], F32, "c_of")
            Cc["osq"] = sb([128, 4, 512], BF16, "c_osq")
            Cc["vt"] = sb([64, 8, 512], BF16, "c_vt")
            Cc["att"] = sb([64, 64], BF16, "c_att")
            Cc["kht"] = sb([64, 256], BF16, "c_kht")
            Cc["rs"] = sb([128, 512], F32, "c_rs")
            Cc["ones64"] = sb([128, 64], F32, "c_ones64")
            kb.op("dve", lambda: V.memset(Cc["ones64"][:], 1.0), writes=[Cc["ones64"]])

        def gla_head(j, h, ntok, C, state_fn, og_cols):
            Cc = CT_
            w_in = I["w_in_c"][j]
            nch = ntok // C
            q, k, b, e1, e2, qt, kt, khh, of, vt = (Cc[n] for n in ("q", "k", "b", "e1", "e2", "qt", "kt", "kh", "of", "vt"))
            for (dst, col0) in ((q, h * 256), (k, 2048 + h * 256)):
                wt, view = load_w_in(w_in, col0, 256)
                for dc in range(2):
                    pq = psA()
                    kb.mm(pq, [(pq[:, 0:ntok], view[:, kc, dc * 128:(dc + 1) * 128], hT[:, kc, 0:ntok]) for kc in range(KC)], reads=[wt, hT])
                    kb.op("act", lambda pq=pq, dst=dst, dc=dc: S.copy(out=dst[:, dc, 0:ntok], in_=pq[:, 0:ntok]), reads=[pq], writes=[dst])
            kb.dma("sp", Cc["wgk"][:], I["w_gk_up_c"][j][:, h * 256:(h + 1) * 256], writes=[Cc["wgk"]])
            for dc in range(2):
                pg = psA()
                cch = 2 * h + dc
                kb.mm(pg, [(pg[:, 0:ntok], Cc["wgk"][:, dc * 128:(dc + 1) * 128], Cc["gkl"][:, 0:ntok])], reads=[Cc["wgk"], Cc["gkl"]])
                kb.op("act", lambda pg=pg, dc=dc, cch=cch: S.activation(out=e1[:, dc, 0:ntok], in_=pg[:, 0:ntok], func=AF.Sigmoid,
                                                                       bias=Cc["bgk"][:, cch:cch + 1]), reads=[pg, Cc["bgk"]], writes=[e1])
            kb.op("act", lambda: S.activation(out=e1[:, :, 0:ntok], in_=e1[:, :, 0:ntok], func=AF.Ln), reads=[e1], writes=[e1])
            for dc in range(2):
                for ci in range(nch):
                    kb.op("dve", lambda dc=dc, ci=ci: V.tensor_tensor_scan(out=b[:, dc, ci * C:(ci + 1) * C], data0=Cc["ones64"][:, 0:C],
                                                                           data1=e1[:, dc, ci * C:(ci + 1) * C], initial=0.0,
                                                                           op0=ALU.mult, op1=ALU.add), reads=[e1, Cc["ones64"]], writes=[b])
            kb.op("act", lambda: S.activation(out=e1[:, :, 0:ntok], in_=b[:, :, 0:ntok], func=AF.Exp, scale=1.0 / 16), reads=[b], writes=[e1])
            kb.op("act", lambda: S.activation(out=e2[:, :, 0:ntok], in_=b[:, :, 0:ntok], func=AF.Exp, scale=-1.0 / 16), reads=[b], writes=[e2])
            kb.op("dve", lambda: V.scalar_tensor_tensor(out=qt[:, :, 0:ntok], in0=q[:, :, 0:ntok], scalar=1.0 / 16, in1=e1[:, :, 0:ntok],
                                                        op0=ALU.mult, op1=ALU.mult), reads=[q, e1], writes=[qt])
            kb.op("dve", lambda: V.tensor_tensor(out=kt[:, :, 0:ntok], in0=k[:, :, 0:ntok], in1=e2[:, :, 0:ntok], op=ALU.mult), reads=[k, e2], writes=[kt])
            for dc in range(2):
                for ci in range(nch):
                    kb.op("dve", lambda dc=dc, ci=ci: V.tensor_scalar(out=khh[:, dc, ci * C:(ci + 1) * C], in0=kt[:, dc, ci * C:(ci + 1) * C],
                                                                      scalar1=e1[:, dc, (ci + 1) * C - 1:(ci + 1) * C], scalar2=None, op0=ALU.mult),
                          reads=[kt, e1], writes=[khh])
            wt, view = load_w_in(w_in, 4096 + h * 512, 512)
            for ci in range(nch):
                pv = psA()
                kb.mm(pv, [(pv[0:C, :], hT[:, kc, ci * C:(ci + 1) * C], view[:, kc, :]) for kc in range(KC)], reads=[wt, hT])
                kb.op("act", lambda ci=ci, pv=pv: S.copy(out=vt[0:C, ci, :], in_=pv[0:C, :]), reads=[pv], writes=[vt])
            wt, view = load_w_in(w_in, 8192 + h * 512, 512)
            for ec in range(4):
                pg = psA()
                kb.mm(pg, [(pg[:, 0:ntok], view[:, kc, ec * 128:(ec + 1) * 128], hT[:, kc, 0:ntok]) for kc in range(KC)], reads=[wt, hT])
                kb.op("act", lambda ec=ec, pg=pg: S.activation(out=og[:, h * 4 + ec, og_cols], in_=pg[:, 0:ntok], func=AF.Silu),
                      reads=[pg], writes=[og])
            for ci in range(nch):
                cc = slice(ci * C, (ci + 1) * C)
                St, Sap = state_fn(ci, "load")
                kb.op("act", lambda Sap=Sap: S.copy(out=Cc["Sb"][:], in_=Sap), reads=[St], writes=[Cc["Sb"]])
                pa = psA()
                kb.mm(pa, [(pa[0:C, 0:C], kt[:, dc, cc], qt[:, dc, cc]) for dc in range(2)], reads=[kt, qt])
                kb.op("dve", lambda pa=pa: V.tensor_tensor(out=Cc["att"][0:C, 0:C], in0=pa[0:C, 0:C], in1=Cc["mk"][0:C, 0:C], op=ALU.mult),
                      reads=[pa, Cc["mk"]], writes=[Cc["att"]])
                po = psA()
                groups = []
                for ec in range(4):
                    o_ap = po[:, ec * C:(ec + 1) * C]
                    groups.append([(o_ap, Cc["Sb"][:, dc, ec * 128:(ec + 1) * 128], qt[:, dc, cc]) for dc in range(2)]
                                  + [(o_ap, vt[0:C, ci, ec * 128:(ec + 1) * 128], Cc["att"][0:C, 0:C])])
                kb.mm_multi(po, groups, reads=[Cc["Sb"], qt, vt, Cc["att"]])
                kb.op("act", lambda po=po, cc=cc: S.copy(out=of[:, :, cc], in_=po[:, 0:4 * C].rearrange("p (e t) -> p e t", e=4)),
                      reads=[po], writes=[of])
                pt = psB()
                kb.transpose(pt, [(pt[0:C, dc * 128:(dc + 1) * 128], khh[:, dc, cc], ident_b[:, :]) for dc in range(2)], reads=[khh, ident_b])
                kb.op("act", lambda pt=pt: S.copy(out=Cc["kht"][0:C, :], in_=pt[0:C, 0:256]), reads=[pt], writes=[Cc["kht"]])
                for dc in range(2):
                    ps_ = psA()
                    kb.mm(ps_, [(ps_[:, :], Cc["kht"][0:C, dc * 128:(dc + 1) * 128], vt[0:C, ci, :])], reads=[Cc["kht"], vt])
                    St, Sap = state_fn(ci, "store")
                    kb.op("dve", lambda ps_=ps_, Sap=Sap, dc=dc, ci=ci: V.scalar_tensor_tensor(
                        out=Sap[:, dc, :], in0=Sap[:, dc, :], scalar=e1[:, dc, (ci + 1) * C - 1:(ci + 1) * C], in1=ps_[:, :],
                        op0=ALU.mult, op1=ALU.add), reads=[St, e1, ps_], writes=[St])
                state_fn(ci, "done")
            osq, rs = Cc["osq"], Cc["rs"]
            kb.op("act", lambda: S.activation(out=osq[:, :, 0:ntok], in_=of[:, :, 0:ntok], func=AF.Square), reads=[of], writes=[osq])
            pn = psA()
            kb.mm(pn, [(pn[:, 0:ntok], ones_b[:, :], osq[:, ec, 0:ntok]) for ec in range(4)], reads=[ones_b, osq])
            kb.op("dve", lambda: V.tensor_scalar(out=rs[:, 0:ntok], in0=pn[:, 0:ntok], scalar1=1.0 / 512, scalar2=NORM_EPS, op0=ALU.mult, op1=ALU.add),
                  reads=[pn], writes=[rs])
            kb.op("act", lambda: S.activation(out=rs[:, 0:ntok], in_=rs[:, 0:ntok], func=AF.Sqrt), reads=[rs], writes=[rs])
            kb.op("dve", lambda: V.reciprocal(out=rs[:, 0:ntok], in_=rs[:, 0:ntok]), reads=[rs], writes=[rs])
            for ec in range(4):
                kb.op("dve", lambda ec=ec: V.scalar_tensor_tensor(out=of[:, ec, 0:ntok], in0=of[:, ec, 0:ntok], scalar=Cc["og_g"][:, ec:ec + 1],
                                                                   in1=rs[:, 0:ntok], op0=ALU.mult, op1=ALU.mult), reads=[of, Cc["og_g"], rs], writes=[of])
                kb.op("dve", lambda ec=ec: V.tensor_tensor(out=og[:, h * 4 + ec, og_cols], in0=of[:, ec, 0:ntok], in1=og[:, h * 4 + ec, og_cols],
                                                            op=ALU.mult), reads=[of, og], writes=[og])

        def gla_layer_tile(l, j, ntok, ti):
            Cc = CT_
            sample = ntok < 128
            pl = psA()
            kb.mm(pl, [(pl[0:16, 0:ntok], Cc["wgl"][:, kc, :], hT[:, kc, 0:ntok]) for kc in range(KC)], reads=[Cc["wgl"], hT])
            kb.op("act", lambda: S.copy(out=Cc["gkl"][:, 0:ntok], in_=pl[0:16, 0:ntok]), reads=[pl], writes=[Cc["gkl"]])
            for h in range(8):
                if not sample:
                    gla_head(j, h, ntok, 64, lambda ci, what, h=h: (Cc["S"], Cc["S"][:, 2 * h:2 * h + 2, :]), slice(0, ntok))
                else:
                    def sfn(ci, what, h=h):
                        if what == "load":
                            kb.dma("sp", Cc["Ss"][:], I["state_gla"][j, ci, h].rearrange("(c p) e -> p c e", p=128), writes=[Cc["Ss"]])
                        elif what == "done":
                            kb.dma("sp", O["glas"][j, ci, h].rearrange("(c p) e -> p c e", p=128), Cc["Ss"][:], reads=[Cc["Ss"]], writes=[])
                            kws_toks.append(kb_last_tok("sp"))
                        return (Cc["Ss"], Cc["Ss"][:, :, :])
                    gla_head(j, h, ntok, 8, sfn, slice(0, ntok))

        def gla_outputs_prompt(j):
            Cc = CT_
            for h in range(8):
                kb.dma("sp", O["glap"][j, h].rearrange("(c p) e -> p c e", p=128), Cc["S"][:, 2 * h:2 * h + 2, :], reads=[Cc["S"]], writes=[])
                kws_toks.append(kb_last_tok("sp"))

        try:
            cnt_kind = {0: 0, 1: 0, 2: 0}
            for l, kind in enumerate(layers):
                j = cnt_kind[kind]
                cnt_kind[kind] += 1
                with contextlib.ExitStack() as les:
                    layer_psum(kind, les)
                    if kind == 0:
                        attn_alloc(les)
                    elif kind == 1:
                        rwkv_alloc(les, j)
                    else:
                        gla_alloc(les, j)
                    for ti in range(NT + 1):
                        sample = ti == NT
                        ntok = NST if sample else 512
                        if sample:
                            src = I["xs"] if l == 0 else O["ys"]
                            src_t = [] if l == 0 else [ys_t]
                            dst, dst_t = O["ys"], ys_t
                        else:
                            src = (I["xp"] if l == 0 else O["yp"])[ti * 512:(ti + 1) * 512, :]
                            src_t = [] if l == 0 else [yp_t[ti]]
                            dst, dst_t = O["yp"][ti * 512:(ti + 1) * 512, :], yp_t[ti]
                        chk("alloc")
                        load_norm(l, src, ntok, src_t)
                        chk("norm")
                        if kind == 0:
                            if sample:
                                attn_sample_prep(j)
                            attn_layer_tile(l, j, ntok, ti, last=(ti == NT - 1))
                            chk("atile")
                            if ti == NT - 1:
                                attn_outputs_prompt(j)
                            if sample:
                                attn_outputs_sample(j)
                            out_proj(I["w_out_a"][j], ntok, dst, dst_t)
                        elif kind == 1:
                            rwkv_layer_tile(l, j, ntok, ti)
                            if ti == NT - 1:
                                rwkv_outputs_prompt(j)
                            out_proj(I["w_out_b"][j], ntok, dst, dst_t)
                        else:
                            gla_layer_tile(l, j, ntok, ti)
                            if ti == NT - 1:
                                gla_outputs_prompt(j)
                            out_proj(I["w_out_c"][j], ntok, dst, dst_t)
                    kb.barrier()
        except _Stop:
            pass
        kb.finish([t.w for t in out_regions] + kws_toks)
    return nc


_PROG = {}


def _get_prog():
    if "nc" not in _PROG:
        _PROG["nc"] = build_program(dict(TP=4096, layers=[0, 1, 2, 0]))
    return _PROG["nc"]


def kernel(**inputs):
    f32 = np.float32
    A = {k: np.ascontiguousarray(np.asarray(v, dtype=f32)) for k, v in inputs.items()}
    n = 8
    consts = host_consts()
    shared = dict(consts)
    for nm in ("norm_g", "rel_bias", "w_in_a", "q_norm_g", "k_norm_g", "sinks", "w_out_a", "mu_b", "w_rkvg_b", "w_lora_down_b",
               "w_lora_up_b", "w0_b", "a0_b", "k_k_b", "k_a_b", "ln_x_g_b", "ln_x_b_b", "w_out_b", "w_in_c", "w_gk_up_c",
               "b_gk_c", "o_norm_g_c", "w_out_c"):
        shared[nm] = A[nm]
    shared["r_k_b"] = A["r_k_b"].reshape(A["r_k_b"].shape[0], -1)
    zeros_p = np.zeros((4096, D), f32)
    in_maps = []
    for c in range(n):
        m = dict(shared)
        m["xp"] = A["x_prompt"][c] if c < 2 else zeros_p
        sl = slice(4 * c, 4 * c + 4)
        m["xs"] = A["x_sample"][sl].reshape(32, D)
        m["cache_k"] = np.ascontiguousarray(A["cache_k_win"][:, sl].reshape(2, 4, 128, 512))
        m["cache_v"] = np.ascontiguousarray(A["cache_v_win"][:, sl].reshape(2, 4, 128, 512))
        m["state_wkv"] = np.ascontiguousarray(A["state_wkv"][:, sl])
        m["state_shift"] = np.ascontiguousarray(A["state_shift"][:, sl])
        m["state_gla"] = np.ascontiguousarray(A["state_gla"][:, sl])
        in_maps.append(m)
    nc = _get_prog()
    res = run_bass_kernel_spmd(nc, in_maps, core_ids=list(range(n)))
    R = res.results

    def cat_s(key, shape_tail, lead=True):
        return np.concatenate([np.asarray(R[c][key], f32) for c in range(n)], axis=1)

    y_prompt = np.stack([np.asarray(R[c]["yp"], f32) for c in range(2)], axis=0)
    y_sample = np.concatenate([np.asarray(R[c]["ys"], f32).reshape(4, 8, D) for c in range(n)], axis=0)
    kwp = np.stack([np.asarray(R[c]["kwp"], f32) for c in range(2)], axis=1).reshape(2, 2, 128, 8, 64)
    vwp = np.stack([np.asarray(R[c]["vwp"], f32) for c in range(2)], axis=1).reshape(2, 2, 128, 8, 64)
    kws = cat_s("kws", None).reshape(2, 32, 128, 8, 64)
    vws = cat_s("vws", None).reshape(2, 32, 128, 8, 64)
    wkvp = np.stack([np.asarray(R[c]["wkvp"], f32) for c in range(2)], axis=1)
    shp = np.stack([np.asarray(R[c]["shp"], f32) for c in range(2)], axis=1)
    wkvs = cat_s("wkvs", None)
    shs = cat_s("shs", None)
    glap = np.stack([np.asarray(R[c]["glap"], f32) for c in range(2)], axis=1)
    glas = cat_s("glas", None)
    return (y_prompt, y_sample, kwp, vwp, kws, vws, wkvp, shp, wkvs, shs, glap, glas)
```

```python
import contextlib
import math
import numpy as np
import concourse.bass as bass
import concourse.mybir as mybir
from concourse.bass_utils import run_bass_kernel_spmd

F32 = mybir.dt.float32
BF16 = mybir.dt.bfloat16
ALU = mybir.AluOpType
AF = mybir.ActivationFunctionType
AX = mybir.AxisListType

D = 2048
DI = 4096
KC = D // 128
IC = DI // 128
WINDOW = 128
NEG = -30000.0
NORM_EPS = 1e-6


class _Stop(Exception):
    pass


class TT:
    def __init__(self, h, name=""):
        self.h = h
        self.name = name
        self.w = None
        self.r = {}
        self.psum = False
        self.pending = False

    def __getitem__(self, idx):
        return self.h[idx]


class AliasTT:
    def __init__(self, base, view):
        self.base = base
        self.h = view
        self.name = base.name + "_alias"
        self.psum = base.psum

    def __getitem__(self, idx):
        return self.h[idx]

    @property
    def w(self):
        return self.base.w

    @w.setter
    def w(self, v):
        self.base.w = v

    @property
    def r(self):
        return self.base.r

    @r.setter
    def r(self, v):
        self.base.r = v


class KB:
    NDS = 12
    LIMIT = 10 ** 9

    def __init__(self, nc, es):
        self.nc = nc
        self.es = es
        self.eng = {"pe": nc.tensor, "act": nc.scalar, "dve": nc.vector, "pool": nc.gpsimd, "sp": nc.sync}
        self.sets = [{}, {}]
        self.phase = {}
        self.cnt = {}
        self.seen = {}
        self.dj = {}
        self.ep = 0
        self.nbar = 0
        for si in range(2):
            for e in self.eng:
                self.sets[si][e] = es.enter_context(nc.semaphore("c%d_%s" % (si, e)))
            for q in ("sp", "pool"):
                for j in range(self.NDS):
                    self.sets[si][("d", q, j)] = es.enter_context(nc.semaphore("d%d_%s_%d" % (si, q, j)))
        for e in self.eng:
            self.phase[e] = es.enter_context(nc.semaphore("ph_" + e))
            self.cnt[e] = 0
            self.seen[e] = {}
        for q in ("sp", "pool"):
            self.dj[q] = 0
        self.nalloc = 0
        self.dead = False
        self.last_dma_tok = None

    def semh(self, k):
        return self.sets[self.ep % 2][k]

    def sb(self, shape, dt, name=None, es=None):
        self.nalloc += 1
        name = (name or "t") + "_%d" % self.nalloc
        return TT((es or self.es).enter_context(self.nc.sbuf_tensor(name, list(shape), dt)), name)

    def ps(self, shape, dt, name=None, es=None):
        self.nalloc += 1
        name = (name or "p") + "_%d" % self.nalloc
        t = TT((es or self.es).enter_context(self.nc.psum_tensor(name, list(shape), dt)), name)
        t.psum = True
        return t

    def barrier(self):
        if self.dead:
            return
        toks = {}
        for e in self.eng:
            if self.cnt[e] > 0:
                toks[e] = self.cnt[e]
        for q in ("sp", "pool"):
            for j in range(self.NDS):
                n = (self.dj[q] - j + self.NDS - 1) // self.NDS if self.dj[q] > j else 0
                if n > 0:
                    toks[("d", q, j)] = 16 * n
        for e in self.eng:
            for k, v in toks.items():
                if k == e:
                    continue
                self._wait(e, (k, v, self.ep))

    def epoch_barrier(self):
        self.barrier()
        nxt = (self.ep + 1) % 2
        self.nbar += 1
        for e in self.eng:
            self.eng[e].sem_clear(self.sets[nxt][e])
            if e in ("sp", "pool"):
                for j in range(self.NDS):
                    self.eng[e].sem_clear(self.sets[nxt][("d", e, j)])
            self.eng[e].sem_inc(self.phase[e], 1)
        for e in self.eng:
            for f in self.eng:
                if f != e:
                    self.eng[e].wait_ge(self.phase[f], self.nbar)
        self.ep += 1
        for e in self.eng:
            self.cnt[e] = 0
            self.seen[e] = {}
        for q in ("sp", "pool"):
            self.dj[q] = 0

    def maybe_epoch(self):
        if max(self.cnt.values()) >= self.LIMIT or max(self.dj.values()) >= self.LIMIT // 2:
            self.epoch_barrier()

    def _wait(self, E, tok):
        if self.dead or tok is None:
            return
        k, v, ep = tok
        if ep < self.ep:
            return
        if self.seen[E].get(k, 0) >= v:
            return
        self.eng[E].wait_ge(self.semh(k), v)
        self.seen[E][k] = v

    def _deps(self, E, reads, writes):
        need = {}

        def add(tok):
            if tok is None:
                return
            k, v, ep = tok
            if ep < self.ep:
                return
            if need.get(k, 0) < v:
                need[k] = v

        for t in reads:
            add(t.w)
            if t.psum:
                for rk, tok in t.r.items():
                    if rk != E:
                        add(tok)
        for t in writes:
            add(t.w)
            for tok in t.r.values():
                add(tok)
        for k, v in need.items():
            if k == "pe" and E == "pe":
                continue
            self._wait(E, (k, v, self.ep))

    def _done(self, E, ins, reads, writes):
        if E == "pe":
            for t in writes:
                t.pending = True
        else:
            for t in reads:
                if t.psum:
                    t.pending = False
        self.cnt[E] += 1
        ins.then_inc(self.semh(E), 1)
        tok = (E, self.cnt[E], self.ep)
        for t in writes:
            t.w = tok
            t.r = {}
        for t in reads:
            if t not in writes:
                t.r[E] = tok
        return tok

    def op(self, E, fn, reads=(), writes=()):
        if self.dead:
            return None
        self.maybe_epoch()
        self._deps(E, reads, writes)
        ins = fn()
        return self._done(E, ins, reads, writes)

    def mm(self, out_t, mms, reads):
        return self.mm_multi(out_t, [mms], reads)

    def mm_multi(self, out_t, groups, reads):
        if self.dead:
            return None
        self.maybe_epoch()
        self._deps("pe", reads, [out_t])
        ins = None
        for mms in groups:
            n = len(mms)
            for i, (o, l, r) in enumerate(mms):
                ins = self.nc.tensor.matmul(o, lhsT=l, rhs=r, start=(i == 0), stop=(i == n - 1))
        return self._done("pe", ins, reads, [out_t])

    def transpose(self, out_t, items, reads):
        if self.dead:
            return None
        self.maybe_epoch()
        self._deps("pe", reads, [out_t])
        ins = None
        for (o, i, idn) in items:
            ins = self.nc.tensor.transpose(o, i, idn)
        return self._done("pe", ins, reads, [out_t])

    def dma(self, q, out_ap, in_ap, reads=(), writes=(), **kw):
        if self.dead:
            return None
        self.maybe_epoch()
        self._deps(q, reads, writes)
        i = self.dj[q]
        self.dj[q] += 1
        j = i % self.NDS
        rnd = i // self.NDS
        key = ("d", q, j)
        if rnd > 0:
            self._wait(q, (key, 16 * rnd, self.ep))
        self.eng[q].dma_start(out=out_ap, in_=in_ap, **kw).then_inc(self.semh(key), 16)
        tok = (key, 16 * (rnd + 1), self.ep)
        for t in writes:
            t.w = tok
            t.r = {}
        for t in reads:
            t.r[key] = tok
        self.last_dma_tok = tok
        return tok

    def finish(self, toks):
        self.dead = False
        for tok in toks:
            if tok is not None:
                self._wait("sp", tok)


def t5_bucket_np(d):
    d = np.maximum(d, 0)
    large = 16 + (np.log(np.maximum(d, 1).astype(np.float32) / np.float32(16)) / np.float32(math.log(128 / 16))
                  * np.float32(16)).astype(np.int32)
    return np.where(d < 16, d, np.minimum(large, 31))


def host_consts():
    c = {}
    c["ident"] = np.eye(128, dtype=np.float32)
    bo = np.zeros((128, 128), np.float32)
    bo[:64, :64] = 1.0
    bo[64:, 64:] = 1.0
    c["blockones"] = bo
    oh = np.zeros((33, 384), np.float32)
    for m in range(384):
        dist = m - 128
        if 0 <= dist < 128:
            oh[int(t5_bucket_np(np.array(dist))), m] = 1.0
        else:
            oh[32, m] = NEG
    c["bucket_oh"] = oh
    t = np.arange(64)
    mk = np.zeros((3, 64, 64), np.float32)
    mk[0] = (t[:, None] < t[None, :])
    mk[1] = (t[None, :] < t[:, None])
    mk[2] = (t[:, None] <= t[None, :])
    c["masks"] = mk
    return c


def build_program(cfg):
    TP = cfg["TP"]
    layers = cfg["layers"]
    NSQ = 4
    NST = NSQ * 8
    nA = sum(1 for k in layers if k == 0)
    nB = sum(1 for k in layers if k == 1)
    nC = sum(1 for k in layers if k == 2)
    NL = len(layers)
    assert TP % 512 == 0
    NT = TP // 512

    nc = bass.Bass("TRN2", target_bir_lowering=False)

    def din(name, shape, dt=F32):
        return nc.dram_tensor(name, list(shape), dt, kind="ExternalInput").ap()

    def dout(name, shape, dt=F32):
        return nc.dram_tensor(name, list(shape), dt, kind="ExternalOutput").ap()

    I = {}
    I["xp"] = din("xp", [TP, D])
    I["xs"] = din("xs", [NST, D])
    I["norm_g"] = din("norm_g", [NL, D])
    I["ident"] = din("ident", [128, 128])
    I["blockones"] = din("blockones", [128, 128])
    I["bucket_oh"] = din("bucket_oh", [33, 384])
    if nA:
        A_IN = DI + 1024 + DI
        I["cache_k"] = din("cache_k", [nA, NSQ, 128, 512])
        I["cache_v"] = din("cache_v", [nA, NSQ, 128, 512])
        I["rel_bias"] = din("rel_bias", [32, 64])
        I["w_in_a"] = din("w_in_a", [nA, D, A_IN])
        I["q_norm_g"] = din("q_norm_g", [nA, 64])
        I["k_norm_g"] = din("k_norm_g", [nA, 64])
        I["sinks"] = din("sinks", [nA, 64])
        I["w_out_a"] = din("w_out_a", [nA, DI, D])
    if nB:
        I["state_wkv"] = din("state_wkv", [nB, NSQ, 64, 64, 64])
        I["state_shift"] = din("state_shift", [nB, NSQ, D])
        I["mu_b"] = din("mu_b", [nB, 6, D])
        I["w_rkvg_b"] = din("w_rkvg_b", [nB, 4, D, DI])
        I["w_lora_down_b"] = din("w_lora_down_b", [nB, 2, D, 96])
        I["w_lora_up_b"] = din("w_lora_up_b", [nB, 2, 96, DI])
        for nm in ("w0_b", "a0_b", "k_k_b", "k_a_b", "r_k_b", "ln_x_g_b", "ln_x_b_b"):
            I[nm] = din(nm, [nB, DI])
        I["w_out_b"] = din("w_out_b", [nB, DI, D])
        I["masks"] = din("masks", [3, 64, 64])
    if nC:
        C_IN = 2 * 2048 + 2 * DI + 16
        I["state_gla"] = din("state_gla", [nC, NSQ, 8, 256, 512])
        I["w_in_c"] = din("w_in_c", [nC, D, C_IN])
        I["w_gk_up_c"] = din("w_gk_up_c", [nC, 16, 2048])
        I["b_gk_c"] = din("b_gk_c", [nC, 2048])
        I["o_norm_g_c"] = din("o_norm_g_c", [nC, 512])
        I["w_out_c"] = din("w_out_c", [nC, DI, D])
        if "masks" not in I:
            I["masks"] = din("masks", [3, 64, 64])
    O = {}
    if nC:
        O["glap"] = dout("glap", [nC, 8, 256, 512])
        O["glas"] = dout("glas", [nC, NSQ, 8, 256, 512])
    if nB:
        O["wkvp"] = dout("wkvp", [nB, 64, 64, 64])
        O["shp"] = dout("shp", [nB, D])
        O["wkvs"] = dout("wkvs", [nB, NSQ, 64, 64, 64])
        O["shs"] = dout("shs", [nB, NSQ, D])
    O["yp"] = dout("yp", [TP, D])
    O["ys"] = dout("ys", [NST, D])
    if nA:
        O["kwp"] = dout("kwp", [nA, 128, 512])
        O["vwp"] = dout("vwp", [nA, 128, 512])
        O["kws"] = dout("kws", [nA, NSQ, 128, 512])
        O["vws"] = dout("vws", [nA, NSQ, 128, 512])
    zrep = nc.dram_tensor("zrep", [64, 128 * 384], F32, kind="Internal").ap() if nA else None

    es = contextlib.ExitStack()
    with es:
        kb = KB(nc, es)
        V = nc.vector
        S = nc.scalar
        G = nc.gpsimd
        out_regions = []

        def chk(stage):
            if cfg.get("stop") == stage:
                kb.dead = True

        yp_t = [TT(None, "yp%d" % t) for t in range(NT)]
        ys_t = TT(None, "ys")
        out_regions += yp_t + [ys_t]

        ident_f = kb.sb([128, 128], F32, "ident_f")
        ident_b = kb.sb([128, 128], BF16, "ident_b")
        bones_b = kb.sb([128, 128], BF16, "bones_b")
        ones_b = kb.sb([128, 128], BF16, "ones_b")
        gT = kb.sb([128, NL, KC], F32, "gT")
        kb.dma("sp", ident_f[:], I["ident"], writes=[ident_f])
        kb.dma("pool", ident_b[:], I["ident"], writes=[ident_b])
        kb.dma("pool", bones_b[:], I["blockones"], writes=[bones_b])
        bones_f = kb.sb([128, 128], F32, "bones_f")
        kb.dma("sp", bones_f[:], I["blockones"], writes=[bones_f])
        kb.op("dve", lambda: V.memset(ones_b[:], 1.0), writes=[ones_b])
        kb.dma("sp", gT[:], I["norm_g"].rearrange("l (kc p) -> p l kc", p=128), writes=[gT],
               allow_slow_non_contiguous=True)

        xt = kb.sb([128, 4, D], F32, "xt")
        hT = kb.sb([128, KC, 512], BF16, "hT")
        og = kb.sb([128, IC, 512], BF16, "og")
        wbuf = [kb.sb([128, 8192], BF16, "wbuf%d" % i) for i in range(2)]
        wsel = [0]
        stat = kb.sb([128, 16], F32, "stat")
        PSA = []
        PSB = []
        PSX = []
        psa_i = [0]
        psb_i = [0]

        def layer_psum(kind, les):
            na, nb_, nx = {0: (4, 2, 2), 1: (5, 1, 2), 2: (6, 2, 0)}[kind]
            PSA[:] = [kb.ps([128, 512], F32, "psA%d" % i, es=les) for i in range(na)]
            PSB[:] = [kb.ps([128, 1024], BF16, "psB%d" % i, es=les) for i in range(nb_)]
            PSX[:] = [kb.ps([128, 512], F32, "psX%d" % i, es=les) for i in range(nx)]

        def psA():
            psa_i[0] += 1
            t = PSA[psa_i[0] % len(PSA)]
            assert kb.dead or not t.pending, "PSUM tile handed out before its previous evacuation was emitted"
            return t

        def psB():
            psb_i[0] += 1
            t = PSB[psb_i[0] % len(PSB)]
            assert kb.dead or not t.pending, "PSUM tile handed out before its previous evacuation was emitted"
            return t

        def next_w():
            wsel[0] += 1
            return wbuf[wsel[0] % 2]

        def load_norm(l, src_ap, ntok, src_reads):
            nsub = (ntok + 127) // 128
            pp = min(ntok, 128)
            if ntok >= 128:
                kb.dma("sp", xt[:, 0:nsub, :], src_ap.rearrange("(s p) d -> p s d", p=128), reads=src_reads, writes=[xt])
            else:
                kb.dma("sp", xt[0:pp, 0, :], src_ap, reads=src_reads, writes=[xt])
            for s in range(nsub):
                kb.op("act", lambda s=s: S.activation(out=hT.h[0:pp, 0:4, :].rearrange("p a b -> p (a b)"), in_=xt[0:pp, s, :],
                                                     func=AF.Square, accum_out=stat[0:pp, s:s + 1]),
                      reads=[xt], writes=[hT, stat])
            kb.op("dve", lambda: V.tensor_scalar(out=stat[0:pp, 4:4 + nsub], in0=stat[0:pp, 0:nsub], scalar1=1.0 / D,
                                                 scalar2=NORM_EPS, op0=ALU.mult, op1=ALU.add), reads=[stat], writes=[stat])
            kb.op("act", lambda: S.activation(out=stat[0:pp, 8:8 + nsub], in_=stat[0:pp, 4:4 + nsub], func=AF.Sqrt),
                  reads=[stat], writes=[stat])
            kb.op("dve", lambda: V.reciprocal(out=stat[0:pp, 12:12 + nsub], in_=stat[0:pp, 8:8 + nsub]),
                  reads=[stat], writes=[stat])
            xn = og.h[:].rearrange("p a b -> p (a b)")
            for s in range(nsub):
                kb.op("dve", lambda s=s: V.tensor_scalar(out=xn[0:pp, s * D:(s + 1) * D], in0=xt[0:pp, s, :],
                                                         scalar1=stat[0:pp, 12 + s:13 + s], scalar2=None, op0=ALU.mult),
                      reads=[xt, stat], writes=[og])
            for kc in range(KC):
                pt = psB()
                kb.transpose(pt, [(pt[:, s * 128:s * 128 + pp], xn[0:pp, s * D + kc * 128:s * D + (kc + 1) * 128],
                                   ident_b[0:pp, 0:pp]) for s in range(nsub)], reads=[og, ident_b])
                kb.op("act", lambda kc=kc, pt=pt: S.activation(out=hT[:, kc, 0:ntok], in_=pt[:, 0:ntok], func=AF.Copy,
                                                                scale=gT[:, l, kc:kc + 1]),
                      reads=[pt, gT], writes=[hT])

        def load_w_in(w_ap, col0, ncols):
            wt = next_w()
            view = wt.h[:, 0:KC * ncols].rearrange("p (k n) -> p k n", k=KC)
            kb.dma("pool", view, w_ap.rearrange("(k p) n -> p k n", p=128)[:, :, col0:col0 + ncols], writes=[wt])
            return wt, view

        def out_proj(w_ap, ntok, dst_ap, dst_t):
            nsub = (ntok + 127) // 128
            pp = min(ntok, 128)
            for cb in range(D // 256):
                wt = next_w()
                view = wt.h[:, 0:IC * 256].rearrange("p (k n) -> p k n", k=IC)
                kb.dma("pool", view, w_ap.rearrange("(k p) n -> p k n", p=128)[:, :, cb * 256:(cb + 1) * 256], writes=[wt])
                for s in range(nsub):
                    pt = psA()
                    kb.mm(pt, [(pt[0:pp, 0:256], og[:, ic, s * 128:s * 128 + pp], view[:, ic, :]) for ic in range(IC)],
                          reads=[og, wt])
                    kb.op("dve", lambda s=s, pt=pt, cb=cb: V.tensor_tensor(out=xt[0:pp, s, cb * 256:(cb + 1) * 256],
                                                                            in0=pt[0:pp, 0:256],
                                                                            in1=xt[0:pp, s, cb * 256:(cb + 1) * 256], op=ALU.add),
                          reads=[pt, xt], writes=[xt])
            if ntok >= 128:
                kb.dma("sp", dst_ap.rearrange("(s p) d -> p s d", p=128), xt[:, 0:nsub, :], reads=[xt], writes=[dst_t])
            else:
                kb.dma("sp", dst_ap, xt[0:pp, 0, :], reads=[xt], writes=[dst_t])

        AT = {}

        def attn_alloc(les):
            def sb(shape, dt, name):
                return kb.sb(shape, dt, name, es=les)
            AT["BTp"] = sb([128, 64, 128], BF16, "BTp")
            AT["BTc"] = sb([128, 64, 128], BF16, "BTc")
            with contextlib.ExitStack() as zes:
                rb33 = kb.sb([33, 64], F32, "rb33", es=zes)
                oh33 = kb.sb([33, 384], F32, "oh33", es=zes)
                zsb = kb.sb([64, 384], F32, "zsb", es=zes)
                kb.op("dve", lambda: V.memset(rb33[:], 1.0), writes=[rb33])
                kb.dma("sp", rb33[0:32, :], I["rel_bias"], writes=[rb33])
                kb.dma("sp", oh33[:], I["bucket_oh"], writes=[oh33])
                pz = psA()
                kb.mm(pz, [(pz[0:64, 0:384], rb33[:, :], oh33[:, :])], reads=[rb33, oh33])
                kb.op("dve", lambda: V.tensor_copy(out=zsb[:], in_=pz[0:64, 0:384]), reads=[pz], writes=[zsb])
                zr3 = zrep.rearrange("h (r m) -> h r m", m=384)
                zrp = kb.sb([64, 16, 384], F32, "zrp", es=zes)
                kb.op("dve", lambda: V.tensor_copy(out=zrp[:], in_=zsb[:, :].unsqueeze(1).to_broadcast([64, 16, 384])),
                      reads=[zsb], writes=[zrp])
                ztoks = []
                for r0 in range(0, 128, 16):
                    kb.dma("sp", zr3[:, r0:r0 + 16, :], zrp[:], reads=[zrp], writes=[])
                    ztoks.append(kb.last_dma_tok)
                for tk in ztoks:
                    kb._wait("pool", tk)
                for (BT, off) in ((AT["BTc"], 128), (AT["BTp"], 256)):
                    src = bass.AP(tensor=zrep.tensor, offset=zrep.offset + off, ap=[[383, 128], [128 * 384, 64], [1, 128]])
                    kb.dma("pool", BT[:], src, reads=[zrep_t], writes=[BT])
                kb.barrier()
            chk("bt")
            AT["kT"] = sb([128, 8, 640], BF16, "kT")
            AT["vtk"] = sb([128, 5, 512], BF16, "vtk")
            AT["knf"] = sb([128, 8, 128], F32, "knf")
            AT["vlf"] = sb([128, 512], F32, "vlf")
            AT["qn"] = sb([128, 4, 512], BF16, "qn")
            AT["sg"] = sb([128, 4, 512], BF16, "sg")
            AT["sqb"] = sb([128, 512], BF16, "sqb")
            AT["rsd"] = sb([128, 512], F32, "rsd")
            AT["pT"] = [sb([128, 512], BF16, "pT%d" % i) for i in range(2)]
            AT["tmpf"] = [sb([128, 512], F32, "tmpf%d" % i) for i in range(2)]
            AT["den"] = sb([128, 512], F32, "den")
            gq = AT["gq"] = sb([128, nA], F32, "gq")
            gk = AT["gk"] = sb([128, nA], F32, "gk")
            esk = AT["esk"] = sb([128, nA, 32], F32, "esk")
            for par in range(2):
                kb.dma("sp", gq[par * 64:(par + 1) * 64, :], I["q_norm_g"].rearrange("l d -> d l"), writes=[gq],
                       allow_slow_non_contiguous=True)
                kb.dma("sp", gk[par * 64:(par + 1) * 64, :], I["k_norm_g"].rearrange("l d -> d l"), writes=[gk],
                       allow_slow_non_contiguous=True)
            kb.op("dve", lambda: V.memset(esk[:], 0.0), writes=[esk])
            for par in range(2):
                sk = I["sinks"].rearrange("l (c two) -> two l c", two=2)[par:par + 1]
                kb.dma("sp", esk[par * 64:par * 64 + 1, :, :], sk, writes=[esk], allow_slow_non_contiguous=True)
            pe_ = psA()
            kb.mm(pe_, [(pe_[:, 0:nA * 32], bones_f[:, :], esk[:].rearrange("p l c -> p (l c)"))], reads=[bones_f, esk])
            kb.op("act", lambda: S.activation(out=esk[:].rearrange("p l c -> p (l c)"), in_=pe_[:, 0:nA * 32], func=AF.Exp),
                  reads=[pe_], writes=[esk])
            kb.op("dve", lambda: V.tensor_scalar(out=gq[:], in0=gq[:], scalar1=0.125, scalar2=None, op0=ALU.mult),
                  reads=[gq], writes=[gq])
            chk("esk")
            AT["kcs"] = sb([128, 8, 128], BF16, "kcs")
            AT["kcs_f"] = AT["tmpf"][0]
            AT["vcs_f"] = AT["tmpf"][1]
            AT["vns"] = sb([8, NSQ, 512], BF16, "vns")
            AT["vnsf"] = sb([32, 512], F32, "vnsf")
            AT["kcT_bufs"] = [sb([128, 8, 128], BF16, "kcTb%d" % i) for i in range(NSQ)]
            AT["vcs_bufs"] = [sb([128, 512], BF16, "vcsb%d" % i) for i in range(NSQ)]
            AT["ktok"] = AT["den"]
            AT["pti"] = [0]
            AT["tfi"] = [0]

        class _A:
            def __getattr__(self, k):
                return AT[k]
        A_ = _A()

        def headnorm(ps_t, ntok, gcol, gt, out_ap, out_t, f32_out=None):
            sqb, rsd = AT["sqb"], AT["rsd"]
            kb.op("act", lambda: S.activation(out=sqb[:, 0:ntok], in_=ps_t[:, 0:ntok], func=AF.Square),
                  reads=[ps_t], writes=[sqb])
            p2 = psA()
            kb.mm(p2, [(p2[:, 0:ntok], bones_b[:, :], sqb[:, 0:ntok])], reads=[bones_b, sqb])
            kb.op("dve", lambda: V.tensor_scalar(out=rsd[:, 0:ntok], in0=p2[:, 0:ntok], scalar1=1.0 / 64, scalar2=NORM_EPS,
                                                 op0=ALU.mult, op1=ALU.add), reads=[p2], writes=[rsd])
            kb.op("act", lambda: S.activation(out=rsd[:, 0:ntok], in_=rsd[:, 0:ntok], func=AF.Sqrt), reads=[rsd], writes=[rsd])
            kb.op("dve", lambda: V.reciprocal(out=rsd[:, 0:ntok], in_=rsd[:, 0:ntok]), reads=[rsd], writes=[rsd])
            if f32_out is not None:
                fo_ap, fo_t, c0, c1 = f32_out
                kb.op("dve", lambda: V.scalar_tensor_tensor(out=fo_ap, in0=ps_t[:, c0:c1], scalar=gcol,
                                                            in1=rsd[:, c0:c1], op0=ALU.mult, op1=ALU.mult),
                      reads=[ps_t, rsd, gt], writes=[fo_t])
            kb.op("dve", lambda: V.scalar_tensor_tensor(out=out_ap, in0=ps_t[:, 0:ntok], scalar=gcol, in1=rsd[:, 0:ntok],
                                                        op0=ALU.mult, op1=ALU.mult),
                  reads=[ps_t, rsd, gt], writes=[out_t])

        def attn_block(j, kh, QB, q_cols, kprev, kcur, vprev, vcur, ncur, og_cols, first, xr=()):
            qn, sg, den, esk, pT, tmpf = AT["qn"], AT["sg"], AT["den"], AT["esk"], AT["pT"], AT["tmpf"]
            po = PSX[0]
            pd = PSX[1]
            groups_o = []
            groups_d = []
            preads = []
            W4 = 4 * QB
            for par in range(2):
                pts = []
                for which in ((0, 1) if not first else (1,)):
                    nk = 128 if which == 0 else ncur
                    lk = kprev(par) if which == 0 else kcur(par)
                    sc = psA()
                    kb.mm(sc, [(sc[0:nk, 0:W4].rearrange("p (c q) -> p c q", c=4), lk,
                                qn[par * 64:(par + 1) * 64, :, q_cols])], reads=[AT["kT"], qn] + list(xr))
                    BT = AT["BTp"] if which == 0 else AT["BTc"]
                    h0 = kh * 8 + par
                    tf = tmpf[AT["tfi"][0] % 2]
                    AT["tfi"][0] += 1
                    kb.op("dve", lambda sc=sc, tf=tf, BT=BT, nk=nk, h0=h0: V.tensor_tensor(
                        out=tf[0:nk, 0:W4].rearrange("p (c q) -> p c q", c=4),
                        in0=sc[0:nk, 0:W4].rearrange("p (c q) -> p c q", c=4),
                        in1=BT[0:nk, h0:h0 + 7:2, 0:QB], op=ALU.add), reads=[sc, BT], writes=[tf])
                    p = pT[AT["pti"][0] % 2]
                    AT["pti"][0] += 1
                    kb.op("act", lambda tf=tf, p=p, nk=nk: S.activation(out=p[0:nk, 0:W4], in_=tf[0:nk, 0:W4], func=AF.Exp),
                          reads=[tf], writes=[p])
                    pts.append((p, nk, which))
                go = []
                gd = []
                preads = []
                for (p, nk, which) in pts:
                    vv = vprev if which == 0 else vcur
                    go.append((po[par * 64:(par + 1) * 64, 0:W4], vv, p[0:nk, 0:W4]))
                    gd.append((pd[par * 64:(par + 1) * 64, 0:W4], ones_b[0:nk, 0:64], p[0:nk, 0:W4]))
                    preads.append(p)
                kb.mm(po, go, reads=preads + [AT["vtk"]] + list(xr))
                kb.mm(pd, gd, reads=preads + [ones_b])
            kb.op("dve", lambda: V.tensor_tensor(out=den[:, 0:W4].rearrange("p (c q) -> p c q", c=4),
                                                 in0=pd[:, 0:W4].rearrange("p (c q) -> p c q", c=4),
                                                 in1=esk[:, j, kh * 4:kh * 4 + 4].unsqueeze(2).to_broadcast([128, 4, QB]),
                                                 op=ALU.add), reads=[pd, esk], writes=[den])
            kb.op("dve", lambda: V.reciprocal(out=den[:, 0:W4], in_=den[:, 0:W4]), reads=[den], writes=[den])
            kb.op("dve", lambda: V.tensor_tensor(out=den[:, 0:W4], in0=po[:, 0:W4], in1=den[:, 0:W4], op=ALU.mult),
                  reads=[po, den], writes=[den])
            kb.op("dve", lambda: V.tensor_tensor(out=og[:, kh * 4:kh * 4 + 4, og_cols],
                                                 in0=den[:, 0:W4].rearrange("p (c q) -> p c q", c=4),
                                                 in1=sg[:, :, q_cols], op=ALU.mult), reads=[den, sg], writes=[og])

        def attn_layer_tile(l, j, ntok, ti, last):
            kT, vtk, knf, vlf, qn, sg, gq, gk, vns, vnsf = (AT[k] for k in
                                                            ("kT", "vtk", "knf", "vlf", "qn", "sg", "gq", "gk", "vns", "vnsf"))
            w_in = I["w_in_a"][j]
            sample = ntok < 128
            for kh in range(8):
                wt = next_w()
                view = wt.h[:, 0:KC * 128].rearrange("p (k n) -> p k n", k=KC)
                src = w_in.rearrange("(k p) n -> p k n", p=128)[:, :, DI + kh * 64:DI + kh * 64 + 64]
                kb.dma("pool", view[:, :, 0:64], src, writes=[wt])
                kb.dma("pool", view[:, :, 64:128], src, writes=[wt])
                pk = psA()
                kb.mm(pk, [(pk[:, 0:ntok], view[:, kc, :], hT[:, kc, 0:ntok]) for kc in range(KC)], reads=[wt, hT])
                if not sample:
                    headnorm(pk, ntok, gk[:, j:j + 1], gk, kT[:, kh, 128:640], kT,
                             f32_out=(knf[:, kh, :], knf, 384, 512) if last else None)
                else:
                    headnorm(pk, ntok, gk[:, j:j + 1], gk, kT[:, kh, 0:ntok], kT,
                             f32_out=(knf[:, kh, 0:ntok], knf, 0, ntok))
            chk("ak")
            wt, view = load_w_in(w_in, DI + 512, 512)
            chk("av0")
            if not sample:
                for s_ in range(4):
                    pv = psA()
                    kb.mm(pv, [(pv[:, :], hT[:, kc, s_ * 128:(s_ + 1) * 128], view[:, kc, :]) for kc in range(KC)], reads=[wt, hT])
                    chk("av1")
                    kb.op("act", lambda s_=s_, pv=pv: S.copy(out=vtk[:, 1 + s_, :], in_=pv[:, :]), reads=[pv], writes=[vtk])
                    chk("av2")
                    if s_ == 1:
                        chk("av3")
                    if s_ == 3:
                        chk("av4")
                    if last and s_ == 3:
                        kb.op("dve", lambda pv=pv: V.tensor_copy(out=vlf[:], in_=pv[:, :]), reads=[pv], writes=[vlf])
                        chk("av5")
            else:
                for sq in range(NSQ):
                    pv = psA()
                    kb.mm(pv, [(pv[0:8, :], hT[:, kc, sq * 8:(sq + 1) * 8], view[:, kc, :]) for kc in range(KC)], reads=[wt, hT])
                    kb.op("act", lambda sq=sq, pv=pv: S.copy(out=vns[:, sq, :], in_=pv[0:8, :]), reads=[pv], writes=[vns])
                pv = psA()
                kb.mm(pv, [(pv[0:NST, :], hT[:, kc, 0:NST], view[:, kc, :]) for kc in range(KC)], reads=[wt, hT])
                kb.op("dve", lambda pv=pv: V.tensor_copy(out=vnsf[:], in_=pv[0:NST, :]), reads=[pv], writes=[vnsf])
            chk("av")
            for kh in range(8):
                wt, view = load_w_in(w_in, kh * 512, 512)
                for c in range(4):
                    pq = psA()
                    kb.mm(pq, [(pq[:, 0:ntok], view[:, kc, c * 128:(c + 1) * 128], hT[:, kc, 0:ntok]) for kc in range(KC)],
                          reads=[wt, hT])
                    headnorm(pq, ntok, gq[:, j:j + 1], gq, qn[:, c, 0:ntok], qn)
                wt, view = load_w_in(w_in, DI + 1024 + kh * 512, 512)
                for c in range(4):
                    pg = psA()
                    kb.mm(pg, [(pg[:, 0:ntok], view[:, kc, c * 128:(c + 1) * 128], hT[:, kc, 0:ntok]) for kc in range(KC)],
                          reads=[wt, hT])
                    kb.op("act", lambda c=c, pg=pg: S.activation(out=sg[:, c, 0:ntok], in_=pg[:, 0:ntok], func=AF.Silu),
                          reads=[pg], writes=[sg])
                chk("aq")
                if not sample:
                    for b in range(4):
                        if b == 1:
                            chk("ab0")
                        if b == 2:
                            chk("ab1")
                        attn_block(j, kh, 128, slice(b * 128, (b + 1) * 128),
                                   lambda par, b=b: kT[par * 64:(par + 1) * 64, kh, b * 128:(b + 1) * 128],
                                   lambda par, b=b: kT[par * 64:(par + 1) * 64, kh, (b + 1) * 128:(b + 2) * 128],
                                   vtk[:, b, kh * 64:(kh + 1) * 64], vtk[:, b + 1, kh * 64:(kh + 1) * 64], 128,
                                   slice(b * 128, (b + 1) * 128), first=(ti == 0 and b == 0))
                else:
                    for sq in range(NSQ):
                        attn_block(j, kh, 8, slice(sq * 8, (sq + 1) * 8),
                                   lambda par, sq=sq: AT["kcT_bufs"][sq][par * 64:(par + 1) * 64, kh, :],
                                   lambda par, sq=sq: kT[par * 64:(par + 1) * 64, kh, sq * 8:(sq + 1) * 8],
                                   AT["vcs_bufs"][sq][:, kh * 64:(kh + 1) * 64], vns[:, sq, kh * 64:(kh + 1) * 64], 8,
                                   slice(sq * 8, (sq + 1) * 8), first=False,
                                   xr=[AT["kcT_bufs"][sq], AT["vcs_bufs"][sq], vns])
            if not sample:
                kb.op("pool", lambda: G.tensor_copy(out=kT[:, :, 0:128], in_=kT[:, :, 512:640]), reads=[kT], writes=[kT])
                kb.op("pool", lambda: G.tensor_copy(out=vtk[:, 0, :], in_=vtk[:, 4, :]), reads=[vtk], writes=[vtk])

        def attn_sample_prep(j):
            kcs, kcs_f, vcs_f = AT["kcs"], AT["kcs_f"], AT["vcs_f"]
            for sq in range(NSQ):
                kb.dma("sp", kcs_f[:], I["cache_k"][j, sq], writes=[kcs_f])
                kb.dma("sp", vcs_f[:], I["cache_v"][j, sq], writes=[vcs_f])
                kcv = kcs_f.h[:].rearrange("p (k d) -> p k d", k=8)
                kb.op("dve", lambda kcv=kcv: V.tensor_copy(out=kcs[:, :, 0:64], in_=kcv), reads=[kcs_f], writes=[kcs])
                kb.op("dve", lambda kcv=kcv: V.tensor_copy(out=kcs[:, :, 64:128], in_=kcv), reads=[kcs_f], writes=[kcs])
                kct = AT["kcT_bufs"][sq]
                vc = AT["vcs_bufs"][sq]
                kb.op("act", lambda vc=vc: S.copy(out=vc[:], in_=vcs_f[:]), reads=[vcs_f], writes=[vc])
                for k2 in range(0, 8, 4):
                    pt = psB()
                    kb.transpose(pt, [(pt[:, i * 128:(i + 1) * 128], kcs[:, k2 + i, :], ident_b[:, :]) for i in range(4)],
                                 reads=[kcs, ident_b])
                    kb.op("act", lambda pt=pt, kct=kct, k2=k2: S.copy(out=kct[:, k2:k2 + 4, :],
                                                                    in_=pt[:, 0:512].rearrange("p (k n) -> p k n", k=4)),
                          reads=[pt], writes=[kct])
                kb.dma("sp", O["kws"][j, sq, 0:120, :], I["cache_k"][j, sq, 8:128, :], writes=[])
                kws_toks.append(kb_last_tok("sp"))
                kb.dma("sp", O["vws"][j, sq, 0:120, :], I["cache_v"][j, sq, 8:128, :], writes=[])
                kws_toks.append(kb_last_tok("sp"))

        def kb_last_tok(q):
            return kb.last_dma_tok

        kws_toks = []
        if nA:
            kws_t = TT(None, "kws")
            vws_t = TT(None, "vws")
            kwp_t = TT(None, "kwp")
            vwp_t = TT(None, "vwp")
            zrep_t = TT(None, "zrep")
            out_regions += [kws_t, vws_t, kwp_t, vwp_t]

        def attn_outputs_prompt(j):
            knf, vlf, ktok = AT["knf"], AT["vlf"], AT["ktok"]
            pt = psA()
            kb.transpose(pt, [(pt[:, kh * 64:(kh + 1) * 64], knf[0:64, kh, :], ident_f[0:64, 0:64]) for kh in range(8)],
                         reads=[knf, ident_f])
            kb.op("dve", lambda: V.tensor_copy(out=ktok[:], in_=pt[:, :]), reads=[pt], writes=[ktok])
            kb.dma("sp", O["kwp"][j], ktok[:], reads=[ktok], writes=[])
            kws_toks.append(kb_last_tok("sp"))
            kb.dma("sp", O["vwp"][j], vlf[:], reads=[vlf], writes=[])
            kws_toks.append(kb_last_tok("sp"))

        def attn_outputs_sample(j):
            knf, vnsf, ktok = AT["knf"], AT["vnsf"], AT["ktok"]
            pt = psA()
            kb.transpose(pt, [(pt[0:NST, kh * 64:(kh + 1) * 64], knf[0:64, kh, 0:NST], ident_f[0:64, 0:64])
                              for kh in range(8)], reads=[knf, ident_f])
            kb.op("dve", lambda: V.tensor_copy(out=ktok[0:NST, :], in_=pt[0:NST, :]), reads=[pt], writes=[ktok])
            for sq in range(NSQ):
                kb.dma("sp", O["kws"][j, sq, 120:128, :], ktok[sq * 8:(sq + 1) * 8, :], reads=[ktok], writes=[])
                kws_toks.append(kb_last_tok("sp"))
                kb.dma("sp", O["vws"][j, sq, 120:128, :], vnsf[sq * 8:(sq + 1) * 8, :], reads=[vnsf], writes=[])
                kws_toks.append(kb_last_tok("sp"))


        BT_ = {}
        KAP = 0.6065306597126334

        def rwkv_alloc(les, j):
            def sb(shape, dt, name):
                return kb.sb(shape, dt, name, es=les)
            B = BT_
            B["ST"] = sb([128, 32, 64], F32, "ST")
            B["STs"] = sb([128, NSQ, 64], F32, "STs")
            B["STs2"] = [B["STs"], TT(B["STs"].h, "STs_b")]
            B["ST2"] = [B["ST"], TT(B["ST"].h, "ST_b")]
            kb.op("dve", lambda: V.memset(B["ST"][:], 0.0), writes=B["ST2"])
            B["hlast"] = sb([128, KC], BF16, "hlast")
            kb.op("dve", lambda: V.memset(B["hlast"][:], 0.0), writes=[B["hlast"]])
            B["dlt"] = AliasTT(wbuf[0], wbuf[0].h[:, 0:KC * 256].rearrange("p (k n) -> p k n", k=KC))
            B["xm"] = [sb([128, KC, 256], BF16, "xm%d" % c) for c in range(5)]
            B["xm"].append(B["xm"][4])
            B["mu"] = sb([128, 6, KC], F32, "mu")
            kb.dma("sp", B["mu"][:], I["mu_b"][j].rearrange("c (k p) -> p c k", p=128), writes=[B["mu"]],
                   allow_slow_non_contiguous=True)
            B["pv"] = sb([128, 7, 32], F32, "pv")
            for wi, nm in enumerate(("w0_b", "a0_b", "k_k_b", "k_a_b", "r_k_b", "ln_x_g_b", "ln_x_b_b")):
                kb.dma("sp", B["pv"][:, wi, :], I[nm][j].rearrange("(c p) -> p c", p=128), writes=[B["pv"]],
                       allow_slow_non_contiguous=True)
            B["wd"] = sb([128, 2, KC, 96], BF16, "wd")
            for c in range(2):
                kb.dma("pool", B["wd"][:, c, :, :], I["w_lora_down_b"][j, c].rearrange("(k p) n -> p k n", p=128), writes=[B["wd"]])
            B["wu"] = [sb([96, 2, 128], BF16, "wu%d" % i) for i in range(2)]
            B["lw"] = sb([96, 2, 256], BF16, "lwlow")
            B["mk"] = sb([128, 3, 64], F32, "mk")
            kb.dma("sp", B["mk"][0:64], I["masks"].rearrange("m a b -> a m b"), writes=[B["mk"]])
            kb.dma("sp", B["mk"][64:128], I["masks"].rearrange("m a b -> a m b"), writes=[B["mk"]])
            for nm in ("r", "k", "kk", "k2", "a", "sgw", "cs", "t1", "t2", "Ep", "of", "gt1"):
                B[nm] = sb([128, 256], F32, "f_" + nm)
            B["E0"] = B["t2"]
            for nm in ("Em", "bon"):
                B[nm] = [sb([128, 256], F32, "f_%s%d" % (nm, i)) for i in range(2)]
            for nm in ("sq", "gsq"):
                B[nm] = sb([128, 256], BF16, "b_" + nm)
            for nm in ("rt", "at", "bt", "kt", "bh", "kh", "vb"):
                B[nm] = [sb([128, 256], BF16, "b_%s%d" % (nm, i)) for i in range(2)]
            B["ones64"] = sb([128, 64], F32, "ones64")
            kb.op("dve", lambda: V.memset(B["ones64"][:], 1.0), writes=[B["ones64"]])
            for nm, shp_ in (("P", [128, 4, 64]), ("M5", [128, 4, 5, 64]), ("T3", [128, 4, 3, 64])):
                t_ = sb(shp_, BF16, "h_" + nm)
                B[nm] = [[TT(t_.h, "%s_%d_%d" % (nm, hp_, ci_)) for ci_ in range(4)] for hp_ in range(2)]
            for nm in ("Y", "U", "Sb"):
                t_ = sb([128, 64], BF16, "m_" + nm)
                B[nm] = [t_, TT(t_.h, t_.name + "_b")]
            B["WK"] = []
            for st_ in range(4):
                row = []
                for nm, shp_ in (("LL0", [128, 2, 64]), ("LL1", [128, 2, 64]), ("P0", [128, 64]), ("P1", [128, 64])):
                    t_ = sb(shp_, BF16, "wk%d_%s" % (st_, nm))
                    row.append([t_, TT(t_.h, t_.name + "_b")])
                B["WK"].append(row)
            B["mk5"] = sb([128, 5, 64], F32, "mk5")
            for half_ in range(2):
                for m_, src_ in enumerate((0, 1, 0, 2, 2)):
                    kb.dma("sp", B["mk5"][half_ * 64:(half_ + 1) * 64, m_, :], I["masks"][src_], writes=[B["mk5"]])
            B["wrk"] = wbuf
            B["Sin"] = B["of"]
            B["Sout"] = B["kk"]
            B["shf"] = sb([128, KC, NSQ], F32, "shf")
            B["wri"] = [0]

        def rwkv_mix_inputs(j, col0, T, first_cols):
            B = BT_
            for c in range(6):
                for kc in range(KC):
                    kb.op("dve", lambda c=c, kc=kc: V.scalar_tensor_tensor(out=B["xm"][c][:, kc, 0:T], in0=B["dlt"][:, kc, 0:T],
                                                                            scalar=B["mu"][:, c, kc:kc + 1], in1=hT[:, kc, col0:col0 + T],
                                                                            op0=ALU.mult, op1=ALU.add),
                          reads=[B["dlt"], B["mu"], hT], writes=[B["xm"][c]])
                if c >= 4:
                    cc_ = c - 4
                    pl = psA()
                    kb.mm(pl, [(pl[0:96, 0:T], B["wd"][:, cc_, kc, :], B["xm"][c][:, kc, 0:T]) for kc in range(KC)],
                          reads=[B["wd"], B["xm"][c]])
                    kb.op("act", lambda cc_=cc_, pl=pl: S.activation(out=B["lw"][:, cc_, 0:T], in_=pl[0:96, 0:T],
                                                                    func=(AF.Tanh if cc_ == 0 else AF.Copy)), reads=[pl], writes=[B["lw"]])

        def rwkv_s1(j, pr, T, C, og_cols):
            B = BT_
            bs = pr % 2
            pvp = B["pv"]
            w4 = I["w_rkvg_b"][j]
            ps_in = {}
            for c in range(4):
                wt = next_w()
                wv = wt.h[:, 0:KC * 128].rearrange("p (k n) -> p k n", k=KC)
                kb.dma("pool", wv, w4[c].rearrange("(k p) n -> p k n", p=128)[:, :, pr * 128:(pr + 1) * 128], writes=[wt])
                yield
                pp_ = psA()
                kb.mm(pp_, [(pp_[:, 0:T], wv[:, kc, :], B["xm"][c][:, kc, 0:T]) for kc in range(KC)], reads=[wt, B["xm"][c]])
                if c == 0:
                    kb.op("act", lambda pp_=pp_: S.copy(out=B["r"][:, 0:T], in_=pp_[:, 0:T]), reads=[pp_], writes=[B["r"]])
                    yield
                elif c == 1:
                    kb.op("act", lambda pp_=pp_: S.copy(out=B["k"][:, 0:T], in_=pp_[:, 0:T]), reads=[pp_], writes=[B["k"]])
                    yield
                elif c == 2:
                    kb.op("act", lambda pp_=pp_: S.copy(out=B["vb"][bs][:, 0:T], in_=pp_[:, 0:T]), reads=[pp_], writes=[B["vb"][bs]])
                    yield
                else:
                    kb.op("act", lambda pp_=pp_: S.activation(out=og[:, pr, og_cols], in_=pp_[:, 0:T], func=AF.Silu),
                          reads=[pp_], writes=[og])
                    yield
            chk("bproj")
            r, k, kk, k2, a, sgw, cs, t1, t2, Ep, E0 = (B[n] for n in ("r", "k", "kk", "k2", "a", "sgw", "cs", "t1", "t2", "Ep", "E0"))
            Em, bon = B["Em"][bs], B["bon"][bs]
            wu = B["wu"][pr % 2]
            kb.dma("pool", wu[:], I["w_lora_up_b"][j][:, :, pr * 128:(pr + 1) * 128].rearrange("c r n -> r c n"), writes=[wu])
            yield
            for c, dst, wi in ((0, sgw, 0), (1, a, 1)):
                pl = psA()
                kb.mm(pl, [(pl[:, 0:T], wu[:, c, :], B["lw"][:, c, 0:T])], reads=[wu, B["lw"]])
                kb.op("act", lambda pl=pl, dst=dst, wi=wi: S.activation(out=dst[:, 0:T], in_=pl[:, 0:T], func=AF.Sigmoid,
                                                                       bias=pvp[:, wi, pr:pr + 1]), reads=[pl, pvp], writes=[dst])
                yield
            kb.op("dve", lambda: V.tensor_scalar(out=t1[:, 0:T], in0=k[:, 0:T], scalar1=pvp[:, 2, pr:pr + 1], scalar2=None, op0=ALU.mult),
                  reads=[k, pvp], writes=[t1])
            yield
            kb.op("act", lambda: S.activation(out=B["sq"][:, 0:T], in_=t1[:, 0:T], func=AF.Square), reads=[t1], writes=[B["sq"]])
            yield
            pn = psA()
            kb.mm(pn, [(pn[:, 0:T], bones_b[:, :], B["sq"][:, 0:T])], reads=[bones_b, B["sq"]])
            kb.op("act", lambda: S.activation(out=t2[:, 0:T], in_=pn[:, 0:T], func=AF.Sqrt), reads=[pn], writes=[t2])
            yield
            kb.op("dve", lambda: V.tensor_scalar(out=t2[:, 0:T], in0=t2[:, 0:T], scalar1=1e-12, scalar2=None, op0=ALU.max),
                  reads=[t2], writes=[t2])
            yield
            kb.op("dve", lambda: V.reciprocal(out=t2[:, 0:T], in_=t2[:, 0:T]), reads=[t2], writes=[t2])
            yield
            kb.op("dve", lambda: V.tensor_tensor(out=kk[:, 0:T], in0=t1[:, 0:T], in1=t2[:, 0:T], op=ALU.mult), reads=[t1, t2], writes=[kk])
            yield
            kb.op("dve", lambda: V.tensor_scalar(out=t1[:, 0:T], in0=a[:, 0:T], scalar1=-1.0, scalar2=pvp[:, 3, pr:pr + 1],
                                                 op0=ALU.add, op1=ALU.mult), reads=[a, pvp], writes=[t1])
            yield
            kb.op("dve", lambda: V.scalar_tensor_tensor(out=k2[:, 0:T], in0=t1[:, 0:T], scalar=1.0, in1=k[:, 0:T], op0=ALU.add, op1=ALU.mult),
                  reads=[t1, k], writes=[k2])
            yield
            kb.op("dve", lambda: V.scalar_tensor_tensor(out=B["sq"][:, 0:T], in0=r[:, 0:T], scalar=pvp[:, 4, pr:pr + 1], in1=k2[:, 0:T],
                                                        op0=ALU.mult, op1=ALU.mult), reads=[r, k2, pvp], writes=[B["sq"]])
            yield
            pb = psA()
            kb.mm(pb, [(pb[:, 0:T], bones_b[:, :], B["sq"][:, 0:T])], reads=[bones_b, B["sq"]])
            kb.op("dve", lambda: V.tensor_tensor(out=bon[:, 0:T], in0=pb[:, 0:T], in1=B["vb"][bs][:, 0:T], op=ALU.mult), reads=[pb, B["vb"][bs]], writes=[bon])
            yield
            nch = T // C
            for ci in range(nch):
                kb.op("dve", lambda ci=ci: V.tensor_tensor_scan(out=cs[:, ci * C:(ci + 1) * C], data0=B["ones64"][:, 0:C],
                                                                 data1=sgw[:, ci * C:(ci + 1) * C], initial=0.0, op0=ALU.mult, op1=ALU.add),
                      reads=[sgw, B["ones64"]], writes=[cs])
                yield
            kb.op("act", lambda: S.activation(out=Em[:, 0:T], in_=cs[:, 0:T], func=AF.Exp, scale=-KAP), reads=[cs], writes=[Em])
            yield
            kb.op("act", lambda: S.activation(out=Ep[:, 0:T], in_=cs[:, 0:T], func=AF.Exp, scale=KAP), reads=[cs], writes=[Ep])
            yield
            kb.op("dve", lambda: V.tensor_tensor(out=t1[:, 0:T], in0=cs[:, 0:T], in1=sgw[:, 0:T], op=ALU.subtract), reads=[cs, sgw], writes=[t1])
            yield
            kb.op("act", lambda: S.activation(out=E0[:, 0:T], in_=t1[:, 0:T], func=AF.Exp, scale=-KAP), reads=[t1], writes=[E0])
            yield
            kb.op("dve", lambda: V.tensor_tensor(out=B["rt"][bs][:, 0:T], in0=r[:, 0:T], in1=Em[:, 0:T], op=ALU.mult), reads=[r, Em], writes=[B["rt"][bs]])
            yield
            kb.op("dve", lambda: V.scalar_tensor_tensor(out=B["at"][bs][:, 0:T], in0=kk[:, 0:T], scalar=-1.0, in1=E0[:, 0:T], op0=ALU.mult, op1=ALU.mult),
                  reads=[kk, E0], writes=[B["at"][bs]])
            yield
            kb.op("dve", lambda: V.tensor_tensor(out=t1[:, 0:T], in0=kk[:, 0:T], in1=a[:, 0:T], op=ALU.mult), reads=[kk, a], writes=[t1])
            yield
            kb.op("dve", lambda: V.tensor_tensor(out=B["bt"][bs][:, 0:T], in0=t1[:, 0:T], in1=Ep[:, 0:T], op=ALU.mult), reads=[t1, Ep], writes=[B["bt"][bs]])
            yield
            kb.op("dve", lambda: V.tensor_tensor(out=B["kt"][bs][:, 0:T], in0=k2[:, 0:T], in1=Ep[:, 0:T], op=ALU.mult), reads=[k2, Ep], writes=[B["kt"][bs]])
            yield
            for ci in range(nch):
                cc = slice(ci * C, (ci + 1) * C)
                wc = Em[:, (ci + 1) * C - 1:(ci + 1) * C]
                kb.op("dve", lambda cc=cc, wc=wc: V.tensor_scalar(out=B["bh"][bs][:, cc], in0=B["bt"][bs][:, cc], scalar1=wc, scalar2=None, op0=ALU.mult),
                      reads=[B["bt"][bs], Em], writes=[B["bh"][bs]])
                yield
                kb.op("dve", lambda cc=cc, wc=wc: V.tensor_scalar(out=B["kh"][bs][:, cc], in0=B["kt"][bs][:, cc], scalar1=wc, scalar2=None, op0=ALU.mult),
                      reads=[B["kt"][bs], Em], writes=[B["kh"][bs]])
                yield
        def rwkv_s2(j, pr, T, C, og_cols, ST, st_ap):
            B = BT_
            bs = pr % 2
            pvp = B["pv"]
            nch = T // C
            Em, bon, t1 = B["Em"][bs], B["bon"][bs], B["gt1"]
            chk("bprep")
            nlev = int(math.log2(C))
            mk = B["mk"]

            def run_rr(gens):
                gens = list(gens)
                while gens:
                    for g in list(gens):
                        try:
                            next(g)
                        except StopIteration:
                            gens.remove(g)
                    yield

            def phaseA(ci, hp):
                cc = slice(ci * C, (ci + 1) * C)
                P = slice(hp * 64, (hp + 1) * 64)
                TBs = slice(hp * 64, hp * 64 + C)
                at_, bt_, kt_, rt_ = B["at"][bs][P, cc], B["bt"][bs][P, cc], B["kt"][bs][P, cc], B["rt"][bs][P, cc]
                M5, T3 = B["M5"][hp][ci], B["T3"][hp][ci]
                wk = B["WK"][ci]
                LL = [wk[0][hp], wk[1][hp]]
                Pp = [wk[2][hp], wk[3][hp]]
                pm = psA()
                kb.mm_multi(pm, [[(pm[TBs, 0:C], bt_, at_)], [(pm[TBs, 64:64 + C], at_, bt_)], [(pm[TBs, 128:128 + C], kt_, at_)],
                                 [(pm[TBs, 192:192 + C], bt_, rt_)], [(pm[TBs, 256:256 + C], kt_, rt_)]],
                            reads=[B["bt"][bs], B["at"][bs], B["kt"][bs], B["rt"][bs]])
                kb.op("dve", lambda: V.tensor_tensor(out=M5[TBs, ci, :, 0:C], in0=pm[TBs, 0:320].rearrange("p (m c) -> p m c", m=5)[:, :, 0:C],
                                                     in1=B["mk5"][TBs, :, 0:C], op=ALU.mult), reads=[pm, B["mk5"]], writes=[M5])
                yield
                pt = psA()
                kb.mm_multi(pt, [[(pt[TBs, m_ * 64:(m_ + 1) * 64], B[nm_][bs][P, cc], ident_b[P, P])] for m_, nm_ in enumerate(("vb", "bh", "kh"))],
                            reads=[B["vb"][bs], B["bh"][bs], B["kh"][bs], ident_b])
                kb.op("act", lambda: S.copy(out=T3[TBs, ci, :, :], in_=pt[TBs, 0:192].rearrange("p (m c) -> p m c", m=3)), reads=[pt], writes=[T3])
                yield
                kb.op("dve", lambda: V.tensor_tensor(out=Pp[0][TBs, 0:C], in0=M5[TBs, ci, 0, 0:C], in1=ident_b[TBs, TBs], op=ALU.add),
                      reads=[M5, ident_b], writes=[Pp[0]])
                yield
                Lt, Lap, LTap = M5, M5[TBs, ci, 1, 0:C], M5[TBs, ci, 0, 0:C]
                cur = 0
                for lev in range(1, nlev):
                    nx = 1 - cur
                    last = lev == nlev - 1
                    p2 = psA()
                    grp = [[(p2[TBs, 0:C], LTap, Lap)]]
                    if not last:
                        grp.append([(p2[TBs, 64:64 + C], Lap, LTap)])
                    kb.mm_multi(p2, grp, reads=[Lt])
                    nq = 1 if last else 2
                    kb.op("act", lambda p2=p2, nx=nx, nq=nq: S.copy(out=LL[nx][TBs, 0:nq, 0:C],
                                                                   in_=p2[TBs, 0:128].rearrange("p (m c) -> p m c", m=2)[:, 0:nq, 0:C]),
                          reads=[p2], writes=[LL[nx]])
                    yield
                    Lt, Lap, LTap = LL[nx], LL[nx][TBs, 0, 0:C], LL[nx][TBs, 1, 0:C]
                    p3 = psA()
                    kb.mm(p3, [(p3[TBs, 0:C], ident_b[TBs, TBs], Pp[cur][TBs, 0:C]), (p3[TBs, 0:C], Lap, Pp[cur][TBs, 0:C])],
                          reads=[ident_b, Pp[cur], Lt])
                    if last:
                        kb.op("dve", lambda p3=p3: V.tensor_copy(out=B["P"][hp][ci][TBs, ci, 0:C], in_=p3[TBs, 0:C]), reads=[p3], writes=[B["P"][hp][ci]])
                    else:
                        kb.op("dve", lambda p3=p3, nx=nx: V.tensor_copy(out=Pp[nx][TBs, 0:C], in_=p3[TBs, 0:C]), reads=[p3], writes=[Pp[nx]])
                    yield
                    cur = nx

            def phaseB(ci, hp):
                cc = slice(ci * C, (ci + 1) * C)
                P = slice(hp * 64, (hp + 1) * 64)
                TBs = slice(hp * 64, hp * 64 + C)
                at_, rt_ = B["at"][bs][P, cc], B["rt"][bs][P, cc]
                Sb, Y, U = B["Sb"][hp], B["Y"][hp], B["U"][hp]
                po = PSX[hp]
                kb.op("act", lambda: S.copy(out=Sb[P, :], in_=st_ap(ci, P)), reads=[ST[hp]], writes=[Sb])
                yield
                py = psA()
                kb.mm(py, [(py[TBs, 0:64], at_, Sb[P, :]), (py[TBs, 0:64], B["M5"][hp][ci][TBs, ci, 2, 0:C], B["T3"][hp][ci][TBs, ci, 0, :])],
                      reads=[B["at"][bs], Sb, B["M5"][hp][ci], B["T3"][hp][ci]])
                kb.op("dve", lambda: V.tensor_copy(out=Y[TBs, 0:64], in_=py[TBs, 0:64]), reads=[py], writes=[Y])
                yield
                pu = psA()
                kb.mm(pu, [(pu[TBs, 0:64], B["P"][hp][ci][TBs, ci, 0:C], Y[TBs, 0:64])], reads=[B["P"][hp][ci], Y])
                kb.op("dve", lambda: V.tensor_copy(out=U[TBs, 0:64], in_=pu[TBs, 0:64]), reads=[pu], writes=[U])
                yield
                kb.mm(po, [(po[P, cc], Sb[P, :], rt_), (po[P, cc], U[TBs, 0:64], B["M5"][hp][ci][TBs, ci, 3, 0:C]),
                           (po[P, cc], B["T3"][hp][ci][TBs, ci, 0, :], B["M5"][hp][ci][TBs, ci, 4, 0:C])],
                      reads=[Sb, B["rt"][bs], U, B["M5"][hp][ci], B["T3"][hp][ci], B["M5"][hp][ci]])
                pd_ = psA()
                kb.mm(pd_, [(pd_[P, 0:64], B["T3"][hp][ci][TBs, ci, 1, :], U[TBs, 0:64]), (pd_[P, 0:64], B["T3"][hp][ci][TBs, ci, 2, :], B["T3"][hp][ci][TBs, ci, 0, :])],
                      reads=[B["T3"][hp][ci], U, B["T3"][hp][ci], B["T3"][hp][ci]])
                wc = Em[P, (ci + 1) * C - 1:(ci + 1) * C]
                kb.op("dve", lambda: V.scalar_tensor_tensor(out=st_ap(ci, P), in0=st_ap(ci, P), scalar=wc, in1=pd_[P, 0:64],
                                                            op0=ALU.mult, op1=ALU.add), reads=[ST[hp], Em, pd_], writes=[ST[hp]])
                yield

            def chainB(hp, cis):
                for ci_ in cis:
                    yield from phaseB(ci_, hp)

            if nch == 4:
                yield from run_rr([phaseA(ci, hp) for ci in (0, 1) for hp in range(2)])
                yield from run_rr([chainB(hp, (0, 1)) for hp in range(2)] + [phaseA(ci, hp) for ci in (2, 3) for hp in range(2)])
                yield from run_rr([chainB(hp, (2, 3)) for hp in range(2)])
            else:
                yield from run_rr([phaseA(ci, hp) for ci in range(nch) for hp in range(2)])
                for ci in range(nch):
                    yield from run_rr([phaseB(ci, hp) for hp in range(2)])
            chk("bchunks")
            of = B["of"]
            for hp in range(2):
                P = slice(hp * 64, (hp + 1) * 64)
                kb.op("act", lambda hp=hp, P=P: S.copy(out=B["gsq"][P, 0:T], in_=PSX[hp][P, 0:T]), reads=[PSX[hp]], writes=[B["gsq"]])
                yield
                kb.op("act", lambda hp=hp, P=P: S.copy(out=of[P, 0:T], in_=PSX[hp][P, 0:T]), reads=[PSX[hp]], writes=[of])
                yield
            pm_ = psA()
            kb.mm(pm_, [(pm_[:, 0:T], bones_b[:, :], B["gsq"][:, 0:T])], reads=[bones_b, B["gsq"]])
            kb.op("dve", lambda: V.scalar_tensor_tensor(out=of[:, 0:T], in0=pm_[:, 0:T], scalar=-1.0 / 64, in1=of[:, 0:T], op0=ALU.mult, op1=ALU.add),
                  reads=[pm_, of], writes=[of])
            yield
            kb.op("act", lambda: S.activation(out=B["gsq"][:, 0:T], in_=of[:, 0:T], func=AF.Square), reads=[of], writes=[B["gsq"]])
            yield
            pv_ = psA()
            kb.mm(pv_, [(pv_[:, 0:T], bones_b[:, :], B["gsq"][:, 0:T])], reads=[bones_b, B["gsq"]])
            kb.op("dve", lambda: V.tensor_scalar(out=t1[:, 0:T], in0=pv_[:, 0:T], scalar1=1.0 / 64, scalar2=64e-5, op0=ALU.mult, op1=ALU.add),
                  reads=[pv_], writes=[t1])
            yield
            kb.op("act", lambda: S.activation(out=t1[:, 0:T], in_=t1[:, 0:T], func=AF.Sqrt), reads=[t1], writes=[t1])
            yield
            kb.op("dve", lambda: V.reciprocal(out=t1[:, 0:T], in_=t1[:, 0:T]), reads=[t1], writes=[t1])
            yield
            kb.op("dve", lambda: V.tensor_tensor(out=of[:, 0:T], in0=of[:, 0:T], in1=t1[:, 0:T], op=ALU.mult), reads=[of, t1], writes=[of])
            yield
            kb.op("dve", lambda: V.tensor_scalar(out=of[:, 0:T], in0=of[:, 0:T], scalar1=pvp[:, 5, pr:pr + 1], scalar2=pvp[:, 6, pr:pr + 1],
                                                 op0=ALU.mult, op1=ALU.add), reads=[of, pvp], writes=[of])
            yield
            kb.op("dve", lambda: V.tensor_tensor(out=of[:, 0:T], in0=of[:, 0:T], in1=bon[:, 0:T], op=ALU.add), reads=[of, bon], writes=[of])
            yield
            kb.op("dve", lambda: V.tensor_tensor(out=og[:, pr, og_cols], in0=of[:, 0:T], in1=og[:, pr, og_cols], op=ALU.mult), reads=[of, og], writes=[og])
            yield


        def _exhaust(g):
            for _ in g:
                pass

        def _interleave(g1, g2):
            d1 = d2 = False
            while not (d1 and d2):
                if not d1:
                    try:
                        next(g1)
                    except StopIteration:
                        d1 = True
                if not d2:
                    try:
                        next(g2)
                    except StopIteration:
                        d2 = True

        def rwkv_layer_tile(l, j, ntok, ti):
            B = BT_
            sample = ntok < 128
            dlt = B["dlt"]
            if not sample:
                for half in range(2):
                    c0 = half * 256
                    kb.op("dve", lambda c0=c0: V.tensor_tensor(out=dlt[:, :, 1:256], in0=hT[:, :, c0:c0 + 255], in1=hT[:, :, c0 + 1:c0 + 256],
                                                               op=ALU.subtract), reads=[hT], writes=[dlt])
                    if half == 0:
                        kb.op("dve", lambda: V.tensor_tensor(out=dlt[:, :, 0], in0=B["hlast"][:, :], in1=hT[:, :, 0], op=ALU.subtract),
                              reads=[hT, B["hlast"]], writes=[dlt])
                    else:
                        kb.op("dve", lambda: V.tensor_tensor(out=dlt[:, :, 0], in0=hT[:, :, 255], in1=hT[:, :, 256], op=ALU.subtract),
                              reads=[hT], writes=[dlt])
                    rwkv_mix_inputs(j, c0, 256, None)
                    chk("bmix")
                    prev = None
                    for pr in range(32):
                        g1 = rwkv_s1(j, pr, 256, 64, slice(half * 256, (half + 1) * 256))
                        if prev is None:
                            _exhaust(g1)
                        else:
                            _interleave(prev, g1)
                        prev = rwkv_s2(j, pr, 256, 64, slice(half * 256, (half + 1) * 256), B["ST2"], lambda ci, P, pr=pr: B["ST"][P, pr, :])
                    _exhaust(prev)
                kb.op("dve", lambda: V.tensor_copy(out=B["hlast"][:, :], in_=hT[:, :, 511]), reads=[hT], writes=[B["hlast"]])
            else:
                shf = B["t1"]
                kb.dma("sp", dlt[:, :, 256:256 + NSQ], I["state_shift"][j].rearrange("s (k p) -> p k s", p=128), writes=[dlt],
                       allow_slow_non_contiguous=True) if False else None
                stg = B["t2"]
                for sq in range(NSQ):
                    kb.dma("sp", stg[:, sq * KC:(sq + 1) * KC], I["state_shift"][j, sq].rearrange("(k p) -> p k", p=128),
                           writes=[stg], allow_slow_non_contiguous=True)
                stg3 = stg.h[:, 0:KC * NSQ].rearrange("p (s k) -> p k s", k=KC)
                h4 = hT.h[:, :, 0:NST].rearrange("p k (s t) -> p k s t", t=8)
                d4 = dlt.h[:, :, 0:NST].rearrange("p k (s t) -> p k s t", t=8)
                kb.op("dve", lambda: V.tensor_tensor(out=d4[:, :, :, 1:8], in0=h4[:, :, :, 0:7], in1=h4[:, :, :, 1:8], op=ALU.subtract),
                      reads=[hT], writes=[dlt])
                kb.op("dve", lambda: V.tensor_tensor(out=d4[:, :, :, 0], in0=stg3, in1=h4[:, :, :, 0], op=ALU.subtract),
                      reads=[hT, stg], writes=[dlt])
                rwkv_mix_inputs(j, 0, NST, None)
                Sin, STp, Sout = B["Sin"], B["STs"], B["Sout"]
                Sin_v = Sin.h[:, 0:NSQ * 64].rearrange("p (s i) -> p s i", s=NSQ)
                Sout_v = Sout.h[:, 0:NSQ * 64].rearrange("p (s i) -> p s i", s=NSQ)
                for pr in range(32):
                    kb.dma("sp", Sin_v, I["state_wkv"][j, :, 2 * pr:2 * pr + 2].rearrange("s h i j -> (h i) s j"), writes=[Sin])
                    for hp in range(2):
                        pt = psA()
                        kb.mm_multi(pt, [[(pt[hp * 64:(hp + 1) * 64, sq * 64:(sq + 1) * 64], Sin_v[hp * 64:(hp + 1) * 64, sq, :],
                                           ident_f[hp * 64:(hp + 1) * 64, hp * 64:(hp + 1) * 64])] for sq in range(NSQ)],
                                    reads=[Sin, ident_f])
                        kb.op("dve", lambda pt=pt, hp=hp: V.tensor_copy(out=STp[hp * 64:(hp + 1) * 64, 0:NSQ, :],
                                                                        in_=pt[hp * 64:(hp + 1) * 64, 0:NSQ * 64].rearrange("p (s i) -> p s i", s=NSQ)),
                              reads=[pt], writes=[B["STs2"][hp]])
                    _exhaust(rwkv_s1(j, pr, NST, 8, slice(0, NST)))
                    _exhaust(rwkv_s2(j, pr, NST, 8, slice(0, NST), B["STs2"], lambda ci, P: STp[P, ci, :]))
                    for hp in range(2):
                        pt = psA()
                        kb.mm_multi(pt, [[(pt[hp * 64:(hp + 1) * 64, sq * 64:(sq + 1) * 64], STp[hp * 64:(hp + 1) * 64, sq, :],
                                           ident_f[hp * 64:(hp + 1) * 64, hp * 64:(hp + 1) * 64])] for sq in range(NSQ)],
                                    reads=[B["STs2"][hp], ident_f])
                        kb.op("dve", lambda pt=pt, hp=hp: V.tensor_copy(out=Sout_v[hp * 64:(hp + 1) * 64],
                                                                        in_=pt[hp * 64:(hp + 1) * 64, 0:NSQ * 64].rearrange("p (s i) -> p s i", s=NSQ)),
                              reads=[pt], writes=[Sout])
                    kb.dma("sp", O["wkvs"][j, :, 2 * pr:2 * pr + 2].rearrange("s h i j -> (h i) s j"), Sout_v, reads=[Sout], writes=[])
                    kws_toks.append(kb_last_tok("sp"))
                shf = B["shf"]
                kb.op("dve", lambda: V.tensor_copy(out=shf[:, :, 0:NSQ], in_=h4[:, :, :, 7]), reads=[hT], writes=[shf])
                for sq in range(NSQ):
                    kb.dma("sp", O["shs"][j, sq].rearrange("(k p) -> p k", p=128), shf[:, :, sq], reads=[shf], writes=[],
                           allow_slow_non_contiguous=True)
                    kws_toks.append(kb_last_tok("sp"))

        def rwkv_outputs_prompt(j):
            B = BT_
            ST, Sout, shf = B["ST"], B["Sout"], B["shf"]
            Sout_v = Sout.h[:, 0:256].rearrange("p (s i) -> p s i", s=4)
            for p4 in range(0, 32, 4):
                for hp in range(2):
                    pt = psA()
                    kb.mm_multi(pt, [[(pt[hp * 64:(hp + 1) * 64, q * 64:(q + 1) * 64], ST[hp * 64:(hp + 1) * 64, p4 + q, :],
                                       ident_f[hp * 64:(hp + 1) * 64, hp * 64:(hp + 1) * 64])] for q in range(4)],
                                reads=[B["ST2"][hp], ident_f])
                    kb.op("dve", lambda pt=pt, hp=hp: V.tensor_copy(out=Sout_v[hp * 64:(hp + 1) * 64],
                                                                    in_=pt[hp * 64:(hp + 1) * 64, 0:256].rearrange("p (s i) -> p s i", s=4)),
                          reads=[pt], writes=[Sout])
                kb.dma("sp", O["wkvp"][j, 2 * p4:2 * p4 + 8].rearrange("(q h) i j -> (h i) q j", h=2), Sout_v, reads=[Sout], writes=[])
                kws_toks.append(kb_last_tok("sp"))
            kb.op("dve", lambda: V.tensor_copy(out=shf[:, :, 0], in_=hT[:, :, 511]), reads=[hT], writes=[shf])
            kb.dma("sp", O["shp"][j].rearrange("(k p) -> p k", p=128), shf[:, :, 0], reads=[shf], writes=[],
                   allow_slow_non_contiguous=True)
            kws_toks.append(kb_last_tok("sp"))


        CT_ = {}

        def gla_alloc(les, j):
            def sb(shape, dt, name):
                return kb.sb(shape, dt, name, es=les)
            Cc = CT_
            Cc["S"] = sb([128, 16, 512], F32, "glaS")
            kb.op("dve", lambda: V.memset(Cc["S"][:], 0.0), writes=[Cc["S"]])
            Cc["Ss"] = sb([128, 2, 512], F32, "glaSs")
            Cc["Sb"] = sb([128, 2, 512], BF16, "glaSb")
            Cc["wgk"] = sb([16, 256], F32, "wgk")
            Cc["bgk"] = sb([128, 16], F32, "bgk")
            kb.dma("sp", Cc["bgk"][:], I["b_gk_c"][j].rearrange("(c p) -> p c", p=128), writes=[Cc["bgk"]], allow_slow_non_contiguous=True)
            Cc["og_g"] = sb([128, 4], F32, "og_g")
            kb.dma("sp", Cc["og_g"][:], I["o_norm_g_c"][j].rearrange("(c p) -> p c", p=128), writes=[Cc["og_g"]], allow_slow_non_contiguous=True)
            Cc["gkl"] = sb([16, 512], F32, "gkl")
            Cc["wgl"] = sb([128, KC, 16], BF16, "wgl")
            kb.dma("pool", Cc["wgl"][:], I["w_in_c"][j].rearrange("(k p) n -> p k n", p=128)[:, :, 12288:12304], writes=[Cc["wgl"]])
            Cc["mk"] = sb([64, 64], F32, "mkc")
            kb.dma("sp", Cc["mk"][:], I["masks"][2], writes=[Cc["mk"]])
            for nm in ("q", "k", "b", "e1", "e2"):
                Cc[nm] = sb([128, 2, 512], F32, "c_" + nm)
            for nm in ("qt", "kt", "kh"):
                Cc[nm] = sb([128, 2, 512], BF16, "cb_" + nm)
            Cc["of"] = sb([128, 4, 512], F32, "c_of")
            Cc["osq"] = sb([128, 4, 512], BF16, "c_osq")
            Cc["vt"] = sb([64, 8, 512], BF16, "c_vt")
            Cc["att"] = sb([64, 64], BF16, "c_att")
            Cc["kht"] = sb([64, 256], BF16, "c_kht")
            Cc["rs"] = sb([128, 512], F32, "c_rs")
            Cc["ones64"] = sb([128, 64], F32, "c_ones64")
            kb.op("dve", lambda: V.memset(Cc["ones64"][:], 1.0), writes=[Cc["ones64"]])

        def gla_head(j, h, ntok, C, state_fn, og_cols):
            Cc = CT_
            w_in = I["w_in_c"][j]
            nch = ntok // C
            q, k, b, e1, e2, qt, kt, khh, of, vt = (Cc[n] for n in ("q", "k", "b", "e1", "e2", "qt", "kt", "kh", "of", "vt"))
            for (dst, col0) in ((q, h * 256), (k, 2048 + h * 256)):
                wt, view = load_w_in(w_in, col0, 256)
                for dc in range(2):
                    pq = psA()
                    kb.mm(pq, [(pq[:, 0:ntok], view[:, kc, dc * 128:(dc + 1) * 128], hT[:, kc, 0:ntok]) for kc in range(KC)], reads=[wt, hT])
                    kb.op("act", lambda pq=pq, dst=dst, dc=dc: S.copy(out=dst[:, dc, 0:ntok], in_=pq[:, 0:ntok]), reads=[pq], writes=[dst])
            kb.dma("sp", Cc["wgk"][:], I["w_gk_up_c"][j][:, h * 256:(h + 1) * 256], writes=[Cc["wgk"]])
            for dc in range(2):
                pg = psA()
                cch = 2 * h + dc
                kb.mm(pg, [(pg[:, 0:ntok], Cc["wgk"][:, dc * 128:(dc + 1) * 128], Cc["gkl"][:, 0:ntok])], reads=[Cc["wgk"], Cc["gkl"]])
                kb.op("act", lambda pg=pg, dc=dc, cch=cch: S.activation(out=e1[:, dc, 0:ntok], in_=pg[:, 0:ntok], func=AF.Sigmoid,
                                                                       bias=Cc["bgk"][:, cch:cch + 1]), reads=[pg, Cc["bgk"]], writes=[e1])
            kb.op("act", lambda: S.activation(out=e1[:, :, 0:ntok], in_=e1[:, :, 0:ntok], func=AF.Ln), reads=[e1], writes=[e1])
            for dc in range(2):
                for ci in range(nch):
                    kb.op("dve", lambda dc=dc, ci=ci: V.tensor_tensor_scan(out=b[:, dc, ci * C:(ci + 1) * C], data0=Cc["ones64"][:, 0:C],
                                                                           data1=e1[:, dc, ci * C:(ci + 1) * C], initial=0.0,
                                                                           op0=ALU.mult, op1=ALU.add), reads=[e1, Cc["ones64"]], writes=[b])
            kb.op("act", lambda: S.activation(out=e1[:, :, 0:ntok], in_=b[:, :, 0:ntok], func=AF.Exp, scale=1.0 / 16), reads=[b], writes=[e1])
            kb.op("act", lambda: S.activation(out=e2[:, :, 0:ntok], in_=b[:, :, 0:ntok], func=AF.Exp, scale=-1.0 / 16), reads=[b], writes=[e2])
            kb.op("dve", lambda: V.scalar_tensor_tensor(out=qt[:, :, 0:ntok], in0=q[:, :, 0:ntok], scalar=1.0 / 16, in1=e1[:, :, 0:ntok],
                                                        op0=ALU.mult, op1=ALU.mult), reads=[q, e1], writes=[qt])
            kb.op("dve", lambda: V.tensor_tensor(out=kt[:, :, 0:ntok], in0=k[:, :, 0:ntok], in1=e2[:, :, 0:ntok], op=ALU.mult), reads=[k, e2], writes=[kt])
            for dc in range(2):
                for ci in range(nch):
                    kb.op("dve", lambda dc=dc, ci=ci: V.tensor_scalar(out=khh[:, dc, ci * C:(ci + 1) * C], in0=kt[:, dc, ci * C:(ci + 1) * C],
                                                                      scalar1=e1[:, dc, (ci + 1) * C - 1:(ci + 1) * C], scalar2=None, op0=ALU.mult),
                          reads=[kt, e1], writes=[khh])
            wt, view = load_w_in(w_in, 4096 + h * 512, 512)
            for ci in range(nch):
                pv = psA()
                kb.mm(pv, [(pv[0:C, :], hT[:, kc, ci * C:(ci + 1) * C], view[:, kc, :]) for kc in range(KC)], reads=[wt, hT])
                kb.op("act", lambda ci=ci, pv=pv: S.copy(out=vt[0:C, ci, :], in_=pv[0:C, :]), reads=[pv], writes=[vt])
            wt, view = load_w_in(w_in, 8192 + h * 512, 512)
            for ec in range(4):
                pg = psA()
                kb.mm(pg, [(pg[:, 0:ntok], view[:, kc, ec * 128:(ec + 1) * 128], hT[:, kc, 0:ntok]) for kc in range(KC)], reads=[wt, hT])
                kb.op("act", lambda ec=ec, pg=pg: S.activation(out=og[:, h * 4 + ec, og_cols], in_=pg[:, 0:ntok], func=AF.Silu),
                      reads=[pg], writes=[og])
            for ci in range(nch):
                cc = slice(ci * C, (ci + 1) * C)
                St, Sap = state_fn(ci, "load")
                kb.op("act", lambda Sap=Sap: S.copy(out=Cc["Sb"][:], in_=Sap), reads=[St], writes=[Cc["Sb"]])
                pa = psA()
                kb.mm(pa, [(pa[0:C, 0:C], kt[:, dc, cc], qt[:, dc, cc]) for dc in range(2)], reads=[kt, qt])
                kb.op("dve", lambda pa=pa: V.tensor_tensor(out=Cc["att"][0:C, 0:C], in0=pa[0:C, 0:C], in1=Cc["mk"][0:C, 0:C], op=ALU.mult),
                      reads=[pa, Cc["mk"]], writes=[Cc["att"]])
                po = psA()
                groups = []
                for ec in range(4):
                    o_ap = po[:, ec * C:(ec + 1) * C]
                    groups.append([(o_ap, Cc["Sb"][:, dc, ec * 128:(ec + 1) * 128], qt[:, dc, cc]) for dc in range(2)]
                                  + [(o_ap, vt[0:C, ci, ec * 128:(ec + 1) * 128], Cc["att"][0:C, 0:C])])
                kb.mm_multi(po, groups, reads=[Cc["Sb"], qt, vt, Cc["att"]])
                kb.op("act", lambda po=po, cc=cc: S.copy(out=of[:, :, cc], in_=po[:, 0:4 * C].rearrange("p (e t) -> p e t", e=4)),
                      reads=[po], writes=[of])
                pt = psB()
                kb.transpose(pt, [(pt[0:C, dc * 128:(dc + 1) * 128], khh[:, dc, cc], ident_b[:, :]) for dc in range(2)], reads=[khh, ident_b])
                kb.op("act", lambda pt=pt: S.copy(out=Cc["kht"][0:C, :], in_=pt[0:C, 0:256]), reads=[pt], writes=[Cc["kht"]])
                for dc in range(2):
                    ps_ = psA()
                    kb.mm(ps_, [(ps_[:, :], Cc["kht"][0:C, dc * 128:(dc + 1) * 128], vt[0:C, ci, :])], reads=[Cc["kht"], vt])
                    St, Sap = state_fn(ci, "store")
                    kb.op("dve", lambda ps_=ps_, Sap=Sap, dc=dc, ci=ci: V.scalar_tensor_tensor(
                        out=Sap[:, dc, :], in0=Sap[:, dc, :], scalar=e1[:, dc, (ci + 1) * C - 1:(ci + 1) * C], in1=ps_[:, :],
                        op0=ALU.mult, op1=ALU.add), reads=[St, e1, ps_], writes=[St])
                state_fn(ci, "done")
            osq, rs = Cc["osq"], Cc["rs"]
            kb.op("act", lambda: S.activation(out=osq[:, :, 0:ntok], in_=of[:, :, 0:ntok], func=AF.Square), reads=[of], writes=[osq])
            pn = psA()
            kb.mm(pn, [(pn[:, 0:ntok], ones_b[:, :], osq[:, ec, 0:ntok]) for ec in range(4)], reads=[ones_b, osq])
            kb.op("dve", lambda: V.tensor_scalar(out=rs[:, 0:ntok], in0=pn[:, 0:ntok], scalar1=1.0 / 512, scalar2=NORM_EPS, op0=ALU.mult, op1=ALU.add),
                  reads=[pn], writes=[rs])
            kb.op("act", lambda: S.activation(out=rs[:, 0:ntok], in_=rs[:, 0:ntok], func=AF.Sqrt), reads=[rs], writes=[rs])
            kb.op("dve", lambda: V.reciprocal(out=rs[:, 0:ntok], in_=rs[:, 0:ntok]), reads=[rs], writes=[rs])
            for ec in range(4):
                kb.op("dve", lambda ec=ec: V.scalar_tensor_tensor(out=of[:, ec, 0:ntok], in0=of[:, ec, 0:ntok], scalar=Cc["og_g"][:, ec:ec + 1],
                                                                   in1=rs[:, 0:ntok], op0=ALU.mult, op1=ALU.mult), reads=[of, Cc["og_g"], rs], writes=[of])
                kb.op("dve", lambda ec=ec: V.tensor_tensor(out=og[:, h * 4 + ec, og_cols], in0=of[:, ec, 0:ntok], in1=og[:, h * 4 + ec, og_cols],
                                                            op=ALU.mult), reads=[of, og], writes=[og])

        def gla_layer_tile(l, j, ntok, ti):
            Cc = CT_
            sample = ntok < 128
            pl = psA()
            kb.mm(pl, [(pl[0:16, 0:ntok], Cc["wgl"][:, kc, :], hT[:, kc, 0:ntok]) for kc in range(KC)], reads=[Cc["wgl"], hT])
            kb.op("act", lambda: S.copy(out=Cc["gkl"][:, 0:ntok], in_=pl[0:16, 0:ntok]), reads=[pl], writes=[Cc["gkl"]])
            for h in range(8):
                if not sample:
                    gla_head(j, h, ntok, 64, lambda ci, what, h=h: (Cc["S"], Cc["S"][:, 2 * h:2 * h + 2, :]), slice(0, ntok))
                else:
                    def sfn(ci, what, h=h):
                        if what == "load":
                            kb.dma("sp", Cc["Ss"][:], I["state_gla"][j, ci, h].rearrange("(c p) e -> p c e", p=128), writes=[Cc["Ss"]])
                        elif what == "done":
                            kb.dma("sp", O["glas"][j, ci, h].rearrange("(c p) e -> p c e", p=128), Cc["Ss"][:], reads=[Cc["Ss"]], writes=[])
                            kws_toks.append(kb_last_tok("sp"))
                        return (Cc["Ss"], Cc["Ss"][:, :, :])
                    gla_head(j, h, ntok, 8, sfn, slice(0, ntok))

        def gla_outputs_prompt(j):
            Cc = CT_
            for h in range(8):
                kb.dma("sp", O["glap"][j, h].rearrange("(c p) e -> p c e", p=128), Cc["S"][:, 2 * h:2 * h + 2, :], reads=[Cc["S"]], writes=[])
                kws_toks.append(kb_last_tok("sp"))

        try:
            cnt_kind = {0: 0, 1: 0, 2: 0}
            for l, kind in enumerate(layers):
                j = cnt_kind[kind]
                cnt_kind[kind] += 1
                with contextlib.ExitStack() as les:
                    layer_psum(kind, les)
                    if kind == 0:
                        attn_alloc(les)
                    elif kind == 1:
                        rwkv_alloc(les, j)
                    else:
                        gla_alloc(les, j)
                    for ti in range(NT + 1):
                        sample = ti == NT
                        ntok = NST if sample else 512
                        if sample:
                            src = I["xs"] if l == 0 else O["ys"]
                            src_t = [] if l == 0 else [ys_t]
                            dst, dst_t = O["ys"], ys_t
                        else:
                            src = (I["xp"] if l == 0 else O["yp"])[ti * 512:(ti + 1) * 512, :]
                            src_t = [] if l == 0 else [yp_t[ti]]
                            dst, dst_t = O["yp"][ti * 512:(ti + 1) * 512, :], yp_t[ti]
                        chk("alloc")
                        load_norm(l, src, ntok, src_t)
                        chk("norm")
                        if kind == 0:
                            if sample:
                                attn_sample_prep(j)
                            attn_layer_tile(l, j, ntok, ti, last=(ti == NT - 1))
                            chk("atile")
                            if ti == NT - 1:
                                attn_outputs_prompt(j)
                            if sample:
                                attn_outputs_sample(j)
                            out_proj(I["w_out_a"][j], ntok, dst, dst_t)
                        elif kind == 1:
                            rwkv_layer_tile(l, j, ntok, ti)
                            if ti == NT - 1:
                                rwkv_outputs_prompt(j)
                            out_proj(I["w_out_b"][j], ntok, dst, dst_t)
                        else:
                            gla_layer_tile(l, j, ntok, ti)
                            if ti == NT - 1:
                                gla_outputs_prompt(j)
                            out_proj(I["w_out_c"][j], ntok, dst, dst_t)
                    kb.barrier()
        except _Stop:
            pass
        kb.finish([t.w for t in out_regions] + kws_toks)
    return nc


_PROG = {}


def _get_prog():
    if "nc" not in _PROG:
        _PROG["nc"] = build_program(dict(TP=4096, layers=[0, 1, 2, 0]))
    return _PROG["nc"]


def kernel(**inputs):
    f32 = np.float32
    A = {k: np.ascontiguousarray(np.asarray(v, dtype=f32)) for k, v in inputs.items()}
    n = 8
    consts = host_consts()
    shared = dict(consts)
    for nm in ("norm_g", "rel_bias", "w_in_a", "q_norm_g", "k_norm_g", "sinks", "w_out_a", "mu_b", "w_rkvg_b", "w_lora_down_b",
               "w_lora_up_b", "w0_b", "a0_b", "k_k_b", "k_a_b", "ln_x_g_b", "ln_x_b_b", "w_out_b", "w_in_c", "w_gk_up_c",
               "b_gk_c", "o_norm_g_c", "w_out_c"):
        shared[nm] = A[nm]
    shared["r_k_b"] = A["r_k_b"].reshape(A["r_k_b"].shape[0], -1)
    zeros_p = np.zeros((4096, D), f32)
    in_maps = []
    for c in range(n):
        m = dict(shared)
        m["xp"] = A["x_prompt"][c] if c < 2 else zeros_p
        sl = slice(4 * c, 4 * c + 4)
        m["xs"] = A["x_sample"][sl].reshape(32, D)
        m["cache_k"] = np.ascontiguousarray(A["cache_k_win"][:, sl].reshape(2, 4, 128, 512))
        m["cache_v"] = np.ascontiguousarray(A["cache_v_win"][:, sl].reshape(2, 4, 128, 512))
        m["state_wkv"] = np.ascontiguousarray(A["state_wkv"][:, sl])
        m["state_shift"] = np.ascontiguousarray(A["state_shift"][:, sl])
        m["state_gla"] = np.ascontiguousarray(A["state_gla"][:, sl])
        in_maps.append(m)
    nc = _get_prog()
    res = run_bass_kernel_spmd(nc, in_maps, core_ids=list(range(n)))
    R = res.results

    def cat_s(key, shape_tail, lead=True):
        return np.concatenate([np.asarray(R[c][key], f32) for c in range(n)], axis=1)

    y_prompt = np.stack([np.asarray(R[c]["yp"], f32) for c in range(2)], axis=0)
    y_sample = np.concatenate([np.asarray(R[c]["ys"], f32).reshape(4, 8, D) for c in range(n)], axis=0)
    kwp = np.stack([np.asarray(R[c]["kwp"], f32) for c in range(2)], axis=1).reshape(2, 2, 128, 8, 64)
    vwp = np.stack([np.asarray(R[c]["vwp"], f32) for c in range(2)], axis=1).reshape(2, 2, 128, 8, 64)
    kws = cat_s("kws", None).reshape(2, 32, 128, 8, 64)
    vws = cat_s("vws", None).reshape(2, 32, 128, 8, 64)
    wkvp = np.stack([np.asarray(R[c]["wkvp"], f32) for c in range(2)], axis=1)
    shp = np.stack([np.asarray(R[c]["shp"], f32) for c in range(2)], axis=1)
    wkvs = cat_s("wkvs", None)
    shs = cat_s("shs", None)
    glap = np.stack([np.asarray(R[c]["glap"], f32) for c in range(2)], axis=1)
    glas = cat_s("glas", None)
    return (y_prompt, y_sample, kwp, vwp, kws, vws, wkvp, shp, wkvs, shs, glap, glas)
```

```python
import contextlib
import math
import numpy as np
import concourse.bass as bass
import concourse.mybir as mybir
from concourse.bass_utils import run_bass_kernel_spmd

F32 = mybir.dt.float32
BF16 = mybir.dt.bfloat16
ALU = mybir.AluOpType
AF = mybir.ActivationFunctionType
AX = mybir.AxisListType

D = 2048
DI = 4096
KC = D // 128
IC = DI // 128
WINDOW = 128
NEG = -30000.0
NORM_EPS = 1e-6


class _Stop(Exception):
    pass


class TT:
    def __init__(self, h, name=""):
        self.h = h
        self.name = name
        self.w = None
        self.r = {}
        self.psum = False
        self.pending = False

    def __getitem__(self, idx):
        return self.h[idx]


class AliasTT:
    def __init__(self, base, view):
        self.base = base
        self.h = view
        self.name = base.name + "_alias"
        self.psum = base.psum

    def __getitem__(self, idx):
        return self.h[idx]

    @property
    def w(self):
        return self.base.w

    @w.setter
    def w(self, v):
        self.base.w = v

    @property
    def r(self):
        return self.base.r

    @r.setter
    def r(self, v):
        self.base.r = v


class KB:
    NDS = 12
    LIMIT = 10 ** 9

    def __init__(self, nc, es):
        self.nc = nc
        self.es = es
        self.eng = {"pe": nc.tensor, "act": nc.scalar, "dve": nc.vector, "pool": nc.gpsimd, "sp": nc.sync}
        self.sets = [{}, {}]
        self.phase = {}
        self.cnt = {}
        self.seen = {}
        self.dj = {}
        self.ep = 0
        self.nbar = 0
        for si in range(2):
            for e in self.eng:
                self.sets[si][e] = es.enter_context(nc.semaphore("c%d_%s" % (si, e)))
            for q in ("sp", "pool"):
                for j in range(self.NDS):
                    self.sets[si][("d", q, j)] = es.enter_context(nc.semaphore("d%d_%s_%d" % (si, q, j)))
        for e in self.eng:
            self.phase[e] = es.enter_context(nc.semaphore("ph_" + e))
            self.cnt[e] = 0
            self.seen[e] = {}
        for q in ("sp", "pool"):
            self.dj[q] = 0
        self.nalloc = 0
        self.dead = False
        self.last_dma_tok = None

    def semh(self, k):
        return self.sets[self.ep % 2][k]

    def sb(self, shape, dt, name=None, es=None):
        self.nalloc += 1
        name = (name or "t") + "_%d" % self.nalloc
        return TT((es or self.es).enter_context(self.nc.sbuf_tensor(name, list(shape), dt)), name)

    def ps(self, shape, dt, name=None, es=None):
        self.nalloc += 1
        name = (name or "p") + "_%d" % self.nalloc
        t = TT((es or self.es).enter_context(self.nc.psum_tensor(name, list(shape), dt)), name)
        t.psum = True
        return t

    def barrier(self):
        if self.dead:
            return
        toks = {}
        for e in self.eng:
            if self.cnt[e] > 0:
                toks[e] = self.cnt[e]
        for q in ("sp", "pool"):
            for j in range(self.NDS):
                n = (self.dj[q] - j + self.NDS - 1) // self.NDS if self.dj[q] > j else 0
                if n > 0:
                    toks[("d", q, j)] = 16 * n
        for e in self.eng:
            for k, v in toks.items():
                if k == e:
                    continue
                self._wait(e, (k, v, self.ep))

    def epoch_barrier(self):
        self.barrier()
        nxt = (self.ep + 1) % 2
        self.nbar += 1
        for e in self.eng:
            self.eng[e].sem_clear(self.sets[nxt][e])
            if e in ("sp", "pool"):
                for j in range(self.NDS):
                    self.eng[e].sem_clear(self.sets[nxt][("d", e, j)])
            self.eng[e].sem_inc(self.phase[e], 1)
        for e in self.eng:
            for f in self.eng:
                if f != e:
                    self.eng[e].wait_ge(self.phase[f], self.nbar)
        self.ep += 1
        for e in self.eng:
            self.cnt[e] = 0
            self.seen[e] = {}
        for q in ("sp", "pool"):
            self.dj[q] = 0

    def maybe_epoch(self):
        if max(self.cnt.values()) >= self.LIMIT or max(self.dj.values()) >= self.LIMIT // 2:
            self.epoch_barrier()

    def _wait(self, E, tok):
        if self.dead or tok is None:
            return
        k, v, ep = tok
        if ep < self.ep:
            return
        if self.seen[E].get(k, 0) >= v:
            return
        self.eng[E].wait_ge(self.semh(k), v)
        self.seen[E][k] = v

    def _deps(self, E, reads, writes):
        need = {}

        def add(tok):
            if tok is None:
                return
            k, v, ep = tok
            if ep < self.ep:
                return
            if need.get(k, 0) < v:
                need[k] = v

        for t in reads:
            add(t.w)
            if t.psum:
                for rk, tok in t.r.items():
                    if rk != E:
                        add(tok)
        for t in writes:
            add(t.w)
            for tok in t.r.values():
                add(tok)
        for k, v in need.items():
            if k == "pe" and E == "pe":
                continue
            self._wait(E, (k, v, self.ep))

    def _done(self, E, ins, reads, writes):
        if E == "pe":
            for t in writes:
                t.pending = True
        else:
            for t in reads:
                if t.psum:
                    t.pending = False
        self.cnt[E] += 1
        ins.then_inc(self.semh(E), 1)
        tok = (E, self.cnt[E], self.ep)
        for t in writes:
            t.w = tok
            t.r = {}
        for t in reads:
            if t not in writes:
                t.r[E] = tok
        return tok

    def op(self, E, fn, reads=(), writes=()):
        if self.dead:
            return None
        self.maybe_epoch()
        self._deps(E, reads, writes)
        ins = fn()
        return self._done(E, ins, reads, writes)

    def mm(self, out_t, mms, reads):
        return self.mm_multi(out_t, [mms], reads)

    def mm_multi(self, out_t, groups, reads):
        if self.dead:
            return None
        self.maybe_epoch()
        self._deps("pe", reads, [out_t])
        ins = None
        for mms in groups:
            n = len(mms)
            for i, (o, l, r) in enumerate(mms):
                ins = self.nc.tensor.matmul(o, lhsT=l, rhs=r, start=(i == 0), stop=(i == n - 1))
        return self._done("pe", ins, reads, [out_t])

    def transpose(self, out_t, items, reads):
        if self.dead:
            return None
        self.maybe_epoch()
        self._deps("pe", reads, [out_t])
        ins = None
        for (o, i, idn) in items:
            ins = self.nc.tensor.transpose(o, i, idn)
        return self._done("pe", ins, reads, [out_t])

    def dma(self, q, out_ap, in_ap, reads=(), writes=(), **kw):
        if self.dead:
            return None
        self.maybe_epoch()
        self._deps(q, reads, writes)
        i = self.dj[q]
        self.dj[q] += 1
        j = i % self.NDS
        rnd = i // self.NDS
        key = ("d", q, j)
        if rnd > 0:
            self._wait(q, (key, 16 * rnd, self.ep))
        self.eng[q].dma_start(out=out_ap, in_=in_ap, **kw).then_inc(self.semh(key), 16)
        tok = (key, 16 * (rnd + 1), self.ep)
        for t in writes:
            t.w = tok
            t.r = {}
        for t in reads:
            t.r[key] = tok
        self.last_dma_tok = tok
        return tok

    def finish(self, toks):
        self.dead = False
        for tok in toks:
            if tok is not None:
                self._wait("sp", tok)


def t5_bucket_np(d):
    d = np.maximum(d, 0)
    large = 16 + (np.log(np.maximum(d, 1).astype(np.float32) / np.float32(16)) / np.float32(math.log(128 / 16))
                  * np.float32(16)).astype(np.int32)
    return np.where(d < 16, d, np.minimum(large, 31))


def host_consts():
    c = {}
    c["ident"] = np.eye(128, dtype=np.float32)
    bo = np.zeros((128, 128), np.float32)
    bo[:64, :64] = 1.0
    bo[64:, 64:] = 1.0
    c["blockones"] = bo
    oh = np.zeros((33, 384), np.float32)
    for m in range(384):
        dist = m - 128
        if 0 <= dist < 128:
            oh[int(t5_bucket_np(np.array(dist))), m] = 1.0
        else:
            oh[32, m] = NEG
    c["bucket_oh"] = oh
    t = np.arange(64)
    mk = np.zeros((3, 64, 64), np.float32)
    mk[0] = (t[:, None] < t[None, :])
    mk[1] = (t[None, :] < t[:, None])
    mk[2] = (t[:, None] <= t[None, :])
    c["masks"] = mk
    return c


def build_program(cfg):
    TP = cfg["TP"]
    layers = cfg["layers"]
    NSQ = 4
    NST = NSQ * 8
    nA = sum(1 for k in layers if k == 0)
    nB = sum(1 for k in layers if k == 1)
    nC = sum(1 for k in layers if k == 2)
    NL = len(layers)
    assert TP % 512 == 0
    NT = TP // 512

    nc = bass.Bass("TRN2", target_bir_lowering=False)

    def din(name, shape, dt=F32):
        return nc.dram_tensor(name, list(shape), dt, kind="ExternalInput").ap()

    def dout(name, shape, dt=F32):
        return nc.dram_tensor(name, list(shape), dt, kind="ExternalOutput").ap()

    I = {}
    I["xp"] = din("xp", [TP, D])
    I["xs"] = din("xs", [NST, D])
    I["norm_g"] = din("norm_g", [NL, D])
    I["ident"] = din("ident", [128, 128])
    I["blockones"] = din("blockones", [128, 128])
    I["bucket_oh"] = din("bucket_oh", [33, 384])
    if nA:
        A_IN = DI + 1024 + DI
        I["cache_k"] = din("cache_k", [nA, NSQ, 128, 512])
        I["cache_v"] = din("cache_v", [nA, NSQ, 128, 512])
        I["rel_bias"] = din("rel_bias", [32, 64])
        I["w_in_a"] = din("w_in_a", [nA, D, A_IN])
        I["q_norm_g"] = din("q_norm_g", [nA, 64])
        I["k_norm_g"] = din("k_norm_g", [nA, 64])
        I["sinks"] = din("sinks", [nA, 64])
        I["w_out_a"] = din("w_out_a", [nA, DI, D])
    if nB:
        I["state_wkv"] = din("state_wkv", [nB, NSQ, 64, 64, 64])
        I["state_shift"] = din("state_shift", [nB, NSQ, D])
        I["mu_b"] = din("mu_b", [nB, 6, D])
        I["w_rkvg_b"] = din("w_rkvg_b", [nB, 4, D, DI])
        I["w_lora_down_b"] = din("w_lora_down_b", [nB, 2, D, 96])
        I["w_lora_up_b"] = din("w_lora_up_b", [nB, 2, 96, DI])
        for nm in ("w0_b", "a0_b", "k_k_b", "k_a_b", "r_k_b", "ln_x_g_b", "ln_x_b_b"):
            I[nm] = din(nm, [nB, DI])
        I["w_out_b"] = din("w_out_b", [nB, DI, D])
        I["masks"] = din("masks", [3, 64, 64])
    if nC:
        C_IN = 2 * 2048 + 2 * DI + 16
        I["state_gla"] = din("state_gla", [nC, NSQ, 8, 256, 512])
        I["w_in_c"] = din("w_in_c", [nC, D, C_IN])
        I["w_gk_up_c"] = din("w_gk_up_c", [nC, 16, 2048])
        I["b_gk_c"] = din("b_gk_c", [nC, 2048])
        I["o_norm_g_c"] = din("o_norm_g_c", [nC, 512])
        I["w_out_c"] = din("w_out_c", [nC, DI, D])
        if "masks" not in I:
            I["masks"] = din("masks", [3, 64, 64])
    O = {}
    if nC:
        O["glap"] = dout("glap", [nC, 8, 256, 512])
        O["glas"] = dout("glas", [nC, NSQ, 8, 256, 512])
    if nB:
        O["wkvp"] = dout("wkvp", [nB, 64, 64, 64])
        O["shp"] = dout("shp", [nB, D])
        O["wkvs"] = dout("wkvs", [nB, NSQ, 64, 64, 64])
        O["shs"] = dout("shs", [nB, NSQ, D])
    O["yp"] = dout("yp", [TP, D])
    O["ys"] = dout("ys", [NST, D])
    if nA:
        O["kwp"] = dout("kwp", [nA, 128, 512])
        O["vwp"] = dout("vwp", [nA, 128, 512])
        O["kws"] = dout("kws", [nA, NSQ, 128, 512])
        O["vws"] = dout("vws", [nA, NSQ, 128, 512])
    zrep = nc.dram_tensor("zrep", [64, 128 * 384], F32, kind="Internal").ap() if nA else None

    es = contextlib.ExitStack()
    with es:
        kb = KB(nc, es)
        V = nc.vector
        S = nc.scalar
        G = nc.gpsimd
        out_regions = []

        def chk(stage):
            if cfg.get("stop") == stage:
                kb.dead = True

        yp_t = [TT(None, "yp%d" % t) for t in range(NT)]
        ys_t = TT(None, "ys")
        out_regions += yp_t + [ys_t]

        ident_f = kb.sb([128, 128], F32, "ident_f")
        ident_b = kb.sb([128, 128], BF16, "ident_b")
        bones_b = kb.sb([128, 128], BF16, "bones_b")
        ones_b = kb.sb([128, 128], BF16, "ones_b")
        gT = kb.sb([128, NL, KC], F32, "gT")
        kb.dma("sp", ident_f[:], I["ident"], writes=[ident_f])
        kb.dma("pool", ident_b[:], I["ident"], writes=[ident_b])
        kb.dma("pool", bones_b[:], I["blockones"], writes=[bones_b])
        bones_f = kb.sb([128, 128], F32, "bones_f")
        kb.dma("sp", bones_f[:], I["blockones"], writes=[bones_f])
        kb.op("dve", lambda: V.memset(ones_b[:], 1.0), writes=[ones_b])
        kb.dma("sp", gT[:], I["norm_g"].rearrange("l (kc p) -> p l kc", p=128), writes=[gT],
               allow_slow_non_contiguous=True)

        xt = kb.sb([128, 4, D], F32, "xt")
        hT = kb.sb([128, KC, 512], BF16, "hT")
        og = kb.sb([128, IC, 512], BF16, "og")
        wbuf = [kb.sb([128, 8192], BF16, "wbuf%d" % i) for i in range(2)]
        wsel = [0]
        stat = kb.sb([128, 16], F32, "stat")
        PSA = []
        PSB = []
        PSX = []
        psa_i = [0]
        psb_i = [0]

        def layer_psum(kind, les):
            na, nb_, nx = {0: (4, 2, 2), 1: (5, 1, 2), 2: (6, 2, 0)}[kind]
            PSA[:] = [kb.ps([128, 512], F32, "psA%d" % i, es=les) for i in range(na)]
            PSB[:] = [kb.ps([128, 1024], BF16, "psB%d" % i, es=les) for i in range(nb_)]
            PSX[:] = [kb.ps([128, 512], F32, "psX%d" % i, es=les) for i in range(nx)]

        def psA():
            psa_i[0] += 1
            t = PSA[psa_i[0] % len(PSA)]
            assert kb.dead or not t.pending, "PSUM tile handed out before its previous evacuation was emitted"
            return t

        def psB():
            psb_i[0] += 1
            t = PSB[psb_i[0] % len(PSB)]
            assert kb.dead or not t.pending, "PSUM tile handed out before its previous evacuation was emitted"
            return t

        def next_w():
            wsel[0] += 1
            return wbuf[wsel[0] % 2]

        def load_norm(l, src_ap, ntok, src_reads):
            nsub = (ntok + 127) // 128
            pp = min(ntok, 128)
            if ntok >= 128:
                kb.dma("sp", xt[:, 0:nsub, :], src_ap.rearrange("(s p) d -> p s d", p=128), reads=src_reads, writes=[xt])
            else:
                kb.dma("sp", xt[0:pp, 0, :], src_ap, reads=src_reads, writes=[xt])
            for s in range(nsub):
                kb.op("act", lambda s=s: S.activation(out=hT.h[0:pp, 0:4, :].rearrange("p a b -> p (a b)"), in_=xt[0:pp, s, :],
                                                     func=AF.Square, accum_out=stat[0:pp, s:s + 1]),
                      reads=[xt], writes=[hT, stat])
            kb.op("dve", lambda: V.tensor_scalar(out=stat[0:pp, 4:4 + nsub], in0=stat[0:pp, 0:nsub], scalar1=1.0 / D,
                                                 scalar2=NORM_EPS, op0=ALU.mult, op1=ALU.add), reads=[stat], writes=[stat])
            kb.op("act", lambda: S.activation(out=stat[0:pp, 8:8 + nsub], in_=stat[0:pp, 4:4 + nsub], func=AF.Sqrt),
                  reads=[stat], writes=[stat])
            kb.op("dve", lambda: V.reciprocal(out=stat[0:pp, 12:12 + nsub], in_=stat[0:pp, 8:8 + nsub]),
                  reads=[stat], writes=[stat])
            xn = og.h[:].rearrange("p a b -> p (a b)")
            for s in range(nsub):
                kb.op("dve", lambda s=s: V.tensor_scalar(out=xn[0:pp, s * D:(s + 1) * D], in0=xt[0:pp, s, :],
                                                         scalar1=stat[0:pp, 12 + s:13 + s], scalar2=None, op0=ALU.mult),
                      reads=[xt, stat], writes=[og])
            for kc in range(KC):
                pt = psB()
                kb.transpose(pt, [(pt[:, s * 128:s * 128 + pp], xn[0:pp, s * D + kc * 128:s * D + (kc + 1) * 128],
                                   ident_b[0:pp, 0:pp]) for s in range(nsub)], reads=[og, ident_b])
                kb.op("act", lambda kc=kc, pt=pt: S.activation(out=hT[:, kc, 0:ntok], in_=pt[:, 0:ntok], func=AF.Copy,
                                                                scale=gT[:, l, kc:kc + 1]),
                      reads=[pt, gT], writes=[hT])

        def load_w_in(w_ap, col0, ncols):
            wt = next_w()
            view = wt.h[:, 0:KC * ncols].rearrange("p (k n) -> p k n", k=KC)
            kb.dma("pool", view, w_ap.rearrange("(k p) n -> p k n", p=128)[:, :, col0:col0 + ncols], writes=[wt])
            return wt, view

        def out_proj(w_ap, ntok, dst_ap, dst_t):
            nsub = (ntok + 127) // 128
            pp = min(ntok, 128)
            for cb in range(D // 256):
                wt = next_w()
                view = wt.h[:, 0:IC * 256].rearrange("p (k n) -> p k n", k=IC)
                kb.dma("pool", view, w_ap.rearrange("(k p) n -> p k n", p=128)[:, :, cb * 256:(cb + 1) * 256], writes=[wt])
                for s in range(nsub):
                    pt = psA()
                    kb.mm(pt, [(pt[0:pp, 0:256], og[:, ic, s * 128:s * 128 + pp], view[:, ic, :]) for ic in range(IC)],
                          reads=[og, wt])
                    kb.op("dve", lambda s=s, pt=pt, cb=cb: V.tensor_tensor(out=xt[0:pp, s, cb * 256:(cb + 1) * 256],
                                                                            in0=pt[0:pp, 0:256],
                                                                            in1=xt[0:pp, s, cb * 256:(cb + 1) * 256], op=ALU.add),
                          reads=[pt, xt], writes=[xt])
            if ntok >= 128:
                kb.dma("sp", dst_ap.rearrange("(s p) d -> p s d", p=128), xt[:, 0:nsub, :], reads=[xt], writes=[dst_t])
            else:
                kb.dma("sp", dst_ap, xt[0:pp, 0, :], reads=[xt], writes=[dst_t])

        AT = {}

        def attn_alloc(les):
            def sb(shape, dt, name):
                return kb.sb(shape, dt, name, es=les)
            AT["BTp"] = sb([128, 64, 128], BF16, "BTp")
            AT["BTc"] = sb([128, 64, 128], BF16, "BTc")
            with contextlib.ExitStack() as zes:
                rb33 = kb.sb([33, 64], F32, "rb33", es=zes)
                oh33 = kb.sb([33, 384], F32, "oh33", es=zes)
                zsb = kb.sb([64, 384], F32, "zsb", es=zes)
                kb.op("dve", lambda: V.memset(rb33[:], 1.0), writes=[rb33])
                kb.dma("sp", rb33[0:32, :], I["rel_bias"], writes=[rb33])
                kb.dma("sp", oh33[:], I["bucket_oh"], writes=[oh33])
                pz = psA()
                kb.mm(pz, [(pz[0:64, 0:384], rb33[:, :], oh33[:, :])], reads=[rb33, oh33])
                kb.op("dve", lambda: V.tensor_copy(out=zsb[:], in_=pz[0:64, 0:384]), reads=[pz], writes=[zsb])
                zr3 = zrep.rearrange("h (r m) -> h r m", m=384)
                zrp = kb.sb([64, 16, 384], F32, "zrp", es=zes)
                kb.op("dve", lambda: V.tensor_copy(out=zrp[:], in_=zsb[:, :].unsqueeze(1).to_broadcast([64, 16, 384])),
                      reads=[zsb], writes=[zrp])
                ztoks = []
                for r0 in range(0, 128, 16):
                    kb.dma("sp", zr3[:, r0:r0 + 16, :], zrp[:], reads=[zrp], writes=[])
                    ztoks.append(kb.last_dma_tok)
                for tk in ztoks:
                    kb._wait("pool", tk)
                for (BT, off) in ((AT["BTc"], 128), (AT["BTp"], 256)):
                    src = bass.AP(tensor=zrep.tensor, offset=zrep.offset + off, ap=[[383, 128], [128 * 384, 64], [1, 128]])
                    kb.dma("pool", BT[:], src, reads=[zrep_t], writes=[BT])
                kb.barrier()
            chk("bt")
            AT["kT"] = sb([128, 8, 640], BF16, "kT")
            AT["vtk"] = sb([128, 5, 512], BF16, "vtk")
            AT["knf"] = sb([128, 8, 128], F32, "knf")
            AT["vlf"] = sb([128, 512], F32, "vlf")
            AT["qn"] = sb([128, 4, 512], BF16, "qn")
            AT["sg"] = sb([128, 4, 512], BF16, "sg")
            AT["sqb"] = sb([128, 512], BF16, "sqb")
            AT["rsd"] = sb([128, 512], F32, "rsd")
            AT["pT"] = [sb([128, 512], BF16, "pT%d" % i) for i in range(2)]
            AT["tmpf"] = [sb([128, 512], F32, "tmpf%d" % i) for i in range(2)]
            AT["den"] = sb([128, 512], F32, "den")
            gq = AT["gq"] = sb([128, nA], F32, "gq")
            gk = AT["gk"] = sb([128, nA], F32, "gk")
            esk = AT["esk"] = sb([128, nA, 32], F32, "esk")
            for par in range(2):
                kb.dma("sp", gq[par * 64:(par + 1) * 64, :], I["q_norm_g"].rearrange("l d -> d l"), writes=[gq],
                       allow_slow_non_contiguous=True)
                kb.dma("sp", gk[par * 64:(par + 1) * 64, :], I["k_norm_g"].rearrange("l d -> d l"), writes=[gk],
                       allow_slow_non_contiguous=True)
            kb.op("dve", lambda: V.memset(esk[:], 0.0), writes=[esk])
            for par in range(2):
                sk = I["sinks"].rearrange("l (c two) -> two l c", two=2)[par:par + 1]
                kb.dma("sp", esk[par * 64:par * 64 + 1, :, :], sk, writes=[esk], allow_slow_non_contiguous=True)
            pe_ = psA()
            kb.mm(pe_, [(pe_[:, 0:nA * 32], bones_f[:, :], esk[:].rearrange("p l c -> p (l c)"))], reads=[bones_f, esk])
            kb.op("act", lambda: S.activation(out=esk[:].rearrange("p l c -> p (l c)"), in_=pe_[:, 0:nA * 32], func=AF.Exp),
                  reads=[pe_], writes=[esk])
            kb.op("dve", lambda: V.tensor_scalar(out=gq[:], in0=gq[:], scalar1=0.125, scalar2=None, op0=ALU.mult),
                  reads=[gq], writes=[gq])
            chk("esk")
            AT["kcs"] = sb([128, 8, 128], BF16, "kcs")
            AT["kcs_f"] = AT["tmpf"][0]
            AT["vcs_f"] = AT["tmpf"][1]
            AT["vns"] = sb([8, NSQ, 512], BF16, "vns")
            AT["vnsf"] = sb([32, 512], F32, "vnsf")
            AT["kcT_bufs"] = [sb([128, 8, 128], BF16, "kcTb%d" % i) for i in range(NSQ)]
            AT["vcs_bufs"] = [sb([128, 512], BF16, "vcsb%d" % i) for i in range(NSQ)]
            AT["ktok"] = AT["den"]
            AT["pti"] = [0]
            AT["tfi"] = [0]

        class _A:
            def __getattr__(self, k):
                return AT[k]
        A_ = _A()

        def headnorm(ps_t, ntok, gcol, gt, out_ap, out_t, f32_out=None):
            sqb, rsd = AT["sqb"], AT["rsd"]
            kb.op("act", lambda: S.activation(out=sqb[:, 0:ntok], in_=ps_t[:, 0:ntok], func=AF.Square),
                  reads=[ps_t], writes=[sqb])
            p2 = psA()
            kb.mm(p2, [(p2[:, 0:ntok], bones_b[:, :], sqb[:, 0:ntok])], reads=[bones_b, sqb])
            kb.op("dve", lambda: V.tensor_scalar(out=rsd[:, 0:ntok], in0=p2[:, 0:ntok], scalar1=1.0 / 64, scalar2=NORM_EPS,
                                                 op0=ALU.mult, op1=ALU.add), reads=[p2], writes=[rsd])
            kb.op("act", lambda: S.activation(out=rsd[:, 0:ntok], in_=rsd[:, 0:ntok], func=AF.Sqrt), reads=[rsd], writes=[rsd])
            kb.op("dve", lambda: V.reciprocal(out=rsd[:, 0:ntok], in_=rsd[:, 0:ntok]), reads=[rsd], writes=[rsd])
            if f32_out is not None:
                fo_ap, fo_t, c0, c1 = f32_out
                kb.op("dve", lambda: V.scalar_tensor_tensor(out=fo_ap, in0=ps_t[:, c0:c1], scalar=gcol,
                                                            in1=rsd[:, c0:c1], op0=ALU.mult, op1=ALU.mult),
                      reads=[ps_t, rsd, gt], writes=[fo_t])
            kb.op("dve", lambda: V.scalar_tensor_tensor(out=out_ap, in0=ps_t[:, 0:ntok], scalar=gcol, in1=rsd[:, 0:ntok],
                                                        op0=ALU.mult, op1=ALU.mult),
                  reads=[ps_t, rsd, gt], writes=[out_t])

        def attn_block(j, kh, QB, q_cols, kprev, kcur, vprev, vcur, ncur, og_cols, first, xr=()):
            qn, sg, den, esk, pT, tmpf = AT["qn"], AT["sg"], AT["den"], AT["esk"], AT["pT"], AT["tmpf"]
            po = PSX[0]
            pd = PSX[1]
            groups_o = []
            groups_d = []
            preads = []
            W4 = 4 * QB
            for par in range(2):
                pts = []
                for which in ((0, 1) if not first else (1,)):
                    nk = 128 if which == 0 else ncur
                    lk = kprev(par) if which == 0 else kcur(par)
                    sc = psA()
                    kb.mm(sc, [(sc[0:nk, 0:W4].rearrange("p (c q) -> p c q", c=4), lk,
                                qn[par * 64:(par + 1) * 64, :, q_cols])], reads=[AT["kT"], qn] + list(xr))
                    BT = AT["BTp"] if which == 0 else AT["BTc"]
                    h0 = kh * 8 + par
                    tf = tmpf[AT["tfi"][0] % 2]
                    AT["tfi"][0] += 1
                    kb.op("dve", lambda sc=sc, tf=tf, BT=BT, nk=nk, h0=h0: V.tensor_tensor(
                        out=tf[0:nk, 0:W4].rearrange("p (c q) -> p c q", c=4),
                        in0=sc[0:nk, 0:W4].rearrange("p (c q) -> p c q", c=4),
                        in1=BT[0:nk, h0:h0 + 7:2, 0:QB], op=ALU.add), reads=[sc, BT], writes=[tf])
                    p = pT[AT["pti"][0] % 2]
                    AT["pti"][0] += 1
                    kb.op("act", lambda tf=tf, p=p, nk=nk: S.activation(out=p[0:nk, 0:W4], in_=tf[0:nk, 0:W4], func=AF.Exp),
                          reads=[tf], writes=[p])
                    pts.append((p, nk, which))
                go = []
                gd = []
                preads = []
                for (p, nk, which) in pts:
                    vv = vprev if which == 0 else vcur
                    go.append((po[par * 64:(par + 1) * 64, 0:W4], vv, p[0:nk, 0:W4]))
                    gd.append((pd[par * 64:(par + 1) * 64, 0:W4], ones_b[0:nk, 0:64], p[0:nk, 0:W4]))
                    preads.append(p)
                kb.mm(po, go, reads=preads + [AT["vtk"]] + list(xr))
                kb.mm(pd, gd, reads=preads + [ones_b])
            kb.op("dve", lambda: V.tensor_tensor(out=den[:, 0:W4].rearrange("p (c q) -> p c q", c=4),
                                                 in0=pd[:, 0:W4].rearrange("p (c q) -> p c q", c=4),
                                                 in1=esk[:, j, kh * 4:kh * 4 + 4].unsqueeze(2).to_broadcast([128, 4, QB]),
                                                 op=ALU.add), reads=[pd, esk], writes=[den])
            kb.op("dve", lambda: V.reciprocal(out=den[:, 0:W4], in_=den[:, 0:W4]), reads=[den], writes=[den])
            kb.op("dve", lambda: V.tensor_tensor(out=den[:, 0:W4], in0=po[:, 0:W4], in1=den[:, 0:W4], op=ALU.mult),
                  reads=[po, den], writes=[den])
            kb.op("dve", lambda: V.tensor_tensor(out=og[:, kh * 4:kh * 4 + 4, og_cols],
                                                 in0=den[:, 0:W4].rearrange("p (c q) -> p c q", c=4),
                                                 in1=sg[:, :, q_cols], op=ALU.mult), reads=[den, sg], writes=[og])

        def attn_layer_tile(l, j, ntok, ti, last):
            kT, vtk, knf, vlf, qn, sg, gq, gk, vns, vnsf = (AT[k] for k in
                                                            ("kT", "vtk", "knf", "vlf", "qn", "sg", "gq", "gk", "vns", "vnsf"))
            w_in = I["w_in_a"][j]
            sample = ntok < 128
            for kh in range(8):
                wt = next_w()
                view = wt.h[:, 0:KC * 128].rearrange("p (k n) -> p k n", k=KC)
                src = w_in.rearrange("(k p) n -> p k n", p=128)[:, :, DI + kh * 64:DI + kh * 64 + 64]
                kb.dma("pool", view[:, :, 0:64], src, writes=[wt])
                kb.dma("pool", view[:, :, 64:128], src, writes=[wt])
                pk = psA()
                kb.mm(pk, [(pk[:, 0:ntok], view[:, kc, :], hT[:, kc, 0:ntok]) for kc in range(KC)], reads=[wt, hT])
                if not sample:
                    headnorm(pk, ntok, gk[:, j:j + 1], gk, kT[:, kh, 128:640], kT,
                             f32_out=(knf[:, kh, :], knf, 384, 512) if last else None)
                else:
                    headnorm(pk, ntok, gk[:, j:j + 1], gk, kT[:, kh, 0:ntok], kT,
                             f32_out=(knf[:, kh, 0:ntok], knf, 0, ntok))
            chk("ak")
            wt, view = load_w_in(w_in, DI + 512, 512)
            chk("av0")
            if not sample:
                for s_ in range(4):
                    pv = psA()
                    kb.mm(pv, [(pv[:, :], hT[:, kc, s_ * 128:(s_ + 1) * 128], view[:, kc, :]) for kc in range(KC)], reads=[wt, hT])
                    chk("av1")
                    kb.op("act", lambda s_=s_, pv=pv: S.copy(out=vtk[:, 1 + s_, :], in_=pv[:, :]), reads=[pv], writes=[vtk])
                    chk("av2")
                    if s_ == 1:
                        chk("av3")
                    if s_ == 3:
                        chk("av4")
                    if last and s_ == 3:
                        kb.op("dve", lambda pv=pv: V.tensor_copy(out=vlf[:], in_=pv[:, :]), reads=[pv], writes=[vlf])
                        chk("av5")
            else:
                for sq in range(NSQ):
                    pv = psA()
                    kb.mm(pv, [(pv[0:8, :], hT[:, kc, sq * 8:(sq + 1) * 8], view[:, kc, :]) for kc in range(KC)], reads=[wt, hT])
                    kb.op("act", lambda sq=sq, pv=pv: S.copy(out=vns[:, sq, :], in_=pv[0:8, :]), reads=[pv], writes=[vns])
                pv = psA()
                kb.mm(pv, [(pv[0:NST, :], hT[:, kc, 0:NST], view[:, kc, :]) for kc in range(KC)], reads=[wt, hT])
                kb.op("dve", lambda pv=pv: V.tensor_copy(out=vnsf[:], in_=pv[0:NST, :]), reads=[pv], writes=[vnsf])
            chk("av")
            for kh in range(8):
                wt, view = load_w_in(w_in, kh * 512, 512)
                for c in range(4):
                    pq = psA()
                    kb.mm(pq, [(pq[:, 0:ntok], view[:, kc, c * 128:(c + 1) * 128], hT[:, kc, 0:ntok]) for kc in range(KC)],
                          reads=[wt, hT])
                    headnorm(pq, ntok, gq[:, j:j + 1], gq, qn[:, c, 0:ntok], qn)
                wt, view = load_w_in(w_in, DI + 1024 + kh * 512, 512)
                for c in range(4):
                    pg = psA()
                    kb.mm(pg, [(pg[:, 0:ntok], view[:, kc, c * 128:(c + 1) * 128], hT[:, kc, 0:ntok]) for kc in range(KC)],
                          reads=[wt, hT])
                    kb.op("act", lambda c=c, pg=pg: S.activation(out=sg[:, c, 0:ntok], in_=pg[:, 0:ntok], func=AF.Silu),
                          reads=[pg], writes=[sg])
                chk("aq")
                if not sample:
                    for b in range(4):
                        if b == 1:
                            chk("ab0")
                        if b == 2:
                            chk("ab1")
                        attn_block(j, kh, 128, slice(b * 128, (b + 1) * 128),
                                   lambda par, b=b: kT[par * 64:(par + 1) * 64, kh, b * 128:(b + 1) * 128],
                                   lambda par, b=b: kT[par * 64:(par + 1) * 64, kh, (b + 1) * 128:(b + 2) * 128],
                                   vtk[:, b, kh * 64:(kh + 1) * 64], vtk[:, b + 1, kh * 64:(kh + 1) * 64], 128,
                                   slice(b * 128, (b + 1) * 128), first=(ti == 0 and b == 0))
                else:
                    for sq in range(NSQ):
                        attn_block(j, kh, 8, slice(sq * 8, (sq + 1) * 8),
                                   lambda par, sq=sq: AT["kcT_bufs"][sq][par * 64:(par + 1) * 64, kh, :],
                                   lambda par, sq=sq: kT[par * 64:(par + 1) * 64, kh, sq * 8:(sq + 1) * 8],
                                   AT["vcs_bufs"][sq][:, kh * 64:(kh + 1) * 64], vns[:, sq, kh * 64:(kh + 1) * 64], 8,
                                   slice(sq * 8, (sq + 1) * 8), first=False,
                                   xr=[AT["kcT_bufs"][sq], AT["vcs_bufs"][sq], vns])
            if not sample:
                kb.op("pool", lambda: G.tensor_copy(out=kT[:, :, 0:128], in_=kT[:, :, 512:640]), reads=[kT], writes=[kT])
                kb.op("pool", lambda: G.tensor_copy(out=vtk[:, 0, :], in_=vtk[:, 4, :]), reads=[vtk], writes=[vtk])

        def attn_sample_prep(j):
            kcs, kcs_f, vcs_f = AT["kcs"], AT["kcs_f"], AT["vcs_f"]
            for sq in range(NSQ):
                kb.dma("sp", kcs_f[:], I["cache_k"][j, sq], writes=[kcs_f])
                kb.dma("sp", vcs_f[:], I["cache_v"][j, sq], writes=[vcs_f])
                kcv = kcs_f.h[:].rearrange("p (k d) -> p k d", k=8)
                kb.op("dve", lambda kcv=kcv: V.tensor_copy(out=kcs[:, :, 0:64], in_=kcv), reads=[kcs_f], writes=[kcs])
                kb.op("dve", lambda kcv=kcv: V.tensor_copy(out=kcs[:, :, 64:128], in_=kcv), reads=[kcs_f], writes=[kcs])
                kct = AT["kcT_bufs"][sq]
                vc = AT["vcs_bufs"][sq]
                kb.op("act", lambda vc=vc: S.copy(out=vc[:], in_=vcs_f[:]), reads=[vcs_f], writes=[vc])
                for k2 in range(0, 8, 4):
                    pt = psB()
                    kb.transpose(pt, [(pt[:, i * 128:(i + 1) * 128], kcs[:, k2 + i, :], ident_b[:, :]) for i in range(4)],
                                 reads=[kcs, ident_b])
                    kb.op("act", lambda pt=pt, kct=kct, k2=k2: S.copy(out=kct[:, k2:k2 + 4, :],
                                                                    in_=pt[:, 0:512].rearrange("p (k n) -> p k n", k=4)),
                          reads=[pt], writes=[kct])
                kb.dma("sp", O["kws"][j, sq, 0:120, :], I["cache_k"][j, sq, 8:128, :], writes=[])
                kws_toks.append(kb_last_tok("sp"))
                kb.dma("sp", O["vws"][j, sq, 0:120, :], I["cache_v"][j, sq, 8:128, :], writes=[])
                kws_toks.append(kb_last_tok("sp"))

        def kb_last_tok(q):
            return kb.last_dma_tok

        kws_toks = []
        if nA:
            kws_t = TT(None, "kws")
            vws_t = TT(None, "vws")
            kwp_t = TT(None, "kwp")
            vwp_t = TT(None, "vwp")
            zrep_t = TT(None, "zrep")
            out_regions += [kws_t, vws_t, kwp_t, vwp_t]

        def attn_outputs_prompt(j):
            knf, vlf, ktok = AT["knf"], AT["vlf"], AT["ktok"]
            pt = psA()
            kb.transpose(pt, [(pt[:, kh * 64:(kh + 1) * 64], knf[0:64, kh, :], ident_f[0:64, 0:64]) for kh in range(8)],
                         reads=[knf, ident_f])
            kb.op("dve", lambda: V.tensor_copy(out=ktok[:], in_=pt[:, :]), reads=[pt], writes=[ktok])
            kb.dma("sp", O["kwp"][j], ktok[:], reads=[ktok], writes=[])
            kws_toks.append(kb_last_tok("sp"))
            kb.dma("sp", O["vwp"][j], vlf[:], reads=[vlf], writes=[])
            kws_toks.append(kb_last_tok("sp"))

        def attn_outputs_sample(j):
            knf, vnsf, ktok = AT["knf"], AT["vnsf"], AT["ktok"]
            pt = psA()
            kb.transpose(pt, [(pt[0:NST, kh * 64:(kh + 1) * 64], knf[0:64, kh, 0:NST], ident_f[0:64, 0:64])
                              for kh in range(8)], reads=[knf, ident_f])
            kb.op("dve", lambda: V.tensor_copy(out=ktok[0:NST, :], in_=pt[0:NST, :]), reads=[pt], writes=[ktok])
            for sq in range(NSQ):
                kb.dma("sp", O["kws"][j, sq, 120:128, :], ktok[sq * 8:(sq + 1) * 8, :], reads=[ktok], writes=[])
                kws_toks.append(kb_last_tok("sp"))
                kb.dma("sp", O["vws"][j, sq, 120:128, :], vnsf[sq * 8:(sq + 1) * 8, :], reads=[vnsf], writes=[])
                kws_toks.append(kb_last_tok("sp"))


        BT_ = {}
        KAP = 0.6065306597126334

        def rwkv_alloc(les, j):
            def sb(shape, dt, name):
                return kb.sb(shape, dt, name, es=les)
            B = BT_
            B["ST"] = sb([128, 32, 64], F32, "ST")
            B["STs"] = sb([128, NSQ, 64], F32, "STs")
            B["STs2"] = [B["STs"], TT(B["STs"].h, "STs_b")]
            B["ST2"] = [B["ST"], TT(B["ST"].h, "ST_b")]
            kb.op("dve", lambda: V.memset(B["ST"][:], 0.0), writes=B["ST2"])
            B["hlast"] = sb([128, KC], BF16, "hlast")
            kb.op("dve", lambda: V.memset(B["hlast"][:], 0.0), writes=[B["hlast"]])
            B["dlt"] = AliasTT(wbuf[0], wbuf[0].h[:, 0:KC * 256].rearrange("p (k n) -> p k n", k=KC))
            B["xm"] = [sb([128, KC, 256], BF16, "xm%d" % c) for c in range(5)]
            B["xm"].append(B["xm"][4])
            B["mu"] = sb([128, 6, KC], F32, "mu")
            kb.dma("sp", B["mu"][:], I["mu_b"][j].rearrange("c (k p) -> p c k", p=128), writes=[B["mu"]],
                   allow_slow_non_contiguous=True)
            B["pv"] = sb([128, 7, 32], F32, "pv")
            for wi, nm in enumerate(("w0_b", "a0_b", "k_k_b", "k_a_b", "r_k_b", "ln_x_g_b", "ln_x_b_b")):
                kb.dma("sp", B["pv"][:, wi, :], I[nm][j].rearrange("(c p) -> p c", p=128), writes=[B["pv"]],
                       allow_slow_non_contiguous=True)
            B["wd"] = sb([128, 2, KC, 96], BF16, "wd")
            for c in range(2):
                kb.dma("pool", B["wd"][:, c, :, :], I["w_lora_down_b"][j, c].rearrange("(k p) n -> p k n", p=128), writes=[B["wd"]])
            B["wu"] = [sb([96, 2, 128], BF16, "wu%d" % i) for i in range(2)]
            B["lw"] = sb([96, 2, 256], BF16, "lwlow")
            B["mk"] = sb([128, 3, 64], F32, "mk")
            kb.dma("sp", B["mk"][0:64], I["masks"].rearrange("m a b -> a m b"), writes=[B["mk"]])
            kb.dma("sp", B["mk"][64:128], I["masks"].rearrange("m a b -> a m b"), writes=[B["mk"]])
            for nm in ("r", "k", "kk", "k2", "a", "sgw", "cs", "t1", "t2", "Ep", "of", "gt1"):
                B[nm] = sb([128, 256], F32, "f_" + nm)
            B["E0"] = B["t2"]
            B["Em"] = [sb([128, 256], F32, "f_Em%d" % i) for i in range(2)]
            B["bon"] = [sb([128, 256], F32, "f_bon%d" % i) for i in range(3)]
            for nm in ("sq", "gsq"):
                B[nm] = sb([128, 256], BF16, "b_" + nm)
            for nm in ("rt", "at", "bt", "kt", "bh", "kh", "vb"):
                B[nm] = [sb([128, 256], BF16, "b_%s%d" % (nm, i)) for i in range(2)]
            B["ones64"] = sb([128, 64], F32, "ones64")
            kb.op("dve", lambda: V.memset(B["ones64"][:], 1.0), writes=[B["ones64"]])
            for nm, shp_ in (("P", [128, 4, 64]), ("M5", [128, 4, 5, 64]), ("T3", [128, 4, 3, 64])):
                t_ = sb(shp_, BF16, "h_" + nm)
                B[nm] = [[TT(t_.h, "%s_%d_%d" % (nm, hp_, ci_)) for ci_ in range(4)] for hp_ in range(2)]
            for nm in ("Y", "U", "Sb"):
                t_ = sb([128, 64], BF16, "m_" + nm)
                B[nm] = [t_, TT(t_.h, t_.name + "_b")]
            B["WK"] = []
            for st_ in range(4):
                row = []
                for nm, shp_ in (("LL0", [128, 2, 64]), ("LL1", [128, 2, 64]), ("P0", [128, 64]), ("P1", [128, 64])):
                    t_ = sb(shp_, BF16, "wk%d_%s" % (st_, nm))
                    row.append([t_, TT(t_.h, t_.name + "_b")])
                B["WK"].append(row)
            B["mk5"] = sb([128, 5, 64], F32, "mk5")
            for half_ in range(2):
                for m_, src_ in enumerate((0, 1, 0, 2, 2)):
                    kb.dma("sp", B["mk5"][half_ * 64:(half_ + 1) * 64, m_, :], I["masks"][src_], writes=[B["mk5"]])
            B["wrk"] = wbuf
            B["Sin"] = B["of"]
            B["Sout"] = B["kk"]
            B["shf"] = sb([128, KC, NSQ], F32, "shf")
            B["wri"] = [0]

        def rwkv_mix_inputs(j, col0, T, first_cols):
            B = BT_
            for c in range(6):
                for kc in range(KC):
                    kb.op("dve", lambda c=c, kc=kc: V.scalar_tensor_tensor(out=B["xm"][c][:, kc, 0:T], in0=B["dlt"][:, kc, 0:T],
                                                                            scalar=B["mu"][:, c, kc:kc + 1], in1=hT[:, kc, col0:col0 + T],
                                                                            op0=ALU.mult, op1=ALU.add),
                          reads=[B["dlt"], B["mu"], hT], writes=[B["xm"][c]])
                if c >= 4:
                    cc_ = c - 4
                    pl = psA()
                    kb.mm(pl, [(pl[0:96, 0:T], B["wd"][:, cc_, kc, :], B["xm"][c][:, kc, 0:T]) for kc in range(KC)],
                          reads=[B["wd"], B["xm"][c]])
                    kb.op("act", lambda cc_=cc_, pl=pl: S.activation(out=B["lw"][:, cc_, 0:T], in_=pl[0:96, 0:T],
                                                                    func=(AF.Tanh if cc_ == 0 else AF.Copy)), reads=[pl], writes=[B["lw"]])

        def rwkv_s1(j, pr, T, C, og_cols):
            B = BT_
            bs = pr % 2
            pvp = B["pv"]
            w4 = I["w_rkvg_b"][j]
            ps_in = {}
            for c in range(4):
                wt = next_w()
                wv = wt.h[:, 0:KC * 128].rearrange("p (k n) -> p k n", k=KC)
                kb.dma("pool", wv, w4[c].rearrange("(k p) n -> p k n", p=128)[:, :, pr * 128:(pr + 1) * 128], writes=[wt])
                yield
                pp_ = psA()
                kb.mm(pp_, [(pp_[:, 0:T], wv[:, kc, :], B["xm"][c][:, kc, 0:T]) for kc in range(KC)], reads=[wt, B["xm"][c]])
                if c == 0:
                    kb.op("act", lambda pp_=pp_: S.copy(out=B["r"][:, 0:T], in_=pp_[:, 0:T]), reads=[pp_], writes=[B["r"]])
                    yield
                elif c == 1:
                    kb.op("act", lambda pp_=pp_: S.copy(out=B["k"][:, 0:T], in_=pp_[:, 0:T]), reads=[pp_], writes=[B["k"]])
                    yield
                elif c == 2:
                    kb.op("act", lambda pp_=pp_: S.copy(out=B["vb"][bs][:, 0:T], in_=pp_[:, 0:T]), reads=[pp_], writes=[B["vb"][bs]])
                    yield
                else:
                    kb.op("act", lambda pp_=pp_: S.activation(out=og[:, pr, og_cols], in_=pp_[:, 0:T], func=AF.Silu),
                          reads=[pp_], writes=[og])
                    yield
            chk("bproj")
            r, k, kk, k2, a, sgw, cs, t1, t2, Ep, E0 = (B[n] for n in ("r", "k", "kk", "k2", "a", "sgw", "cs", "t1", "t2", "Ep", "E0"))
            Em, bon = B["Em"][bs], B["bon"][pr % 3]
            wu = B["wu"][pr % 2]
            kb.dma("pool", wu[:], I["w_lora_up_b"][j][:, :, pr * 128:(pr + 1) * 128].rearrange("c r n -> r c n"), writes=[wu])
            yield
            for c, dst, wi in ((0, sgw, 0), (1, a, 1)):
                pl = psA()
                kb.mm(pl, [(pl[:, 0:T], wu[:, c, :], B["lw"][:, c, 0:T])], reads=[wu, B["lw"]])
                kb.op("act", lambda pl=pl, dst=dst, wi=wi: S.activation(out=dst[:, 0:T], in_=pl[:, 0:T], func=AF.Sigmoid,
                                                                       bias=pvp[:, wi, pr:pr + 1]), reads=[pl, pvp], writes=[dst])
                yield
            kb.op("dve", lambda: V.tensor_scalar(out=t1[:, 0:T], in0=k[:, 0:T], scalar1=pvp[:, 2, pr:pr + 1], scalar2=None, op0=ALU.mult),
                  reads=[k, pvp], writes=[t1])
            yield
            kb.op("act", lambda: S.activation(out=B["sq"][:, 0:T], in_=t1[:, 0:T], func=AF.Square), reads=[t1], writes=[B["sq"]])
            yield
            pn = psA()
            kb.mm(pn, [(pn[:, 0:T], bones_b[:, :], B["sq"][:, 0:T])], reads=[bones_b, B["sq"]])
            kb.op("act", lambda: S.activation(out=t2[:, 0:T], in_=pn[:, 0:T], func=AF.Sqrt), reads=[pn], writes=[t2])
            yield
            kb.op("dve", lambda: V.tensor_scalar(out=t2[:, 0:T], in0=t2[:, 0:T], scalar1=1e-12, scalar2=None, op0=ALU.max),
                  reads=[t2], writes=[t2])
            yield
            kb.op("dve", lambda: V.reciprocal(out=t2[:, 0:T], in_=t2[:, 0:T]), reads=[t2], writes=[t2])
            yield
            kb.op("dve", lambda: V.tensor_tensor(out=kk[:, 0:T], in0=t1[:, 0:T], in1=t2[:, 0:T], op=ALU.mult), reads=[t1, t2], writes=[kk])
            yield
            kb.op("dve", lambda: V.tensor_scalar(out=t1[:, 0:T], in0=a[:, 0:T], scalar1=-1.0, scalar2=pvp[:, 3, pr:pr + 1],
                                                 op0=ALU.add, op1=ALU.mult), reads=[a, pvp], writes=[t1])
            yield
            kb.op("dve", lambda: V.scalar_tensor_tensor(out=k2[:, 0:T], in0=t1[:, 0:T], scalar=1.0, in1=k[:, 0:T], op0=ALU.add, op1=ALU.mult),
                  reads=[t1, k], writes=[k2])
            yield
            kb.op("dve", lambda: V.scalar_tensor_tensor(out=B["sq"][:, 0:T], in0=r[:, 0:T], scalar=pvp[:, 4, pr:pr + 1], in1=k2[:, 0:T],
                                                        op0=ALU.mult, op1=ALU.mult), reads=[r, k2, pvp], writes=[B["sq"]])
            yield
            pb = psA()
            kb.mm(pb, [(pb[:, 0:T], bones_b[:, :], B["sq"][:, 0:T])], reads=[bones_b, B["sq"]])
            kb.op("dve", lambda: V.tensor_tensor(out=bon[:, 0:T], in0=pb[:, 0:T], in1=B["vb"][bs][:, 0:T], op=ALU.mult), reads=[pb, B["vb"][bs]], writes=[bon])
            yield
            nch = T // C
            for ci in range(nch):
                kb.op("dve", lambda ci=ci: V.tensor_tensor_scan(out=cs[:, ci * C:(ci + 1) * C], data0=B["ones64"][:, 0:C],
                                                                 data1=sgw[:, ci * C:(ci + 1) * C], initial=0.0, op0=ALU.mult, op1=ALU.add),
                      reads=[sgw, B["ones64"]], writes=[cs])
                yield
            kb.op("act", lambda: S.activation(out=Em[:, 0:T], in_=cs[:, 0:T], func=AF.Exp, scale=-KAP), reads=[cs], writes=[Em])
            yield
            kb.op("act", lambda: S.activation(out=Ep[:, 0:T], in_=cs[:, 0:T], func=AF.Exp, scale=KAP), reads=[cs], writes=[Ep])
            yield
            kb.op("dve", lambda: V.tensor_tensor(out=t1[:, 0:T], in0=cs[:, 0:T], in1=sgw[:, 0:T], op=ALU.subtract), reads=[cs, sgw], writes=[t1])
            yield
            kb.op("act", lambda: S.activation(out=E0[:, 0:T], in_=t1[:, 0:T], func=AF.Exp, scale=-KAP), reads=[t1], writes=[E0])
            yield
            kb.op("dve", lambda: V.tensor_tensor(out=B["rt"][bs][:, 0:T], in0=r[:, 0:T], in1=Em[:, 0:T], op=ALU.mult), reads=[r, Em], writes=[B["rt"][bs]])
            yield
            kb.op("dve", lambda: V.scalar_tensor_tensor(out=B["at"][bs][:, 0:T], in0=kk[:, 0:T], scalar=-1.0, in1=E0[:, 0:T], op0=ALU.mult, op1=ALU.mult),
                  reads=[kk, E0], writes=[B["at"][bs]])
            yield
            kb.op("dve", lambda: V.tensor_tensor(out=t1[:, 0:T], in0=kk[:, 0:T], in1=a[:, 0:T], op=ALU.mult), reads=[kk, a], writes=[t1])
            yield
            kb.op("dve", lambda: V.tensor_tensor(out=B["bt"][bs][:, 0:T], in0=t1[:, 0:T], in1=Ep[:, 0:T], op=ALU.mult), reads=[t1, Ep], writes=[B["bt"][bs]])
            yield
            kb.op("dve", lambda: V.tensor_tensor(out=B["kt"][bs][:, 0:T], in0=k2[:, 0:T], in1=Ep[:, 0:T], op=ALU.mult), reads=[k2, Ep], writes=[B["kt"][bs]])
            yield
            for ci in range(nch):
                cc = slice(ci * C, (ci + 1) * C)
                wc = Em[:, (ci + 1) * C - 1:(ci + 1) * C]
                kb.op("dve", lambda cc=cc, wc=wc: V.tensor_scalar(out=B["bh"][bs][:, cc], in0=B["bt"][bs][:, cc], scalar1=wc, scalar2=None, op0=ALU.mult),
                      reads=[B["bt"][bs], Em], writes=[B["bh"][bs]])
                yield
                kb.op("dve", lambda cc=cc, wc=wc: V.tensor_scalar(out=B["kh"][bs][:, cc], in0=B["kt"][bs][:, cc], scalar1=wc, scalar2=None, op0=ALU.mult),
                      reads=[B["kt"][bs], Em], writes=[B["kh"][bs]])
                yield
        def rwkv_s2(j, pr, T, C, og_cols, ST, st_ap):
            B = BT_
            bs = pr % 2
            pvp = B["pv"]
            nch = T // C
            Em, t1 = B["Em"][bs], B["gt1"]
            chk("bprep")
            nlev = int(math.log2(C))
            mk = B["mk"]

            def run_rr(gens):
                gens = list(gens)
                while gens:
                    for g in list(gens):
                        try:
                            next(g)
                        except StopIteration:
                            gens.remove(g)
                    yield

            def phaseA(ci, hp):
                cc = slice(ci * C, (ci + 1) * C)
                P = slice(hp * 64, (hp + 1) * 64)
                TBs = slice(hp * 64, hp * 64 + C)
                at_, bt_, kt_, rt_ = B["at"][bs][P, cc], B["bt"][bs][P, cc], B["kt"][bs][P, cc], B["rt"][bs][P, cc]
                M5, T3 = B["M5"][hp][ci], B["T3"][hp][ci]
                wk = B["WK"][ci]
                LL = [wk[0][hp], wk[1][hp]]
                Pp = [wk[2][hp], wk[3][hp]]
                pm = psA()
                kb.mm_multi(pm, [[(pm[TBs, 0:C], bt_, at_)], [(pm[TBs, 64:64 + C], at_, bt_)], [(pm[TBs, 128:128 + C], kt_, at_)],
                                 [(pm[TBs, 192:192 + C], bt_, rt_)], [(pm[TBs, 256:256 + C], kt_, rt_)]],
                            reads=[B["bt"][bs], B["at"][bs], B["kt"][bs], B["rt"][bs]])
                kb.op("dve", lambda: V.tensor_tensor(out=M5[TBs, ci, :, 0:C], in0=pm[TBs, 0:320].rearrange("p (m c) -> p m c", m=5)[:, :, 0:C],
                                                     in1=B["mk5"][TBs, :, 0:C], op=ALU.mult), reads=[pm, B["mk5"]], writes=[M5])
                yield
                pt = psA()
                kb.mm_multi(pt, [[(pt[TBs, m_ * 64:(m_ + 1) * 64], B[nm_][bs][P, cc], ident_b[P, P])] for m_, nm_ in enumerate(("vb", "bh", "kh"))],
                            reads=[B["vb"][bs], B["bh"][bs], B["kh"][bs], ident_b])
                kb.op("act", lambda: S.copy(out=T3[TBs, ci, :, :], in_=pt[TBs, 0:192].rearrange("p (m c) -> p m c", m=3)), reads=[pt], writes=[T3])
                yield
                kb.op("dve", lambda: V.tensor_tensor(out=Pp[0][TBs, 0:C], in0=M5[TBs, ci, 0, 0:C], in1=ident_b[TBs, TBs], op=ALU.add),
                      reads=[M5, ident_b], writes=[Pp[0]])
                yield
                Lt, Lap, LTap = M5, M5[TBs, ci, 1, 0:C], M5[TBs, ci, 0, 0:C]
                cur = 0
                for lev in range(1, nlev):
                    nx = 1 - cur
                    last = lev == nlev - 1
                    p2 = psA()
                    grp = [[(p2[TBs, 0:C], LTap, Lap)]]
                    if not last:
                        grp.append([(p2[TBs, 64:64 + C], Lap, LTap)])
                    kb.mm_multi(p2, grp, reads=[Lt])
                    nq = 1 if last else 2
                    kb.op("act", lambda p2=p2, nx=nx, nq=nq: S.copy(out=LL[nx][TBs, 0:nq, 0:C],
                                                                   in_=p2[TBs, 0:128].rearrange("p (m c) -> p m c", m=2)[:, 0:nq, 0:C]),
                          reads=[p2], writes=[LL[nx]])
                    yield
                    Lt, Lap, LTap = LL[nx], LL[nx][TBs, 0, 0:C], LL[nx][TBs, 1, 0:C]
                    p3 = psA()
                    kb.mm(p3, [(p3[TBs, 0:C], ident_b[TBs, TBs], Pp[cur][TBs, 0:C]), (p3[TBs, 0:C], Lap, Pp[cur][TBs, 0:C])],
                          reads=[ident_b, Pp[cur], Lt])
                    if last:
                        kb.op("dve", lambda p3=p3: V.tensor_copy(out=B["P"][hp][ci][TBs, ci, 0:C], in_=p3[TBs, 0:C]), reads=[p3], writes=[B["P"][hp][ci]])
                    else:
                        kb.op("dve", lambda p3=p3, nx=nx: V.tensor_copy(out=Pp[nx][TBs, 0:C], in_=p3[TBs, 0:C]), reads=[p3], writes=[Pp[nx]])
                    yield
                    cur = nx

            def phaseB(ci, hp):
                cc = slice(ci * C, (ci + 1) * C)
                P = slice(hp * 64, (hp + 1) * 64)
                TBs = slice(hp * 64, hp * 64 + C)
                at_, rt_ = B["at"][bs][P, cc], B["rt"][bs][P, cc]
                Sb, Y, U = B["Sb"][hp], B["Y"][hp], B["U"][hp]
                po = PSX[hp]
                kb.op("act", lambda: S.copy(out=Sb[P, :], in_=st_ap(ci, P)), reads=[ST[hp]], writes=[Sb])
                yield
                py = psA()
                kb.mm(py, [(py[TBs, 0:64], at_, Sb[P, :]), (py[TBs, 0:64], B["M5"][hp][ci][TBs, ci, 2, 0:C], B["T3"][hp][ci][TBs, ci, 0, :])],
                      reads=[B["at"][bs], Sb, B["M5"][hp][ci], B["T3"][hp][ci]])
                kb.op("dve", lambda: V.tensor_copy(out=Y[TBs, 0:64], in_=py[TBs, 0:64]), reads=[py], writes=[Y])
                yield
                pu = psA()
                kb.mm(pu, [(pu[TBs, 0:64], B["P"][hp][ci][TBs, ci, 0:C], Y[TBs, 0:64])], reads=[B["P"][hp][ci], Y])
                kb.op("dve", lambda: V.tensor_copy(out=U[TBs, 0:64], in_=pu[TBs, 0:64]), reads=[pu], writes=[U])
                yield
                kb.mm(po, [(po[P, cc], Sb[P, :], rt_), (po[P, cc], U[TBs, 0:64], B["M5"][hp][ci][TBs, ci, 3, 0:C]),
                           (po[P, cc], B["T3"][hp][ci][TBs, ci, 0, :], B["M5"][hp][ci][TBs, ci, 4, 0:C])],
                      reads=[Sb, B["rt"][bs], U, B["M5"][hp][ci], B["T3"][hp][ci], B["M5"][hp][ci]])
                pd_ = psA()
                kb.mm(pd_, [(pd_[P, 0:64], B["T3"][hp][ci][TBs, ci, 1, :], U[TBs, 0:64]), (pd_[P, 0:64], B["T3"][hp][ci][TBs, ci, 2, :], B["T3"][hp][ci][TBs, ci, 0, :])],
                      reads=[B["T3"][hp][ci], U, B["T3"][hp][ci], B["T3"][hp][ci]])
                wc = Em[P, (ci + 1) * C - 1:(ci + 1) * C]
                kb.op("dve", lambda: V.scalar_tensor_tensor(out=st_ap(ci, P), in0=st_ap(ci, P), scalar=wc, in1=pd_[P, 0:64],
                                                            op0=ALU.mult, op1=ALU.add), reads=[ST[hp], Em, pd_], writes=[ST[hp]])
                yield

            def chainB(hp, cis):
                for ci_ in cis:
                    yield from phaseB(ci_, hp)

            if nch == 4:
                yield from run_rr([phaseA(ci, hp) for ci in (0, 1) for hp in range(2)])
                yield from run_rr([chainB(hp, (0, 1)) for hp in range(2)] + [phaseA(ci, hp) for ci in (2, 3) for hp in range(2)])
                yield from run_rr([chainB(hp, (2, 3)) for hp in range(2)])
            else:
                yield from run_rr([phaseA(ci, hp) for ci in range(nch) for hp in range(2)])
                for ci in range(nch):
                    yield from run_rr([phaseB(ci, hp) for hp in range(2)])
            chk("bchunks")
            of = B["of"]
            for hp in range(2):
                P = slice(hp * 64, (hp + 1) * 64)
                kb.op("act", lambda hp=hp, P=P: S.copy(out=B["gsq"][P, 0:T], in_=PSX[hp][P, 0:T]), reads=[PSX[hp]], writes=[B["gsq"]])
                yield
                kb.op("act", lambda hp=hp, P=P: S.copy(out=of[P, 0:T], in_=PSX[hp][P, 0:T]), reads=[PSX[hp]], writes=[of])
                yield

        def rwkv_gn(j, pr, T, og_cols):
            B = BT_
            pvp = B["pv"]
            of, t1, bon = B["of"], B["gt1"], B["bon"][pr % 3]
            pm_ = psA()
            kb.mm(pm_, [(pm_[:, 0:T], bones_b[:, :], B["gsq"][:, 0:T])], reads=[bones_b, B["gsq"]])
            kb.op("dve", lambda: V.scalar_tensor_tensor(out=of[:, 0:T], in0=pm_[:, 0:T], scalar=-1.0 / 64, in1=of[:, 0:T], op0=ALU.mult, op1=ALU.add),
                  reads=[pm_, of], writes=[of])
            yield
            kb.op("act", lambda: S.activation(out=B["gsq"][:, 0:T], in_=of[:, 0:T], func=AF.Square), reads=[of], writes=[B["gsq"]])
            yield
            pv_ = psA()
            kb.mm(pv_, [(pv_[:, 0:T], bones_b[:, :], B["gsq"][:, 0:T])], reads=[bones_b, B["gsq"]])
            kb.op("dve", lambda: V.tensor_scalar(out=t1[:, 0:T], in0=pv_[:, 0:T], scalar1=1.0 / 64, scalar2=64e-5, op0=ALU.mult, op1=ALU.add),
                  reads=[pv_], writes=[t1])
            yield
            kb.op("act", lambda: S.activation(out=t1[:, 0:T], in_=t1[:, 0:T], func=AF.Sqrt), reads=[t1], writes=[t1])
            yield
            kb.op("dve", lambda: V.reciprocal(out=t1[:, 0:T], in_=t1[:, 0:T]), reads=[t1], writes=[t1])
            yield
            kb.op("dve", lambda: V.tensor_tensor(out=of[:, 0:T], in0=of[:, 0:T], in1=t1[:, 0:T], op=ALU.mult), reads=[of, t1], writes=[of])
            yield
            kb.op("dve", lambda: V.tensor_scalar(out=of[:, 0:T], in0=of[:, 0:T], scalar1=pvp[:, 5, pr:pr + 1], scalar2=pvp[:, 6, pr:pr + 1],
                                                 op0=ALU.mult, op1=ALU.add), reads=[of, pvp], writes=[of])
            yield
            kb.op("dve", lambda: V.tensor_tensor(out=of[:, 0:T], in0=of[:, 0:T], in1=bon[:, 0:T], op=ALU.add), reads=[of, bon], writes=[of])
            yield
            kb.op("dve", lambda: V.tensor_tensor(out=og[:, pr, og_cols], in0=of[:, 0:T], in1=og[:, pr, og_cols], op=ALU.mult), reads=[of, og], writes=[og])
            yield


        def _exhaust(g):
            for _ in g:
                pass

        def _interleave(*gens):
            gens = [g for g in gens if g is not None]
            while gens:
                for g in list(gens):
                    try:
                        next(g)
                    except StopIteration:
                        gens.remove(g)

        def rwkv_layer_tile(l, j, ntok, ti):
            B = BT_
            sample = ntok < 128
            dlt = B["dlt"]
            if not sample:
                for half in range(2):
                    c0 = half * 256
                    kb.op("dve", lambda c0=c0: V.tensor_tensor(out=dlt[:, :, 1:256], in0=hT[:, :, c0:c0 + 255], in1=hT[:, :, c0 + 1:c0 + 256],
                                                               op=ALU.subtract), reads=[hT], writes=[dlt])
                    if half == 0:
                        kb.op("dve", lambda: V.tensor_tensor(out=dlt[:, :, 0], in0=B["hlast"][:, :], in1=hT[:, :, 0], op=ALU.subtract),
                              reads=[hT, B["hlast"]], writes=[dlt])
                    else:
                        kb.op("dve", lambda: V.tensor_tensor(out=dlt[:, :, 0], in0=hT[:, :, 255], in1=hT[:, :, 256], op=ALU.subtract),
                              reads=[hT], writes=[dlt])
                    rwkv_mix_inputs(j, c0, 256, None)
                    chk("bmix")
                    prev = None
                    gnp = None
                    cols_ = slice(half * 256, (half + 1) * 256)
                    for pr in range(32):
                        g1 = rwkv_s1(j, pr, 256, 64, cols_)
                        _interleave(prev, g1, gnp)
                        gnp = rwkv_gn(j, pr - 1, 256, cols_) if pr >= 1 else None
                        prev = rwkv_s2(j, pr, 256, 64, cols_, B["ST2"], lambda ci, P, pr=pr: B["ST"][P, pr, :])
                    _interleave(prev, gnp)
                    _exhaust(rwkv_gn(j, 31, 256, cols_))
                kb.op("dve", lambda: V.tensor_copy(out=B["hlast"][:, :], in_=hT[:, :, 511]), reads=[hT], writes=[B["hlast"]])
            else:
                shf = B["t1"]
                kb.dma("sp", dlt[:, :, 256:256 + NSQ], I["state_shift"][j].rearrange("s (k p) -> p k s", p=128), writes=[dlt],
                       allow_slow_non_contiguous=True) if False else None
                stg = B["t2"]
                for sq in range(NSQ):
                    kb.dma("sp", stg[:, sq * KC:(sq + 1) * KC], I["state_shift"][j, sq].rearrange("(k p) -> p k", p=128),
                           writes=[stg], allow_slow_non_contiguous=True)
                stg3 = stg.h[:, 0:KC * NSQ].rearrange("p (s k) -> p k s", k=KC)
                h4 = hT.h[:, :, 0:NST].rearrange("p k (s t) -> p k s t", t=8)
                d4 = dlt.h[:, :, 0:NST].rearrange("p k (s t) -> p k s t", t=8)
                kb.op("dve", lambda: V.tensor_tensor(out=d4[:, :, :, 1:8], in0=h4[:, :, :, 0:7], in1=h4[:, :, :, 1:8], op=ALU.subtract),
                      reads=[hT], writes=[dlt])
                kb.op("dve", lambda: V.tensor_tensor(out=d4[:, :, :, 0], in0=stg3, in1=h4[:, :, :, 0], op=ALU.subtract),
                      reads=[hT, stg], writes=[dlt])
                rwkv_mix_inputs(j, 0, NST, None)
                Sin, STp, Sout = B["Sin"], B["STs"], B["Sout"]
                Sin_v = Sin.h[:, 0:NSQ * 64].rearrange("p (s i) -> p s i", s=NSQ)
                Sout_v = Sout.h[:, 0:NSQ * 64].rearrange("p (s i) -> p s i", s=NSQ)
                for pr in range(32):
                    kb.dma("sp", Sin_v, I["state_wkv"][j, :, 2 * pr:2 * pr + 2].rearrange("s h i j -> (h i) s j"), writes=[Sin])
                    for hp in range(2):
                        pt = psA()
                        kb.mm_multi(pt, [[(pt[hp * 64:(hp + 1) * 64, sq * 64:(sq + 1) * 64], Sin_v[hp * 64:(hp + 1) * 64, sq, :],
                                           ident_f[hp * 64:(hp + 1) * 64, hp * 64:(hp + 1) * 64])] for sq in range(NSQ)],
                                    reads=[Sin, ident_f])
                        kb.op("dve", lambda pt=pt, hp=hp: V.tensor_copy(out=STp[hp * 64:(hp + 1) * 64, 0:NSQ, :],
                                                                        in_=pt[hp * 64:(hp + 1) * 64, 0:NSQ * 64].rearrange("p (s i) -> p s i", s=NSQ)),
                              reads=[pt], writes=[B["STs2"][hp]])
                    _exhaust(rwkv_s1(j, pr, NST, 8, slice(0, NST)))
                    _exhaust(rwkv_s2(j, pr, NST, 8, slice(0, NST), B["STs2"], lambda ci, P: STp[P, ci, :]))
                    _exhaust(rwkv_gn(j, pr, NST, slice(0, NST)))
                    for hp in range(2):
                        pt = psA()
                        kb.mm_multi(pt, [[(pt[hp * 64:(hp + 1) * 64, sq * 64:(sq + 1) * 64], STp[hp * 64:(hp + 1) * 64, sq, :],
                                           ident_f[hp * 64:(hp + 1) * 64, hp * 64:(hp + 1) * 64])] for sq in range(NSQ)],
                                    reads=[B["STs2"][hp], ident_f])
                        kb.op("dve", lambda pt=pt, hp=hp: V.tensor_copy(out=Sout_v[hp * 64:(hp + 1) * 64],
                                                                        in_=pt[hp * 64:(hp + 1) * 64, 0:NSQ * 64].rearrange("p (s i) -> p s i", s=NSQ)),
                              reads=[pt], writes=[Sout])
                    kb.dma("sp", O["wkvs"][j, :, 2 * pr:2 * pr + 2].rearrange("s h i j -> (h i) s j"), Sout_v, reads=[Sout], writes=[])
                    kws_toks.append(kb_last_tok("sp"))
                shf = B["shf"]
                kb.op("dve", lambda: V.tensor_copy(out=shf[:, :, 0:NSQ], in_=h4[:, :, :, 7]), reads=[hT], writes=[shf])
                for sq in range(NSQ):
                    kb.dma("sp", O["shs"][j, sq].rearrange("(k p) -> p k", p=128), shf[:, :, sq], reads=[shf], writes=[],
                           allow_slow_non_contiguous=True)
                    kws_toks.append(kb_last_tok("sp"))

        def rwkv_outputs_prompt(j):
            B = BT_
            ST, Sout, shf = B["ST"], B["Sout"], B["shf"]
            Sout_v = Sout.h[:, 0:256].rearrange("p (s i) -> p s i", s=4)
            for p4 in range(0, 32, 4):
                for hp in range(2):
                    pt = psA()
                    kb.mm_multi(pt, [[(pt[hp * 64:(hp + 1) * 64, q * 64:(q + 1) * 64], ST[hp * 64:(hp + 1) * 64, p4 + q, :],
                                       ident_f[hp * 64:(hp + 1) * 64, hp * 64:(hp + 1) * 64])] for q in range(4)],
                                reads=[B["ST2"][hp], ident_f])
                    kb.op("dve", lambda pt=pt, hp=hp: V.tensor_copy(out=Sout_v[hp * 64:(hp + 1) * 64],
                                                                    in_=pt[hp * 64:(hp + 1) * 64, 0:256].rearrange("p (s i) -> p s i", s=4)),
                          reads=[pt], writes=[Sout])
                kb.dma("sp", O["wkvp"][j, 2 * p4:2 * p4 + 8].rearrange("(q h) i j -> (h i) q j", h=2), Sout_v, reads=[Sout], writes=[])
                kws_toks.append(kb_last_tok("sp"))
            kb.op("dve", lambda: V.tensor_copy(out=shf[:, :, 0], in_=hT[:, :, 511]), reads=[hT], writes=[shf])
            kb.dma("sp", O["shp"][j].rearrange("(k p) -> p k", p=128), shf[:, :, 0], reads=[shf], writes=[],
                   allow_slow_non_contiguous=True)
            kws_toks.append(kb_last_tok("sp"))


        CT_ = {}

        def gla_alloc(les, j):
            def sb(shape, dt, name):
                return kb.sb(shape, dt, name, es=les)
            Cc = CT_
            Cc["S"] = sb([128, 16, 512], F32, "glaS")
            kb.op("dve", lambda: V.memset(Cc["S"][:], 0.0), writes=[Cc["S"]])
            Cc["Ss"] = sb([128, 2, 512], F32, "glaSs")
            Cc["Sb"] = sb([128, 2, 512], BF16, "glaSb")
            Cc["wgk"] = sb([16, 256], F32, "wgk")
            Cc["bgk"] = sb([128, 16], F32, "bgk")
            kb.dma("sp", Cc["bgk"][:], I["b_gk_c"][j].rearrange("(c p) -> p c", p=128), writes=[Cc["bgk"]], allow_slow_non_contiguous=True)
            Cc["og_g"] = sb([128, 4], F32, "og_g")
            kb.dma("sp", Cc["og_g"][:], I["o_norm_g_c"][j].rearrange("(c p) -> p c", p=128), writes=[Cc["og_g"]], allow_slow_non_contiguous=True)
            Cc["gkl"] = sb([16, 512], F32, "gkl")
            Cc["wgl"] = sb([128, KC, 16], BF16, "wgl")
            kb.dma("pool", Cc["wgl"][:], I["w_in_c"][j].rearrange("(k p) n -> p k n", p=128)[:, :, 12288:12304], writes=[Cc["wgl"]])
            Cc["mk"] = sb([64, 64], F32, "mkc")
            kb.dma("sp", Cc["mk"][:], I["masks"][2], writes=[Cc["mk"]])
            for nm in ("q", "k", "b", "e1", "e2"):
                Cc[nm] = sb([128, 2, 512], F32, "c_" + nm)
            for nm in ("qt", "kt", "kh"):
                Cc[nm] = sb([128, 2, 512], BF16, "cb_" + nm)
            Cc["of"] = sb([128, 4, 512], F32, "c_of")
            Cc["osq"] = sb([128, 4, 512], BF16, "c_osq")
            Cc["vt"] = sb([64, 8, 512], BF16, "c_vt")
            Cc["att"] = sb([64, 64], BF16, "c_att")
            Cc["kht"] = sb([64, 256], BF16, "c_kht")
            Cc["rs"] = sb([128, 512], F32, "c_rs")
            Cc["ones64"] = sb([128, 64], F32, "c_ones64")
            kb.op("dve", lambda: V.memset(Cc["ones64"][:], 1.0), writes=[Cc["ones64"]])

        def gla_head(j, h, ntok, C, state_fn, og_cols):
            Cc = CT_
            w_in = I["w_in_c"][j]
            nch = ntok // C
            q, k, b, e1, e2, qt, kt, khh, of, vt = (Cc[n] for n in ("q", "k", "b", "e1", "e2", "qt", "kt", "kh", "of", "vt"))
            for (dst, col0) in ((q, h * 256), (k, 2048 + h * 256)):
                wt, view = load_w_in(w_in, col0, 256)
                for dc in range(2):
                    pq = psA()
                    kb.mm(pq, [(pq[:, 0:ntok], view[:, kc, dc * 128:(dc + 1) * 128], hT[:, kc, 0:ntok]) for kc in range(KC)], reads=[wt, hT])
                    kb.op("act", lambda pq=pq, dst=dst, dc=dc: S.copy(out=dst[:, dc, 0:ntok], in_=pq[:, 0:ntok]), reads=[pq], writes=[dst])
            kb.dma("sp", Cc["wgk"][:], I["w_gk_up_c"][j][:, h * 256:(h + 1) * 256], writes=[Cc["wgk"]])
            for dc in range(2):
                pg = psA()
                cch = 2 * h + dc
                kb.mm(pg, [(pg[:, 0:ntok], Cc["wgk"][:, dc * 128:(dc + 1) * 128], Cc["gkl"][:, 0:ntok])], reads=[Cc["wgk"], Cc["gkl"]])
                kb.op("act", lambda pg=pg, dc=dc, cch=cch: S.activation(out=e1[:, dc, 0:ntok], in_=pg[:, 0:ntok], func=AF.Sigmoid,
                                                                       bias=Cc["bgk"][:, cch:cch + 1]), reads=[pg, Cc["bgk"]], writes=[e1])
            kb.op("act", lambda: S.activation(out=e1[:, :, 0:ntok], in_=e1[:, :, 0:ntok], func=AF.Ln), reads=[e1], writes=[e1])
            for dc in range(2):
                for ci in range(nch):
                    kb.op("dve", lambda dc=dc, ci=ci: V.tensor_tensor_scan(out=b[:, dc, ci * C:(ci + 1) * C], data0=Cc["ones64"][:, 0:C],
                                                                           data1=e1[:, dc, ci * C:(ci + 1) * C], initial=0.0,
                                                                           op0=ALU.mult, op1=ALU.add), reads=[e1, Cc["ones64"]], writes=[b])
            kb.op("act", lambda: S.activation(out=e1[:, :, 0:ntok], in_=b[:, :, 0:ntok], func=AF.Exp, scale=1.0 / 16), reads=[b], writes=[e1])
            kb.op("act", lambda: S.activation(out=e2[:, :, 0:ntok], in_=b[:, :, 0:ntok], func=AF.Exp, scale=-1.0 / 16), reads=[b], writes=[e2])
            kb.op("dve", lambda: V.scalar_tensor_tensor(out=qt[:, :, 0:ntok], in0=q[:, :, 0:ntok], scalar=1.0 / 16, in1=e1[:, :, 0:ntok],
                                                        op0=ALU.mult, op1=ALU.mult), reads=[q, e1], writes=[qt])
            kb.op("dve", lambda: V.tensor_tensor(out=kt[:, :, 0:ntok], in0=k[:, :, 0:ntok], in1=e2[:, :, 0:ntok], op=ALU.mult), reads=[k, e2], writes=[kt])
            for dc in range(2):
                for ci in range(nch):
                    kb.op("dve", lambda dc=dc, ci=ci: V.tensor_scalar(out=khh[:, dc, ci * C:(ci + 1) * C], in0=kt[:, dc, ci * C:(ci + 1) * C],
                                                                      scalar1=e1[:, dc, (ci + 1) * C - 1:(ci + 1) * C], scalar2=None, op0=ALU.mult),
                          reads=[kt, e1], writes=[khh])
            wt, view = load_w_in(w_in, 4096 + h * 512, 512)
            for ci in range(nch):
                pv = psA()
                kb.mm(pv, [(pv[0:C, :], hT[:, kc, ci * C:(ci + 1) * C], view[:, kc, :]) for kc in range(KC)], reads=[wt, hT])
                kb.op("act", lambda ci=ci, pv=pv: S.copy(out=vt[0:C, ci, :], in_=pv[0:C, :]), reads=[pv], writes=[vt])
            wt, view = load_w_in(w_in, 8192 + h * 512, 512)
            for ec in range(4):
                pg = psA()
                kb.mm(pg, [(pg[:, 0:ntok], view[:, kc, ec * 128:(ec + 1) * 128], hT[:, kc, 0:ntok]) for kc in range(KC)], reads=[wt, hT])
                kb.op("act", lambda ec=ec, pg=pg: S.activation(out=og[:, h * 4 + ec, og_cols], in_=pg[:, 0:ntok], func=AF.Silu),
                      reads=[pg], writes=[og])
            for ci in range(nch):
                cc = slice(ci * C, (ci + 1) * C)
                St, Sap = state_fn(ci, "load")
                kb.op("act", lambda Sap=Sap: S.copy(out=Cc["Sb"][:], in_=Sap), reads=[St], writes=[Cc["Sb"]])
                pa = psA()
                kb.mm(pa, [(pa[0:C, 0:C], kt[:, dc, cc], qt[:, dc, cc]) for dc in range(2)], reads=[kt, qt])
                kb.op("dve", lambda pa=pa: V.tensor_tensor(out=Cc["att"][0:C, 0:C], in0=pa[0:C, 0:C], in1=Cc["mk"][0:C, 0:C], op=ALU.mult),
                      reads=[pa, Cc["mk"]], writes=[Cc["att"]])
                po = psA()
                groups = []
                for ec in range(4):
                    o_ap = po[:, ec * C:(ec + 1) * C]
                    groups.append([(o_ap, Cc["Sb"][:, dc, ec * 128:(ec + 1) * 128], qt[:, dc, cc]) for dc in range(2)]
                                  + [(o_ap, vt[0:C, ci, ec * 128:(ec + 1) * 128], Cc["att"][0:C, 0:C])])
                kb.mm_multi(po, groups, reads=[Cc["Sb"], qt, vt, Cc["att"]])
                kb.op("act", lambda po=po, cc=cc: S.copy(out=of[:, :, cc], in_=po[:, 0:4 * C].rearrange("p (e t) -> p e t", e=4)),
                      reads=[po], writes=[of])
                pt = psB()
                kb.transpose(pt, [(pt[0:C, dc * 128:(dc + 1) * 128], khh[:, dc, cc], ident_b[:, :]) for dc in range(2)], reads=[khh, ident_b])
                kb.op("act", lambda pt=pt: S.copy(out=Cc["kht"][0:C, :], in_=pt[0:C, 0:256]), reads=[pt], writes=[Cc["kht"]])
                for dc in range(2):
                    ps_ = psA()
                    kb.mm(ps_, [(ps_[:, :], Cc["kht"][0:C, dc * 128:(dc + 1) * 128], vt[0:C, ci, :])], reads=[Cc["kht"], vt])
                    St, Sap = state_fn(ci, "store")
                    kb.op("dve", lambda ps_=ps_, Sap=Sap, dc=dc, ci=ci: V.scalar_tensor_tensor(
                        out=Sap[:, dc, :], in0=Sap[:, dc, :], scalar=e1[:, dc, (ci + 1) * C - 1:(ci + 1) * C], in1=ps_[:, :],
                        op0=ALU.mult, op1=ALU.add), reads=[St, e1, ps_], writes=[St])
                state_fn(ci, "done")
            osq, rs = Cc["osq"], Cc["rs"]
            kb.op("act", lambda: S.activation(out=osq[:, :, 0:ntok], in_=of[:, :, 0:ntok], func=AF.Square), reads=[of], writes=[osq])
            pn = psA()
            kb.mm(pn, [(pn[:, 0:ntok], ones_b[:, :], osq[:, ec, 0:ntok]) for ec in range(4)], reads=[ones_b, osq])
            kb.op("dve", lambda: V.tensor_scalar(out=rs[:, 0:ntok], in0=pn[:, 0:ntok], scalar1=1.0 / 512, scalar2=NORM_EPS, op0=ALU.mult, op1=ALU.add),
                  reads=[pn], writes=[rs])
            kb.op("act", lambda: S.activation(out=rs[:, 0:ntok], in_=rs[:, 0:ntok], func=AF.Sqrt), reads=[rs], writes=[rs])
            kb.op("dve", lambda: V.reciprocal(out=rs[:, 0:ntok], in_=rs[:, 0:ntok]), reads=[rs], writes=[rs])
            for ec in range(4):
                kb.op("dve", lambda ec=ec: V.scalar_tensor_tensor(out=of[:, ec, 0:ntok], in0=of[:, ec, 0:ntok], scalar=Cc["og_g"][:, ec:ec + 1],
                                                                   in1=rs[:, 0:ntok], op0=ALU.mult, op1=ALU.mult), reads=[of, Cc["og_g"], rs], writes=[of])
                kb.op("dve", lambda ec=ec: V.tensor_tensor(out=og[:, h * 4 + ec, og_cols], in0=of[:, ec, 0:ntok], in1=og[:, h * 4 + ec, og_cols],
                                                            op=ALU.mult), reads=[of, og], writes=[og])

        def gla_layer_tile(l, j, ntok, ti):
            Cc = CT_
            sample = ntok < 128
            pl = psA()
            kb.mm(pl, [(pl[0:16, 0:ntok], Cc["wgl"][:, kc, :], hT[:, kc, 0:ntok]) for kc in range(KC)], reads=[Cc["wgl"], hT])
            kb.op("act", lambda: S.copy(out=Cc["gkl"][:, 0:ntok], in_=pl[0:16, 0:ntok]), reads=[pl], writes=[Cc["gkl"]])
            for h in range(8):
                if not sample:
                    gla_head(j, h, ntok, 64, lambda ci, what, h=h: (Cc["S"], Cc["S"][:, 2 * h:2 * h + 2, :]), slice(0, ntok))
                else:
                    def sfn(ci, what, h=h):
                        if what == "load":
                            kb.dma("sp", Cc["Ss"][:], I["state_gla"][j, ci, h].rearrange("(c p) e -> p c e", p=128), writes=[Cc["Ss"]])
                        elif what == "done":
                            kb.dma("sp", O["glas"][j, ci, h].rearrange("(c p) e -> p c e", p=128), Cc["Ss"][:], reads=[Cc["Ss"]], writes=[])
                            kws_toks.append(kb_last_tok("sp"))
                        return (Cc["Ss"], Cc["Ss"][:, :, :])
                    gla_head(j, h, ntok, 8, sfn, slice(0, ntok))

        def gla_outputs_prompt(j):
            Cc = CT_
            for h in range(8):
                kb.dma("sp", O["glap"][j, h].rearrange("(c p) e -> p c e", p=128), Cc["S"][:, 2 * h:2 * h + 2, :], reads=[Cc["S"]], writes=[])
                kws_toks.append(kb_last_tok("sp"))

        try:
            cnt_kind = {0: 0, 1: 0, 2: 0}
            for l, kind in enumerate(layers):
                j = cnt_kind[kind]
                cnt_kind[kind] += 1
                with contextlib.ExitStack() as les:
                    layer_psum(kind, les)
                    if kind == 0:
                        attn_alloc(les)
                    elif kind == 1:
                        rwkv_alloc(les, j)
                    else:
                        gla_alloc(les, j)
                    for ti in range(NT + 1):
                        sample = ti == NT
                        ntok = NST if sample else 512
                        if sample:
                            src = I["xs"] if l == 0 else O["ys"]
                            src_t = [] if l == 0 else [ys_t]
                            dst, dst_t = O["ys"], ys_t
                        else:
                            src = (I["xp"] if l == 0 else O["yp"])[ti * 512:(ti + 1) * 512, :]
                            src_t = [] if l == 0 else [yp_t[ti]]
                            dst, dst_t = O["yp"][ti * 512:(ti + 1) * 512, :], yp_t[ti]
                        chk("alloc")
                        load_norm(l, src, ntok, src_t)
                        chk("norm")
                        if kind == 0:
                            if sample:
                                attn_sample_prep(j)
                            attn_layer_tile(l, j, ntok, ti, last=(ti == NT - 1))
                            chk("atile")
                            if ti == NT - 1:
                                attn_outputs_prompt(j)
                            if sample:
                                attn_outputs_sample(j)
                            out_proj(I["w_out_a"][j], ntok, dst, dst_t)
                        elif kind == 1:
                            rwkv_layer_tile(l, j, ntok, ti)
                            if ti == NT - 1:
                                rwkv_outputs_prompt(j)
                            out_proj(I["w_out_b"][j], ntok, dst, dst_t)
                        else:
                            gla_layer_tile(l, j, ntok, ti)
                            if ti == NT - 1:
                                gla_outputs_prompt(j)
                            out_proj(I["w_out_c"][j], ntok, dst, dst_t)
                    kb.barrier()
        except _Stop:
            pass
        kb.finish([t.w for t in out_regions] + kws_toks)
    return nc


_PROG = {}


def _get_prog():
    if "nc" not in _PROG:
        _PROG["nc"] = build_program(dict(TP=4096, layers=[0, 1, 2, 0]))
    return _PROG["nc"]


def kernel(**inputs):
    f32 = np.float32
    A = {k: np.ascontiguousarray(np.asarray(v, dtype=f32)) for k, v in inputs.items()}
    n = 8
    consts = host_consts()
    shared = dict(consts)
    for nm in ("norm_g", "rel_bias", "w_in_a", "q_norm_g", "k_norm_g", "sinks", "w_out_a", "mu_b", "w_rkvg_b", "w_lora_down_b",
               "w_lora_up_b", "w0_b", "a0_b", "k_k_b", "k_a_b", "ln_x_g_b", "ln_x_b_b", "w_out_b", "w_in_c", "w_gk_up_c",
               "b_gk_c", "o_norm_g_c", "w_out_c"):
        shared[nm] = A[nm]
    shared["r_k_b"] = A["r_k_b"].reshape(A["r_k_b"].shape[0], -1)
    zeros_p = np.zeros((4096, D), f32)
    in_maps = []
    for c in range(n):
        m = dict(shared)
        m["xp"] = A["x_prompt"][c] if c < 2 else zeros_p
        sl = slice(4 * c, 4 * c + 4)
        m["xs"] = A["x_sample"][sl].reshape(32, D)
        m["cache_k"] = np.ascontiguousarray(A["cache_k_win"][:, sl].reshape(2, 4, 128, 512))
        m["cache_v"] = np.ascontiguousarray(A["cache_v_win"][:, sl].reshape(2, 4, 128, 512))
        m["state_wkv"] = np.ascontiguousarray(A["state_wkv"][:, sl])
        m["state_shift"] = np.ascontiguousarray(A["state_shift"][:, sl])
        m["state_gla"] = np.ascontiguousarray(A["state_gla"][:, sl])
        in_maps.append(m)
    nc = _get_prog()
    res = run_bass_kernel_spmd(nc, in_maps, core_ids=list(range(n)))
    R = res.results

    def cat_s(key, shape_tail, lead=True):
        return np.concatenate([np.asarray(R[c][key], f32) for c in range(n)], axis=1)

    y_prompt = np.stack([np.asarray(R[c]["yp"], f32) for c in range(2)], axis=0)
    y_sample = np.concatenate([np.asarray(R[c]["ys"], f32).reshape(4, 8, D) for c in range(n)], axis=0)
    kwp = np.stack([np.asarray(R[c]["kwp"], f32) for c in range(2)], axis=1).reshape(2, 2, 128, 8, 64)
    vwp = np.stack([np.asarray(R[c]["vwp"], f32) for c in range(2)], axis=1).reshape(2, 2, 128, 8, 64)
    kws = cat_s("kws", None).reshape(2, 32, 128, 8, 64)
    vws = cat_s("vws", None).reshape(2, 32, 128, 8, 64)
    wkvp = np.stack([np.asarray(R[c]["wkvp"], f32) for c in range(2)], axis=1)
    shp = np.stack([np.asarray(R[c]["shp"], f32) for c in range(2)], axis=1)
    wkvs = cat_s("wkvs", None)
    shs = cat_s("shs", None)
    glap = np.stack([np.asarray(R[c]["glap"], f32) for c in range(2)], axis=1)
    glas = cat_s("glas", None)
    return (y_prompt, y_sample, kwp, vwp, kws, vws, wkvp, shp, wkvs, shs, glap, glas)
```

```python
import contextlib
import math
import numpy as np
import concourse.bass as bass
import concourse.mybir as mybir
from concourse.bass_utils import run_bass_kernel_spmd

F32 = mybir.dt.float32
BF16 = mybir.dt.bfloat16
ALU = mybir.AluOpType
AF = mybir.ActivationFunctionType
AX = mybir.AxisListType

D = 2048
DI = 4096
KC = D // 128
IC = DI // 128
WINDOW = 128
NEG = -30000.0
NORM_EPS = 1e-6


class _Stop(Exception):
    pass


class TT:
    def __init__(self, h, name=""):
        self.h = h
        self.name = name
        self.w = None
        self.r = {}
        self.psum = False
        self.pending = False

    def __getitem__(self, idx):
        return self.h[idx]


class AliasTT:
    def __init__(self, base, view):
        self.base = base
        self.h = view
        self.name = base.name + "_alias"
        self.psum = base.psum

    def __getitem__(self, idx):
        return self.h[idx]

    @property
    def w(self):
        return self.base.w

    @w.setter
    def w(self, v):
        self.base.w = v

    @property
    def r(self):
        return self.base.r

    @r.setter
    def r(self, v):
        self.base.r = v


class KB:
    NDS = 12
    LIMIT = 10 ** 9

    def __init__(self, nc, es):
        self.nc = nc
        self.es = es
        self.eng = {"pe": nc.tensor, "act": nc.scalar, "dve": nc.vector, "pool": nc.gpsimd, "sp": nc.sync}
        self.sets = [{}, {}]
        self.phase = {}
        self.cnt = {}
        self.seen = {}
        self.dj = {}
        self.ep = 0
        self.nbar = 0
        for si in range(2):
            for e in self.eng:
                self.sets[si][e] = es.enter_context(nc.semaphore("c%d_%s" % (si, e)))
            for q in ("sp", "pool"):
                for j in range(self.NDS):
                    self.sets[si][("d", q, j)] = es.enter_context(nc.semaphore("d%d_%s_%d" % (si, q, j)))
        for e in self.eng:
            self.phase[e] = es.enter_context(nc.semaphore("ph_" + e))
            self.cnt[e] = 0
            self.seen[e] = {}
        for q in ("sp", "pool"):
            self.dj[q] = 0
        self.nalloc = 0
        self.dead = False
        self.last_dma_tok = None

    def semh(self, k):
        return self.sets[self.ep % 2][k]

    def sb(self, shape, dt, name=None, es=None):
        self.nalloc += 1
        name = (name or "t") + "_%d" % self.nalloc
        return TT((es or self.es).enter_context(self.nc.sbuf_tensor(name, list(shape), dt)), name)

    def ps(self, shape, dt, name=None, es=None):
        self.nalloc += 1
        name = (name or "p") + "_%d" % self.nalloc
        t = TT((es or self.es).enter_context(self.nc.psum_tensor(name, list(shape), dt)), name)
        t.psum = True
        return t

    def barrier(self):
        if self.dead:
            return
        toks = {}
        for e in self.eng:
            if self.cnt[e] > 0:
                toks[e] = self.cnt[e]
        for q in ("sp", "pool"):
            for j in range(self.NDS):
                n = (self.dj[q] - j + self.NDS - 1) // self.NDS if self.dj[q] > j else 0
                if n > 0:
                    toks[("d", q, j)] = 16 * n
        for e in self.eng:
            for k, v in toks.items():
                if k == e:
                    continue
                self._wait(e, (k, v, self.ep))

    def epoch_barrier(self):
        self.barrier()
        nxt = (self.ep + 1) % 2
        self.nbar += 1
        for e in self.eng:
            self.eng[e].sem_clear(self.sets[nxt][e])
            if e in ("sp", "pool"):
                for j in range(self.NDS):
                    self.eng[e].sem_clear(self.sets[nxt][("d", e, j)])
            self.eng[e].sem_inc(self.phase[e], 1)
        for e in self.eng:
            for f in self.eng:
                if f != e:
                    self.eng[e].wait_ge(self.phase[f], self.nbar)
        self.ep += 1
        for e in self.eng:
            self.cnt[e] = 0
            self.seen[e] = {}
        for q in ("sp", "pool"):
            self.dj[q] = 0

    def maybe_epoch(self):
        if max(self.cnt.values()) >= self.LIMIT or max(self.dj.values()) >= self.LIMIT // 2:
            self.epoch_barrier()

    def _wait(self, E, tok):
        if self.dead or tok is None:
            return
        k, v, ep = tok
        if ep < self.ep:
            return
        if self.seen[E].get(k, 0) >= v:
            return
        self.eng[E].wait_ge(self.semh(k), v)
        self.seen[E][k] = v

    def _deps(self, E, reads, writes):
        need = {}

        def add(tok):
            if tok is None:
                return
            k, v, ep = tok
            if ep < self.ep:
                return
            if need.get(k, 0) < v:
                need[k] = v

        for t in reads:
            add(t.w)
            if t.psum:
                for rk, tok in t.r.items():
                    if rk != E:
                        add(tok)
        for t in writes:
            add(t.w)
            for tok in t.r.values():
                add(tok)
        for k, v in need.items():
            if k == "pe" and E == "pe":
                continue
            self._wait(E, (k, v, self.ep))

    def _done(self, E, ins, reads, writes):
        if E == "pe":
            for t in writes:
                t.pending = True
        else:
            for t in reads:
                if t.psum:
                    t.pending = False
        self.cnt[E] += 1
        ins.then_inc(self.semh(E), 1)
        tok = (E, self.cnt[E], self.ep)
        for t in writes:
            t.w = tok
            t.r = {}
        for t in reads:
            if t not in writes:
                t.r[E] = tok
        return tok

    def op(self, E, fn, reads=(), writes=()):
        if self.dead:
            return None
        self.maybe_epoch()
        self._deps(E, reads, writes)
        ins = fn()
        return self._done(E, ins, reads, writes)

    def mm(self, out_t, mms, reads):
        return self.mm_multi(out_t, [mms], reads)

    def mm_multi(self, out_t, groups, reads):
        if self.dead:
            return None
        self.maybe_epoch()
        self._deps("pe", reads, [out_t])
        ins = None
        for mms in groups:
            n = len(mms)
            for i, (o, l, r) in enumerate(mms):
                ins = self.nc.tensor.matmul(o, lhsT=l, rhs=r, start=(i == 0), stop=(i == n - 1))
        return self._done("pe", ins, reads, [out_t])

    def transpose(self, out_t, items, reads):
        if self.dead:
            return None
        self.maybe_epoch()
        self._deps("pe", reads, [out_t])
        ins = None
        for (o, i, idn) in items:
            ins = self.nc.tensor.transpose(o, i, idn)
        return self._done("pe", ins, reads, [out_t])

    def dma(self, q, out_ap, in_ap, reads=(), writes=(), **kw):
        if self.dead:
            return None
        self.maybe_epoch()
        self._deps(q, reads, writes)
        i = self.dj[q]
        self.dj[q] += 1
        j = i % self.NDS
        rnd = i // self.NDS
        key = ("d", q, j)
        if rnd > 0:
            self._wait(q, (key, 16 * rnd, self.ep))
        self.eng[q].dma_start(out=out_ap, in_=in_ap, **kw).then_inc(self.semh(key), 16)
        tok = (key, 16 * (rnd + 1), self.ep)
        for t in writes:
            t.w = tok
            t.r = {}
        for t in reads:
            t.r[key] = tok
        self.last_dma_tok = tok
        return tok

    def finish(self, toks):
        self.dead = False
        for tok in toks:
            if tok is not None:
                self._wait("sp", tok)


def t5_bucket_np(d):
    d = np.maximum(d, 0)
    large = 16 + (np.log(np.maximum(d, 1).astype(np.float32) / np.float32(16)) / np.float32(math.log(128 / 16))
                  * np.float32(16)).astype(np.int32)
    return np.where(d < 16, d, np.minimum(large, 31))


def host_consts():
    c = {}
    c["ident"] = np.eye(128, dtype=np.float32)
    bo = np.zeros((128, 128), np.float32)
    bo[:64, :64] = 1.0
    bo[64:, 64:] = 1.0
    c["blockones"] = bo
    oh = np.zeros((33, 384), np.float32)
    for m in range(384):
        dist = m - 128
        if 0 <= dist < 128:
            oh[int(t5_bucket_np(np.array(dist))), m] = 1.0
        else:
            oh[32, m] = NEG
    c["bucket_oh"] = oh
    t = np.arange(64)
    mk = np.zeros((3, 64, 64), np.float32)
    mk[0] = (t[:, None] < t[None, :])
    mk[1] = (t[None, :] < t[:, None])
    mk[2] = (t[:, None] <= t[None, :])
    c["masks"] = mk
    return c


def build_program(cfg):
    TP = cfg["TP"]
    layers = cfg["layers"]
    NSQ = 4
    NST = NSQ * 8
    nA = sum(1 for k in layers if k == 0)
    nB = sum(1 for k in layers if k == 1)
    nC = sum(1 for k in layers if k == 2)
    NL = len(layers)
    assert TP % 512 == 0
    NT = TP // 512

    nc = bass.Bass("TRN2", target_bir_lowering=False)

    def din(name, shape, dt=F32):
        return nc.dram_tensor(name, list(shape), dt, kind="ExternalInput").ap()

    def dout(name, shape, dt=F32):
        return nc.dram_tensor(name, list(shape), dt, kind="ExternalOutput").ap()

    I = {}
    I["xp"] = din("xp", [TP, D])
    I["xs"] = din("xs", [NST, D])
    I["norm_g"] = din("norm_g", [NL, D])
    I["ident"] = din("ident", [128, 128])
    I["blockones"] = din("blockones", [128, 128])
    I["bucket_oh"] = din("bucket_oh", [33, 384])
    if nA:
        A_IN = DI + 1024 + DI
        I["cache_k"] = din("cache_k", [nA, NSQ, 128, 512])
        I["cache_v"] = din("cache_v", [nA, NSQ, 128, 512])
        I["rel_bias"] = din("rel_bias", [32, 64])
        I["w_in_a"] = din("w_in_a", [nA, D, A_IN])
        I["q_norm_g"] = din("q_norm_g", [nA, 64])
        I["k_norm_g"] = din("k_norm_g", [nA, 64])
        I["sinks"] = din("sinks", [nA, 64])
        I["w_out_a"] = din("w_out_a", [nA, DI, D])
    if nB:
        I["state_wkv"] = din("state_wkv", [nB, NSQ, 64, 64, 64])
        I["state_shift"] = din("state_shift", [nB, NSQ, D])
        I["mu_b"] = din("mu_b", [nB, 6, D])
        I["w_rkvg_b"] = din("w_rkvg_b", [nB, 4, D, DI])
        I["w_lora_down_b"] = din("w_lora_down_b", [nB, 2, D, 96])
        I["w_lora_up_b"] = din("w_lora_up_b", [nB, 2, 96, DI])
        for nm in ("w0_b", "a0_b", "k_k_b", "k_a_b", "r_k_b", "ln_x_g_b", "ln_x_b_b"):
            I[nm] = din(nm, [nB, DI])
        I["w_out_b"] = din("w_out_b", [nB, DI, D])
        I["masks"] = din("masks", [3, 64, 64])
    if nC:
        C_IN = 2 * 2048 + 2 * DI + 16
        I["state_gla"] = din("state_gla", [nC, NSQ, 8, 256, 512])
        I["w_in_c"] = din("w_in_c", [nC, D, C_IN])
        I["w_gk_up_c"] = din("w_gk_up_c", [nC, 16, 2048])
        I["b_gk_c"] = din("b_gk_c", [nC, 2048])
        I["o_norm_g_c"] = din("o_norm_g_c", [nC, 512])
        I["w_out_c"] = din("w_out_c", [nC, DI, D])
        if "masks" not in I:
            I["masks"] = din("masks", [3, 64, 64])
    O = {}
    if nC:
        O["glap"] = dout("glap", [nC, 8, 256, 512])
        O["glas"] = dout("glas", [nC, NSQ, 8, 256, 512])
    if nB:
        O["wkvp"] = dout("wkvp", [nB, 64, 64, 64])
        O["shp"] = dout("shp", [nB, D])
        O["wkvs"] = dout("wkvs", [nB, NSQ, 64, 64, 64])
        O["shs"] = dout("shs", [nB, NSQ, D])
    O["yp"] = dout("yp", [TP, D])
    O["ys"] = dout("ys", [NST, D])
    if nA:
        O["kwp"] = dout("kwp", [nA, 128, 512])
        O["vwp"] = dout("vwp", [nA, 128, 512])
        O["kws"] = dout("kws", [nA, NSQ, 128, 512])
        O["vws"] = dout("vws", [nA, NSQ, 128, 512])
    zrep = nc.dram_tensor("zrep", [64, 128 * 384], F32, kind="Internal").ap() if nA else None

    es = contextlib.ExitStack()
    with es:
        kb = KB(nc, es)
        V = nc.vector
        S = nc.scalar
        G = nc.gpsimd
        out_regions = []

        def chk(stage):
            if cfg.get("stop") == stage:
                kb.dead = True

        yp_t = [TT(None, "yp%d" % t) for t in range(NT)]
        ys_t = TT(None, "ys")
        out_regions += yp_t + [ys_t]

        ident_f = kb.sb([128, 128], F32, "ident_f")
        ident_b = kb.sb([128, 128], BF16, "ident_b")
        bones_b = kb.sb([128, 128], BF16, "bones_b")
        ones_b = kb.sb([128, 128], BF16, "ones_b")
        gT = kb.sb([128, NL, KC], F32, "gT")
        kb.dma("sp", ident_f[:], I["ident"], writes=[ident_f])
        kb.dma("pool", ident_b[:], I["ident"], writes=[ident_b])
        kb.dma("pool", bones_b[:], I["blockones"], writes=[bones_b])
        bones_f = kb.sb([128, 128], F32, "bones_f")
        kb.dma("sp", bones_f[:], I["blockones"], writes=[bones_f])
        kb.op("dve", lambda: V.memset(ones_b[:], 1.0), writes=[ones_b])
        kb.dma("sp", gT[:], I["norm_g"].rearrange("l (kc p) -> p l kc", p=128), writes=[gT],
               allow_slow_non_contiguous=True)

        xt = kb.sb([128, 4, D], F32, "xt")
        hT = kb.sb([128, KC, 512], BF16, "hT")
        og = kb.sb([128, IC, 512], BF16, "og")
        wbuf = [kb.sb([128, 8192], BF16, "wbuf%d" % i) for i in range(2)]
        wsel = [0]
        stat = kb.sb([128, 16], F32, "stat")
        PSA = []
        PSB = []
        PSX = []
        psa_i = [0]
        psb_i = [0]

        def layer_psum(kind, les):
            na, nb_, nx = {0: (4, 2, 2), 1: (5, 1, 2), 2: (6, 2, 0)}[kind]
            PSA[:] = [kb.ps([128, 512], F32, "psA%d" % i, es=les) for i in range(na)]
            PSB[:] = [kb.ps([128, 1024], BF16, "psB%d" % i, es=les) for i in range(nb_)]
            PSX[:] = [kb.ps([128, 512], F32, "psX%d" % i, es=les) for i in range(nx)]

        def psA():
            psa_i[0] += 1
            t = PSA[psa_i[0] % len(PSA)]
            assert kb.dead or not t.pending, "PSUM tile handed out before its previous evacuation was emitted"
            return t

        def psB():
            psb_i[0] += 1
            t = PSB[psb_i[0] % len(PSB)]
            assert kb.dead or not t.pending, "PSUM tile handed out before its previous evacuation was emitted"
            return t

        def next_w():
            wsel[0] += 1
            return wbuf[wsel[0] % 2]

        def load_norm(l, src_ap, ntok, src_reads):
            nsub = (ntok + 127) // 128
            pp = min(ntok, 128)
            if ntok >= 128:
                kb.dma("sp", xt[:, 0:nsub, :], src_ap.rearrange("(s p) d -> p s d", p=128), reads=src_reads, writes=[xt])
            else:
                kb.dma("sp", xt[0:pp, 0, :], src_ap, reads=src_reads, writes=[xt])
            for s in range(nsub):
                kb.op("act", lambda s=s: S.activation(out=hT.h[0:pp, 0:4, :].rearrange("p a b -> p (a b)"), in_=xt[0:pp, s, :],
                                                     func=AF.Square, accum_out=stat[0:pp, s:s + 1]),
                      reads=[xt], writes=[hT, stat])
            kb.op("dve", lambda: V.tensor_scalar(out=stat[0:pp, 4:4 + nsub], in0=stat[0:pp, 0:nsub], scalar1=1.0 / D,
                                                 scalar2=NORM_EPS, op0=ALU.mult, op1=ALU.add), reads=[stat], writes=[stat])
            kb.op("act", lambda: S.activation(out=stat[0:pp, 8:8 + nsub], in_=stat[0:pp, 4:4 + nsub], func=AF.Sqrt),
                  reads=[stat], writes=[stat])
            kb.op("dve", lambda: V.reciprocal(out=stat[0:pp, 12:12 + nsub], in_=stat[0:pp, 8:8 + nsub]),
                  reads=[stat], writes=[stat])
            xn = og.h[:].rearrange("p a b -> p (a b)")
            for s in range(nsub):
                kb.op("dve", lambda s=s: V.tensor_scalar(out=xn[0:pp, s * D:(s + 1) * D], in0=xt[0:pp, s, :],
                                                         scalar1=stat[0:pp, 12 + s:13 + s], scalar2=None, op0=ALU.mult),
                      reads=[xt, stat], writes=[og])
            for kc in range(KC):
                pt = psB()
                kb.transpose(pt, [(pt[:, s * 128:s * 128 + pp], xn[0:pp, s * D + kc * 128:s * D + (kc + 1) * 128],
                                   ident_b[0:pp, 0:pp]) for s in range(nsub)], reads=[og, ident_b])
                kb.op("act", lambda kc=kc, pt=pt: S.activation(out=hT[:, kc, 0:ntok], in_=pt[:, 0:ntok], func=AF.Copy,
                                                                scale=gT[:, l, kc:kc + 1]),
                      reads=[pt, gT], writes=[hT])

        def load_w_in(w_ap, col0, ncols):
            wt = next_w()
            view = wt.h[:, 0:KC * ncols].rearrange("p (k n) -> p k n", k=KC)
            kb.dma("pool", view, w_ap.rearrange("(k p) n -> p k n", p=128)[:, :, col0:col0 + ncols], writes=[wt])
            return wt, view

        def out_proj(w_ap, ntok, dst_ap, dst_t):
            nsub = (ntok + 127) // 128
            pp = min(ntok, 128)
            for cb in range(D // 256):
                wt = next_w()
                view = wt.h[:, 0:IC * 256].rearrange("p (k n) -> p k n", k=IC)
                kb.dma("pool", view, w_ap.rearrange("(k p) n -> p k n", p=128)[:, :, cb * 256:(cb + 1) * 256], writes=[wt])
                for s in range(nsub):
                    pt = psA()
                    kb.mm(pt, [(pt[0:pp, 0:256], og[:, ic, s * 128:s * 128 + pp], view[:, ic, :]) for ic in range(IC)],
                          reads=[og, wt])
                    kb.op("dve", lambda s=s, pt=pt, cb=cb: V.tensor_tensor(out=xt[0:pp, s, cb * 256:(cb + 1) * 256],
                                                                            in0=pt[0:pp, 0:256],
                                                                            in1=xt[0:pp, s, cb * 256:(cb + 1) * 256], op=ALU.add),
                          reads=[pt, xt], writes=[xt])
            if ntok >= 128:
                kb.dma("sp", dst_ap.rearrange("(s p) d -> p s d", p=128), xt[:, 0:nsub, :], reads=[xt], writes=[dst_t])
            else:
                kb.dma("sp", dst_ap, xt[0:pp, 0, :], reads=[xt], writes=[dst_t])

        AT = {}

        def attn_alloc(les):
            def sb(shape, dt, name):
                return kb.sb(shape, dt, name, es=les)
            AT["BTp"] = sb([128, 64, 128], BF16, "BTp")
            AT["BTc"] = sb([128, 64, 128], BF16, "BTc")
            with contextlib.ExitStack() as zes:
                rb33 = kb.sb([33, 64], F32, "rb33", es=zes)
                oh33 = kb.sb([33, 384], F32, "oh33", es=zes)
                zsb = kb.sb([64, 384], F32, "zsb", es=zes)
                kb.op("dve", lambda: V.memset(rb33[:], 1.0), writes=[rb33])
                kb.dma("sp", rb33[0:32, :], I["rel_bias"], writes=[rb33])
                kb.dma("sp", oh33[:], I["bucket_oh"], writes=[oh33])
                pz = psA()
                kb.mm(pz, [(pz[0:64, 0:384], rb33[:, :], oh33[:, :])], reads=[rb33, oh33])
                kb.op("dve", lambda: V.tensor_copy(out=zsb[:], in_=pz[0:64, 0:384]), reads=[pz], writes=[zsb])
                zr3 = zrep.rearrange("h (r m) -> h r m", m=384)
                zrp = kb.sb([64, 16, 384], F32, "zrp", es=zes)
                kb.op("dve", lambda: V.tensor_copy(out=zrp[:], in_=zsb[:, :].unsqueeze(1).to_broadcast([64, 16, 384])),
                      reads=[zsb], writes=[zrp])
                ztoks = []
                for r0 in range(0, 128, 16):
                    kb.dma("sp", zr3[:, r0:r0 + 16, :], zrp[:], reads=[zrp], writes=[])
                    ztoks.append(kb.last_dma_tok)
                for tk in ztoks:
                    kb._wait("pool", tk)
                for (BT, off) in ((AT["BTc"], 128), (AT["BTp"], 256)):
                    src = bass.AP(tensor=zrep.tensor, offset=zrep.offset + off, ap=[[383, 128], [128 * 384, 64], [1, 128]])
                    kb.dma("pool", BT[:], src, reads=[zrep_t], writes=[BT])
                kb.barrier()
            chk("bt")
            AT["kT"] = sb([128, 8, 640], BF16, "kT")
            AT["vtk"] = sb([128, 5, 512], BF16, "vtk")
            AT["knf"] = sb([128, 8, 128], F32, "knf")
            AT["vlf"] = sb([128, 512], F32, "vlf")
            AT["qn"] = sb([128, 4, 512], BF16, "qn")
            AT["sg"] = sb([128, 4, 512], BF16, "sg")
            AT["sqb"] = sb([128, 512], BF16, "sqb")
            AT["rsd"] = sb([128, 512], F32, "rsd")
            AT["pT"] = [sb([128, 512], BF16, "pT%d" % i) for i in range(2)]
            AT["tmpf"] = [sb([128, 512], F32, "tmpf%d" % i) for i in range(2)]
            AT["den"] = sb([128, 512], F32, "den")
            gq = AT["gq"] = sb([128, nA], F32, "gq")
            gk = AT["gk"] = sb([128, nA], F32, "gk")
            esk = AT["esk"] = sb([128, nA, 32], F32, "esk")
            for par in range(2):
                kb.dma("sp", gq[par * 64:(par + 1) * 64, :], I["q_norm_g"].rearrange("l d -> d l"), writes=[gq],
                       allow_slow_non_contiguous=True)
                kb.dma("sp", gk[par * 64:(par + 1) * 64, :], I["k_norm_g"].rearrange("l d -> d l"), writes=[gk],
                       allow_slow_non_contiguous=True)
            kb.op("dve", lambda: V.memset(esk[:], 0.0), writes=[esk])
            for par in range(2):
                sk = I["sinks"].rearrange("l (c two) -> two l c", two=2)[par:par + 1]
                kb.dma("sp", esk[par * 64:par * 64 + 1, :, :], sk, writes=[esk], allow_slow_non_contiguous=True)
            pe_ = psA()
            kb.mm(pe_, [(pe_[:, 0:nA * 32], bones_f[:, :], esk[:].rearrange("p l c -> p (l c)"))], reads=[bones_f, esk])
            kb.op("act", lambda: S.activation(out=esk[:].rearrange("p l c -> p (l c)"), in_=pe_[:, 0:nA * 32], func=AF.Exp),
                  reads=[pe_], writes=[esk])
            kb.op("dve", lambda: V.tensor_scalar(out=gq[:], in0=gq[:], scalar1=0.125, scalar2=None, op0=ALU.mult),
                  reads=[gq], writes=[gq])
            chk("esk")
            AT["kcs"] = sb([128, 8, 128], BF16, "kcs")
            AT["kcs_f"] = AT["tmpf"][0]
            AT["vcs_f"] = AT["tmpf"][1]
            AT["vns"] = sb([8, NSQ, 512], BF16, "vns")
            AT["vnsf"] = sb([32, 512], F32, "vnsf")
            AT["kcT_bufs"] = [sb([128, 8, 128], BF16, "kcTb%d" % i) for i in range(NSQ)]
            AT["vcs_bufs"] = [sb([128, 512], BF16, "vcsb%d" % i) for i in range(NSQ)]
            AT["ktok"] = AT["den"]
            AT["pti"] = [0]
            AT["tfi"] = [0]

        class _A:
            def __getattr__(self, k):
                return AT[k]
        A_ = _A()

        def headnorm(ps_t, ntok, gcol, gt, out_ap, out_t, f32_out=None):
            sqb, rsd = AT["sqb"], AT["rsd"]
            kb.op("act", lambda: S.activation(out=sqb[:, 0:ntok], in_=ps_t[:, 0:ntok], func=AF.Square),
                  reads=[ps_t], writes=[sqb])
            p2 = psA()
            kb.mm(p2, [(p2[:, 0:ntok], bones_b[:, :], sqb[:, 0:ntok])], reads=[bones_b, sqb])
            kb.op("act", lambda: S.activation(out=rsd[:, 0:ntok], in_=p2[:, 0:ntok], func=AF.Sqrt, scale=1.0 / 64, bias=NORM_EPS),
                  reads=[p2], writes=[rsd])
            kb.op("dve", lambda: V.reciprocal(out=rsd[:, 0:ntok], in_=rsd[:, 0:ntok]), reads=[rsd], writes=[rsd])
            if f32_out is not None:
                fo_ap, fo_t, c0, c1 = f32_out
                kb.op("dve", lambda: V.scalar_tensor_tensor(out=fo_ap, in0=ps_t[:, c0:c1], scalar=gcol,
                                                            in1=rsd[:, c0:c1], op0=ALU.mult, op1=ALU.mult),
                      reads=[ps_t, rsd, gt], writes=[fo_t])
            kb.op("dve", lambda: V.scalar_tensor_tensor(out=out_ap, in0=ps_t[:, 0:ntok], scalar=gcol, in1=rsd[:, 0:ntok],
                                                        op0=ALU.mult, op1=ALU.mult),
                  reads=[ps_t, rsd, gt], writes=[out_t])

        def attn_block(j, kh, QB, q_cols, kprev, kcur, vprev, vcur, ncur, og_cols, first, xr=()):
            qn, sg, den, esk, pT, tmpf = AT["qn"], AT["sg"], AT["den"], AT["esk"], AT["pT"], AT["tmpf"]
            po = PSX[0]
            pd = PSX[1]
            groups_o = []
            groups_d = []
            preads = []
            W4 = 4 * QB
            for par in range(2):
                pts = []
                for which in ((0, 1) if not first else (1,)):
                    nk = 128 if which == 0 else ncur
                    lk = kprev(par) if which == 0 else kcur(par)
                    sc = psA()
                    kb.mm(sc, [(sc[0:nk, 0:W4].rearrange("p (c q) -> p c q", c=4), lk,
                                qn[par * 64:(par + 1) * 64, :, q_cols])], reads=[AT["kT"], qn] + list(xr))
                    BT = AT["BTp"] if which == 0 else AT["BTc"]
                    h0 = kh * 8 + par
                    tf = tmpf[AT["tfi"][0] % 2]
                    AT["tfi"][0] += 1
                    kb.op("dve", lambda sc=sc, tf=tf, BT=BT, nk=nk, h0=h0: V.tensor_tensor(
                        out=tf[0:nk, 0:W4].rearrange("p (c q) -> p c q", c=4),
                        in0=sc[0:nk, 0:W4].rearrange("p (c q) -> p c q", c=4),
                        in1=BT[0:nk, h0:h0 + 7:2, 0:QB], op=ALU.add), reads=[sc, BT], writes=[tf])
                    p = pT[AT["pti"][0] % 2]
                    AT["pti"][0] += 1
                    kb.op("act", lambda tf=tf, p=p, nk=nk: S.activation(out=p[0:nk, 0:W4], in_=tf[0:nk, 0:W4], func=AF.Exp),
                          reads=[tf], writes=[p])
                    pts.append((p, nk, which))
                go = []
                gd = []
                preads = []
                for (p, nk, which) in pts:
                    vv = vprev if which == 0 else vcur
                    go.append((po[par * 64:(par + 1) * 64, 0:W4], vv, p[0:nk, 0:W4]))
                    gd.append((pd[par * 64:(par + 1) * 64, 0:W4], ones_b[0:nk, 0:64], p[0:nk, 0:W4]))
                    preads.append(p)
                kb.mm(po, go, reads=preads + [AT["vtk"]] + list(xr))
                kb.mm(pd, gd, reads=preads + [ones_b])
            kb.op("dve", lambda: V.tensor_tensor(out=den[:, 0:W4].rearrange("p (c q) -> p c q", c=4),
                                                 in0=pd[:, 0:W4].rearrange("p (c q) -> p c q", c=4),
                                                 in1=esk[:, j, kh * 4:kh * 4 + 4].unsqueeze(2).to_broadcast([128, 4, QB]),
                                                 op=ALU.add), reads=[pd, esk], writes=[den])
            kb.op("dve", lambda: V.reciprocal(out=den[:, 0:W4], in_=den[:, 0:W4]), reads=[den], writes=[den])
            kb.op("dve", lambda: V.tensor_tensor(out=den[:, 0:W4], in0=po[:, 0:W4], in1=den[:, 0:W4], op=ALU.mult),
                  reads=[po, den], writes=[den])
            kb.op("dve", lambda: V.tensor_tensor(out=og[:, kh * 4:kh * 4 + 4, og_cols],
                                                 in0=den[:, 0:W4].rearrange("p (c q) -> p c q", c=4),
                                                 in1=sg[:, :, q_cols], op=ALU.mult), reads=[den, sg], writes=[og])

        def attn_layer_tile(l, j, ntok, ti, last):
            kT, vtk, knf, vlf, qn, sg, gq, gk, vns, vnsf = (AT[k] for k in
                                                            ("kT", "vtk", "knf", "vlf", "qn", "sg", "gq", "gk", "vns", "vnsf"))
            w_in = I["w_in_a"][j]
            sample = ntok < 128
            for kh in range(8):
                wt = next_w()
                view = wt.h[:, 0:KC * 128].rearrange("p (k n) -> p k n", k=KC)
                src = w_in.rearrange("(k p) n -> p k n", p=128)[:, :, DI + kh * 64:DI + kh * 64 + 64]
                kb.dma("pool", view[:, :, 0:64], src, writes=[wt])
                kb.dma("pool", view[:, :, 64:128], src, writes=[wt])
                pk = psA()
                kb.mm(pk, [(pk[:, 0:ntok], view[:, kc, :], hT[:, kc, 0:ntok]) for kc in range(KC)], reads=[wt, hT])
                if not sample:
                    headnorm(pk, ntok, gk[:, j:j + 1], gk, kT[:, kh, 128:640], kT,
                             f32_out=(knf[:, kh, :], knf, 384, 512) if last else None)
                else:
                    headnorm(pk, ntok, gk[:, j:j + 1], gk, kT[:, kh, 0:ntok], kT,
                             f32_out=(knf[:, kh, 0:ntok], knf, 0, ntok))
            chk("ak")
            wt, view = load_w_in(w_in, DI + 512, 512)
            chk("av0")
            if not sample:
                for s_ in range(4):
                    pv = psA()
                    kb.mm(pv, [(pv[:, :], hT[:, kc, s_ * 128:(s_ + 1) * 128], view[:, kc, :]) for kc in range(KC)], reads=[wt, hT])
                    chk("av1")
                    kb.op("act", lambda s_=s_, pv=pv: S.copy(out=vtk[:, 1 + s_, :], in_=pv[:, :]), reads=[pv], writes=[vtk])
                    chk("av2")
                    if s_ == 1:
                        chk("av3")
                    if s_ == 3:
                        chk("av4")
                    if last and s_ == 3:
                        kb.op("dve", lambda pv=pv: V.tensor_copy(out=vlf[:], in_=pv[:, :]), reads=[pv], writes=[vlf])
                        chk("av5")
            else:
                for sq in range(NSQ):
                    pv = psA()
                    kb.mm(pv, [(pv[0:8, :], hT[:, kc, sq * 8:(sq + 1) * 8], view[:, kc, :]) for kc in range(KC)], reads=[wt, hT])
                    kb.op("act", lambda sq=sq, pv=pv: S.copy(out=vns[:, sq, :], in_=pv[0:8, :]), reads=[pv], writes=[vns])
                pv = psA()
                kb.mm(pv, [(pv[0:NST, :], hT[:, kc, 0:NST], view[:, kc, :]) for kc in range(KC)], reads=[wt, hT])
                kb.op("dve", lambda pv=pv: V.tensor_copy(out=vnsf[:], in_=pv[0:NST, :]), reads=[pv], writes=[vnsf])
            chk("av")
            for kh in range(8):
                wt, view = load_w_in(w_in, kh * 512, 512)
                for c in range(4):
                    pq = psA()
                    kb.mm(pq, [(pq[:, 0:ntok], view[:, kc, c * 128:(c + 1) * 128], hT[:, kc, 0:ntok]) for kc in range(KC)],
                          reads=[wt, hT])
                    headnorm(pq, ntok, gq[:, j:j + 1], gq, qn[:, c, 0:ntok], qn)
                wt, view = load_w_in(w_in, DI + 1024 + kh * 512, 512)
                for c in range(4):
                    pg = psA()
                    kb.mm(pg, [(pg[:, 0:ntok], view[:, kc, c * 128:(c + 1) * 128], hT[:, kc, 0:ntok]) for kc in range(KC)],
                          reads=[wt, hT])
                    kb.op("act", lambda c=c, pg=pg: S.activation(out=sg[:, c, 0:ntok], in_=pg[:, 0:ntok], func=AF.Silu),
                          reads=[pg], writes=[sg])
                chk("aq")
                if not sample:
                    for b in range(4):
                        if b == 1:
                            chk("ab0")
                        if b == 2:
                            chk("ab1")
                        attn_block(j, kh, 128, slice(b * 128, (b + 1) * 128),
                                   lambda par, b=b: kT[par * 64:(par + 1) * 64, kh, b * 128:(b + 1) * 128],
                                   lambda par, b=b: kT[par * 64:(par + 1) * 64, kh, (b + 1) * 128:(b + 2) * 128],
                                   vtk[:, b, kh * 64:(kh + 1) * 64], vtk[:, b + 1, kh * 64:(kh + 1) * 64], 128,
                                   slice(b * 128, (b + 1) * 128), first=(ti == 0 and b == 0))
                else:
                    for sq in range(NSQ):
                        attn_block(j, kh, 8, slice(sq * 8, (sq + 1) * 8),
                                   lambda par, sq=sq: AT["kcT_bufs"][sq][par * 64:(par + 1) * 64, kh, :],
                                   lambda par, sq=sq: kT[par * 64:(par + 1) * 64, kh, sq * 8:(sq + 1) * 8],
                                   AT["vcs_bufs"][sq][:, kh * 64:(kh + 1) * 64], vns[:, sq, kh * 64:(kh + 1) * 64], 8,
                                   slice(sq * 8, (sq + 1) * 8), first=False,
                                   xr=[AT["kcT_bufs"][sq], AT["vcs_bufs"][sq], vns])
            if not sample:
                kb.op("pool", lambda: G.tensor_copy(out=kT[:, :, 0:128], in_=kT[:, :, 512:640]), reads=[kT], writes=[kT])
                kb.op("pool", lambda: G.tensor_copy(out=vtk[:, 0, :], in_=vtk[:, 4, :]), reads=[vtk], writes=[vtk])

        def attn_sample_prep(j):
            kcs, kcs_f, vcs_f = AT["kcs"], AT["kcs_f"], AT["vcs_f"]
            for sq in range(NSQ):
                kb.dma("sp", kcs_f[:], I["cache_k"][j, sq], writes=[kcs_f])
                kb.dma("sp", vcs_f[:], I["cache_v"][j, sq], writes=[vcs_f])
                kcv = kcs_f.h[:].rearrange("p (k d) -> p k d", k=8)
                kb.op("dve", lambda kcv=kcv: V.tensor_copy(out=kcs[:, :, 0:64], in_=kcv), reads=[kcs_f], writes=[kcs])
                kb.op("dve", lambda kcv=kcv: V.tensor_copy(out=kcs[:, :, 64:128], in_=kcv), reads=[kcs_f], writes=[kcs])
                kct = AT["kcT_bufs"][sq]
                vc = AT["vcs_bufs"][sq]
                kb.op("act", lambda vc=vc: S.copy(out=vc[:], in_=vcs_f[:]), reads=[vcs_f], writes=[vc])
                for k2 in range(0, 8, 4):
                    pt = psB()
                    kb.transpose(pt, [(pt[:, i * 128:(i + 1) * 128], kcs[:, k2 + i, :], ident_b[:, :]) for i in range(4)],
                                 reads=[kcs, ident_b])
                    kb.op("act", lambda pt=pt, kct=kct, k2=k2: S.copy(out=kct[:, k2:k2 + 4, :],
                                                                    in_=pt[:, 0:512].rearrange("p (k n) -> p k n", k=4)),
                          reads=[pt], writes=[kct])
                kb.dma("sp", O["kws"][j, sq, 0:120, :], I["cache_k"][j, sq, 8:128, :], writes=[])
                kws_toks.append(kb_last_tok("sp"))
                kb.dma("sp", O["vws"][j, sq, 0:120, :], I["cache_v"][j, sq, 8:128, :], writes=[])
                kws_toks.append(kb_last_tok("sp"))

        def kb_last_tok(q):
            return kb.last_dma_tok

        kws_toks = []
        if nA:
            kws_t = TT(None, "kws")
            vws_t = TT(None, "vws")
            kwp_t = TT(None, "kwp")
            vwp_t = TT(None, "vwp")
            zrep_t = TT(None, "zrep")
            out_regions += [kws_t, vws_t, kwp_t, vwp_t]

        def attn_outputs_prompt(j):
            knf, vlf, ktok = AT["knf"], AT["vlf"], AT["ktok"]
            pt = psA()
            kb.transpose(pt, [(pt[:, kh * 64:(kh + 1) * 64], knf[0:64, kh, :], ident_f[0:64, 0:64]) for kh in range(8)],
                         reads=[knf, ident_f])
            kb.op("dve", lambda: V.tensor_copy(out=ktok[:], in_=pt[:, :]), reads=[pt], writes=[ktok])
            kb.dma("sp", O["kwp"][j], ktok[:], reads=[ktok], writes=[])
            kws_toks.append(kb_last_tok("sp"))
            kb.dma("sp", O["vwp"][j], vlf[:], reads=[vlf], writes=[])
            kws_toks.append(kb_last_tok("sp"))

        def attn_outputs_sample(j):
            knf, vnsf, ktok = AT["knf"], AT["vnsf"], AT["ktok"]
            pt = psA()
            kb.transpose(pt, [(pt[0:NST, kh * 64:(kh + 1) * 64], knf[0:64, kh, 0:NST], ident_f[0:64, 0:64])
                              for kh in range(8)], reads=[knf, ident_f])
            kb.op("dve", lambda: V.tensor_copy(out=ktok[0:NST, :], in_=pt[0:NST, :]), reads=[pt], writes=[ktok])
            for sq in range(NSQ):
                kb.dma("sp", O["kws"][j, sq, 120:128, :], ktok[sq * 8:(sq + 1) * 8, :], reads=[ktok], writes=[])
                kws_toks.append(kb_last_tok("sp"))
                kb.dma("sp", O["vws"][j, sq, 120:128, :], vnsf[sq * 8:(sq + 1) * 8, :], reads=[vnsf], writes=[])
                kws_toks.append(kb_last_tok("sp"))


        BT_ = {}
        KAP = 0.6065306597126334

        def rwkv_alloc(les, j):
            def sb(shape, dt, name):
                return kb.sb(shape, dt, name, es=les)
            B = BT_
            B["ST"] = sb([128, 32, 64], F32, "ST")
            B["STs"] = sb([128, NSQ, 64], F32, "STs")
            B["STs2"] = [B["STs"], TT(B["STs"].h, "STs_b")]
            B["ST2"] = [B["ST"], TT(B["ST"].h, "ST_b")]
            kb.op("dve", lambda: V.memset(B["ST"][:], 0.0), writes=B["ST2"])
            B["hlast"] = sb([128, KC], BF16, "hlast")
            kb.op("dve", lambda: V.memset(B["hlast"][:], 0.0), writes=[B["hlast"]])
            B["dlt"] = AliasTT(wbuf[0], wbuf[0].h[:, 0:KC * 256].rearrange("p (k n) -> p k n", k=KC))
            B["xm"] = [sb([128, KC, 256], BF16, "xm%d" % c) for c in range(5)]
            B["xm"].append(B["xm"][4])
            B["mu"] = sb([128, 6, KC], F32, "mu")
            kb.dma("sp", B["mu"][:], I["mu_b"][j].rearrange("c (k p) -> p c k", p=128), writes=[B["mu"]],
                   allow_slow_non_contiguous=True)
            B["pv"] = sb([128, 7, 32], F32, "pv")
            for wi, nm in enumerate(("w0_b", "a0_b", "k_k_b", "k_a_b", "r_k_b", "ln_x_g_b", "ln_x_b_b")):
                kb.dma("sp", B["pv"][:, wi, :], I[nm][j].rearrange("(c p) -> p c", p=128), writes=[B["pv"]],
                       allow_slow_non_contiguous=True)
            B["wd"] = sb([128, 2, KC, 96], BF16, "wd")
            for c in range(2):
                kb.dma("pool", B["wd"][:, c, :, :], I["w_lora_down_b"][j, c].rearrange("(k p) n -> p k n", p=128), writes=[B["wd"]])
            B["wu"] = [sb([96, 2, 128], BF16, "wu%d" % i) for i in range(2)]
            B["lw"] = sb([96, 2, 256], BF16, "lwlow")
            B["mk"] = sb([128, 3, 64], F32, "mk")
            kb.dma("sp", B["mk"][0:64], I["masks"].rearrange("m a b -> a m b"), writes=[B["mk"]])
            kb.dma("sp", B["mk"][64:128], I["masks"].rearrange("m a b -> a m b"), writes=[B["mk"]])
            for nm in ("r", "k", "kk", "k2", "a", "sgw", "cs", "t1", "t2", "Ep", "of", "gt1"):
                B[nm] = sb([128, 256], F32, "f_" + nm)
            B["E0"] = B["t2"]
            B["Em"] = [sb([128, 256], F32, "f_Em%d" % i) for i in range(2)]
            B["bon"] = [sb([128, 256], F32, "f_bon%d" % i) for i in range(3)]
            for nm in ("sq", "gsq"):
                B[nm] = sb([128, 256], BF16, "b_" + nm)
            for nm in ("rt", "at", "bt", "kt", "bh", "kh", "vb"):
                B[nm] = [sb([128, 256], BF16, "b_%s%d" % (nm, i)) for i in range(2)]
            B["ones64"] = sb([128, 64], F32, "ones64")
            kb.op("dve", lambda: V.memset(B["ones64"][:], 1.0), writes=[B["ones64"]])
            for nm, shp_ in (("P", [128, 4, 64]), ("M5", [128, 4, 5, 64]), ("T3", [128, 4, 3, 64])):
                t_ = sb(shp_, BF16, "h_" + nm)
                B[nm] = [[TT(t_.h, "%s_%d_%d" % (nm, hp_, ci_)) for ci_ in range(4)] for hp_ in range(2)]
            for nm in ("Y", "U", "Sb"):
                t_ = sb([128, 64], BF16, "m_" + nm)
                B[nm] = [t_, TT(t_.h, t_.name + "_b")]
            B["WK"] = []
            for st_ in range(4):
                row = []
                for nm, shp_ in (("LL0", [128, 2, 64]), ("LL1", [128, 2, 64]), ("P0", [128, 64]), ("P1", [128, 64])):
                    t_ = sb(shp_, BF16, "wk%d_%s" % (st_, nm))
                    row.append([t_, TT(t_.h, t_.name + "_b")])
                B["WK"].append(row)
            B["mk5"] = sb([128, 5, 64], F32, "mk5")
            for half_ in range(2):
                for m_, src_ in enumerate((0, 1, 0, 2, 2)):
                    kb.dma("sp", B["mk5"][half_ * 64:(half_ + 1) * 64, m_, :], I["masks"][src_], writes=[B["mk5"]])
            B["wrk"] = wbuf
            B["Sin"] = B["of"]
            B["Sout"] = B["kk"]
            B["shf"] = sb([128, KC, NSQ], F32, "shf")
            B["wri"] = [0]

        def rwkv_mix_inputs(j, col0, T, first_cols):
            B = BT_
            for c in range(6):
                for kc in range(KC):
                    kb.op("dve", lambda c=c, kc=kc: V.scalar_tensor_tensor(out=B["xm"][c][:, kc, 0:T], in0=B["dlt"][:, kc, 0:T],
                                                                            scalar=B["mu"][:, c, kc:kc + 1], in1=hT[:, kc, col0:col0 + T],
                                                                            op0=ALU.mult, op1=ALU.add),
                          reads=[B["dlt"], B["mu"], hT], writes=[B["xm"][c]])
                if c >= 4:
                    cc_ = c - 4
                    pl = psA()
                    kb.mm(pl, [(pl[0:96, 0:T], B["wd"][:, cc_, kc, :], B["xm"][c][:, kc, 0:T]) for kc in range(KC)],
                          reads=[B["wd"], B["xm"][c]])
                    kb.op("act", lambda cc_=cc_, pl=pl: S.activation(out=B["lw"][:, cc_, 0:T], in_=pl[0:96, 0:T],
                                                                    func=(AF.Tanh if cc_ == 0 else AF.Copy)), reads=[pl], writes=[B["lw"]])

        def rwkv_s1(j, pr, T, C, og_cols):
            B = BT_
            bs = pr % 2
            pvp = B["pv"]
            w4 = I["w_rkvg_b"][j]
            ps_in = {}
            for c in range(4):
                wt = next_w()
                wv = wt.h[:, 0:KC * 128].rearrange("p (k n) -> p k n", k=KC)
                kb.dma("pool", wv, w4[c].rearrange("(k p) n -> p k n", p=128)[:, :, pr * 128:(pr + 1) * 128], writes=[wt])
                yield
                pp_ = psA()
                kb.mm(pp_, [(pp_[:, 0:T], wv[:, kc, :], B["xm"][c][:, kc, 0:T]) for kc in range(KC)], reads=[wt, B["xm"][c]])
                if c == 0:
                    kb.op("act", lambda pp_=pp_: S.copy(out=B["r"][:, 0:T], in_=pp_[:, 0:T]), reads=[pp_], writes=[B["r"]])
                    yield
                elif c == 1:
                    kb.op("act", lambda pp_=pp_: S.copy(out=B["k"][:, 0:T], in_=pp_[:, 0:T]), reads=[pp_], writes=[B["k"]])
                    yield
                elif c == 2:
                    kb.op("act", lambda pp_=pp_: S.copy(out=B["vb"][bs][:, 0:T], in_=pp_[:, 0:T]), reads=[pp_], writes=[B["vb"][bs]])
                    yield
                else:
                    kb.op("act", lambda pp_=pp_: S.activation(out=og[:, pr, og_cols], in_=pp_[:, 0:T], func=AF.Silu),
                          reads=[pp_], writes=[og])
                    yield
            chk("bproj")
            r, k, kk, k2, a, sgw, cs, t1, t2, Ep, E0 = (B[n] for n in ("r", "k", "kk", "k2", "a", "sgw", "cs", "t1", "t2", "Ep", "E0"))
            Em, bon = B["Em"][bs], B["bon"][pr % 3]
            wu = B["wu"][pr % 2]
            kb.dma("pool", wu[:], I["w_lora_up_b"][j][:, :, pr * 128:(pr + 1) * 128].rearrange("c r n -> r c n"), writes=[wu])
            yield
            for c, dst, wi in ((0, sgw, 0), (1, a, 1)):
                pl = psA()
                kb.mm(pl, [(pl[:, 0:T], wu[:, c, :], B["lw"][:, c, 0:T])], reads=[wu, B["lw"]])
                kb.op("act", lambda pl=pl, dst=dst, wi=wi: S.activation(out=dst[:, 0:T], in_=pl[:, 0:T], func=AF.Sigmoid,
                                                                       bias=pvp[:, wi, pr:pr + 1]), reads=[pl, pvp], writes=[dst])
                yield
            kb.op("dve", lambda: V.tensor_scalar(out=t1[:, 0:T], in0=k[:, 0:T], scalar1=pvp[:, 2, pr:pr + 1], scalar2=None, op0=ALU.mult),
                  reads=[k, pvp], writes=[t1])
            yield
            kb.op("act", lambda: S.activation(out=B["sq"][:, 0:T], in_=t1[:, 0:T], func=AF.Square), reads=[t1], writes=[B["sq"]])
            yield
            pn = psA()
            kb.mm(pn, [(pn[:, 0:T], bones_b[:, :], B["sq"][:, 0:T])], reads=[bones_b, B["sq"]])
            kb.op("act", lambda: S.activation(out=t2[:, 0:T], in_=pn[:, 0:T], func=AF.Sqrt), reads=[pn], writes=[t2])
            yield
            kb.op("dve", lambda: V.tensor_scalar(out=t2[:, 0:T], in0=t2[:, 0:T], scalar1=1e-12, scalar2=None, op0=ALU.max),
                  reads=[t2], writes=[t2])
            yield
            kb.op("dve", lambda: V.reciprocal(out=t2[:, 0:T], in_=t2[:, 0:T]), reads=[t2], writes=[t2])
            yield
            kb.op("dve", lambda: V.tensor_tensor(out=kk[:, 0:T], in0=t1[:, 0:T], in1=t2[:, 0:T], op=ALU.mult), reads=[t1, t2], writes=[kk])
            yield
            kb.op("dve", lambda: V.tensor_scalar(out=t1[:, 0:T], in0=a[:, 0:T], scalar1=-1.0, scalar2=pvp[:, 3, pr:pr + 1],
                                                 op0=ALU.add, op1=ALU.mult), reads=[a, pvp], writes=[t1])
            yield
            kb.op("dve", lambda: V.scalar_tensor_tensor(out=k2[:, 0:T], in0=t1[:, 0:T], scalar=1.0, in1=k[:, 0:T], op0=ALU.add, op1=ALU.mult),
                  reads=[t1, k], writes=[k2])
            yield
            kb.op("dve", lambda: V.scalar_tensor_tensor(out=B["sq"][:, 0:T], in0=r[:, 0:T], scalar=pvp[:, 4, pr:pr + 1], in1=k2[:, 0:T],
                                                        op0=ALU.mult, op1=ALU.mult), reads=[r, k2, pvp], writes=[B["sq"]])
            yield
            pb = psA()
            kb.mm(pb, [(pb[:, 0:T], bones_b[:, :], B["sq"][:, 0:T])], reads=[bones_b, B["sq"]])
            kb.op("dve", lambda: V.tensor_tensor(out=bon[:, 0:T], in0=pb[:, 0:T], in1=B["vb"][bs][:, 0:T], op=ALU.mult), reads=[pb, B["vb"][bs]], writes=[bon])
            yield
            nch = T // C
            for ci in range(nch):
                kb.op("dve", lambda ci=ci: V.tensor_tensor_scan(out=cs[:, ci * C:(ci + 1) * C], data0=B["ones64"][:, 0:C],
                                                                 data1=sgw[:, ci * C:(ci + 1) * C], initial=0.0, op0=ALU.mult, op1=ALU.add),
                      reads=[sgw, B["ones64"]], writes=[cs])
                yield
            kb.op("act", lambda: S.activation(out=Em[:, 0:T], in_=cs[:, 0:T], func=AF.Exp, scale=-KAP), reads=[cs], writes=[Em])
            yield
            kb.op("act", lambda: S.activation(out=Ep[:, 0:T], in_=cs[:, 0:T], func=AF.Exp, scale=KAP), reads=[cs], writes=[Ep])
            yield
            kb.op("dve", lambda: V.tensor_tensor(out=t1[:, 0:T], in0=cs[:, 0:T], in1=sgw[:, 0:T], op=ALU.subtract), reads=[cs, sgw], writes=[t1])
            yield
            kb.op("act", lambda: S.activation(out=E0[:, 0:T], in_=t1[:, 0:T], func=AF.Exp, scale=-KAP), reads=[t1], writes=[E0])
            yield
            kb.op("dve", lambda: V.tensor_tensor(out=B["rt"][bs][:, 0:T], in0=r[:, 0:T], in1=Em[:, 0:T], op=ALU.mult), reads=[r, Em], writes=[B["rt"][bs]])
            yield
            kb.op("dve", lambda: V.scalar_tensor_tensor(out=B["at"][bs][:, 0:T], in0=kk[:, 0:T], scalar=-1.0, in1=E0[:, 0:T], op0=ALU.mult, op1=ALU.mult),
                  reads=[kk, E0], writes=[B["at"][bs]])
            yield
            kb.op("dve", lambda: V.tensor_tensor(out=t1[:, 0:T], in0=kk[:, 0:T], in1=a[:, 0:T], op=ALU.mult), reads=[kk, a], writes=[t1])
            yield
            kb.op("dve", lambda: V.tensor_tensor(out=B["bt"][bs][:, 0:T], in0=t1[:, 0:T], in1=Ep[:, 0:T], op=ALU.mult), reads=[t1, Ep], writes=[B["bt"][bs]])
            yield
            kb.op("dve", lambda: V.tensor_tensor(out=B["kt"][bs][:, 0:T], in0=k2[:, 0:T], in1=Ep[:, 0:T], op=ALU.mult), reads=[k2, Ep], writes=[B["kt"][bs]])
            yield
            for ci in range(nch):
                cc = slice(ci * C, (ci + 1) * C)
                wc = Em[:, (ci + 1) * C - 1:(ci + 1) * C]
                kb.op("dve", lambda cc=cc, wc=wc: V.tensor_scalar(out=B["bh"][bs][:, cc], in0=B["bt"][bs][:, cc], scalar1=wc, scalar2=None, op0=ALU.mult),
                      reads=[B["bt"][bs], Em], writes=[B["bh"][bs]])
                yield
                kb.op("dve", lambda cc=cc, wc=wc: V.tensor_scalar(out=B["kh"][bs][:, cc], in0=B["kt"][bs][:, cc], scalar1=wc, scalar2=None, op0=ALU.mult),
                      reads=[B["kt"][bs], Em], writes=[B["kh"][bs]])
                yield
        def rwkv_s2(j, pr, T, C, og_cols, ST, st_ap):
            B = BT_
            bs = pr % 2
            pvp = B["pv"]
            nch = T // C
            Em, t1 = B["Em"][bs], B["gt1"]
            chk("bprep")
            nlev = int(math.log2(C))
            mk = B["mk"]

            def run_rr(gens):
                gens = list(gens)
                while gens:
                    for g in list(gens):
                        try:
                            next(g)
                        except StopIteration:
                            gens.remove(g)
                    yield

            def phaseA(ci, hp):
                cc = slice(ci * C, (ci + 1) * C)
                P = slice(hp * 64, (hp + 1) * 64)
                TBs = slice(hp * 64, hp * 64 + C)
                at_, bt_, kt_, rt_ = B["at"][bs][P, cc], B["bt"][bs][P, cc], B["kt"][bs][P, cc], B["rt"][bs][P, cc]
                M5, T3 = B["M5"][hp][ci], B["T3"][hp][ci]
                wk = B["WK"][ci]
                LL = [wk[0][hp], wk[1][hp]]
                Pp = [wk[2][hp], wk[3][hp]]
                pm = psA()
                kb.mm_multi(pm, [[(pm[TBs, 0:C], bt_, at_)], [(pm[TBs, 64:64 + C], at_, bt_)], [(pm[TBs, 128:128 + C], kt_, at_)],
                                 [(pm[TBs, 192:192 + C], bt_, rt_)], [(pm[TBs, 256:256 + C], kt_, rt_)]],
                            reads=[B["bt"][bs], B["at"][bs], B["kt"][bs], B["rt"][bs]])
                kb.op("dve", lambda: V.tensor_tensor(out=M5[TBs, ci, :, 0:C], in0=pm[TBs, 0:320].rearrange("p (m c) -> p m c", m=5)[:, :, 0:C],
                                                     in1=B["mk5"][TBs, :, 0:C], op=ALU.mult), reads=[pm, B["mk5"]], writes=[M5])
                yield
                pt = psA()
                kb.mm_multi(pt, [[(pt[TBs, m_ * 64:(m_ + 1) * 64], B[nm_][bs][P, cc], ident_b[P, P])] for m_, nm_ in enumerate(("vb", "bh", "kh"))],
                            reads=[B["vb"][bs], B["bh"][bs], B["kh"][bs], ident_b])
                kb.op("act", lambda: S.copy(out=T3[TBs, ci, :, :], in_=pt[TBs, 0:192].rearrange("p (m c) -> p m c", m=3)), reads=[pt], writes=[T3])
                yield
                kb.op("dve", lambda: V.tensor_tensor(out=Pp[0][TBs, 0:C], in0=M5[TBs, ci, 0, 0:C], in1=ident_b[TBs, TBs], op=ALU.add),
                      reads=[M5, ident_b], writes=[Pp[0]])
                yield
                Lt, Lap, LTap = M5, M5[TBs, ci, 1, 0:C], M5[TBs, ci, 0, 0:C]
                cur = 0
                for lev in range(1, nlev):
                    nx = 1 - cur
                    last = lev == nlev - 1
                    p2 = psA()
                    grp = [[(p2[TBs, 0:C], LTap, Lap)]]
                    if not last:
                        grp.append([(p2[TBs, 64:64 + C], Lap, LTap)])
                    kb.mm_multi(p2, grp, reads=[Lt])
                    nq = 1 if last else 2
                    kb.op("act", lambda p2=p2, nx=nx, nq=nq: S.copy(out=LL[nx][TBs, 0:nq, 0:C],
                                                                   in_=p2[TBs, 0:128].rearrange("p (m c) -> p m c", m=2)[:, 0:nq, 0:C]),
                          reads=[p2], writes=[LL[nx]])
                    yield
                    Lt, Lap, LTap = LL[nx], LL[nx][TBs, 0, 0:C], LL[nx][TBs, 1, 0:C]
                    p3 = psA()
                    kb.mm(p3, [(p3[TBs, 0:C], ident_b[TBs, TBs], Pp[cur][TBs, 0:C]), (p3[TBs, 0:C], Lap, Pp[cur][TBs, 0:C])],
                          reads=[ident_b, Pp[cur], Lt])
                    if last:
                        kb.op("dve", lambda p3=p3: V.tensor_copy(out=B["P"][hp][ci][TBs, ci, 0:C], in_=p3[TBs, 0:C]), reads=[p3], writes=[B["P"][hp][ci]])
                    else:
                        kb.op("dve", lambda p3=p3, nx=nx: V.tensor_copy(out=Pp[nx][TBs, 0:C], in_=p3[TBs, 0:C]), reads=[p3], writes=[Pp[nx]])
                    yield
                    cur = nx

            def phaseB(ci, hp):
                cc = slice(ci * C, (ci + 1) * C)
                P = slice(hp * 64, (hp + 1) * 64)
                TBs = slice(hp * 64, hp * 64 + C)
                at_, rt_ = B["at"][bs][P, cc], B["rt"][bs][P, cc]
                Sb, Y, U = B["Sb"][hp], B["Y"][hp], B["U"][hp]
                po = PSX[hp]
                kb.op("act", lambda: S.copy(out=Sb[P, :], in_=st_ap(ci, P)), reads=[ST[hp]], writes=[Sb])
                yield
                py = psA()
                kb.mm(py, [(py[TBs, 0:64], at_, Sb[P, :]), (py[TBs, 0:64], B["M5"][hp][ci][TBs, ci, 2, 0:C], B["T3"][hp][ci][TBs, ci, 0, :])],
                      reads=[B["at"][bs], Sb, B["M5"][hp][ci], B["T3"][hp][ci]])
                kb.op("dve", lambda: V.tensor_copy(out=Y[TBs, 0:64], in_=py[TBs, 0:64]), reads=[py], writes=[Y])
                yield
                pu = psA()
                kb.mm(pu, [(pu[TBs, 0:64], B["P"][hp][ci][TBs, ci, 0:C], Y[TBs, 0:64])], reads=[B["P"][hp][ci], Y])
                kb.op("dve", lambda: V.tensor_copy(out=U[TBs, 0:64], in_=pu[TBs, 0:64]), reads=[pu], writes=[U])
                yield
                kb.mm(po, [(po[P, cc], Sb[P, :], rt_), (po[P, cc], U[TBs, 0:64], B["M5"][hp][ci][TBs, ci, 3, 0:C]),
                           (po[P, cc], B["T3"][hp][ci][TBs, ci, 0, :], B["M5"][hp][ci][TBs, ci, 4, 0:C])],
                      reads=[Sb, B["rt"][bs], U, B["M5"][hp][ci], B["T3"][hp][ci], B["M5"][hp][ci]])
                pd_ = psA()
                kb.mm(pd_, [(pd_[P, 0:64], B["T3"][hp][ci][TBs, ci, 1, :], U[TBs, 0:64]), (pd_[P, 0:64], B["T3"][hp][ci][TBs, ci, 2, :], B["T3"][hp][ci][TBs, ci, 0, :])],
                      reads=[B["T3"][hp][ci], U, B["T3"][hp][ci], B["T3"][hp][ci]])
                wc = Em[P, (ci + 1) * C - 1:(ci + 1) * C]
                kb.op("dve", lambda: V.scalar_tensor_tensor(out=st_ap(ci, P), in0=st_ap(ci, P), scalar=wc, in1=pd_[P, 0:64],
                                                            op0=ALU.mult, op1=ALU.add), reads=[ST[hp], Em, pd_], writes=[ST[hp]])
                yield

            def chainB(hp, cis):
                for ci_ in cis:
                    yield from phaseB(ci_, hp)

            if nch == 4:
                yield from run_rr([phaseA(ci, hp) for ci in (0, 1) for hp in range(2)])
                yield from run_rr([chainB(hp, (0, 1)) for hp in range(2)] + [phaseA(ci, hp) for ci in (2, 3) for hp in range(2)])
                yield from run_rr([chainB(hp, (2, 3)) for hp in range(2)])
            else:
                yield from run_rr([phaseA(ci, hp) for ci in range(nch) for hp in range(2)])
                for ci in range(nch):
                    yield from run_rr([phaseB(ci, hp) for hp in range(2)])
            chk("bchunks")
            of = B["of"]
            for hp in range(2):
                P = slice(hp * 64, (hp + 1) * 64)
                kb.op("act", lambda hp=hp, P=P: S.copy(out=B["gsq"][P, 0:T], in_=PSX[hp][P, 0:T]), reads=[PSX[hp]], writes=[B["gsq"]])
                yield
                kb.op("act", lambda hp=hp, P=P: S.copy(out=of[P, 0:T], in_=PSX[hp][P, 0:T]), reads=[PSX[hp]], writes=[of])
                yield

        def rwkv_gn(j, pr, T, og_cols):
            B = BT_
            pvp = B["pv"]
            of, t1, bon = B["of"], B["gt1"], B["bon"][pr % 3]
            pm_ = psA()
            kb.mm(pm_, [(pm_[:, 0:T], bones_b[:, :], B["gsq"][:, 0:T])], reads=[bones_b, B["gsq"]])
            kb.op("dve", lambda: V.scalar_tensor_tensor(out=of[:, 0:T], in0=pm_[:, 0:T], scalar=-1.0 / 64, in1=of[:, 0:T], op0=ALU.mult, op1=ALU.add),
                  reads=[pm_, of], writes=[of])
            yield
            kb.op("act", lambda: S.activation(out=B["gsq"][:, 0:T], in_=of[:, 0:T], func=AF.Square), reads=[of], writes=[B["gsq"]])
            yield
            pv_ = psA()
            kb.mm(pv_, [(pv_[:, 0:T], bones_b[:, :], B["gsq"][:, 0:T])], reads=[bones_b, B["gsq"]])
            kb.op("dve", lambda: V.tensor_scalar(out=t1[:, 0:T], in0=pv_[:, 0:T], scalar1=1.0 / 64, scalar2=64e-5, op0=ALU.mult, op1=ALU.add),
                  reads=[pv_], writes=[t1])
            yield
            kb.op("act", lambda: S.activation(out=t1[:, 0:T], in_=t1[:, 0:T], func=AF.Sqrt), reads=[t1], writes=[t1])
            yield
            kb.op("dve", lambda: V.reciprocal(out=t1[:, 0:T], in_=t1[:, 0:T]), reads=[t1], writes=[t1])
            yield
            kb.op("dve", lambda: V.tensor_tensor(out=of[:, 0:T], in0=of[:, 0:T], in1=t1[:, 0:T], op=ALU.mult), reads=[of, t1], writes=[of])
            yield
            kb.op("dve", lambda: V.tensor_scalar(out=of[:, 0:T], in0=of[:, 0:T], scalar1=pvp[:, 5, pr:pr + 1], scalar2=pvp[:, 6, pr:pr + 1],
                                                 op0=ALU.mult, op1=ALU.add), reads=[of, pvp], writes=[of])
            yield
            kb.op("dve", lambda: V.tensor_tensor(out=of[:, 0:T], in0=of[:, 0:T], in1=bon[:, 0:T], op=ALU.add), reads=[of, bon], writes=[of])
            yield
            kb.op("dve", lambda: V.tensor_tensor(out=og[:, pr, og_cols], in0=of[:, 0:T], in1=og[:, pr, og_cols], op=ALU.mult), reads=[of, og], writes=[og])
            yield


        def _exhaust(g):
            for _ in g:
                pass

        def _interleave(*gens):
            gens = [g for g in gens if g is not None]
            while gens:
                for g in list(gens):
                    try:
                        next(g)
                    except StopIteration:
                        gens.remove(g)

        def rwkv_layer_tile(l, j, ntok, ti):
            B = BT_
            sample = ntok < 128
            dlt = B["dlt"]
            if not sample:
                for half in range(2):
                    c0 = half * 256
                    kb.op("dve", lambda c0=c0: V.tensor_tensor(out=dlt[:, :, 1:256], in0=hT[:, :, c0:c0 + 255], in1=hT[:, :, c0 + 1:c0 + 256],
                                                               op=ALU.subtract), reads=[hT], writes=[dlt])
                    if half == 0:
                        kb.op("dve", lambda: V.tensor_tensor(out=dlt[:, :, 0], in0=B["hlast"][:, :], in1=hT[:, :, 0], op=ALU.subtract),
                              reads=[hT, B["hlast"]], writes=[dlt])
                    else:
                        kb.op("dve", lambda: V.tensor_tensor(out=dlt[:, :, 0], in0=hT[:, :, 255], in1=hT[:, :, 256], op=ALU.subtract),
                              reads=[hT], writes=[dlt])
                    rwkv_mix_inputs(j, c0, 256, None)
                    chk("bmix")
                    prev = None
                    gnp = None
                    cols_ = slice(half * 256, (half + 1) * 256)
                    for pr in range(32):
                        g1 = rwkv_s1(j, pr, 256, 64, cols_)
                        _interleave(prev, g1, gnp)
                        gnp = rwkv_gn(j, pr - 1, 256, cols_) if pr >= 1 else None
                        prev = rwkv_s2(j, pr, 256, 64, cols_, B["ST2"], lambda ci, P, pr=pr: B["ST"][P, pr, :])
                    _interleave(prev, gnp)
                    _exhaust(rwkv_gn(j, 31, 256, cols_))
                kb.op("dve", lambda: V.tensor_copy(out=B["hlast"][:, :], in_=hT[:, :, 511]), reads=[hT], writes=[B["hlast"]])
            else:
                shf = B["t1"]
                kb.dma("sp", dlt[:, :, 256:256 + NSQ], I["state_shift"][j].rearrange("s (k p) -> p k s", p=128), writes=[dlt],
                       allow_slow_non_contiguous=True) if False else None
                stg = B["t2"]
                for sq in range(NSQ):
                    kb.dma("sp", stg[:, sq * KC:(sq + 1) * KC], I["state_shift"][j, sq].rearrange("(k p) -> p k", p=128),
                           writes=[stg], allow_slow_non_contiguous=True)
                stg3 = stg.h[:, 0:KC * NSQ].rearrange("p (s k) -> p k s", k=KC)
                h4 = hT.h[:, :, 0:NST].rearrange("p k (s t) -> p k s t", t=8)
                d4 = dlt.h[:, :, 0:NST].rearrange("p k (s t) -> p k s t", t=8)
                kb.op("dve", lambda: V.tensor_tensor(out=d4[:, :, :, 1:8], in0=h4[:, :, :, 0:7], in1=h4[:, :, :, 1:8], op=ALU.subtract),
                      reads=[hT], writes=[dlt])
                kb.op("dve", lambda: V.tensor_tensor(out=d4[:, :, :, 0], in0=stg3, in1=h4[:, :, :, 0], op=ALU.subtract),
                      reads=[hT, stg], writes=[dlt])
                rwkv_mix_inputs(j, 0, NST, None)
                Sin, STp, Sout = B["Sin"], B["STs"], B["Sout"]
                Sin_v = Sin.h[:, 0:NSQ * 64].rearrange("p (s i) -> p s i", s=NSQ)
                Sout_v = Sout.h[:, 0:NSQ * 64].rearrange("p (s i) -> p s i", s=NSQ)
                for pr in range(32):
                    kb.dma("sp", Sin_v, I["state_wkv"][j, :, 2 * pr:2 * pr + 2].rearrange("s h i j -> (h i) s j"), writes=[Sin])
                    for hp in range(2):
                        pt = psA()
                        kb.mm_multi(pt, [[(pt[hp * 64:(hp + 1) * 64, sq * 64:(sq + 1) * 64], Sin_v[hp * 64:(hp + 1) * 64, sq, :],
                                           ident_f[hp * 64:(hp + 1) * 64, hp * 64:(hp + 1) * 64])] for sq in range(NSQ)],
                                    reads=[Sin, ident_f])
                        kb.op("dve", lambda pt=pt, hp=hp: V.tensor_copy(out=STp[hp * 64:(hp + 1) * 64, 0:NSQ, :],
                                                                        in_=pt[hp * 64:(hp + 1) * 64, 0:NSQ * 64].rearrange("p (s i) -> p s i", s=NSQ)),
                              reads=[pt], writes=[B["STs2"][hp]])
                    _exhaust(rwkv_s1(j, pr, NST, 8, slice(0, NST)))
                    _exhaust(rwkv_s2(j, pr, NST, 8, slice(0, NST), B["STs2"], lambda ci, P: STp[P, ci, :]))
                    _exhaust(rwkv_gn(j, pr, NST, slice(0, NST)))
                    for hp in range(2):
                        pt = psA()
                        kb.mm_multi(pt, [[(pt[hp * 64:(hp + 1) * 64, sq * 64:(sq + 1) * 64], STp[hp * 64:(hp + 1) * 64, sq, :],
                                           ident_f[hp * 64:(hp + 1) * 64, hp * 64:(hp + 1) * 64])] for sq in range(NSQ)],
                                    reads=[B["STs2"][hp], ident_f])
                        kb.op("dve", lambda pt=pt, hp=hp: V.tensor_copy(out=Sout_v[hp * 64:(hp + 1) * 64],
                                                                        in_=pt[hp * 64:(hp + 1) * 64, 0:NSQ * 64].rearrange("p (s i) -> p s i", s=NSQ)),
                              reads=[pt], writes=[Sout])
                    kb.dma("sp", O["wkvs"][j, :, 2 * pr:2 * pr + 2].rearrange("s h i j -> (h i) s j"), Sout_v, reads=[Sout], writes=[])
                    kws_toks.append(kb_last_tok("sp"))
                shf = B["shf"]
                kb.op("dve", lambda: V.tensor_copy(out=shf[:, :, 0:NSQ], in_=h4[:, :, :, 7]), reads=[hT], writes=[shf])
                for sq in range(NSQ):
                    kb.dma("sp", O["shs"][j, sq].rearrange("(k p) -> p k", p=128), shf[:, :, sq], reads=[shf], writes=[],
                           allow_slow_non_contiguous=True)
                    kws_toks.append(kb_last_tok("sp"))

        def rwkv_outputs_prompt(j):
            B = BT_
            ST, Sout, shf = B["ST"], B["Sout"], B["shf"]
            Sout_v = Sout.h[:, 0:256].rearrange("p (s i) -> p s i", s=4)
            for p4 in range(0, 32, 4):
                for hp in range(2):
                    pt = psA()
                    kb.mm_multi(pt, [[(pt[hp * 64:(hp + 1) * 64, q * 64:(q + 1) * 64], ST[hp * 64:(hp + 1) * 64, p4 + q, :],
                                       ident_f[hp * 64:(hp + 1) * 64, hp * 64:(hp + 1) * 64])] for q in range(4)],
                                reads=[B["ST2"][hp], ident_f])
                    kb.op("dve", lambda pt=pt, hp=hp: V.tensor_copy(out=Sout_v[hp * 64:(hp + 1) * 64],
                                                                    in_=pt[hp * 64:(hp + 1) * 64, 0:256].rearrange("p (s i) -> p s i", s=4)),
                          reads=[pt], writes=[Sout])
                kb.dma("sp", O["wkvp"][j, 2 * p4:2 * p4 + 8].rearrange("(q h) i j -> (h i) q j", h=2), Sout_v, reads=[Sout], writes=[])
                kws_toks.append(kb_last_tok("sp"))
            kb.op("dve", lambda: V.tensor_copy(out=shf[:, :, 0], in_=hT[:, :, 511]), reads=[hT], writes=[shf])
            kb.dma("sp", O["shp"][j].rearrange("(k p) -> p k", p=128), shf[:, :, 0], reads=[shf], writes=[],
                   allow_slow_non_contiguous=True)
            kws_toks.append(kb_last_tok("sp"))


        CT_ = {}

        def gla_alloc(les, j):
            def sb(shape, dt, name):
                return kb.sb(shape, dt, name, es=les)
            Cc = CT_
            Cc["S"] = sb([128, 16, 512], F32, "glaS")
            kb.op("dve", lambda: V.memset(Cc["S"][:], 0.0), writes=[Cc["S"]])
            Cc["Ss"] = sb([128, 2, 512], F32, "glaSs")
            Cc["Sb"] = sb([128, 2, 512], BF16, "glaSb")
            Cc["wgk"] = sb([16, 256], F32, "wgk")
            Cc["bgk"] = sb([128, 16], F32, "bgk")
            kb.dma("sp", Cc["bgk"][:], I["b_gk_c"][j].rearrange("(c p) -> p c", p=128), writes=[Cc["bgk"]], allow_slow_non_contiguous=True)
            Cc["og_g"] = sb([128, 4], F32, "og_g")
            kb.dma("sp", Cc["og_g"][:], I["o_norm_g_c"][j].rearrange("(c p) -> p c", p=128), writes=[Cc["og_g"]], allow_slow_non_contiguous=True)
            Cc["gkl"] = sb([16, 512], F32, "gkl")
            Cc["wgl"] = sb([128, KC, 16], BF16, "wgl")
            kb.dma("pool", Cc["wgl"][:], I["w_in_c"][j].rearrange("(k p) n -> p k n", p=128)[:, :, 12288:12304], writes=[Cc["wgl"]])
            Cc["mk"] = sb([64, 64], F32, "mkc")
            kb.dma("sp", Cc["mk"][:], I["masks"][2], writes=[Cc["mk"]])
            for nm in ("q", "k", "b", "e1", "e2"):
                Cc[nm] = sb([128, 2, 512], F32, "c_" + nm)
            for nm in ("qt", "kt", "kh"):
                Cc[nm] = sb([128, 2, 512], BF16, "cb_" + nm)
            Cc["of"] = sb([128, 4, 512], F32, "c_of")
            Cc["osq"] = sb([128, 4, 512], BF16, "c_osq")
            Cc["vt"] = sb([64, 8, 512], BF16, "c_vt")
            Cc["att"] = sb([64, 64], BF16, "c_att")
            Cc["kht"] = sb([64, 256], BF16, "c_kht")
            Cc["rs"] = sb([128, 512], F32, "c_rs")
            Cc["ones64"] = sb([128, 64], F32, "c_ones64")
            kb.op("dve", lambda: V.memset(Cc["ones64"][:], 1.0), writes=[Cc["ones64"]])

        def gla_head(j, h, ntok, C, state_fn, og_cols):
            Cc = CT_
            w_in = I["w_in_c"][j]
            nch = ntok // C
            q, k, b, e1, e2, qt, kt, khh, of, vt = (Cc[n] for n in ("q", "k", "b", "e1", "e2", "qt", "kt", "kh", "of", "vt"))
            for (dst, col0) in ((q, h * 256), (k, 2048 + h * 256)):
                wt, view = load_w_in(w_in, col0, 256)
                for dc in range(2):
                    pq = psA()
                    kb.mm(pq, [(pq[:, 0:ntok], view[:, kc, dc * 128:(dc + 1) * 128], hT[:, kc, 0:ntok]) for kc in range(KC)], reads=[wt, hT])
                    kb.op("act", lambda pq=pq, dst=dst, dc=dc: S.copy(out=dst[:, dc, 0:ntok], in_=pq[:, 0:ntok]), reads=[pq], writes=[dst])
            kb.dma("sp", Cc["wgk"][:], I["w_gk_up_c"][j][:, h * 256:(h + 1) * 256], writes=[Cc["wgk"]])
            for dc in range(2):
                pg = psA()
                cch = 2 * h + dc
                kb.mm(pg, [(pg[:, 0:ntok], Cc["wgk"][:, dc * 128:(dc + 1) * 128], Cc["gkl"][:, 0:ntok])], reads=[Cc["wgk"], Cc["gkl"]])
                kb.op("act", lambda pg=pg, dc=dc, cch=cch: S.activation(out=e1[:, dc, 0:ntok], in_=pg[:, 0:ntok], func=AF.Sigmoid,
                                                                       bias=Cc["bgk"][:, cch:cch + 1]), reads=[pg, Cc["bgk"]], writes=[e1])
            kb.op("act", lambda: S.activation(out=e1[:, :, 0:ntok], in_=e1[:, :, 0:ntok], func=AF.Ln), reads=[e1], writes=[e1])
            for dc in range(2):
                for ci in range(nch):
                    kb.op("dve", lambda dc=dc, ci=ci: V.tensor_tensor_scan(out=b[:, dc, ci * C:(ci + 1) * C], data0=Cc["ones64"][:, 0:C],
                                                                           data1=e1[:, dc, ci * C:(ci + 1) * C], initial=0.0,
                                                                           op0=ALU.mult, op1=ALU.add), reads=[e1, Cc["ones64"]], writes=[b])
            kb.op("act", lambda: S.activation(out=e1[:, :, 0:ntok], in_=b[:, :, 0:ntok], func=AF.Exp, scale=1.0 / 16), reads=[b], writes=[e1])
            kb.op("act", lambda: S.activation(out=e2[:, :, 0:ntok], in_=b[:, :, 0:ntok], func=AF.Exp, scale=-1.0 / 16), reads=[b], writes=[e2])
            kb.op("dve", lambda: V.scalar_tensor_tensor(out=qt[:, :, 0:ntok], in0=q[:, :, 0:ntok], scalar=1.0 / 16, in1=e1[:, :, 0:ntok],
                                                        op0=ALU.mult, op1=ALU.mult), reads=[q, e1], writes=[qt])
            kb.op("dve", lambda: V.tensor_tensor(out=kt[:, :, 0:ntok], in0=k[:, :, 0:ntok], in1=e2[:, :, 0:ntok], op=ALU.mult), reads=[k, e2], writes=[kt])
            for dc in range(2):
                for ci in range(nch):
                    kb.op("dve", lambda dc=dc, ci=ci: V.tensor_scalar(out=khh[:, dc, ci * C:(ci + 1) * C], in0=kt[:, dc, ci * C:(ci + 1) * C],
                                                                      scalar1=e1[:, dc, (ci + 1) * C - 1:(ci + 1) * C], scalar2=None, op0=ALU.mult),
                          reads=[kt, e1], writes=[khh])
            wt, view = load_w_in(w_in, 4096 + h * 512, 512)
            for ci in range(nch):
                pv = psA()
                kb.mm(pv, [(pv[0:C, :], hT[:, kc, ci * C:(ci + 1) * C], view[:, kc, :]) for kc in range(KC)], reads=[wt, hT])
                kb.op("act", lambda ci=ci, pv=pv: S.copy(out=vt[0:C, ci, :], in_=pv[0:C, :]), reads=[pv], writes=[vt])
            wt, view = load_w_in(w_in, 8192 + h * 512, 512)
            for ec in range(4):
                pg = psA()
                kb.mm(pg, [(pg[:, 0:ntok], view[:, kc, ec * 128:(ec + 1) * 128], hT[:, kc, 0:ntok]) for kc in range(KC)], reads=[wt, hT])
                kb.op("act", lambda ec=ec, pg=pg: S.activation(out=og[:, h * 4 + ec, og_cols], in_=pg[:, 0:ntok], func=AF.Silu),
                      reads=[pg], writes=[og])
            for ci in range(nch):
                cc = slice(ci * C, (ci + 1) * C)
                St, Sap = state_fn(ci, "load")
                kb.op("act", lambda Sap=Sap: S.copy(out=Cc["Sb"][:], in_=Sap), reads=[St], writes=[Cc["Sb"]])
                pa = psA()
                kb.mm(pa, [(pa[0:C, 0:C], kt[:, dc, cc], qt[:, dc, cc]) for dc in range(2)], reads=[kt, qt])
                kb.op("dve", lambda pa=pa: V.tensor_tensor(out=Cc["att"][0:C, 0:C], in0=pa[0:C, 0:C], in1=Cc["mk"][0:C, 0:C], op=ALU.mult),
                      reads=[pa, Cc["mk"]], writes=[Cc["att"]])
                po = psA()
                groups = []
                for ec in range(4):
                    o_ap = po[:, ec * C:(ec + 1) * C]
                    groups.append([(o_ap, Cc["Sb"][:, dc, ec * 128:(ec + 1) * 128], qt[:, dc, cc]) for dc in range(2)]
                                  + [(o_ap, vt[0:C, ci, ec * 128:(ec + 1) * 128], Cc["att"][0:C, 0:C])])
                kb.mm_multi(po, groups, reads=[Cc["Sb"], qt, vt, Cc["att"]])
                kb.op("act", lambda po=po, cc=cc: S.copy(out=of[:, :, cc], in_=po[:, 0:4 * C].rearrange("p (e t) -> p e t", e=4)),
                      reads=[po], writes=[of])
                pt = psB()
                kb.transpose(pt, [(pt[0:C, dc * 128:(dc + 1) * 128], khh[:, dc, cc], ident_b[:, :]) for dc in range(2)], reads=[khh, ident_b])
                kb.op("act", lambda pt=pt: S.copy(out=Cc["kht"][0:C, :], in_=pt[0:C, 0:256]), reads=[pt], writes=[Cc["kht"]])
                for dc in range(2):
                    ps_ = psA()
                    kb.mm(ps_, [(ps_[:, :], Cc["kht"][0:C, dc * 128:(dc + 1) * 128], vt[0:C, ci, :])], reads=[Cc["kht"], vt])
                    St, Sap = state_fn(ci, "store")
                    kb.op("dve", lambda ps_=ps_, Sap=Sap, dc=dc, ci=ci: V.scalar_tensor_tensor(
                        out=Sap[:, dc, :], in0=Sap[:, dc, :], scalar=e1[:, dc, (ci + 1) * C - 1:(ci + 1) * C], in1=ps_[:, :],
                        op0=ALU.mult, op1=ALU.add), reads=[St, e1, ps_], writes=[St])
                state_fn(ci, "done")
            osq, rs = Cc["osq"], Cc["rs"]
            kb.op("act", lambda: S.activation(out=osq[:, :, 0:ntok], in_=of[:, :, 0:ntok], func=AF.Square), reads=[of], writes=[osq])
            pn = psA()
            kb.mm(pn, [(pn[:, 0:ntok], ones_b[:, :], osq[:, ec, 0:ntok]) for ec in range(4)], reads=[ones_b, osq])
            kb.op("act", lambda: S.activation(out=rs[:, 0:ntok], in_=pn[:, 0:ntok], func=AF.Sqrt, scale=1.0 / 512, bias=NORM_EPS),
                  reads=[pn], writes=[rs])
            kb.op("dve", lambda: V.reciprocal(out=rs[:, 0:ntok], in_=rs[:, 0:ntok]), reads=[rs], writes=[rs])
            for ec in range(4):
                kb.op("dve", lambda ec=ec: V.scalar_tensor_tensor(out=of[:, ec, 0:ntok], in0=of[:, ec, 0:ntok], scalar=Cc["og_g"][:, ec:ec + 1],
                                                                   in1=rs[:, 0:ntok], op0=ALU.mult, op1=ALU.mult), reads=[of, Cc["og_g"], rs], writes=[of])
                kb.op("dve", lambda ec=ec: V.tensor_tensor(out=og[:, h * 4 + ec, og_cols], in0=of[:, ec, 0:ntok], in1=og[:, h * 4 + ec, og_cols],
                                                            op=ALU.mult), reads=[of, og], writes=[og])

        def gla_layer_tile(l, j, ntok, ti):
            Cc = CT_
            sample = ntok < 128
            pl = psA()
            kb.mm(pl, [(pl[0:16, 0:ntok], Cc["wgl"][:, kc, :], hT[:, kc, 0:ntok]) for kc in range(KC)], reads=[Cc["wgl"], hT])
            kb.op("act", lambda: S.copy(out=Cc["gkl"][:, 0:ntok], in_=pl[0:16, 0:ntok]), reads=[pl], writes=[Cc["gkl"]])
            for h in range(8):
                if not sample:
                    gla_head(j, h, ntok, 64, lambda ci, what, h=h: (Cc["S"], Cc["S"][:, 2 * h:2 * h + 2, :]), slice(0, ntok))
                else:
                    def sfn(ci, what, h=h):
                        if what == "load":
                            kb.dma("sp", Cc["Ss"][:], I["state_gla"][j, ci, h].rearrange("(c p) e -> p c e", p=128), writes=[Cc["Ss"]])
                        elif what == "done":
                            kb.dma("sp", O["glas"][j, ci, h].rearrange("(c p) e -> p c e", p=128), Cc["Ss"][:], reads=[Cc["Ss"]], writes=[])
                            kws_toks.append(kb_last_tok("sp"))
                        return (Cc["Ss"], Cc["Ss"][:, :, :])
                    gla_head(j, h, ntok, 8, sfn, slice(0, ntok))

        def gla_outputs_prompt(j):
            Cc = CT_
            for h in range(8):
                kb.dma("sp", O["glap"][j, h].rearrange("(c p) e -> p c e", p=128), Cc["S"][:, 2 * h:2 * h + 2, :], reads=[Cc["S"]], writes=[])
                kws_toks.append(kb_last_tok("sp"))

        try:
            cnt_kind = {0: 0, 1: 0, 2: 0}
            for l, kind in enumerate(layers):
                j = cnt_kind[kind]
                cnt_kind[kind] += 1
                with contextlib.ExitStack() as les:
                    layer_psum(kind, les)
                    if kind == 0:
                        attn_alloc(les)
                    elif kind == 1:
                        rwkv_alloc(les, j)
                    else:
                        gla_alloc(les, j)
                    for ti in range(NT + 1):
                        sample = ti == NT
                        ntok = NST if sample else 512
                        if sample:
                            src = I["xs"] if l == 0 else O["ys"]
                            src_t = [] if l == 0 else [ys_t]
                            dst, dst_t = O["ys"], ys_t
                        else:
                            src = (I["xp"] if l == 0 else O["yp"])[ti * 512:(ti + 1) * 512, :]
                            src_t = [] if l == 0 else [yp_t[ti]]
                            dst, dst_t = O["yp"][ti * 512:(ti + 1) * 512, :], yp_t[ti]
                        chk("alloc")
                        load_norm(l, src, ntok, src_t)
                        chk("norm")
                        if kind == 0:
                            if sample:
                                attn_sample_prep(j)
                            attn_layer_tile(l, j, ntok, ti, last=(ti == NT - 1))
                            chk("atile")
                            if ti == NT - 1:
                                attn_outputs_prompt(j)
                            if sample:
                                attn_outputs_sample(j)
                            out_proj(I["w_out_a"][j], ntok, dst, dst_t)
                        elif kind == 1:
                            rwkv_layer_tile(l, j, ntok, ti)
                            if ti == NT - 1:
                                rwkv_outputs_prompt(j)
                            out_proj(I["w_out_b"][j], ntok, dst, dst_t)
                        else:
                            gla_layer_tile(l, j, ntok, ti)
                            if ti == NT - 1:
                                gla_outputs_prompt(j)
                            out_proj(I["w_out_c"][j], ntok, dst, dst_t)
                    kb.barrier()
        except _Stop:
            pass
        kb.finish([t.w for t in out_regions] + kws_toks)
    return nc


_PROG = {}


def _get_prog():
    if "nc" not in _PROG:
        _PROG["nc"] = build_program(dict(TP=4096, layers=[0, 1, 2, 0]))
    return _PROG["nc"]


def kernel(**inputs):
    f32 = np.float32
    A = {k: np.ascontiguousarray(np.asarray(v, dtype=f32)) for k, v in inputs.items()}
    n = 8
    consts = host_consts()
    shared = dict(consts)
    for nm in ("norm_g", "rel_bias", "w_in_a", "q_norm_g", "k_norm_g", "sinks", "w_out_a", "mu_b", "w_rkvg_b", "w_lora_down_b",
               "w_lora_up_b", "w0_b", "a0_b", "k_k_b", "k_a_b", "ln_x_g_b", "ln_x_b_b", "w_out_b", "w_in_c", "w_gk_up_c",
               "b_gk_c", "o_norm_g_c", "w_out_c"):
        shared[nm] = A[nm]
    shared["r_k_b"] = A["r_k_b"].reshape(A["r_k_b"].shape[0], -1)
    zeros_p = np.zeros((4096, D), f32)
    in_maps = []
    for c in range(n):
        m = dict(shared)
        m["xp"] = A["x_prompt"][c] if c < 2 else zeros_p
        sl = slice(4 * c, 4 * c + 4)
        m["xs"] = A["x_sample"][sl].reshape(32, D)
        m["cache_k"] = np.ascontiguousarray(A["cache_k_win"][:, sl].reshape(2, 4, 128, 512))
        m["cache_v"] = np.ascontiguousarray(A["cache_v_win"][:, sl].reshape(2, 4, 128, 512))
        m["state_wkv"] = np.ascontiguousarray(A["state_wkv"][:, sl])
        m["state_shift"] = np.ascontiguousarray(A["state_shift"][:, sl])
        m["state_gla"] = np.ascontiguousarray(A["state_gla"][:, sl])
        in_maps.append(m)
    nc = _get_prog()
    res = run_bass_kernel_spmd(nc, in_maps, core_ids=list(range(n)))
    R = res.results

    def cat_s(key, shape_tail, lead=True):
        return np.concatenate([np.asarray(R[c][key], f32) for c in range(n)], axis=1)

    y_prompt = np.stack([np.asarray(R[c]["yp"], f32) for c in range(2)], axis=0)
    y_sample = np.concatenate([np.asarray(R[c]["ys"], f32).reshape(4, 8, D) for c in range(n)], axis=0)
    kwp = np.stack([np.asarray(R[c]["kwp"], f32) for c in range(2)], axis=1).reshape(2, 2, 128, 8, 64)
    vwp = np.stack([np.asarray(R[c]["vwp"], f32) for c in range(2)], axis=1).reshape(2, 2, 128, 8, 64)
    kws = cat_s("kws", None).reshape(2, 32, 128, 8, 64)
    vws = cat_s("vws", None).reshape(2, 32, 128, 8, 64)
    wkvp = np.stack([np.asarray(R[c]["wkvp"], f32) for c in range(2)], axis=1)
    shp = np.stack([np.asarray(R[c]["shp"], f32) for c in range(2)], axis=1)
    wkvs = cat_s("wkvs", None)
    shs = cat_s("shs", None)
    glap = np.stack([np.asarray(R[c]["glap"], f32) for c in range(2)], axis=1)
    glas = cat_s("glas", None)
    return (y_prompt, y_sample, kwp, vwp, kws, vws, wkvp, shp, wkvs, shs, glap, glas)
```
